# Optimizing a Trainium2 kernel written in Bass

```python
import jax, jax.numpy as jnp
from jax import lax
import numpy as np

D_MODEL = 1024
BATCH = 2
SEQ = 16384
DEPTH = 4

PLE_DIM = 256
GM_HEADS = 8
GM_HEAD_DIM = 64
GM_WIDTH = GM_HEADS * GM_HEAD_DIM
GM_CHUNK = 128
ML_HEADS = 4
ML_HEAD_DIM = 128
ML_WIDTH = ML_HEADS * ML_HEAD_DIM
ML_CHUNK = 128
QK_CONV = 4
FFN_CONV = 3
D_FF = 2816
D_MIX = GM_WIDTH + ML_WIDTH
IN_COLS = 2 * GM_WIDTH + 4 * ML_WIDTH + 2 * ML_HEADS
EPS = 1e-6

kernel_name = "hymba_style_gmlp_mlstm_hybrid"


def rms_norm(x, g):
    xf = x.astype(jnp.float32)
    y = xf * lax.rsqrt(jnp.mean(xf * xf, axis=-1, keepdims=True) + EPS)
    return (y * g).astype(x.dtype)


def layer_norm(x, g, b):
    xf = x.astype(jnp.float32)
    mu = jnp.mean(xf, axis=-1, keepdims=True)
    xc = xf - mu
    y = xc * lax.rsqrt(jnp.mean(xc * xc, axis=-1, keepdims=True) + EPS)
    return (y * g + b).astype(x.dtype)


def causal_dwconv(x, w, b):
    K = w.shape[0]
    S = x.shape[1]
    xp = jnp.pad(x, ((0, 0), (K - 1, 0), (0, 0)))
    out = xp[:, 0:S] * w[0]
    for j in range(1, K):
        out = out + xp[:, j:j + S] * w[j]
    return out + b


def gmlp_mixer(u, v, ln_g, ln_b, ws, bs):
    Bn, S, _ = u.shape
    n = S // GM_CHUNK
    u = jax.nn.gelu(u, approximate=False)
    v = layer_norm(jax.nn.gelu(v, approximate=False), ln_g, ln_b)
    vc = v.reshape(Bn, n, GM_CHUNK, GM_HEADS, GM_HEAD_DIM)
    causal = jnp.tril(jnp.ones((GM_CHUNK, GM_CHUNK), dtype=bool))
    ws = jnp.where(causal[None], ws, jnp.zeros_like(ws))
    mixed = jnp.einsum('hts,bcshd->bcthd', ws, vc) + bs.T[:, :, None]
    return u * mixed.reshape(Bn, S, GM_WIDTH)


def _mlstm_state_step(carry, xs):
    C, nv, m = carry
    k_c, v_c, a_c, bL_c = xs
    m_new = jnp.maximum(bL_c + m, jnp.max(a_c, axis=-1))
    decay = jnp.exp(bL_c + m - m_new)
    w = jnp.exp(a_c - m_new[..., None])
    C_new = decay[..., None, None] * C + jnp.einsum('bhl,bhld,bhle->bhde', w, k_c, v_c)
    n_new = decay[..., None] * nv + jnp.einsum('bhl,bhld->bhd', w, k_c)
    return (C_new, n_new, m_new), (C, nv, m)


def mlstm_mixer(q, k, v, ig, fg, o, norm_g):
    Bn, S, _ = q.shape
    n = S // ML_CHUNK
    L, H, dh = ML_CHUNK, ML_HEADS, ML_HEAD_DIM
    f32 = jnp.float32

    def heads(t):
        return t.astype(f32).reshape(Bn, n, L, H, dh).transpose(0, 3, 1, 2, 4)

    def gates(t):
        return t.astype(f32).reshape(Bn, n, L, H).transpose(0, 3, 1, 2)

    qh = heads(q)
    kh = heads(k) * (dh ** -0.5)
    vh = heads(v)
    ig = gates(ig)
    lf = jax.nn.log_sigmoid(gates(fg))
    b = jnp.cumsum(lf, axis=-1)
    bL = b[..., -1]
    a = bL[..., None] - b + ig

    mv = lambda t: jnp.moveaxis(t, 2, 0)
    init = (jnp.zeros((Bn, H, dh, dh), f32), jnp.zeros((Bn, H, dh), f32), jnp.zeros((Bn, H), f32))
    _, (C0, n0, m0) = lax.scan(_mlstm_state_step, init, (mv(kh), mv(vh), mv(a), mv(bL)))
    C0 = jnp.moveaxis(C0, 0, 2)
    n0 = jnp.moveaxis(n0, 0, 2)
    m0 = jnp.moveaxis(m0, 0, 2)

    causal = jnp.tril(jnp.ones((L, L), dtype=bool))
    logD = jnp.where(causal, b[..., :, None] - b[..., None, :] + ig[..., None, :], -jnp.inf)
    inter = b + m0[..., None]
    m_t = jnp.maximum(jnp.max(logD, axis=-1), inter)
    wD = jnp.exp(logD - m_t[..., None])
    inter_w = jnp.exp(inter - m_t)
    s = jnp.einsum('bhntd,bhnsd->bhnts', qh, kh) * wD
    num = inter_w[..., None] * jnp.einsum('bhntd,bhnde->bhnte', qh, C0) + jnp.einsum('bhnts,bhnse->bhnte', s, vh)
    den = inter_w * jnp.einsum('bhntd,bhnd->bhnt', qh, n0) + jnp.sum(s, axis=-1)
    hr = num / jnp.maximum(jnp.abs(den), jnp.exp(-m_t))[..., None]
    hr = hr.transpose(0, 2, 3, 1, 4).reshape(Bn, S, H, dh)
    hc = jax.nn.sigmoid(o.astype(f32)).reshape(Bn, S, H, dh) * hr
    hn = hc * lax.rsqrt(jnp.mean(hc * hc, axis=-1, keepdims=True) + EPS) * norm_g.reshape(H, dh)
    return hn.reshape(Bn, S, ML_WIDTH).astype(v.dtype)


def setup_inputs(seed: int = 0) -> dict:
    key = jax.random.key(seed)
    ks = jax.random.split(key, 24)
    nrm = lambda k, shape, sc: jax.random.normal(k, shape, jnp.float32) * sc
    f_bias = jnp.broadcast_to(jnp.linspace(3.0, 6.0, ML_HEADS, dtype=jnp.float32), (DEPTH, ML_HEADS))
    return {
        "x": nrm(ks[0], (BATCH, SEQ, D_MODEL), 1.0),
        "p": nrm(ks[1], (DEPTH, BATCH, SEQ, PLE_DIM), 1.0),
        "g_mix": 1.0 + nrm(ks[2], (DEPTH, D_MODEL), 0.02),
        "w_in": nrm(ks[3], (DEPTH, D_MODEL, IN_COLS), D_MODEL ** -0.5),
        "gm_ln_g": 1.0 + nrm(ks[4], (DEPTH, GM_WIDTH), 0.02),
        "gm_ln_b": nrm(ks[5], (DEPTH, GM_WIDTH), 0.02),
        "gm_ws": nrm(ks[6], (DEPTH, GM_HEADS, GM_CHUNK, GM_CHUNK), GM_CHUNK ** -0.5),
        "gm_bs": 1.0 + nrm(ks[7], (DEPTH, GM_HEADS, GM_CHUNK), 0.02),
        "ml_conv_w": nrm(ks[8], (DEPTH, QK_CONV, 2 * ML_WIDTH), QK_CONV ** -0.5),
        "ml_conv_b": nrm(ks[9], (DEPTH, 2 * ML_WIDTH), 0.02),
        "ml_b_i": nrm(ks[10], (DEPTH, ML_HEADS), 0.1),
        "ml_b_f": f_bias + nrm(ks[11], (DEPTH, ML_HEADS), 0.1),
        "ml_norm_g": 1.0 + nrm(ks[12], (DEPTH, ML_WIDTH), 0.02),
        "w_out": nrm(ks[13], (DEPTH, D_MIX, D_MODEL), D_MIX ** -0.5),
        "g_ffn": 1.0 + nrm(ks[14], (DEPTH, D_MODEL), 0.02),
        "w_up": nrm(ks[15], (DEPTH, D_MODEL, 2 * D_FF), D_MODEL ** -0.5),
        "ffn_conv_w": nrm(ks[16], (DEPTH, FFN_CONV, 2 * D_FF), FFN_CONV ** -0.5),
        "ffn_conv_b": nrm(ks[17], (DEPTH, 2 * D_FF), 0.02),
        "w_down": nrm(ks[18], (DEPTH, D_FF, D_MODEL), D_FF ** -0.5),
        "g_ple": 1.0 + nrm(ks[19], (DEPTH, D_MODEL), 0.02),
        "w_ple_gate": nrm(ks[20], (DEPTH, D_MODEL, D_MODEL), D_MODEL ** -0.5),
        "w_ple": nrm(ks[21], (DEPTH, PLE_DIM, D_MODEL), PLE_DIM ** -0.5),
        "g_final": 1.0 + nrm(ks[22], (D_MODEL,), 0.02),
    }


def reference(x, p, g_mix, w_in, gm_ln_g, gm_ln_b, gm_ws, gm_bs, ml_conv_w, ml_conv_b, ml_b_i, ml_b_f, ml_norm_g, w_out, g_ffn, w_up, ffn_conv_w, ffn_conv_b, w_down, g_ple, w_ple_gate, w_ple, g_final):
    cuts = [GM_WIDTH, 2 * GM_WIDTH, 2 * GM_WIDTH + 2 * ML_WIDTH, 2 * GM_WIDTH + 3 * ML_WIDTH,
            2 * GM_WIDTH + 4 * ML_WIDTH, 2 * GM_WIDTH + 4 * ML_WIDTH + ML_HEADS]
    for i in range(DEPTH):
        h = rms_norm(x, g_mix[i])
        proj = h @ w_in[i]
        u, v, qk, vm, o, ig, fg = jnp.split(proj, cuts, axis=-1)
        gm = gmlp_mixer(u, v, gm_ln_g[i], gm_ln_b[i], gm_ws[i], gm_bs[i])
        qk = jax.nn.silu(causal_dwconv(qk, ml_conv_w[i], ml_conv_b[i]))
        q, k = jnp.split(qk, 2, axis=-1)
        ml = mlstm_mixer(q, k, vm, ig + ml_b_i[i], fg + ml_b_f[i], o, ml_norm_g[i])
        x = x + jnp.concatenate([gm, ml], axis=-1) @ w_out[i]
        h = rms_norm(x, g_ffn[i])
        up = causal_dwconv(h @ w_up[i], ffn_conv_w[i], ffn_conv_b[i])
        a, bv = jnp.split(up, 2, axis=-1)
        x = x + (jax.nn.silu(a) * bv) @ w_down[i]
        gate = jax.nn.sigmoid(rms_norm(x, g_ple[i]) @ w_ple_gate[i])
        x = x + gate * (p[i] @ w_ple[i])
    return rms_norm(x, g_final)
```

```python
import math
import numpy as np
from contextlib import ExitStack
import concourse.bass as bass
import concourse.mybir as mybir
from concourse.bass_utils import run_bass_kernel_spmd

F32 = mybir.dt.float32
BF16 = mybir.dt.bfloat16
AF = mybir.ActivationFunctionType
ALU = mybir.AluOpType

D = 1024
NT = 512
NCH = NT // 128
DFF = 2816
EPS = 1e-6
NSLOT = 4
WBLK = 4096
SAME_ENGINE_SYNC = True
SAME_ENGINE_ALL = True
SAME_ENGINE_SYNC_Q = {"act", "dve", "pool"}
ANNOTATE = False


class Op:
    __slots__ = ("q", "fn", "reads", "writes", "sem", "inc", "needs_inc", "count", "waits", "tag")


class Sched:
    def __init__(self):
        self.ops = []
        self.tag = None

    def add(self, q, fn, reads=(), writes=(), dma_sem=None):
        o = Op()
        o.q, o.fn, o.reads, o.writes = q, fn, tuple(reads), tuple(writes)
        o.sem = dma_sem
        o.inc = 16 if dma_sem is not None else 1
        o.needs_inc = dma_sem is not None
        o.count = None
        o.waits = {}
        o.tag = self.tag
        self.ops.append(o)
        return o

    def analyze(self, qsems):
        last_w, readers, need = {}, {}, []
        for o in self.ops:
            deps = {}
            for r in o.reads:
                w = last_w.get(r)
                if w is not None:
                    deps[id(w)] = (w, True)
            for k in o.writes:
                w = last_w.get(k)
                if w is not None and id(w) not in deps:
                    deps[id(w)] = (w, False)
                for rd in readers.get(k, ()):
                    if id(rd) not in deps:
                        deps[id(rd)] = (rd, False)
            deps.pop(id(o), None)
            nd = []
            for a, raw in deps.values():
                if a.sem is None and a.q == o.q:
                    if o.q == "pe" or not SAME_ENGINE_SYNC or o.q not in SAME_ENGINE_SYNC_Q:
                        continue
                    if not raw and not SAME_ENGINE_ALL:
                        continue
                nd.append(a)
                a.needs_inc = True
            need.append(nd)
            for r in o.reads:
                readers.setdefault(r, []).append(o)
            for k in o.writes:
                last_w[k] = o
                readers[k] = []
        cnt = {}
        for o in self.ops:
            if o.needs_inc:
                s = o.sem if o.sem is not None else qsems[o.q]
                cnt[s] = cnt.get(s, 0) + o.inc
                o.count = (s, cnt[s])
        waited = {}
        for o, nd in zip(self.ops, need):
            w = {}
            for a in nd:
                s, c = a.count
                if c > w.get(s, 0):
                    w[s] = c
            qw = waited.setdefault(o.q, {})
            for s, c in list(w.items()):
                if qw.get(s, 0) >= c:
                    del w[s]
                else:
                    qw[s] = c
            o.waits = w
        self.final_counts = cnt

    def emit(self, block):
        byq = {}
        for o in self.ops:
            byq.setdefault(o.q, []).append(o)

        def run(eng, ops):
            for o in ops:
                for s, c in o.waits.items():
                    eng.wait_ge(s, c)
                ins = o.fn(eng)
                if ANNOTATE and o.tag is not None:
                    ins.annotate(o.tag)
                if o.needs_inc:
                    ins.then_inc(o.count[0], o.inc)

        names = {"pe": "tensor", "act": "scalar", "dve": "vector", "pool": "gpsimd", "sp": "sync"}
        for q in ["sp", "pe", "act", "dve", "pool"]:
            if q in byq:
                getattr(block, names[q])(lambda eng, ops=byq[q]: run(eng, ops))


class Ring:
    def __init__(self, items):
        self.items = items
        self.i = 0

    def next(self):
        it = self.items[self.i % len(self.items)]
        self.i += 1
        return it


NL = 4


def weight_blocks():
    b = []
    for i in range(2):
        b.append([("w_in", 8, 1024 + 512 * i, 512, 0, 512, 0)])
    for c0 in (0, 512, 2048, 2560):
        b.append([("w_in", 8, c0, 512, 0, 512, 0)])
    b.append(None)
    for i in range(2):
        b.append([("w_out", 8, 512 * i, 512, 0, 512, 0)])
    for i in range(11):
        b.append([("w_up", 8, 256 * i, 256, 0, 512, 0), ("w_up", 8, DFF + 256 * i, 256, 0, 512, 256)])
    for f in range(8):
        b.append([("w_down", 22, 128 * f, 128, 0, 128, 0)])
    for j in range(4):
        b.append([("w_gate", 8, 256 * j, 256, 0, 256, 0), ("w_ple", 2, 256 * j, 256, 2048, 256, 0)])
    return b


WB = weight_blocks()
NWB = len(WB)
B_QK, B_U, B_V, B_VM, B_O, B_WS, B_OUT, B_UP, B_DOWN, B_GP = 0, 2, 3, 4, 5, 6, 7, 9, 20, 28
WB_SIZE = []
for _b in WB:
    if _b is None:
        WB_SIZE.append(1024)
    else:
        WB_SIZE.append(max(base + K * W for (_, K, _, _, base, W, _) in _b))


def MM(out, lhsT, rhs, start, stop):
    return lambda e: e.matmul(out, lhsT=lhsT, rhs=rhs, start=start, stop=stop)


def TR(out, in_, ident):
    return lambda e: e.transpose(out, in_, ident)


def ACT(out, in_, func, **kw):
    return lambda e: e.activation(out=out, in_=in_, func=func, **kw)


def TT(out, in0, in1, op):
    return lambda e: e.tensor_tensor(out=out, in0=in0, in1=in1, op=op)


def TS(out, in0, s1, s2, op0, op1=None):
    if op1 is None:
        return lambda e: e.tensor_scalar(out=out, in0=in0, scalar1=s1, scalar2=None, op0=op0)
    return lambda e: e.tensor_scalar(out=out, in0=in0, scalar1=s1, scalar2=s2, op0=op0, op1=op1)


def TSS(out, in_, scalar, op):
    return lambda e: e.tensor_single_scalar(out=out, in_=in_, scalar=scalar, op=op)


def STT(out, in0, scalar, in1, op0, op1):
    return lambda e: e.scalar_tensor_tensor(out=out, in0=in0, scalar=scalar, in1=in1, op0=op0, op1=op1)


def CP(out, in_):
    return lambda e: e.tensor_copy(out=out, in_=in_)


def MS(out, val):
    return lambda e: e.memset(out, val)


def DMA(out, in_):
    return lambda e: e.dma_start(out=out, in_=in_)


def pipeline(n, stages):
    ns = len(stages)
    for i in range(n + ns - 1):
        for si in reversed(range(ns)):
            c = i - si
            if 0 <= c < n:
                stages[si](c)


def build_program(S, n_layers=NL):
    n_tiles = S // NT
    nc = bass.Bass("TRN2", target_bir_lowering=False)

    def din(name, shape):
        return nc.dram_tensor(name, list(shape), F32, kind="ExternalInput").ap()

    xin = din("xin", [D, S])
    pin = din("pin", [NL, 256, S])
    wsrc = {
        "w_in": din("w_in", [NL, D, 3080]), "w_out": din("w_out", [NL, D, D]), "w_up": din("w_up", [NL, D, 2 * DFF]),
        "w_down": din("w_down", [NL, DFF, D]), "w_gate": din("w_gate", [NL, D, D]), "w_ple": din("w_ple", [NL, 256, D]),
    }
    gv_d = din("gv", [128, NL * 24 + 8])
    mlng_d = din("mlng", [128, NL * 4])
    lngb_d = din("lngb", [NL, 128, 2, 512])
    wsT_d = din("wsT", [NL, 128, 8, 128])
    bs8_d = din("bs8", [8, NL, 128])
    cwqk_d = din("cwqk", [128, NL, 8, 4])
    cbqk_d = din("cbqk", [128, NL, 8])
    cwff_d = din("cwff", [128, NL, 44, 3])
    cbff_d = din("cbff", [128, NL, 44])
    bif_d = din("bif", [128, NL, 8])
    ident_d = din("ident", [128, 128])
    tri_d = din("tri", [128, 128])
    sel_d = din("sel", [128, 512])
    yout = nc.dram_tensor("yout", [D, S], F32, kind="ExternalOutput").ap()
    wb_d = nc.dram_tensor("wb_scratch", [NL, NWB, 128, WBLK], BF16).ap()

    xin_v = xin.rearrange("(f p) s -> p f s", p=128)
    yout_v = yout.rearrange("(f p) s -> p f s", p=128)

    with ExitStack() as es:
        def sb(name, shape, dt=F32):
            return es.enter_context(nc.sbuf_tensor("sb_" + name, list(shape), dt))

        def psum(name, shape, dt=F32):
            return es.enter_context(nc.psum_tensor(name, list(shape), dt))

        def sem(name):
            return es.enter_context(nc.semaphore(name))

        qsems = {q: sem("q_" + q) for q in ["pe", "act", "dve", "pool"]}
        S_ = Sched()
        A = S_.add

        xT = sb("xT", [128, 8, NT])
        hT = sb("hT", [128, 8, NT], BF16)
        sqr = [sb(f"sq{i}", [128, NT], BF16) for i in range(2)]
        rstd = sb("rstd", [128, NT])
        ntmp = [sb(f"ntmp{i}", [128, NT]) for i in range(2)]
        qkraw = sb("qkraw", [128, 8, NT + 4])
        qkT = sb("qkT", [128, 8, NT], BF16)
        cacc = [sb(f"cacc{i}", [128, NT]) for i in range(4)]
        gpre = sb("gpre", [128, NCH, 8])
        e1 = sb("e1", [128, NCH, 4])
        l1 = sb("l1", [128, NCH, 4])
        t1 = sb("t1", [128, NCH, 4])
        sc = sb("sc", [128, NCH, 4])
        einv = sb("einv", [128, NCH, 4])
        ebL = sb("ebL", [128, NCH, 4])
        ebLs = sb("ebLs", [128, NL, 4])
        ug = sb("ug", [128, NCH, 512], BF16)
        vg = sb("vg", [128, NCH, 512], BF16)
        vt = [sb(f"vt{i}", [128, 512]) for i in range(2)]
        vn = sb("vn", [128, NCH, 512], BF16)
        st6 = sb("st6", [128, NCH, 6])
        mv = sb("mv", [128, NCH, 2])
        lvar = sb("lvar", [128, NCH])
        lrstd = sb("lrstd", [128, NCH])
        vaug = sb("vaug", [128, NCH, 4, 132], BF16)
        so = sb("so", [128, NCH, 512], BF16)
        gm = [sb(f"gm{i}", [128, 512], BF16) for i in range(2)]
        Amat = [sb(f"Amat{i}", [128, 4, 128], BF16) for i in range(2)]
        Kp = [sb(f"Kp{i}", [128, 4, 128], BF16) for i in range(2)]
        dtmp = [sb(f"dtmp{i}", [128, 4]) for i in range(2)]
        hc = [sb(f"hc{i}", [128, 512]) for i in range(2)]
        junk = sb("junk", [128, 128])
        ss = [sb(f"ss{i}", [128, 4]) for i in range(2)]
        lss = [sb(f"lss{i}", [128, 4]) for i in range(2)]
        hrs = [sb(f"hrs{i}", [128, 4]) for i in range(2)]
        ml = [sb(f"ml{i}", [128, 512], BF16) for i in range(2)]
        Dst = sb("Dst", [128, NL, 4, 132])
        Cbf = sb("Cbf", [128, NL, 4, 132], BF16)
        raw = [sb(f"raw{i}", [128, NT + 2]) for i in range(4)]
        facc = [sb(f"facc{i}", [128, NT]) for i in range(3)]
        sa = [sb(f"sa{i}", [128, NT]) for i in range(2)]
        actT = sb("actT", [128, 22, NT], BF16)
        halo = sb("halo", [128, NL, 44, 2])
        qhalo = sb("qhalo", [128, NL, 8, 4])
        pT = sb("pT", [128, 2, NT], BF16)
        gt = [sb(f"gt{i}", [128, NT]) for i in range(2)]
        wring = sb("wring", [128, NSLOT, WBLK], BF16)
        lngb = sb("lngb", [128, 2, 512])
        ident = sb("ident", [128, 128], BF16)
        tri = sb("tri", [128, 128])
        trib = sb("trib", [128, 128], BF16)
        onesf = sb("onesf", [128, 128])
        onesm = sb("onesm", [128, 128], BF16)
        sel = sb("sel", [128, 512], BF16)
        bs128 = sb("bs128", [128, NL, 128], BF16)
        gv = sb("gv", [128, NL * 24 + 8])
        mlng = sb("mlng", [128, NL * 4])
        cwqk = sb("cwqk", [128, NL, 8, 4])
        cbqk = sb("cbqk", [128, NL, 8])
        cwff = sb("cwff", [128, NL, 44, 3])
        cbff = sb("cbff", [128, NL, 44])
        bif = sb("bif", [128, NL, 8])
        epsc = sb("epsc", [128, 1])
        wg = sb("wg", [128, NL, 8, 8], BF16)

        PBt = [psum(f"pb{i}", [128, 512]) for i in range(4)]
        PTt = [psum(f"pt{i}", [128, 1024], BF16) for i in range(2)]
        PSt = [psum(f"ps{i}", [128, 512]) for i in range(2)]
        PB = Ring([(PBt[i], ("pb", i)) for i in range(4)])
        PT = Ring([(PTt[i], ("pt", i)) for i in range(2)])
        PS = Ring([(PSt[i], ("ps", i)) for i in range(2)])

        s_setup = sem("d_setup")
        s_w = [sem(f"d_w{i}") for i in range(NSLOT)]
        s_x = sem("d_x")
        s_p = sem("d_p")
        s_ln = sem("d_ln")
        s_yo = [sem(f"d_yo{i}") for i in range(2)]
        s_prep = [sem(f"d_prep{i}") for i in range(4)]
        s_ws = sem("d_ws")
        s_wg = sem("d_wg")
        s_ws2 = sem("d_ws2")

        XK = [("x", f) for f in range(8)]
        HT = [("hT", f) for f in range(8)]

        S_.tag = "setup"

        def load_const(dst_ap, src_ap, keys):
            A("sp", DMA(dst_ap, src_ap), writes=keys, dma_sem=s_setup)

        load_const(gv[:], gv_d, ["gv"])
        load_const(mlng[:], mlng_d, ["mlng"])
        load_const(cwqk[:], cwqk_d, ["cwqk"])
        load_const(cbqk[:], cbqk_d, ["cbqk"])
        load_const(cwff[:], cwff_d, ["cwff"])
        load_const(cbff[:], cbff_d, ["cbff"])
        load_const(bif[:], bif_d, ["bif"])
        load_const(tri[:], tri_d, ["tri"])
        load_const(xT[:, 0, :], sel_d, [("x", 0)])
        load_const(xT[:, 1, 0:128], ident_d, [("x", 1)])
        load_const(xT[0:8, 2, :].rearrange("p (l t) -> p l t", l=NL), bs8_d, [("x", 2)])
        setup_keys = ["gv", "mlng", "cwqk", "cbqk", "cwff", "cbff", "bif", "tri"] + XK[0:3]
        A("pool", MS(epsc[:], EPS), reads=setup_keys, writes=["epsc", "setup_done"])
        SD = ["setup_done"]
        A("pool", CP(sel[:], xT[:, 0, :]), reads=SD + [("x", 0)], writes=["sel"])
        A("pool", CP(ident[:], xT[:, 1, 0:128]), reads=SD + [("x", 1)], writes=["ident"])
        A("pool", CP(trib[:], tri[:]), reads=SD, writes=["trib"])
        A("pool", MS(bs128[:], 0.0), reads=SD, writes=["bs128"])
        A("pool", CP(bs128[0:8, :, :], xT[0:8, 2, :].rearrange("p (l t) -> p l t", l=NL)), reads=SD + ["bs128", ("x", 2)], writes=["bs128"])
        A("dve", MS(onesf[:], 1.0), writes=["onesf"])
        A("dve", MS(onesm[:], 1.0 / D), writes=["onesm"])
        A("dve", MS(qhalo[:], 0.0), writes=["qhalo"])
        A("dve", MS(halo[:], 0.0), writes=["halo"])
        A("dve", MS(Dst[:], 0.0), writes=[("Dst", l, h) for l in range(NL) for h in range(4)])
        A("dve", MS(Cbf[:], 0.0), writes=[("Cbf", l, h) for l in range(NL) for h in range(4)])
        A("dve", MS(ebLs[:], 1.0), writes=["ebLs"])
        A("pool", MS(vaug[:], 1.0), writes=[("vaug", c) for c in range(NCH)])
        for l in range(n_layers):
            A("pool", DMA(wg[:, l, :, :], wsrc["w_in"][l].rearrange("(k p) n -> p k n", p=128)[:, :, 3072:3080]),
              writes=[("wg", l), "wgslot"], dma_sem=s_wg)
        xflat = xT[:, 4:6, :].rearrange("p a b -> p (a b)")
        hflat = hT[:, 0:2, :].rearrange("p a b -> p (a b)")
        for l in range(n_layers):
            A("sp", DMA(xflat.rearrange("p (h t) -> p h t", h=8), wsT_d[l]), writes=[("x", 4), ("x", 5)], dma_sem=s_ws)
            for h in range(8):
                A("dve", TT(hflat[:, h * 128:(h + 1) * 128], xflat[:, h * 128:(h + 1) * 128], tri[:], ALU.mult),
                  reads=SD + [("x", 4), ("x", 5)], writes=[("hT", 0), ("hT", 1)])
            A("sp", DMA(wb_d[l, B_WS, :, 0:1024], hflat), reads=[("hT", 0), ("hT", 1)], writes=[("wbd", l, B_WS, 0)], dma_sem=s_ws2)

        np_ = [0]
        wbd_keys = {}
        prep_queue = {l: [] for l in range(n_layers)}
        for l in range(n_layers):
            for bi, parts in enumerate(WB):
                if parts is None:
                    wbd_keys[(l, bi)] = [("wbd", l, bi, 0)]
                    continue
                keys = []
                for pi, (src, K, c0, n, base, W, off) in enumerate(parts):
                    srcv = wsrc[src][l].rearrange("(k p) n -> p k n", p=128)[:, :, c0:c0 + n]
                    dstv = wb_d[l, bi, :, base:base + K * W].rearrange("p (k n) -> p k n", k=K)[:, :, off:off + n]
                    key = ("wbd", l, bi, pi)
                    prep_queue[l].append((dstv, srcv, key))
                    keys.append(key)
                wbd_keys[(l, bi)] = keys

        def emit_prep(l, n):
            for _ in range(n):
                if l >= n_layers or not prep_queue[l]:
                    return
                dstv, srcv, key = prep_queue[l].pop(0)
                sl = np_[0] % 4
                np_[0] += 1
                A("pool", DMA(dstv, srcv), writes=[key, ("prepslot", sl)], dma_sem=s_prep[sl])

        emit_prep(0, 10 ** 6)

        gblk = [0]
        total_blocks = n_tiles * n_layers * NWB

        def issue_wload(g):
            if g >= total_blocks:
                return
            bi = g % NWB
            l = (g // NWB) % n_layers
            sl = g % NSLOT
            n = WB_SIZE[bi]
            A("sp", DMA(wring[:, sl, 0:n], wb_d[l, bi, :, 0:n]), reads=wbd_keys[(l, bi)], writes=[("wr", sl)], dma_sem=s_w[sl])

        def next_block(expect_bi):
            g = gblk[0]
            assert g % NWB == expect_bi, (g % NWB, expect_bi)
            issue_wload(g + NSLOT - 1)
            gblk[0] += 1
            sl = g % NSLOT
            return wring[:, sl, :], ("wr", sl)

        def norm_begin():
            return PS.next()

        def norm_accum(nst, f):
            ps, pk = nst
            i = f % 2
            A("act", ACT(sqr[i][:], xT[:, f, :], AF.Square), reads=[("x", f)], writes=[("sq", i)])
            A("pe", MM(ps[:, 0:NT], onesm[:], sqr[i][:], f == 0, f == 7), reads=[("sq", i), "onesm"], writes=[pk])

        def norm_finish(nst, gcol, final_t0=None):
            ps, pk = nst
            A("act", ACT(rstd[:], ps[:, 0:NT], AF.Ln, bias=epsc[:, 0:1]), reads=[pk, "epsc"], writes=["rstd"])
            A("act", ACT(rstd[:], rstd[:], AF.Exp, scale=-0.5), reads=["rstd"], writes=["rstd"])
            for f in range(8):
                if final_t0 is not None:
                    i = f % 2
                    A("dve", STT(ntmp[i][:], xT[:, f, :], gv[:, gcol + f:gcol + f + 1], rstd[:], ALU.mult, ALU.mult),
                      reads=[("x", f), "rstd"] + SD, writes=[("ntmp", i)])
                    A("sp", DMA(yout_v[:, f, final_t0:final_t0 + NT], ntmp[i][:]), reads=[("ntmp", i)], writes=["yout"], dma_sem=s_yo[i])
                else:
                    A("dve", STT(hT[:, f, :], xT[:, f, :], gv[:, gcol + f:gcol + f + 1], rstd[:], ALU.mult, ALU.mult),
                      reads=[("x", f), "rstd"] + SD, writes=[("hT", f)])

        def norm(gcol, final_t0=None):
            nst = norm_begin()
            for f in range(8):
                norm_accum(nst, f)
            norm_finish(nst, gcol, final_t0)

        for g in range(NSLOT - 1):
            issue_wload(g)

        def layer_step(t, l, nst_in):
            t0 = t * NT
            tg = f"L{l}"
            S_.tag = tg + "norm1"
            A("pool", DMA(pT[:], pin[l].rearrange("(k p) s -> p k s", p=128)[:, :, t0:t0 + NT]), writes=["pT"], dma_sem=s_p)
            A("sp", DMA(lngb[:], lngb_d[l]), writes=["lngb"], dma_sem=s_ln)

            if nst_in is None:
                norm(l * 24)
            else:
                norm_finish(nst_in, l * 24)
            S_.tag = tg + "gates"
            psg, pgk = PS.next()
            for c in range(NCH):
                for k in range(8):
                    A("pe", MM(psg[:, c * 8:(c + 1) * 8], hT[:, k, c * 128:(c + 1) * 128], wg[:, l, k, :], k == 0, k == 7),
                      reads=[("hT", k), ("wg", l)], writes=[pgk])
            for c in range(NCH):
                A("dve", TT(gpre[:, c, :], psg[:, c * 8:(c + 1) * 8], bif[:, l, :], ALU.add), reads=[pgk] + SD, writes=["gpre"])
            A("act", ACT(e1[:], gpre[:, :, 4:8], AF.Exp, scale=-1.0), reads=["gpre"], writes=["e1"])
            A("act", ACT(l1[:], e1[:], AF.Ln, bias=1.0), reads=["e1"], writes=["l1"])
            def gates_tail():
                S_.tag = tg + "gates"
                psb, pbk = PS.next()
                l1v = l1[:].rearrange("p c h -> p (c h)")
                A("pe", MM(psb[:, 0:16], tri[:], l1v, True, True), reads=["l1"] + SD, writes=[pbk])
                A("pe", MM(psb[:, 16:32], onesf[:], l1v, True, True), reads=["l1", "onesf"], writes=[pbk])
                bneg = psb[:, 0:16].rearrange("p (c h) -> p c h", c=NCH)
                bLneg = psb[:, 16:32].rearrange("p (c h) -> p c h", c=NCH)
                A("dve", STT(t1[:], gpre[:, :, 0:4], math.log(128.0 ** -0.5), bneg, ALU.add, ALU.add), reads=["gpre", pbk], writes=["t1"])
                A("act", ACT(sc[:], t1[:], AF.Exp), reads=["t1"], writes=["sc"])
                A("act", ACT(einv[:], bneg, AF.Exp), reads=[pbk], writes=["einv"])
                A("act", ACT(ebL[:], bLneg, AF.Exp, scale=-1.0), reads=[pbk], writes=["ebL"])
                S_.tag = tg + "qk"

            S_.tag = tg + "qk"
            Wq = {}
            qpb = {}

            def qk_s0(f):
                wbi, j = divmod(f, 4)
                if j == 0:
                    W, wk = next_block(B_QK + wbi)
                    Wq[wbi] = (W[:, 0:4096].rearrange("p (k n) -> p k n", k=8), wk)
                Wv, wk = Wq[wbi]
                ps, pk = PB.next()
                qpb[f] = (ps, pk)
                for k in range(8):
                    A("pe", MM(ps[:, 0:NT], Wv[:, k, j * 128:(j + 1) * 128], hT[:, k, :], k == 0, k == 7), reads=[wk, ("hT", k)], writes=[pk])
                qk_ = ("qkraw", f)
                A("pool", CP(qkraw[:, f, 0:3], qhalo[:, l, f, 0:3]), reads=["qhalo"], writes=[qk_])
                A("act", ACT(qkraw[:, f, 3:3 + NT], ps[:, 0:NT], AF.Copy), reads=[pk, qk_], writes=[qk_])
                A("pool", CP(qhalo[:, l, f, 0:3], qkraw[:, f, NT:NT + 3]), reads=[qk_], writes=["qhalo"])

            def qk_s1(f):
                A("act", ACT(cacc[f % 4][:], qkraw[:, f, 0:NT], AF.Identity, scale=cwqk[:, l, f, 0:1], bias=cbqk[:, l, f:f + 1]),
                  reads=[("qkraw", f)] + SD, writes=[("cacc", f % 4)])

            def qk_s2(f):
                ps, pk = qpb.pop(f)
                ca = cacc[f % 4]
                ck = ("cacc", f % 4)
                A("dve", STT(ca[:], ps[:, 0:NT], cwqk[:, l, f, 3:4], ca[:], ALU.mult, ALU.add), reads=[pk, ck] + SD, writes=[ck])

            def qk_tap(tap):
                def fn(f):
                    ca = cacc[f % 4]
                    ck = ("cacc", f % 4)
                    A("dve", STT(ca[:], qkraw[:, f, tap:tap + NT], cwqk[:, l, f, tap:tap + 1], ca[:], ALU.mult, ALU.add), reads=[("qkraw", f), ck] + SD, writes=[ck])
                return fn

            def qk_s5(f):
                A("act", ACT(qkT[:, f, :], cacc[f % 4][:], AF.Silu), reads=[("cacc", f % 4)], writes=[("qkT", f)])

            gates_tail_done = [False]

            def qk_s0_wrap(f):
                qk_s0(f)
                if f == 1 and not gates_tail_done[0]:
                    gates_tail()
                    gates_tail_done[0] = True

            pipeline(8, [qk_s0_wrap, qk_s1, qk_s2, qk_tap(1), qk_tap(2), qk_s5])

            S_.tag = tg + "uvo"
            for which, bidx in (("u", B_U), ("v", B_V), ("vm", B_VM), ("o", B_O)):
                W, wk = next_block(bidx)
                Wv = W[:, 0:4096].rearrange("p (k n) -> p k n", k=8)
                for c in range(NCH):
                    ps, pk = PB.next()
                    for k in range(8):
                        A("pe", MM(ps[:, 0:512], hT[:, k, c * 128:(c + 1) * 128], Wv[:, k, :], k == 0, k == 7), reads=[wk, ("hT", k)], writes=[pk])
                    if which == "u":
                        A("act", ACT(ug[:, c, :], ps[:, 0:512], AF.Gelu), reads=[pk], writes=[("ug", c)])
                    elif which == "v":
                        A("act", ACT(vg[:, c, :], ps[:, 0:512], AF.Gelu), reads=[pk], writes=[("vg", c)])
                        A("dve", lambda e, c=c: e.bn_stats(out=st6[:, c, :], in_=vg[:, c, :]), reads=[("vg", c)], writes=[("st6", c)])
                        A("dve", lambda e, c=c: e.bn_aggr(out=mv[:, c, :], in_=st6[:, c, :]), reads=[("st6", c)], writes=[("mv", c)])
                    elif which == "vm":
                        A("dve", CP(vaug[:, c, :, 0:128], ps[:, 0:512].rearrange("p (h d) -> p h d", h=4)), reads=[pk, ("vaug", c)], writes=[("vaug", c)])
                    else:
                        A("act", ACT(so[:, c, :], ps[:, 0:512], AF.Sigmoid), reads=[pk], writes=[("so", c)])
                if which == "v":
                    A("act", ACT(lvar[:], mv[:, :, 1], AF.Ln, bias=epsc[:, 0:1]), reads=[("mv", c) for c in range(NCH)] + ["epsc"], writes=["lvar"])
                    A("act", ACT(lrstd[:], lvar[:], AF.Exp, scale=-0.5), reads=["lvar"], writes=["lrstd"])
                    for c in range(NCH):
                        v_ = vt[c % 2]
                        vk = ("vt", c % 2)
                        A("dve", TS(v_[:], vg[:, c, :], mv[:, c, 0:1], lrstd[:, c:c + 1], ALU.subtract, ALU.mult),
                          reads=[("vg", c), ("mv", c), "lrstd"], writes=[vk])
                        A("pool", TT(v_[:], v_[:], lngb[:, 0, :], ALU.mult), reads=[vk, "lngb"], writes=[vk])
                        A("pool", TT(vn[:, c, :], v_[:], lngb[:, 1, :], ALU.add), reads=[vk, "lngb"], writes=[("vn", c)])

            Wws, wsk = next_block(B_WS)
            wsv = Wws[:, 0:1024].rearrange("p (h t) -> p h t", h=8)
            stg = {}

            def mix_P1(c):
                S_.tag = tg + "mixP1"
                cs = slice(c * 128, (c + 1) * 128)
                gi = c % 2
                psm, pmk = PB.next()
                A("pe", MM(psm[:, 0:512], bs128[:, l, :], sel[:], True, False), reads=["bs128", "sel"], writes=[pmk])
                for h in range(8):
                    A("pe", MM(psm[:, h * 64:(h + 1) * 64], wsv[:, h, :], vn[:, c, h * 64:(h + 1) * 64], False, h == 7), reads=[wsk, ("vn", c)], writes=[pmk])
                st, stk = PS.next()
                for h in range(4):
                    A("pe", MM(st[:, h * 128:(h + 1) * 128], qkT[:, 4 + h, cs], qkT[:, h, cs], True, True), reads=[("qkT", 4 + h), ("qkT", h)], writes=[stk])
                kt, ktk = PT.next()
                for h in range(4):
                    A("pe", TR(kt[:, h * 128:(h + 1) * 128], qkT[:, 4 + h, cs], ident[:]), reads=[("qkT", 4 + h), "ident"], writes=[ktk])
                A("dve", TT(gm[gi][:], psm[:, 0:512], ug[:, c, :], ALU.mult), reads=[pmk, ("ug", c)], writes=[("gm", gi)])
                for h in range(4):
                    A("dve", STT(Amat[gi][:, h, :], st[:, h * 128:(h + 1) * 128], sc[:, c, h:h + 1], trib[:], ALU.mult, ALU.mult),
                      reads=[stk, "sc", "trib"], writes=[("Amat", gi, h)])
                    A("dve", TS(Kp[gi][:, h, :], kt[:, h * 128:(h + 1) * 128], sc[:, c, h:h + 1], None, ALU.mult), reads=[ktk, "sc"], writes=[("Kp", gi, h)])

            def mix_P2(c):
                S_.tag = tg + "mixP2"
                cs = slice(c * 128, (c + 1) * 128)
                gi = c % 2
                pt, ptk = PT.next()
                for j in range(4):
                    A("pe", TR(pt[:, j * 128:(j + 1) * 128], gm[gi][:, j * 128:(j + 1) * 128], ident[:]), reads=[("gm", gi), "ident"], writes=[ptk])
                A("act", ACT(hT[:, 0:4, cs], pt[:, 0:512].rearrange("p (a b) -> p a b", a=4), AF.Copy), reads=[ptk], writes=HT[0:4])
                for pr in range(2):
                    nd, ndk = PB.next()
                    up, upk = PB.next()
                    for j in range(2):
                        h = 2 * pr + j
                        A("pe", MM(nd[:, j * 129:(j + 1) * 129], Amat[gi][:, h, :], vaug[:, c, h, 0:129], True, False), reads=[("Amat", gi, h), ("vaug", c)], writes=[ndk])
                        A("pe", MM(nd[:, j * 129:(j + 1) * 129], qkT[:, h, cs], Cbf[:, l, h, 0:129], False, True), reads=[("qkT", h), ("Cbf", l, h)], writes=[ndk])
                    for j in range(2):
                        h = 2 * pr + j
                        A("pe", MM(up[:, j * 129:(j + 1) * 129], Kp[gi][:, h, :], vaug[:, c, h, 0:129], True, True), reads=[("Kp", gi, h), ("vaug", c)], writes=[upk])
                    d_ = dtmp[gi]
                    dk = ("dtmp", gi, pr)
                    dsl = d_[:, 2 * pr:2 * pr + 2]
                    A("act", ACT(dsl, nd[:, 0:258].rearrange("p (j d) -> p j d", j=2)[:, :, 128], AF.Abs), reads=[ndk], writes=[dk])
                    A("dve", TT(dsl, dsl, einv[:, c, 2 * pr:2 * pr + 2], ALU.max), reads=[dk, "einv"], writes=[dk])
                    A("dve", lambda e, dsl=dsl: e.reciprocal(out=dsl, in_=dsl), reads=[dk], writes=[dk])
                    for j in range(2):
                        h = 2 * pr + j
                        A("dve", STT(hc[gi][:, h * 128:(h + 1) * 128], nd[:, j * 129:j * 129 + 128], d_[:, h:h + 1], so[:, c, h * 128:(h + 1) * 128], ALU.mult, ALU.mult),
                          reads=[ndk, dk, ("so", c)], writes=[("hc", gi, h)])
                    for j in range(2):
                        h = 2 * pr + j
                        prev = ebL[:, c - 1, h:h + 1] if c > 0 else ebLs[:, l, h:h + 1]
                        A("dve", STT(Dst[:, l, h, 0:129], Dst[:, l, h, 0:129], prev, up[:, j * 129:(j + 1) * 129], ALU.mult, ALU.add),
                          reads=[upk, ("Dst", l, h), "ebL", "ebLs"], writes=[("Dst", l, h)])
                        A("act", ACT(Cbf[:, l, h, 0:129], Dst[:, l, h, 0:129], AF.Copy, scale=ebL[:, c, h:h + 1]), reads=[("Dst", l, h), "ebL"], writes=[("Cbf", l, h)])
                for h in range(4):
                    A("act", ACT(junk[:], hc[gi][:, h * 128:(h + 1) * 128], AF.Square, accum_out=ss[gi][:, h:h + 1]), reads=[("hc", gi, h)], writes=["junk", ("ss", gi)])
                A("act", ACT(lss[gi][:], ss[gi][:], AF.Ln, scale=1.0 / 128.0, bias=epsc[:, 0:1]), reads=[("ss", gi), "epsc"], writes=[("lss", gi)])
                A("act", ACT(hrs[gi][:], lss[gi][:], AF.Exp, scale=-0.5), reads=[("lss", gi)], writes=[("hrs", gi)])
                for h in range(4):
                    A("act", ACT(ml[gi][:, h * 128:(h + 1) * 128], hc[gi][:, h * 128:(h + 1) * 128], AF.Copy, scale=hrs[gi][:, h:h + 1]),
                      reads=[("hc", gi, h), ("hrs", gi)], writes=[("ml", gi)])

            def mix_P3(c):
                S_.tag = tg + "mixP3"
                cs = slice(c * 128, (c + 1) * 128)
                gi = c % 2
                pt, ptk = PT.next()
                for h in range(4):
                    A("pe", TR(pt[:, h * 128:(h + 1) * 128], ml[gi][:, h * 128:(h + 1) * 128], ident[:]), reads=[("ml", gi), "ident"], writes=[ptk])
                for h in range(4):
                    A("act", ACT(hT[:, 4 + h, cs], pt[:, h * 128:(h + 1) * 128], AF.Copy, scale=mlng[:, l * 4 + h:l * 4 + h + 1]), reads=[ptk] + SD, writes=[("hT", 4 + h)])

            mix_P1(0)
            for c in range(NCH):
                if c + 1 < NCH:
                    mix_P1(c + 1)
                mix_P2(c)
                if c >= 1:
                    mix_P3(c - 1)
            mix_P3(NCH - 1)
            A("dve", CP(ebLs[:, l, :], ebL[:, NCH - 1, :]), reads=["ebL", "ebLs"], writes=["ebLs"])

            S_.tag = tg + "wout"
            Wo = {}
            nst2 = norm_begin()

            def wout_s0(f):
                wbi, j = divmod(f, 4)
                if j == 0:
                    W, wk = next_block(B_OUT + wbi)
                    Wo[wbi] = (W[:, 0:4096].rearrange("p (k n) -> p k n", k=8), wk)
                Wv, wk = Wo[wbi]
                ps, pk = PB.next()
                for k in range(8):
                    A("pe", MM(ps[:, 0:NT], Wv[:, k, j * 128:(j + 1) * 128], hT[:, k, :], k == 0, k == 7), reads=[wk, ("hT", k)], writes=[pk])
                A("dve", TT(xT[:, f, :], ps[:, 0:NT], xT[:, f, :], ALU.add), reads=[pk, ("x", f)], writes=[("x", f)])

            pipeline(8, [wout_s0, lambda f: None, lambda f: norm_accum(nst2, f)])

            S_.tag = tg + "norm2"
            norm_finish(nst2, l * 24 + 8)
            S_.tag = tg + "wup"
            Wu = {}
            pbk_ = {}

            def up_pre(fi):
                A("pool", CP(raw[fi % 4][:, 0:2], halo[:, l, fi, :]), reads=["halo"], writes=[("raw", fi % 4)])

            def up_s0(fi):
                i, jj = divmod(fi, 4)
                if jj == 0:
                    W, wk = next_block(B_UP + i)
                    Wu[i] = (W[:, 0:4096].rearrange("p (k n) -> p k n", k=8), wk)
                Wv, wk = Wu[i]
                ps, pk = PB.next()
                pbk_[fi] = (ps, pk)
                for k in range(8):
                    A("pe", MM(ps[:, 0:NT], Wv[:, k, jj * 128:(jj + 1) * 128], hT[:, k, :], k == 0, k == 7), reads=[wk, ("hT", k)], writes=[pk])
                r_ = raw[fi % 4]
                rk_ = ("raw", fi % 4)
                A("act", ACT(r_[:, 2:2 + NT], ps[:, 0:NT], AF.Copy), reads=[pk, rk_], writes=[rk_])
                if t == 0:
                    emit_prep(l + 1, 2)

            def up_s1(fi):
                A("act", ACT(facc[fi % 3][:], raw[fi % 4][:, 0:NT], AF.Identity, scale=cwff[:, l, fi, 0:1], bias=cbff[:, l, fi:fi + 1]),
                  reads=[("raw", fi % 4)] + SD, writes=[("facc", fi % 3)])
                A("pool", CP(halo[:, l, fi, :], raw[fi % 4][:, NT:NT + 2]), reads=[("raw", fi % 4)], writes=["halo"])

            def up_s2(fi):
                ps, pk = pbk_.pop(fi)
                fa = facc[fi % 3]
                fk = ("facc", fi % 3)
                A("dve", STT(fa[:], ps[:, 0:NT], cwff[:, l, fi, 2:3], fa[:], ALU.mult, ALU.add), reads=[pk, fk] + SD, writes=[fk])

            def up_s3(fi):
                fa = facc[fi % 3]
                fk = ("facc", fi % 3)
                A("dve", STT(fa[:], raw[fi % 4][:, 1:1 + NT], cwff[:, l, fi, 1:2], fa[:], ALU.mult, ALU.add), reads=[("raw", fi % 4), fk] + SD, writes=[fk])

            def up_s4(fi):
                i, jj = divmod(fi, 4)
                fa = facc[fi % 3]
                fk = ("facc", fi % 3)
                if jj < 2:
                    A("act", ACT(sa[jj][:], fa[:], AF.Silu), reads=[fk], writes=[("sa", jj)])
                else:
                    A("pool", TT(actT[:, 2 * i + jj - 2, :], sa[jj - 2][:], fa[:], ALU.mult), reads=[("sa", jj - 2), fk], writes=[("actT", 2 * i + jj - 2)])

            pipeline(44, [up_pre, up_s0, up_s1, up_s2, up_s3, up_s4])
            if t == 0:
                emit_prep(l + 1, 10 ** 6)

            S_.tag = tg + "wdown"
            nst3 = norm_begin()

            def wdown_s0(f):
                W, wk = next_block(B_DOWN + f)
                Wv = W[:, 0:22 * 128].rearrange("p (k n) -> p k n", k=22)
                ps, pk = PB.next()
                for k in range(22):
                    A("pe", MM(ps[:, 0:NT], Wv[:, k, :], actT[:, k, :], k == 0, k == 21), reads=[wk, ("actT", k)], writes=[pk])
                A("dve", TT(xT[:, f, :], ps[:, 0:NT], xT[:, f, :], ALU.add), reads=[pk, ("x", f)], writes=[("x", f)])

            pipeline(8, [wdown_s0, lambda f: None, lambda f: norm_accum(nst3, f)])

            S_.tag = tg + "norm3"
            norm_finish(nst3, l * 24 + 16)
            S_.tag = tg + "ple"
            Wgp = {}
            nst_next = norm_begin()

            for it in range(8 + 3):
                f = it
                fa_ = it - 3
                if 0 <= fa_ < 8:
                    A("act", ACT(sqr[fa_ % 2][:], xT[:, fa_, :], AF.Square), reads=[("x", fa_)], writes=[("sq", fa_ % 2)])
                if f < 8:
                    jb, jj = divmod(f, 2)
                    if jj == 0:
                        W, wk = next_block(B_GP + jb)
                        Wgp[jb] = (W[:, 0:2048].rearrange("p (k n) -> p k n", k=8), W[:, 2048:2560].rearrange("p (k n) -> p k n", k=2), wk)
                    Wg, Wp, wk = Wgp[jb]
                    psg2, pgk2 = PB.next()
                    for k in range(8):
                        A("pe", MM(psg2[:, 0:NT], Wg[:, k, jj * 128:(jj + 1) * 128], hT[:, k, :], k == 0, k == 7), reads=[wk, ("hT", k)], writes=[pgk2])
                if 0 <= fa_ < 8:
                    ps_n, pk_n = nst_next
                    A("pe", MM(ps_n[:, 0:NT], onesm[:], sqr[fa_ % 2][:], fa_ == 0, fa_ == 7), reads=[("sq", fa_ % 2), "onesm"], writes=[pk_n])
                if f < 8:
                    g_ = gt[f % 2]
                    gk_ = ("gt", f % 2)
                    A("act", ACT(g_[:], psg2[:, 0:NT], AF.Sigmoid), reads=[pgk2], writes=[gk_])
                    psp, ppk = PB.next()
                    for k in range(2):
                        A("pe", MM(psp[:, 0:NT], Wp[:, k, jj * 128:(jj + 1) * 128], pT[:, k, :], k == 0, k == 1), reads=[wk, "pT"], writes=[ppk])
                    A("dve", TT(g_[:], psp[:, 0:NT], g_[:], ALU.mult), reads=[ppk, gk_], writes=[gk_])
                    A("dve", TT(xT[:, f, :], xT[:, f, :], g_[:], ALU.add), reads=[gk_, ("x", f)], writes=[("x", f)])
            return nst_next

        for t in range(n_tiles):
            S_.tag = "xload"
            A("sp", DMA(xT[:], xin_v[:, :, t * NT:(t + 1) * NT]), writes=XK, dma_sem=s_x)
            nst = None
            for l in range(n_layers):
                nst = layer_step(t, l, nst)
            S_.tag = "final"
            norm_finish(nst, NL * 24, final_t0=t * NT)

        A("sp", lambda e: e.nop(), reads=["yout"])

        S_.analyze(qsems)
        with nc.Block() as block:
            S_.emit(block)
    return nc


def _pk(v):
    v = np.asarray(v, np.float32)
    return np.ascontiguousarray(v.reshape(-1, 128).T)


def ffn_col_order():
    cols = []
    for i in range(11):
        for j in (2 * i, 2 * i + 1):
            cols.append(np.arange(128 * j, 128 * j + 128))
        for j in (2 * i, 2 * i + 1):
            cols.append(DFF + np.arange(128 * j, 128 * j + 128))
    return np.stack(cols)


def shared_inputs(inp):
    f32 = np.float32
    d = {}
    for nm in ("w_in", "w_out", "w_up", "w_down", "w_ple"):
        d[nm] = np.ascontiguousarray(inp[nm], f32)
    d["w_gate"] = np.ascontiguousarray(inp["w_ple_gate"], f32)
    cols = []
    for i in range(NL):
        cols += [_pk(inp["g_mix"][i]), _pk(inp["g_ffn"][i]), _pk(inp["g_ple"][i])]
    cols.append(_pk(inp["g_final"]))
    d["gv"] = np.ascontiguousarray(np.concatenate(cols, axis=1))
    d["mlng"] = np.ascontiguousarray(np.concatenate([_pk(inp["ml_norm_g"][i]) for i in range(NL)], axis=1))
    lngb = np.stack([np.stack([inp["gm_ln_g"][i], inp["gm_ln_b"][i]]) for i in range(NL)])
    d["lngb"] = np.ascontiguousarray(np.broadcast_to(lngb[:, None], (NL, 128, 2, 512)), f32)
    d["wsT"] = np.ascontiguousarray(np.transpose(np.asarray(inp["gm_ws"], f32), (0, 3, 1, 2)))
    d["bs8"] = np.ascontiguousarray(np.transpose(np.asarray(inp["gm_bs"], f32), (1, 0, 2)))
    cw = np.asarray(inp["ml_conv_w"], f32)
    d["cwqk"] = np.ascontiguousarray(np.transpose(cw.reshape(NL, 4, 8, 128), (3, 0, 2, 1)))
    cb = np.asarray(inp["ml_conv_b"], f32)
    d["cbqk"] = np.ascontiguousarray(np.transpose(cb.reshape(NL, 8, 128), (2, 0, 1)))
    order = ffn_col_order()
    fw = np.asarray(inp["ffn_conv_w"], f32)
    d["cwff"] = np.ascontiguousarray(np.transpose(fw[:, :, order], (3, 0, 2, 1)))
    fb = np.asarray(inp["ffn_conv_b"], f32)
    d["cbff"] = np.ascontiguousarray(np.transpose(fb[:, order], (2, 0, 1)))
    bif = np.concatenate([np.asarray(inp["ml_b_i"], f32), np.asarray(inp["ml_b_f"], f32)], axis=1)
    d["bif"] = np.ascontiguousarray(np.broadcast_to(bif[None], (128, NL, 8)), f32)
    d["ident"] = np.eye(128, dtype=f32)
    d["tri"] = np.triu(np.ones((128, 128), f32))
    sel = np.zeros((128, 512), f32)
    for h in range(8):
        sel[h, h * 64:(h + 1) * 64] = 1.0
    d["sel"] = sel
    return d


_PROG = {}


def get_prog(S, n_layers=NL):
    if (S, n_layers) not in _PROG:
        _PROG[(S, n_layers)] = build_program(S, n_layers)
    return _PROG[(S, n_layers)]


def kernel(**inp):
    x = np.asarray(inp["x"], np.float32)
    p = np.asarray(inp["p"], np.float32)
    B, S, _ = x.shape
    sh = shared_inputs(inp)
    maps = []
    for b in range(B):
        m = dict(sh)
        m["xin"] = np.ascontiguousarray(x[b].T)
        m["pin"] = np.ascontiguousarray(np.transpose(p[:, b], (0, 2, 1)))
        maps.append(m)
    res = run_bass_kernel_spmd(get_prog(S), maps, core_ids=list(range(B)))
    return np.stack([np.asarray(res.results[b]["yout"]).T for b in range(B)]).astype(np.float32)
```

```python
import math
import numpy as np
from contextlib import ExitStack
import concourse.bass as bass
import concourse.mybir as mybir
from concourse.bass_utils import run_bass_kernel_spmd

F32 = mybir.dt.float32
BF16 = mybir.dt.bfloat16
AF = mybir.ActivationFunctionType
ALU = mybir.AluOpType

D = 1024
NT = 512
NCH = NT // 128
DFF = 2816
EPS = 1e-6
NSLOT = 4
WBLK = 4096
SAME_ENGINE_SYNC = True
SAME_ENGINE_ALL = True
SAME_ENGINE_SYNC_Q = {"act", "dve", "pool"}
ANNOTATE = False


class Op:
    __slots__ = ("q", "fn", "reads", "writes", "sem", "inc", "needs_inc", "count", "waits", "tag")


class Sched:
    def __init__(self):
        self.ops = []
        self.tag = None

    def add(self, q, fn, reads=(), writes=(), dma_sem=None):
        o = Op()
        o.q, o.fn, o.reads, o.writes = q, fn, tuple(reads), tuple(writes)
        o.sem = dma_sem
        o.inc = 16 if dma_sem is not None else 1
        o.needs_inc = dma_sem is not None
        o.count = None
        o.waits = {}
        o.tag = self.tag
        self.ops.append(o)
        return o

    def analyze(self, qsems):
        last_w, readers, need = {}, {}, []
        for o in self.ops:
            deps = {}
            for r in o.reads:
                w = last_w.get(r)
                if w is not None:
                    deps[id(w)] = (w, True)
            for k in o.writes:
                w = last_w.get(k)
                if w is not None and id(w) not in deps:
                    deps[id(w)] = (w, False)
                for rd in readers.get(k, ()):
                    if id(rd) not in deps:
                        deps[id(rd)] = (rd, False)
            deps.pop(id(o), None)
            nd = []
            for a, raw in deps.values():
                if a.sem is None and a.q == o.q:
                    if o.q == "pe" or not SAME_ENGINE_SYNC or o.q not in SAME_ENGINE_SYNC_Q:
                        continue
                    if not raw and not SAME_ENGINE_ALL:
                        continue
                nd.append(a)
                a.needs_inc = True
            need.append(nd)
            for r in o.reads:
                readers.setdefault(r, []).append(o)
            for k in o.writes:
                last_w[k] = o
                readers[k] = []
        cnt = {}
        for o in self.ops:
            if o.needs_inc:
                s = o.sem if o.sem is not None else qsems[o.q]
                cnt[s] = cnt.get(s, 0) + o.inc
                o.count = (s, cnt[s])
        waited = {}
        for o, nd in zip(self.ops, need):
            w = {}
            for a in nd:
                s, c = a.count
                if c > w.get(s, 0):
                    w[s] = c
            qw = waited.setdefault(o.q, {})
            for s, c in list(w.items()):
                if qw.get(s, 0) >= c:
                    del w[s]
                else:
                    qw[s] = c
            o.waits = w
        self.final_counts = cnt

    def emit(self, block):
        byq = {}
        for o in self.ops:
            byq.setdefault(o.q, []).append(o)

        def run(eng, ops):
            for o in ops:
                for s, c in o.waits.items():
                    eng.wait_ge(s, c)
                ins = o.fn(eng)
                if ANNOTATE and o.tag is not None:
                    ins.annotate(o.tag)
                if o.needs_inc:
                    ins.then_inc(o.count[0], o.inc)

        names = {"pe": "tensor", "act": "scalar", "dve": "vector", "pool": "gpsimd", "sp": "sync"}
        for q in ["sp", "pe", "act", "dve", "pool"]:
            if q in byq:
                getattr(block, names[q])(lambda eng, ops=byq[q]: run(eng, ops))


class Ring:
    def __init__(self, items):
        self.items = items
        self.i = 0

    def next(self):
        it = self.items[self.i % len(self.items)]
        self.i += 1
        return it


NL = 4


def weight_blocks():
    b = []
    for i in range(2):
        b.append([("w_in", 8, 1024 + 512 * i, 512, 0, 512, 0)])
    for c0 in (0, 512, 2048, 2560):
        b.append([("w_in", 8, c0, 512, 0, 512, 0)])
    b.append(None)
    for i in range(2):
        b.append([("w_out", 8, 512 * i, 512, 0, 512, 0)])
    for i in range(11):
        b.append([("w_up", 8, 256 * i, 256, 0, 512, 0), ("w_up", 8, DFF + 256 * i, 256, 0, 512, 256)])
    for f in range(8):
        b.append([("w_down", 22, 128 * f, 128, 0, 128, 0)])
    for j in range(4):
        b.append([("w_gate", 8, 256 * j, 256, 0, 256, 0), ("w_ple", 2, 256 * j, 256, 2048, 256, 0)])
    return b


WB = weight_blocks()
NWB = len(WB)
B_QK, B_U, B_V, B_VM, B_O, B_WS, B_OUT, B_UP, B_DOWN, B_GP = 0, 2, 3, 4, 5, 6, 7, 9, 20, 28
WB_SIZE = []
for _b in WB:
    if _b is None:
        WB_SIZE.append(1024)
    else:
        WB_SIZE.append(max(base + K * W for (_, K, _, _, base, W, _) in _b))


def MM(out, lhsT, rhs, start, stop):
    return lambda e: e.matmul(out, lhsT=lhsT, rhs=rhs, start=start, stop=stop)


def TR(out, in_, ident):
    return lambda e: e.transpose(out, in_, ident)


def ACT(out, in_, func, **kw):
    return lambda e: e.activation(out=out, in_=in_, func=func, **kw)


def TT(out, in0, in1, op):
    return lambda e: e.tensor_tensor(out=out, in0=in0, in1=in1, op=op)


def TS(out, in0, s1, s2, op0, op1=None):
    if op1 is None:
        return lambda e: e.tensor_scalar(out=out, in0=in0, scalar1=s1, scalar2=None, op0=op0)
    return lambda e: e.tensor_scalar(out=out, in0=in0, scalar1=s1, scalar2=s2, op0=op0, op1=op1)


def TSS(out, in_, scalar, op):
    return lambda e: e.tensor_single_scalar(out=out, in_=in_, scalar=scalar, op=op)


def STT(out, in0, scalar, in1, op0, op1):
    return lambda e: e.scalar_tensor_tensor(out=out, in0=in0, scalar=scalar, in1=in1, op0=op0, op1=op1)


def CP(out, in_):
    return lambda e: e.tensor_copy(out=out, in_=in_)


def MS(out, val):
    return lambda e: e.memset(out, val)


def DMA(out, in_):
    return lambda e: e.dma_start(out=out, in_=in_)


def pipeline(n, stages):
    ns = len(stages)
    for i in range(n + ns - 1):
        for si in reversed(range(ns)):
            c = i - si
            if 0 <= c < n:
                stages[si](c)


def build_program(S, n_layers=NL):
    n_tiles = S // NT
    nc = bass.Bass("TRN2", target_bir_lowering=False)

    def din(name, shape):
        return nc.dram_tensor(name, list(shape), F32, kind="ExternalInput").ap()

    xin = din("xin", [D, S])
    pin = din("pin", [NL, 256, S])
    wsrc = {
        "w_in": din("w_in", [NL, D, 3080]), "w_out": din("w_out", [NL, D, D]), "w_up": din("w_up", [NL, D, 2 * DFF]),
        "w_down": din("w_down", [NL, DFF, D]), "w_gate": din("w_gate", [NL, D, D]), "w_ple": din("w_ple", [NL, 256, D]),
    }
    gv_d = din("gv", [128, NL * 24 + 8])
    mlng_d = din("mlng", [128, NL * 4])
    lngb_d = din("lngb", [NL, 128, 2, 512])
    wsT_d = din("wsT", [NL, 128, 8, 128])
    bs8_d = din("bs8", [8, NL, 128])
    cwqk_d = din("cwqk", [128, NL, 8, 4])
    cbqk_d = din("cbqk", [128, NL, 8])
    cwff_d = din("cwff", [128, NL, 44, 3])
    cbff_d = din("cbff", [128, NL, 44])
    bif_d = din("bif", [128, NL, 8])
    ident_d = din("ident", [128, 128])
    tri_d = din("tri", [128, 128])
    sel_d = din("sel", [128, 512])
    yout = nc.dram_tensor("yout", [D, S], F32, kind="ExternalOutput").ap()
    wb_d = nc.dram_tensor("wb_scratch", [NL, NWB, 128, WBLK], BF16).ap()

    xin_v = xin.rearrange("(f p) s -> p f s", p=128)
    yout_v = yout.rearrange("(f p) s -> p f s", p=128)

    with ExitStack() as es:
        def sb(name, shape, dt=F32):
            return es.enter_context(nc.sbuf_tensor("sb_" + name, list(shape), dt))

        def psum(name, shape, dt=F32):
            return es.enter_context(nc.psum_tensor(name, list(shape), dt))

        def sem(name):
            return es.enter_context(nc.semaphore(name))

        qsems = {q: sem("q_" + q) for q in ["pe", "act", "dve", "pool"]}
        S_ = Sched()
        A = S_.add

        xT = sb("xT", [128, 8, NT])
        hT = sb("hT", [128, 8, NT], BF16)
        sqr = [sb(f"sq{i}", [128, NT], BF16) for i in range(2)]
        rstd = sb("rstd", [128, NT])
        ntmp = [sb(f"ntmp{i}", [128, NT]) for i in range(2)]
        qkraw = sb("qkraw", [128, 8, NT + 4])
        qkT = sb("qkT", [128, 8, NT], BF16)
        cacc = [sb(f"cacc{i}", [128, NT]) for i in range(4)]
        gpre = sb("gpre", [128, NCH, 8])
        e1 = sb("e1", [128, NCH, 4])
        l1 = sb("l1", [128, NCH, 4])
        t1 = sb("t1", [128, NCH, 4])
        sc = sb("sc", [128, NCH, 4])
        einv = sb("einv", [128, NCH, 4])
        ebL = sb("ebL", [128, NCH, 4])
        ebLs = sb("ebLs", [128, NL, 4])
        ug = sb("ug", [128, NCH, 512], BF16)
        vg = sb("vg", [128, NCH, 512], BF16)
        vt = [sb(f"vt{i}", [128, 512]) for i in range(2)]
        vn = sb("vn", [128, NCH, 512], BF16)
        st6 = sb("st6", [128, NCH, 6])
        mv = sb("mv", [128, NCH, 2])
        lvar = sb("lvar", [128, NCH])
        lrstd = sb("lrstd", [128, NCH])
        vaug = sb("vaug", [128, NCH, 4, 132], BF16)
        so = sb("so", [128, NCH, 512], BF16)
        gm = [sb(f"gm{i}", [128, 512], BF16) for i in range(2)]
        Amat = [sb(f"Amat{i}", [128, 4, 128], BF16) for i in range(2)]
        Kp = [sb(f"Kp{i}", [128, 4, 128], BF16) for i in range(2)]
        dtmp = [sb(f"dtmp{i}", [128, 4]) for i in range(2)]
        hc = [sb(f"hc{i}", [128, 512]) for i in range(2)]
        junk = sb("junk", [128, 128])
        ss = [sb(f"ss{i}", [128, 4]) for i in range(2)]
        lss = [sb(f"lss{i}", [128, 4]) for i in range(2)]
        hrs = [sb(f"hrs{i}", [128, 4]) for i in range(2)]
        ml = [sb(f"ml{i}", [128, 512], BF16) for i in range(2)]
        Dst = sb("Dst", [128, NL, 4, 132])
        Cbf = sb("Cbf", [128, NL, 4, 132], BF16)
        raw = [sb(f"raw{i}", [128, NT + 2]) for i in range(4)]
        facc = [sb(f"facc{i}", [128, NT]) for i in range(3)]
        sa = [sb(f"sa{i}", [128, NT]) for i in range(2)]
        actT = sb("actT", [128, 22, NT], BF16)
        halo = sb("halo", [128, NL, 44, 2])
        qhalo = sb("qhalo", [128, NL, 8, 4])
        pT = sb("pT", [128, 2, NT], BF16)
        gt = [sb(f"gt{i}", [128, NT]) for i in range(2)]
        wring = sb("wring", [128, NSLOT, WBLK], BF16)
        lngb = sb("lngb", [128, 2, 512])
        ident = sb("ident", [128, 128], BF16)
        tri = sb("tri", [128, 128])
        trib = sb("trib", [128, 128], BF16)
        trib4 = sb("trib4", [128, 4, 128], BF16)
        onesf = sb("onesf", [128, 128])
        onesm = sb("onesm", [128, 128], BF16)
        sel = sb("sel", [128, 512], BF16)
        bs128 = sb("bs128", [128, NL, 128], BF16)
        gv = sb("gv", [128, NL * 24 + 8])
        mlng = sb("mlng", [128, NL * 4])
        cwqk = sb("cwqk", [128, NL, 8, 4])
        cbqk = sb("cbqk", [128, NL, 8])
        cwff = sb("cwff", [128, NL, 44, 3])
        cbff = sb("cbff", [128, NL, 44])
        bif = sb("bif", [128, NL, 8])
        epsc = sb("epsc", [128, 1])
        wg = sb("wg", [128, NL, 8, 8], BF16)

        PBt = [psum(f"pb{i}", [128, 512]) for i in range(4)]
        PTt = [psum(f"pt{i}", [128, 1024], BF16) for i in range(2)]
        PSt = [psum(f"ps{i}", [128, 512]) for i in range(2)]
        PB = Ring([(PBt[i], ("pb", i)) for i in range(4)])
        PT = Ring([(PTt[i], ("pt", i)) for i in range(2)])
        PS = Ring([(PSt[i], ("ps", i)) for i in range(2)])

        s_setup = sem("d_setup")
        s_w = [sem(f"d_w{i}") for i in range(NSLOT)]
        s_x = sem("d_x")
        s_p = sem("d_p")
        s_ln = sem("d_ln")
        s_yo = [sem(f"d_yo{i}") for i in range(2)]
        s_prep = [sem(f"d_prep{i}") for i in range(4)]
        s_ws = sem("d_ws")
        s_wg = sem("d_wg")
        s_ws2 = sem("d_ws2")

        XK = [("x", f) for f in range(8)]
        HT = [("hT", f) for f in range(8)]

        S_.tag = "setup"

        def load_const(dst_ap, src_ap, keys):
            A("sp", DMA(dst_ap, src_ap), writes=keys, dma_sem=s_setup)

        load_const(gv[:], gv_d, ["gv"])
        load_const(mlng[:], mlng_d, ["mlng"])
        load_const(cwqk[:], cwqk_d, ["cwqk"])
        load_const(cbqk[:], cbqk_d, ["cbqk"])
        load_const(cwff[:], cwff_d, ["cwff"])
        load_const(cbff[:], cbff_d, ["cbff"])
        load_const(bif[:], bif_d, ["bif"])
        load_const(tri[:], tri_d, ["tri"])
        load_const(xT[:, 0, :], sel_d, [("x", 0)])
        load_const(xT[:, 1, 0:128], ident_d, [("x", 1)])
        load_const(xT[0:8, 2, :].rearrange("p (l t) -> p l t", l=NL), bs8_d, [("x", 2)])
        setup_keys = ["gv", "mlng", "cwqk", "cbqk", "cwff", "cbff", "bif", "tri"] + XK[0:3]
        A("pool", MS(epsc[:], EPS), reads=setup_keys, writes=["epsc", "setup_done"])
        SD = ["setup_done"]
        A("pool", CP(sel[:], xT[:, 0, :]), reads=SD + [("x", 0)], writes=["sel"])
        A("pool", CP(ident[:], xT[:, 1, 0:128]), reads=SD + [("x", 1)], writes=["ident"])
        A("pool", CP(trib[:], tri[:]), reads=SD, writes=["trib"])
        for h in range(4):
            A("pool", CP(trib4[:, h, :], tri[:]), reads=SD, writes=["trib4"])
        A("pool", MS(bs128[:], 0.0), reads=SD, writes=["bs128"])
        A("pool", CP(bs128[0:8, :, :], xT[0:8, 2, :].rearrange("p (l t) -> p l t", l=NL)), reads=SD + ["bs128", ("x", 2)], writes=["bs128"])
        A("dve", MS(onesf[:], 1.0), writes=["onesf"])
        A("dve", MS(onesm[:], 1.0 / D), writes=["onesm"])
        A("dve", MS(qhalo[:], 0.0), writes=["qhalo"])
        A("dve", MS(halo[:], 0.0), writes=["halo"])
        A("dve", MS(Dst[:], 0.0), writes=[("Dst", l, h) for l in range(NL) for h in range(4)])
        A("dve", MS(Cbf[:], 0.0), writes=[("Cbf", l, h) for l in range(NL) for h in range(4)])
        A("dve", MS(ebLs[:], 1.0), writes=["ebLs"])
        A("pool", MS(vaug[:], 1.0), writes=[("vaug", c) for c in range(NCH)])
        for l in range(n_layers):
            A("pool", DMA(wg[:, l, :, :], wsrc["w_in"][l].rearrange("(k p) n -> p k n", p=128)[:, :, 3072:3080]),
              writes=[("wg", l), "wgslot"], dma_sem=s_wg)
        xflat = xT[:, 4:6, :].rearrange("p a b -> p (a b)")
        hflat = hT[:, 0:2, :].rearrange("p a b -> p (a b)")
        for l in range(n_layers):
            A("sp", DMA(xflat.rearrange("p (h t) -> p h t", h=8), wsT_d[l]), writes=[("x", 4), ("x", 5)], dma_sem=s_ws)
            for h in range(8):
                A("dve", TT(hflat[:, h * 128:(h + 1) * 128], xflat[:, h * 128:(h + 1) * 128], tri[:], ALU.mult),
                  reads=SD + [("x", 4), ("x", 5)], writes=[("hT", 0), ("hT", 1)])
            A("sp", DMA(wb_d[l, B_WS, :, 0:1024], hflat), reads=[("hT", 0), ("hT", 1)], writes=[("wbd", l, B_WS, 0)], dma_sem=s_ws2)

        np_ = [0]
        wbd_keys = {}
        prep_queue = {l: [] for l in range(n_layers)}
        for l in range(n_layers):
            for bi, parts in enumerate(WB):
                if parts is None:
                    wbd_keys[(l, bi)] = [("wbd", l, bi, 0)]
                    continue
                keys = []
                for pi, (src, K, c0, n, base, W, off) in enumerate(parts):
                    srcv = wsrc[src][l].rearrange("(k p) n -> p k n", p=128)[:, :, c0:c0 + n]
                    dstv = wb_d[l, bi, :, base:base + K * W].rearrange("p (k n) -> p k n", k=K)[:, :, off:off + n]
                    key = ("wbd", l, bi, pi)
                    prep_queue[l].append((dstv, srcv, key))
                    keys.append(key)
                wbd_keys[(l, bi)] = keys

        def emit_prep(l, n):
            for _ in range(n):
                if l >= n_layers or not prep_queue[l]:
                    return
                dstv, srcv, key = prep_queue[l].pop(0)
                sl = np_[0] % 4
                np_[0] += 1
                A("pool", DMA(dstv, srcv), writes=[key, ("prepslot", sl)], dma_sem=s_prep[sl])

        emit_prep(0, 10 ** 6)

        gblk = [0]
        total_blocks = n_tiles * n_layers * NWB

        def issue_wload(g):
            if g >= total_blocks:
                return
            bi = g % NWB
            l = (g // NWB) % n_layers
            sl = g % NSLOT
            n = WB_SIZE[bi]
            A("sp", DMA(wring[:, sl, 0:n], wb_d[l, bi, :, 0:n]), reads=wbd_keys[(l, bi)], writes=[("wr", sl)], dma_sem=s_w[sl])

        def next_block(expect_bi):
            g = gblk[0]
            assert g % NWB == expect_bi, (g % NWB, expect_bi)
            issue_wload(g + NSLOT - 1)
            gblk[0] += 1
            sl = g % NSLOT
            return wring[:, sl, :], ("wr", sl)

        def norm_begin():
            return PS.next()

        def norm_accum(nst, f):
            ps, pk = nst
            i = f % 2
            A("act", ACT(sqr[i][:], xT[:, f, :], AF.Square), reads=[("x", f)], writes=[("sq", i)])
            A("pe", MM(ps[:, 0:NT], onesm[:], sqr[i][:], f == 0, f == 7), reads=[("sq", i), "onesm"], writes=[pk])

        def norm_finish(nst, gcol, final_t0=None):
            ps, pk = nst
            A("act", ACT(rstd[:], ps[:, 0:NT], AF.Ln, bias=epsc[:, 0:1]), reads=[pk, "epsc"], writes=["rstd"])
            A("act", ACT(rstd[:], rstd[:], AF.Exp, scale=-0.5), reads=["rstd"], writes=["rstd"])
            for f in range(8):
                if final_t0 is not None:
                    i = f % 2
                    A("dve", STT(ntmp[i][:], xT[:, f, :], gv[:, gcol + f:gcol + f + 1], rstd[:], ALU.mult, ALU.mult),
                      reads=[("x", f), "rstd"] + SD, writes=[("ntmp", i)])
                    A("sp", DMA(yout_v[:, f, final_t0:final_t0 + NT], ntmp[i][:]), reads=[("ntmp", i)], writes=["yout"], dma_sem=s_yo[i])
                else:
                    A("dve", STT(hT[:, f, :], xT[:, f, :], gv[:, gcol + f:gcol + f + 1], rstd[:], ALU.mult, ALU.mult),
                      reads=[("x", f), "rstd"] + SD, writes=[("hT", f)])

        def norm(gcol, final_t0=None):
            nst = norm_begin()
            for f in range(8):
                norm_accum(nst, f)
            norm_finish(nst, gcol, final_t0)

        for g in range(NSLOT - 1):
            issue_wload(g)

        def layer_step(t, l, nst_in):
            t0 = t * NT
            tg = f"L{l}"
            S_.tag = tg + "norm1"
            A("pool", DMA(pT[:], pin[l].rearrange("(k p) s -> p k s", p=128)[:, :, t0:t0 + NT]), writes=["pT"], dma_sem=s_p)
            A("sp", DMA(lngb[:], lngb_d[l]), writes=["lngb"], dma_sem=s_ln)

            if nst_in is None:
                norm(l * 24)
            else:
                norm_finish(nst_in, l * 24)
            S_.tag = tg + "gates"
            psg, pgk = PS.next()
            for c in range(NCH):
                for k in range(8):
                    A("pe", MM(psg[:, c * 8:(c + 1) * 8], hT[:, k, c * 128:(c + 1) * 128], wg[:, l, k, :], k == 0, k == 7),
                      reads=[("hT", k), ("wg", l)], writes=[pgk])
            for c in range(NCH):
                A("dve", TT(gpre[:, c, :], psg[:, c * 8:(c + 1) * 8], bif[:, l, :], ALU.add), reads=[pgk] + SD, writes=["gpre"])
            A("act", ACT(e1[:], gpre[:, :, 4:8], AF.Exp, scale=-1.0), reads=["gpre"], writes=["e1"])
            A("act", ACT(l1[:], e1[:], AF.Ln, bias=1.0), reads=["e1"], writes=["l1"])
            def gates_tail():
                S_.tag = tg + "gates"
                psb, pbk = PS.next()
                l1v = l1[:].rearrange("p c h -> p (c h)")
                A("pe", MM(psb[:, 0:16], tri[:], l1v, True, True), reads=["l1"] + SD, writes=[pbk])
                A("pe", MM(psb[:, 16:32], onesf[:], l1v, True, True), reads=["l1", "onesf"], writes=[pbk])
                bneg = psb[:, 0:16].rearrange("p (c h) -> p c h", c=NCH)
                bLneg = psb[:, 16:32].rearrange("p (c h) -> p c h", c=NCH)
                A("dve", STT(t1[:], gpre[:, :, 0:4], math.log(128.0 ** -0.5), bneg, ALU.add, ALU.add), reads=["gpre", pbk], writes=["t1"])
                A("act", ACT(sc[:], t1[:], AF.Exp), reads=["t1"], writes=["sc"])
                A("act", ACT(einv[:], bneg, AF.Exp), reads=[pbk], writes=["einv"])
                A("act", ACT(ebL[:], bLneg, AF.Exp, scale=-1.0), reads=[pbk], writes=["ebL"])
                S_.tag = tg + "qk"

            S_.tag = tg + "qk"
            Wq = {}
            qpb = {}

            def qk_s0(f):
                wbi, j = divmod(f, 4)
                if j == 0:
                    W, wk = next_block(B_QK + wbi)
                    Wq[wbi] = (W[:, 0:4096].rearrange("p (k n) -> p k n", k=8), wk)
                Wv, wk = Wq[wbi]
                ps, pk = PB.next()
                qpb[f] = (ps, pk)
                for k in range(8):
                    A("pe", MM(ps[:, 0:NT], Wv[:, k, j * 128:(j + 1) * 128], hT[:, k, :], k == 0, k == 7), reads=[wk, ("hT", k)], writes=[pk])
                qk_ = ("qkraw", f)
                A("pool", CP(qkraw[:, f, 0:3], qhalo[:, l, f, 0:3]), reads=["qhalo"], writes=[qk_])
                A("act", ACT(qkraw[:, f, 3:3 + NT], ps[:, 0:NT], AF.Copy), reads=[pk, qk_], writes=[qk_])
                A("pool", CP(qhalo[:, l, f, 0:3], qkraw[:, f, NT:NT + 3]), reads=[qk_], writes=["qhalo"])

            def qk_s1(f):
                A("act", ACT(cacc[f % 4][:], qkraw[:, f, 0:NT], AF.Identity, scale=cwqk[:, l, f, 0:1], bias=cbqk[:, l, f:f + 1]),
                  reads=[("qkraw", f)] + SD, writes=[("cacc", f % 4)])

            def qk_s2(f):
                ps, pk = qpb.pop(f)
                ca = cacc[f % 4]
                ck = ("cacc", f % 4)
                A("dve", STT(ca[:], ps[:, 0:NT], cwqk[:, l, f, 3:4], ca[:], ALU.mult, ALU.add), reads=[pk, ck] + SD, writes=[ck])

            def qk_tap(tap):
                def fn(f):
                    ca = cacc[f % 4]
                    ck = ("cacc", f % 4)
                    A("dve", STT(ca[:], qkraw[:, f, tap:tap + NT], cwqk[:, l, f, tap:tap + 1], ca[:], ALU.mult, ALU.add), reads=[("qkraw", f), ck] + SD, writes=[ck])
                return fn

            def qk_s5(f):
                A("act", ACT(qkT[:, f, :], cacc[f % 4][:], AF.Silu), reads=[("cacc", f % 4)], writes=[("qkT", f)])

            gates_tail_done = [False]

            def qk_s0_wrap(f):
                qk_s0(f)
                if f == 1 and not gates_tail_done[0]:
                    gates_tail()
                    gates_tail_done[0] = True

            pipeline(8, [qk_s0_wrap, qk_s1, qk_s2, qk_tap(1), qk_tap(2), qk_s5])

            S_.tag = tg + "uvo"
            for which, bidx in (("u", B_U), ("v", B_V), ("vm", B_VM), ("o", B_O)):
                W, wk = next_block(bidx)
                Wv = W[:, 0:4096].rearrange("p (k n) -> p k n", k=8)
                for c in range(NCH):
                    ps, pk = PB.next()
                    for k in range(8):
                        A("pe", MM(ps[:, 0:512], hT[:, k, c * 128:(c + 1) * 128], Wv[:, k, :], k == 0, k == 7), reads=[wk, ("hT", k)], writes=[pk])
                    if which == "u":
                        A("act", ACT(ug[:, c, :], ps[:, 0:512], AF.Gelu), reads=[pk], writes=[("ug", c)])
                    elif which == "v":
                        A("act", ACT(vg[:, c, :], ps[:, 0:512], AF.Gelu), reads=[pk], writes=[("vg", c)])
                        A("dve", lambda e, c=c: e.bn_stats(out=st6[:, c, :], in_=vg[:, c, :]), reads=[("vg", c)], writes=[("st6", c)])
                        A("dve", lambda e, c=c: e.bn_aggr(out=mv[:, c, :], in_=st6[:, c, :]), reads=[("st6", c)], writes=[("mv", c)])
                    elif which == "vm":
                        for h in range(4):
                            A("act", ACT(vaug[:, c, h, 0:128], ps[:, h * 128:(h + 1) * 128], AF.Copy, scale=sc[:, c, h:h + 1]), reads=[pk, ("vaug", c), "sc"], writes=[("vaug", c)])
                        A("pool", CP(vaug[:, c, :, 128], sc[:, c, :]), reads=["sc", ("vaug", c)], writes=[("vaug", c)])
                    else:
                        A("act", ACT(so[:, c, :], ps[:, 0:512], AF.Sigmoid), reads=[pk], writes=[("so", c)])
                if which == "v":
                    A("act", ACT(lvar[:], mv[:, :, 1], AF.Ln, bias=epsc[:, 0:1]), reads=[("mv", c) for c in range(NCH)] + ["epsc"], writes=["lvar"])
                    A("act", ACT(lrstd[:], lvar[:], AF.Exp, scale=-0.5), reads=["lvar"], writes=["lrstd"])
                    for c in range(NCH):
                        v_ = vt[c % 2]
                        vk = ("vt", c % 2)
                        A("dve", TS(v_[:], vg[:, c, :], mv[:, c, 0:1], lrstd[:, c:c + 1], ALU.subtract, ALU.mult),
                          reads=[("vg", c), ("mv", c), "lrstd"], writes=[vk])
                        A("pool", TT(v_[:], v_[:], lngb[:, 0, :], ALU.mult), reads=[vk, "lngb"], writes=[vk])
                        A("pool", TT(vn[:, c, :], v_[:], lngb[:, 1, :], ALU.add), reads=[vk, "lngb"], writes=[("vn", c)])

            Wws, wsk = next_block(B_WS)
            wsv = Wws[:, 0:1024].rearrange("p (h t) -> p h t", h=8)
            stg = {}

            def mix_P1(c):
                S_.tag = tg + "mixP1"
                cs = slice(c * 128, (c + 1) * 128)
                gi = c % 2
                psm, pmk = PB.next()
                A("pe", MM(psm[:, 0:512], bs128[:, l, :], sel[:], True, False), reads=["bs128", "sel"], writes=[pmk])
                for h in range(8):
                    A("pe", MM(psm[:, h * 64:(h + 1) * 64], wsv[:, h, :], vn[:, c, h * 64:(h + 1) * 64], False, h == 7), reads=[wsk, ("vn", c)], writes=[pmk])
                st, stk = PS.next()
                for h in range(4):
                    A("pe", MM(st[:, h * 128:(h + 1) * 128], qkT[:, 4 + h, cs], qkT[:, h, cs], True, True), reads=[("qkT", 4 + h), ("qkT", h)], writes=[stk])
                kt, ktk = PT.next()
                for h in range(4):
                    A("pe", TR(kt[:, h * 128:(h + 1) * 128], qkT[:, 4 + h, cs], ident[:]), reads=[("qkT", 4 + h), "ident"], writes=[ktk])
                A("dve", TT(gm[gi][:], psm[:, 0:512], ug[:, c, :], ALU.mult), reads=[pmk, ("ug", c)], writes=[("gm", gi)])
                A("dve", TT(Amat[gi][:], st[:, 0:512].rearrange("p (h t) -> p h t", h=4), trib4[:], ALU.mult),
                  reads=[stk, "trib4"], writes=[("Amat", gi, h) for h in range(4)])
                A("dve", CP(Kp[gi][:], kt[:, 0:512].rearrange("p (h t) -> p h t", h=4)), reads=[ktk], writes=[("Kp", gi, h) for h in range(4)])

            def mix_P2(c):
                S_.tag = tg + "mixP2"
                cs = slice(c * 128, (c + 1) * 128)
                gi = c % 2
                pt, ptk = PT.next()
                for j in range(4):
                    A("pe", TR(pt[:, j * 128:(j + 1) * 128], gm[gi][:, j * 128:(j + 1) * 128], ident[:]), reads=[("gm", gi), "ident"], writes=[ptk])
                A("act", ACT(hT[:, 0:4, cs], pt[:, 0:512].rearrange("p (a b) -> p a b", a=4), AF.Copy), reads=[ptk], writes=HT[0:4])
                for pr in range(2):
                    nd, ndk = PB.next()
                    up, upk = PB.next()
                    for j in range(2):
                        h = 2 * pr + j
                        A("pe", MM(nd[:, j * 129:(j + 1) * 129], Amat[gi][:, h, :], vaug[:, c, h, 0:129], True, False), reads=[("Amat", gi, h), ("vaug", c)], writes=[ndk])
                        A("pe", MM(nd[:, j * 129:(j + 1) * 129], qkT[:, h, cs], Cbf[:, l, h, 0:129], False, True), reads=[("qkT", h), ("Cbf", l, h)], writes=[ndk])
                    for j in range(2):
                        h = 2 * pr + j
                        A("pe", MM(up[:, j * 129:(j + 1) * 129], Kp[gi][:, h, :], vaug[:, c, h, 0:129], True, True), reads=[("Kp", gi, h), ("vaug", c)], writes=[upk])
                    d_ = dtmp[gi]
                    dk = ("dtmp", gi, pr)
                    dsl = d_[:, 2 * pr:2 * pr + 2]
                    A("act", ACT(dsl, nd[:, 0:258].rearrange("p (j d) -> p j d", j=2)[:, :, 128], AF.Abs), reads=[ndk], writes=[dk])
                    A("dve", TT(dsl, dsl, einv[:, c, 2 * pr:2 * pr + 2], ALU.max), reads=[dk, "einv"], writes=[dk])
                    A("dve", lambda e, dsl=dsl: e.reciprocal(out=dsl, in_=dsl), reads=[dk], writes=[dk])
                    for j in range(2):
                        h = 2 * pr + j
                        A("dve", STT(hc[gi][:, h * 128:(h + 1) * 128], nd[:, j * 129:j * 129 + 128], d_[:, h:h + 1], so[:, c, h * 128:(h + 1) * 128], ALU.mult, ALU.mult),
                          reads=[ndk, dk, ("so", c)], writes=[("hc", gi, h)])
                    for j in range(2):
                        h = 2 * pr + j
                        prev = ebL[:, c - 1, h:h + 1] if c > 0 else ebLs[:, l, h:h + 1]
                        A("dve", STT(Dst[:, l, h, 0:129], Dst[:, l, h, 0:129], prev, up[:, j * 129:(j + 1) * 129], ALU.mult, ALU.add),
                          reads=[upk, ("Dst", l, h), "ebL", "ebLs"], writes=[("Dst", l, h)])
                        A("act", ACT(Cbf[:, l, h, 0:129], Dst[:, l, h, 0:129], AF.Copy, scale=ebL[:, c, h:h + 1]), reads=[("Dst", l, h), "ebL"], writes=[("Cbf", l, h)])
                for h in range(4):
                    A("dve", lambda e, h=h, gi=gi: e.scalar_tensor_tensor(out=junk[:], in0=hc[gi][:, h * 128:(h + 1) * 128], scalar=1.0, in1=hc[gi][:, h * 128:(h + 1) * 128],
                                                                          op0=ALU.mult, op1=ALU.mult, accum_out=ss[gi][:, h:h + 1]),
                      reads=[("hc", gi, h)], writes=["junk", ("ss", gi)])
                A("act", ACT(lss[gi][:], ss[gi][:], AF.Ln, scale=1.0 / 128.0, bias=epsc[:, 0:1]), reads=[("ss", gi), "epsc"], writes=[("lss", gi)])
                A("act", ACT(hrs[gi][:], lss[gi][:], AF.Exp, scale=-0.5), reads=[("lss", gi)], writes=[("hrs", gi)])
                for h in range(4):
                    A("act", ACT(ml[gi][:, h * 128:(h + 1) * 128], hc[gi][:, h * 128:(h + 1) * 128], AF.Copy, scale=hrs[gi][:, h:h + 1]),
                      reads=[("hc", gi, h), ("hrs", gi)], writes=[("ml", gi)])

            def mix_P3(c):
                S_.tag = tg + "mixP3"
                cs = slice(c * 128, (c + 1) * 128)
                gi = c % 2
                pt, ptk = PT.next()
                for h in range(4):
                    A("pe", TR(pt[:, h * 128:(h + 1) * 128], ml[gi][:, h * 128:(h + 1) * 128], ident[:]), reads=[("ml", gi), "ident"], writes=[ptk])
                for h in range(4):
                    A("act", ACT(hT[:, 4 + h, cs], pt[:, h * 128:(h + 1) * 128], AF.Copy, scale=mlng[:, l * 4 + h:l * 4 + h + 1]), reads=[ptk] + SD, writes=[("hT", 4 + h)])

            mix_P1(0)
            for c in range(NCH):
                if c + 1 < NCH:
                    mix_P1(c + 1)
                mix_P2(c)
                if c >= 1:
                    mix_P3(c - 1)
            mix_P3(NCH - 1)
            A("dve", CP(ebLs[:, l, :], ebL[:, NCH - 1, :]), reads=["ebL", "ebLs"], writes=["ebLs"])

            S_.tag = tg + "wout"
            Wo = {}
            nst2 = norm_begin()

            def wout_s0(f):
                wbi, j = divmod(f, 4)
                if j == 0:
                    W, wk = next_block(B_OUT + wbi)
                    Wo[wbi] = (W[:, 0:4096].rearrange("p (k n) -> p k n", k=8), wk)
                Wv, wk = Wo[wbi]
                ps, pk = PB.next()
                for k in range(8):
                    A("pe", MM(ps[:, 0:NT], Wv[:, k, j * 128:(j + 1) * 128], hT[:, k, :], k == 0, k == 7), reads=[wk, ("hT", k)], writes=[pk])
                A("dve", TT(xT[:, f, :], ps[:, 0:NT], xT[:, f, :], ALU.add), reads=[pk, ("x", f)], writes=[("x", f)])

            pipeline(8, [wout_s0, lambda f: None, lambda f: norm_accum(nst2, f)])

            S_.tag = tg + "norm2"
            norm_finish(nst2, l * 24 + 8)
            S_.tag = tg + "wup"
            Wu = {}
            pbk_ = {}

            def up_pre(fi):
                A("pool", CP(raw[fi % 4][:, 0:2], halo[:, l, fi, :]), reads=["halo"], writes=[("raw", fi % 4)])

            def up_s0(fi):
                i, jj = divmod(fi, 4)
                if jj == 0:
                    W, wk = next_block(B_UP + i)
                    Wu[i] = (W[:, 0:4096].rearrange("p (k n) -> p k n", k=8), wk)
                Wv, wk = Wu[i]
                ps, pk = PB.next()
                pbk_[fi] = (ps, pk)
                for k in range(8):
                    A("pe", MM(ps[:, 0:NT], Wv[:, k, jj * 128:(jj + 1) * 128], hT[:, k, :], k == 0, k == 7), reads=[wk, ("hT", k)], writes=[pk])
                r_ = raw[fi % 4]
                rk_ = ("raw", fi % 4)
                A("act", ACT(r_[:, 2:2 + NT], ps[:, 0:NT], AF.Copy), reads=[pk, rk_], writes=[rk_])
                if t == 0:
                    emit_prep(l + 1, 2)

            def up_s1(fi):
                A("act", ACT(facc[fi % 3][:], raw[fi % 4][:, 0:NT], AF.Identity, scale=cwff[:, l, fi, 0:1], bias=cbff[:, l, fi:fi + 1]),
                  reads=[("raw", fi % 4)] + SD, writes=[("facc", fi % 3)])
                A("pool", CP(halo[:, l, fi, :], raw[fi % 4][:, NT:NT + 2]), reads=[("raw", fi % 4)], writes=["halo"])

            def up_s2(fi):
                ps, pk = pbk_.pop(fi)
                fa = facc[fi % 3]
                fk = ("facc", fi % 3)
                A("dve", STT(fa[:], ps[:, 0:NT], cwff[:, l, fi, 2:3], fa[:], ALU.mult, ALU.add), reads=[pk, fk] + SD, writes=[fk])

            def up_s3(fi):
                fa = facc[fi % 3]
                fk = ("facc", fi % 3)
                A("dve", STT(fa[:], raw[fi % 4][:, 1:1 + NT], cwff[:, l, fi, 1:2], fa[:], ALU.mult, ALU.add), reads=[("raw", fi % 4), fk] + SD, writes=[fk])

            def up_s4(fi):
                i, jj = divmod(fi, 4)
                fa = facc[fi % 3]
                fk = ("facc", fi % 3)
                if jj < 2:
                    A("act", ACT(sa[jj][:], fa[:], AF.Silu), reads=[fk], writes=[("sa", jj)])
                else:
                    A("pool", TT(actT[:, 2 * i + jj - 2, :], sa[jj - 2][:], fa[:], ALU.mult), reads=[("sa", jj - 2), fk], writes=[("actT", 2 * i + jj - 2)])

            pipeline(44, [up_pre, up_s0, up_s1, up_s2, up_s3, up_s4])
            if t == 0:
                emit_prep(l + 1, 10 ** 6)

            S_.tag = tg + "wdown"
            nst3 = norm_begin()

            def wdown_s0(f):
                W, wk = next_block(B_DOWN + f)
                Wv = W[:, 0:22 * 128].rearrange("p (k n) -> p k n", k=22)
                ps, pk = PB.next()
                for k in range(22):
                    A("pe", MM(ps[:, 0:NT], Wv[:, k, :], actT[:, k, :], k == 0, k == 21), reads=[wk, ("actT", k)], writes=[pk])
                A("dve", TT(xT[:, f, :], ps[:, 0:NT], xT[:, f, :], ALU.add), reads=[pk, ("x", f)], writes=[("x", f)])

            pipeline(8, [wdown_s0, lambda f: None, lambda f: norm_accum(nst3, f)])

            S_.tag = tg + "norm3"
            norm_finish(nst3, l * 24 + 16)
            S_.tag = tg + "ple"
            Wgp = {}
            nst_next = norm_begin()

            for it in range(8 + 3):
                f = it
                fa_ = it - 3
                if 0 <= fa_ < 8:
                    A("act", ACT(sqr[fa_ % 2][:], xT[:, fa_, :], AF.Square), reads=[("x", fa_)], writes=[("sq", fa_ % 2)])
                if f < 8:
                    jb, jj = divmod(f, 2)
                    if jj == 0:
                        W, wk = next_block(B_GP + jb)
                        Wgp[jb] = (W[:, 0:2048].rearrange("p (k n) -> p k n", k=8), W[:, 2048:2560].rearrange("p (k n) -> p k n", k=2), wk)
                    Wg, Wp, wk = Wgp[jb]
                    psg2, pgk2 = PB.next()
                    for k in range(8):
                        A("pe", MM(psg2[:, 0:NT], Wg[:, k, jj * 128:(jj + 1) * 128], hT[:, k, :], k == 0, k == 7), reads=[wk, ("hT", k)], writes=[pgk2])
                if 0 <= fa_ < 8:
                    ps_n, pk_n = nst_next
                    A("pe", MM(ps_n[:, 0:NT], onesm[:], sqr[fa_ % 2][:], fa_ == 0, fa_ == 7), reads=[("sq", fa_ % 2), "onesm"], writes=[pk_n])
                if f < 8:
                    g_ = gt[f % 2]
                    gk_ = ("gt", f % 2)
                    A("act", ACT(g_[:], psg2[:, 0:NT], AF.Sigmoid), reads=[pgk2], writes=[gk_])
                    psp, ppk = PB.next()
                    for k in range(2):
                        A("pe", MM(psp[:, 0:NT], Wp[:, k, jj * 128:(jj + 1) * 128], pT[:, k, :], k == 0, k == 1), reads=[wk, "pT"], writes=[ppk])
                    A("dve", TT(g_[:], psp[:, 0:NT], g_[:], ALU.mult), reads=[ppk, gk_], writes=[gk_])
                    A("dve", TT(xT[:, f, :], xT[:, f, :], g_[:], ALU.add), reads=[gk_, ("x", f)], writes=[("x", f)])
            return nst_next

        for t in range(n_tiles):
            S_.tag = "xload"
            A("sp", DMA(xT[:], xin_v[:, :, t * NT:(t + 1) * NT]), writes=XK, dma_sem=s_x)
            nst = None
            for l in range(n_layers):
                nst = layer_step(t, l, nst)
            S_.tag = "final"
            norm_finish(nst, NL * 24, final_t0=t * NT)

        A("sp", lambda e: e.nop(), reads=["yout"])

        S_.analyze(qsems)
        with nc.Block() as block:
            S_.emit(block)
    return nc


def _pk(v):
    v = np.asarray(v, np.float32)
    return np.ascontiguousarray(v.reshape(-1, 128).T)


def ffn_col_order():
    cols = []
    for i in range(11):
        for j in (2 * i, 2 * i + 1):
            cols.append(np.arange(128 * j, 128 * j + 128))
        for j in (2 * i, 2 * i + 1):
            cols.append(DFF + np.arange(128 * j, 128 * j + 128))
    return np.stack(cols)


def shared_inputs(inp):
    f32 = np.float32
    d = {}
    for nm in ("w_in", "w_out", "w_up", "w_down", "w_ple"):
        d[nm] = np.ascontiguousarray(inp[nm], f32)
    d["w_gate"] = np.ascontiguousarray(inp["w_ple_gate"], f32)
    cols = []
    for i in range(NL):
        cols += [_pk(inp["g_mix"][i]), _pk(inp["g_ffn"][i]), _pk(inp["g_ple"][i])]
    cols.append(_pk(inp["g_final"]))
    d["gv"] = np.ascontiguousarray(np.concatenate(cols, axis=1))
    d["mlng"] = np.ascontiguousarray(np.concatenate([_pk(inp["ml_norm_g"][i]) for i in range(NL)], axis=1))
    lngb = np.stack([np.stack([inp["gm_ln_g"][i], inp["gm_ln_b"][i]]) for i in range(NL)])
    d["lngb"] = np.ascontiguousarray(np.broadcast_to(lngb[:, None], (NL, 128, 2, 512)), f32)
    d["wsT"] = np.ascontiguousarray(np.transpose(np.asarray(inp["gm_ws"], f32), (0, 3, 1, 2)))
    d["bs8"] = np.ascontiguousarray(np.transpose(np.asarray(inp["gm_bs"], f32), (1, 0, 2)))
    cw = np.asarray(inp["ml_conv_w"], f32)
    d["cwqk"] = np.ascontiguousarray(np.transpose(cw.reshape(NL, 4, 8, 128), (3, 0, 2, 1)))
    cb = np.asarray(inp["ml_conv_b"], f32)
    d["cbqk"] = np.ascontiguousarray(np.transpose(cb.reshape(NL, 8, 128), (2, 0, 1)))
    order = ffn_col_order()
    fw = np.asarray(inp["ffn_conv_w"], f32)
    d["cwff"] = np.ascontiguousarray(np.transpose(fw[:, :, order], (3, 0, 2, 1)))
    fb = np.asarray(inp["ffn_conv_b"], f32)
    d["cbff"] = np.ascontiguousarray(np.transpose(fb[:, order], (2, 0, 1)))
    bif = np.concatenate([np.asarray(inp["ml_b_i"], f32), np.asarray(inp["ml_b_f"], f32)], axis=1)
    d["bif"] = np.ascontiguousarray(np.broadcast_to(bif[None], (128, NL, 8)), f32)
    d["ident"] = np.eye(128, dtype=f32)
    d["tri"] = np.triu(np.ones((128, 128), f32))
    sel = np.zeros((128, 512), f32)
    for h in range(8):
        sel[h, h * 64:(h + 1) * 64] = 1.0
    d["sel"] = sel
    return d


_PROG = {}


def get_prog(S, n_layers=NL):
    if (S, n_layers) not in _PROG:
        _PROG[(S, n_layers)] = build_program(S, n_layers)
    return _PROG[(S, n_layers)]


def kernel(**inp):
    x = np.asarray(inp["x"], np.float32)
    p = np.asarray(inp["p"], np.float32)
    B, S, _ = x.shape
    sh = shared_inputs(inp)
    maps = []
    for b in range(B):
        m = dict(sh)
        m["xin"] = np.ascontiguousarray(x[b].T)
        m["pin"] = np.ascontiguousarray(np.transpose(p[:, b], (0, 2, 1)))
        maps.append(m)
    res = run_bass_kernel_spmd(get_prog(S), maps, core_ids=list(range(B)))
    return np.stack([np.asarray(res.results[b]["yout"]).T for b in range(B)]).astype(np.float32)
```

```python
import math
import numpy as np
from contextlib import ExitStack
import concourse.bass as bass
import concourse.mybir as mybir
from concourse.bass_utils import run_bass_kernel_spmd

F32 = mybir.dt.float32
BF16 = mybir.dt.bfloat16
AF = mybir.ActivationFunctionType
ALU = mybir.AluOpType

D = 1024
NT = 512
NCH = NT // 128
DFF = 2816
EPS = 1e-6
NSLOT = 4
WBLK = 4096
SAME_ENGINE_SYNC = True
SAME_ENGINE_ALL = True
SAME_ENGINE_SYNC_Q = {"act", "dve", "pool"}
ANNOTATE = False


class Op:
    __slots__ = ("q", "fn", "reads", "writes", "sem", "inc", "needs_inc", "count", "waits", "tag")


class Sched:
    def __init__(self):
        self.ops = []
        self.tag = None

    def add(self, q, fn, reads=(), writes=(), dma_sem=None):
        o = Op()
        o.q, o.fn, o.reads, o.writes = q, fn, tuple(reads), tuple(writes)
        o.sem = dma_sem
        o.inc = 16 if dma_sem is not None else 1
        o.needs_inc = dma_sem is not None
        o.count = None
        o.waits = {}
        o.tag = self.tag
        self.ops.append(o)
        return o

    def analyze(self, qsems):
        last_w, readers, need = {}, {}, []
        for o in self.ops:
            deps = {}
            for r in o.reads:
                w = last_w.get(r)
                if w is not None:
                    deps[id(w)] = (w, True)
            for k in o.writes:
                w = last_w.get(k)
                if w is not None and id(w) not in deps:
                    deps[id(w)] = (w, False)
                for rd in readers.get(k, ()):
                    if id(rd) not in deps:
                        deps[id(rd)] = (rd, False)
            deps.pop(id(o), None)
            nd = []
            for a, raw in deps.values():
                if a.sem is None and a.q == o.q:
                    if o.q == "pe" or not SAME_ENGINE_SYNC or o.q not in SAME_ENGINE_SYNC_Q:
                        continue
                    if not raw and not SAME_ENGINE_ALL:
                        continue
                nd.append(a)
                a.needs_inc = True
            need.append(nd)
            for r in o.reads:
                readers.setdefault(r, []).append(o)
            for k in o.writes:
                last_w[k] = o
                readers[k] = []
        cnt = {}
        for o in self.ops:
            if o.needs_inc:
                s = o.sem if o.sem is not None else qsems[o.q]
                cnt[s] = cnt.get(s, 0) + o.inc
                o.count = (s, cnt[s])
        waited = {}
        for o, nd in zip(self.ops, need):
            w = {}
            for a in nd:
                s, c = a.count
                if c > w.get(s, 0):
                    w[s] = c
            qw = waited.setdefault(o.q, {})
            for s, c in list(w.items()):
                if qw.get(s, 0) >= c:
                    del w[s]
                else:
                    qw[s] = c
            o.waits = w
        self.final_counts = cnt

    def emit(self, block):
        byq = {}
        for o in self.ops:
            byq.setdefault(o.q, []).append(o)

        def run(eng, ops):
            for o in ops:
                for s, c in o.waits.items():
                    eng.wait_ge(s, c)
                ins = o.fn(eng)
                if ANNOTATE and o.tag is not None:
                    ins.annotate(o.tag)
                if o.needs_inc:
                    ins.then_inc(o.count[0], o.inc)

        names = {"pe": "tensor", "act": "scalar", "dve": "vector", "pool": "gpsimd", "sp": "sync"}
        for q in ["sp", "pe", "act", "dve", "pool"]:
            if q in byq:
                getattr(block, names[q])(lambda eng, ops=byq[q]: run(eng, ops))


class Ring:
    def __init__(self, items):
        self.items = items
        self.i = 0

    def next(self):
        it = self.items[self.i % len(self.items)]
        self.i += 1
        return it


NL = 4


def weight_blocks():
    b = []
    for i in range(2):
        b.append([("w_in", 8, 1024 + 512 * i, 512, 0, 512, 0)])
    for c0 in (0, 512, 2048, 2560):
        b.append([("w_in", 8, c0, 512, 0, 512, 0)])
    b.append(None)
    for i in range(2):
        b.append([("w_out", 8, 512 * i, 512, 0, 512, 0)])
    for i in range(11):
        b.append([("w_up", 8, 256 * i, 256, 0, 512, 0), ("w_up", 8, DFF + 256 * i, 256, 0, 512, 256)])
    for f in range(8):
        b.append([("w_down", 22, 128 * f, 128, 0, 128, 0)])
    for j in range(4):
        b.append([("w_gate", 8, 256 * j, 256, 0, 256, 0), ("w_ple", 2, 256 * j, 256, 2048, 256, 0)])
    return b


WB = weight_blocks()
NWB = len(WB)
B_QK, B_U, B_V, B_VM, B_O, B_WS, B_OUT, B_UP, B_DOWN, B_GP = 0, 2, 3, 4, 5, 6, 7, 9, 20, 28
WB_SIZE = []
for _b in WB:
    if _b is None:
        WB_SIZE.append(1024)
    else:
        WB_SIZE.append(max(base + K * W for (_, K, _, _, base, W, _) in _b))


def MM(out, lhsT, rhs, start, stop):
    return lambda e: e.matmul(out, lhsT=lhsT, rhs=rhs, start=start, stop=stop)


def TR(out, in_, ident):
    return lambda e: e.transpose(out, in_, ident)


def ACT(out, in_, func, **kw):
    return lambda e: e.activation(out=out, in_=in_, func=func, **kw)


def TT(out, in0, in1, op):
    return lambda e: e.tensor_tensor(out=out, in0=in0, in1=in1, op=op)


def TS(out, in0, s1, s2, op0, op1=None):
    if op1 is None:
        return lambda e: e.tensor_scalar(out=out, in0=in0, scalar1=s1, scalar2=None, op0=op0)
    return lambda e: e.tensor_scalar(out=out, in0=in0, scalar1=s1, scalar2=s2, op0=op0, op1=op1)


def TSS(out, in_, scalar, op):
    return lambda e: e.tensor_single_scalar(out=out, in_=in_, scalar=scalar, op=op)


def STT(out, in0, scalar, in1, op0, op1):
    return lambda e: e.scalar_tensor_tensor(out=out, in0=in0, scalar=scalar, in1=in1, op0=op0, op1=op1)


def CP(out, in_):
    return lambda e: e.tensor_copy(out=out, in_=in_)


def MS(out, val):
    return lambda e: e.memset(out, val)


def DMA(out, in_):
    return lambda e: e.dma_start(out=out, in_=in_)


def pipeline(n, stages):
    ns = len(stages)
    for i in range(n + ns - 1):
        for si in reversed(range(ns)):
            c = i - si
            if 0 <= c < n:
                stages[si](c)


def build_program(S, n_layers=NL):
    n_tiles = S // NT
    nc = bass.Bass("TRN2", target_bir_lowering=False)

    def din(name, shape):
        return nc.dram_tensor(name, list(shape), F32, kind="ExternalInput").ap()

    xin = din("xin", [D, S])
    pin = din("pin", [NL, 256, S])
    wsrc = {
        "w_in": din("w_in", [NL, D, 3080]), "w_out": din("w_out", [NL, D, D]), "w_up": din("w_up", [NL, D, 2 * DFF]),
        "w_down": din("w_down", [NL, DFF, D]), "w_gate": din("w_gate", [NL, D, D]), "w_ple": din("w_ple", [NL, 256, D]),
    }
    gv_d = din("gv", [128, NL * 24 + 8])
    mlng_d = din("mlng", [128, NL * 4])
    lngb_d = din("lngb", [NL, 128, 2, 512])
    wsT_d = din("wsT", [NL, 128, 8, 128])
    bs8_d = din("bs8", [8, NL, 128])
    cwqk_d = din("cwqk", [128, NL, 8, 4])
    cbqk_d = din("cbqk", [128, NL, 8])
    cwff_d = din("cwff", [128, NL, 44, 3])
    cbff_d = din("cbff", [128, NL, 44])
    bif_d = din("bif", [128, NL, 8])
    ident_d = din("ident", [128, 128])
    tri_d = din("tri", [128, 128])
    sel_d = din("sel", [128, 512])
    yout = nc.dram_tensor("yout", [D, S], F32, kind="ExternalOutput").ap()
    wb_d = nc.dram_tensor("wb_scratch", [NL, NWB, 128, WBLK], BF16).ap()

    xin_v = xin.rearrange("(f p) s -> p f s", p=128)
    yout_v = yout.rearrange("(f p) s -> p f s", p=128)

    with ExitStack() as es:
        def sb(name, shape, dt=F32):
            return es.enter_context(nc.sbuf_tensor("sb_" + name, list(shape), dt))

        def psum(name, shape, dt=F32):
            return es.enter_context(nc.psum_tensor(name, list(shape), dt))

        def sem(name):
            return es.enter_context(nc.semaphore(name))

        qsems = {q: sem("q_" + q) for q in ["pe", "act", "dve", "pool"]}
        S_ = Sched()
        A = S_.add

        xT = sb("xT", [128, 8, NT])
        hT = sb("hT", [128, 8, NT], BF16)
        sqr = [sb(f"sq{i}", [128, NT], BF16) for i in range(2)]
        rstd = sb("rstd", [128, NT])
        ntmp = [sb(f"ntmp{i}", [128, NT]) for i in range(2)]
        qkraw = sb("qkraw", [128, 8, NT + 4])
        qkT = sb("qkT", [128, 8, NT], BF16)
        cacc = [sb(f"cacc{i}", [128, NT]) for i in range(4)]
        gpre = sb("gpre", [128, NCH, 8])
        e1 = sb("e1", [128, NCH, 4])
        l1 = sb("l1", [128, NCH, 4])
        t1 = sb("t1", [128, NCH, 4])
        sc = sb("sc", [128, NCH, 4])
        einv = sb("einv", [128, NCH, 4])
        ebL = sb("ebL", [128, NCH, 4])
        ebLs = sb("ebLs", [128, NL, 4])
        ug = sb("ug", [128, NCH, 512], BF16)
        vg = sb("vg", [128, NCH, 512], BF16)
        vt = [sb(f"vt{i}", [128, 512]) for i in range(2)]
        vn = sb("vn", [128, NCH, 512], BF16)
        st6 = sb("st6", [128, NCH, 6])
        mv = sb("mv", [128, NCH, 2])
        lvar = sb("lvar", [128, NCH])
        lrstd = sb("lrstd", [128, NCH])
        vaug = sb("vaug", [128, NCH, 4, 132], BF16)
        so = sb("so", [128, NCH, 512], BF16)
        gm = [sb(f"gm{i}", [128, 512], BF16) for i in range(2)]
        Amat = [sb(f"Amat{i}", [128, 4, 128], BF16) for i in range(2)]
        Kp = [sb(f"Kp{i}", [128, 4, 128], BF16) for i in range(2)]
        dtmp = [sb(f"dtmp{i}", [128, 4]) for i in range(2)]
        hc = [sb(f"hc{i}", [128, 512]) for i in range(2)]
        junk = sb("junk", [128, 128])
        ss = [sb(f"ss{i}", [128, 4]) for i in range(2)]
        lss = [sb(f"lss{i}", [128, 4]) for i in range(2)]
        hrs = [sb(f"hrs{i}", [128, 4]) for i in range(2)]
        ml = [sb(f"ml{i}", [128, 512], BF16) for i in range(2)]
        Dst = sb("Dst", [128, NL, 4, 132])
        Cbf = sb("Cbf", [128, NL, 4, 132], BF16)
        raw = [sb(f"raw{i}", [128, NT + 2]) for i in range(4)]
        facc = [sb(f"facc{i}", [128, NT]) for i in range(3)]
        sa = [sb(f"sa{i}", [128, NT]) for i in range(2)]
        actT = sb("actT", [128, 22, NT], BF16)
        halo = sb("halo", [128, NL, 44, 2])
        qhalo = sb("qhalo", [128, NL, 8, 4])
        pT = sb("pT", [128, 2, NT], BF16)
        gt = [sb(f"gt{i}", [128, NT]) for i in range(2)]
        wring = sb("wring", [128, NSLOT, WBLK], BF16)
        lngb = sb("lngb", [128, 2, 512])
        ident = sb("ident", [128, 128], BF16)
        tri = sb("tri", [128, 128])
        trib = sb("trib", [128, 128], BF16)
        trib4 = sb("trib4", [128, 4, 128], BF16)
        onesf = sb("onesf", [128, 128])
        onesm = sb("onesm", [128, 128], BF16)
        sel = sb("sel", [128, 512], BF16)
        bs128 = sb("bs128", [128, NL, 128], BF16)
        gv = sb("gv", [128, NL * 24 + 8])
        mlng = sb("mlng", [128, NL * 4])
        cwqk = sb("cwqk", [128, NL, 8, 4])
        cbqk = sb("cbqk", [128, NL, 8])
        cwff = sb("cwff", [128, NL, 44, 3])
        cbff = sb("cbff", [128, NL, 44])
        bif = sb("bif", [128, NL, 8])
        epsc = sb("epsc", [128, 1])
        wg = sb("wg", [128, NL, 8, 8], BF16)

        PBt = [psum(f"pb{i}", [128, 512]) for i in range(4)]
        PTt = [psum(f"pt{i}", [128, 1024], BF16) for i in range(2)]
        PSt = [psum(f"ps{i}", [128, 512]) for i in range(2)]
        PB = Ring([(PBt[i], ("pb", i)) for i in range(4)])
        PT = Ring([(PTt[i], ("pt", i)) for i in range(2)])
        PS = Ring([(PSt[i], ("ps", i)) for i in range(2)])

        s_setup = sem("d_setup")
        s_w = [sem(f"d_w{i}") for i in range(NSLOT)]
        s_x = sem("d_x")
        s_p = sem("d_p")
        s_ln = sem("d_ln")
        s_yo = [sem(f"d_yo{i}") for i in range(2)]
        s_prep = [sem(f"d_prep{i}") for i in range(4)]
        s_ws = sem("d_ws")
        s_wg = sem("d_wg")
        s_ws2 = sem("d_ws2")

        XK = [("x", f) for f in range(8)]
        HT = [("hT", f) for f in range(8)]

        S_.tag = "setup"

        def load_const(dst_ap, src_ap, keys):
            A("sp", DMA(dst_ap, src_ap), writes=keys, dma_sem=s_setup)

        load_const(gv[:], gv_d, ["gv"])
        load_const(mlng[:], mlng_d, ["mlng"])
        load_const(cwqk[:], cwqk_d, ["cwqk"])
        load_const(cbqk[:], cbqk_d, ["cbqk"])
        load_const(cwff[:], cwff_d, ["cwff"])
        load_const(cbff[:], cbff_d, ["cbff"])
        load_const(bif[:], bif_d, ["bif"])
        load_const(tri[:], tri_d, ["tri"])
        load_const(xT[:, 0, :], sel_d, [("x", 0)])
        load_const(xT[:, 1, 0:128], ident_d, [("x", 1)])
        load_const(xT[0:8, 2, :].rearrange("p (l t) -> p l t", l=NL), bs8_d, [("x", 2)])
        setup_keys = ["gv", "mlng", "cwqk", "cbqk", "cwff", "cbff", "bif", "tri"] + XK[0:3]
        A("pool", MS(epsc[:], EPS), reads=setup_keys, writes=["epsc", "setup_done"])
        SD = ["setup_done"]
        A("pool", CP(sel[:], xT[:, 0, :]), reads=SD + [("x", 0)], writes=["sel"])
        A("pool", CP(ident[:], xT[:, 1, 0:128]), reads=SD + [("x", 1)], writes=["ident"])
        A("pool", CP(trib[:], tri[:]), reads=SD, writes=["trib"])
        for h in range(4):
            A("pool", CP(trib4[:, h, :], tri[:]), reads=SD, writes=["trib4"])
        A("pool", MS(bs128[:], 0.0), reads=SD, writes=["bs128"])
        A("pool", CP(bs128[0:8, :, :], xT[0:8, 2, :].rearrange("p (l t) -> p l t", l=NL)), reads=SD + ["bs128", ("x", 2)], writes=["bs128"])
        A("dve", MS(onesf[:], 1.0), writes=["onesf"])
        A("dve", MS(onesm[:], 1.0 / D), writes=["onesm"])
        A("dve", MS(qhalo[:], 0.0), writes=["qhalo"])
        A("dve", MS(halo[:], 0.0), writes=["halo"])
        A("dve", MS(Dst[:], 0.0), writes=[("Dst", l, h) for l in range(NL) for h in range(4)])
        A("dve", MS(Cbf[:], 0.0), writes=[("Cbf", l, h) for l in range(NL) for h in range(4)])
        A("dve", MS(ebLs[:], 1.0), writes=["ebLs"])
        A("pool", MS(vaug[:], 1.0), writes=[("vaug", c) for c in range(NCH)])
        for l in range(n_layers):
            A("pool", DMA(wg[:, l, :, :], wsrc["w_in"][l].rearrange("(k p) n -> p k n", p=128)[:, :, 3072:3080]),
              writes=[("wg", l), "wgslot"], dma_sem=s_wg)
        xflat = xT[:, 4:6, :].rearrange("p a b -> p (a b)")
        hflat = hT[:, 0:2, :].rearrange("p a b -> p (a b)")
        for l in range(n_layers):
            A("sp", DMA(xflat.rearrange("p (h t) -> p h t", h=8), wsT_d[l]), writes=[("x", 4), ("x", 5)], dma_sem=s_ws)
            for h in range(8):
                A("dve", TT(hflat[:, h * 128:(h + 1) * 128], xflat[:, h * 128:(h + 1) * 128], tri[:], ALU.mult),
                  reads=SD + [("x", 4), ("x", 5)], writes=[("hT", 0), ("hT", 1)])
            A("sp", DMA(wb_d[l, B_WS, :, 0:1024], hflat), reads=[("hT", 0), ("hT", 1)], writes=[("wbd", l, B_WS, 0)], dma_sem=s_ws2)

        np_ = [0]
        wbd_keys = {}
        prep_queue = {l: [] for l in range(n_layers)}
        for l in range(n_layers):
            for bi, parts in enumerate(WB):
                if parts is None:
                    wbd_keys[(l, bi)] = [("wbd", l, bi, 0)]
                    continue
                keys = []
                for pi, (src, K, c0, n, base, W, off) in enumerate(parts):
                    srcv = wsrc[src][l].rearrange("(k p) n -> p k n", p=128)[:, :, c0:c0 + n]
                    dstv = wb_d[l, bi, :, base:base + K * W].rearrange("p (k n) -> p k n", k=K)[:, :, off:off + n]
                    key = ("wbd", l, bi, pi)
                    prep_queue[l].append((dstv, srcv, key))
                    keys.append(key)
                wbd_keys[(l, bi)] = keys

        def emit_prep(l, n):
            for _ in range(n):
                if l >= n_layers or not prep_queue[l]:
                    return
                dstv, srcv, key = prep_queue[l].pop(0)
                sl = np_[0] % 4
                np_[0] += 1
                A("pool", DMA(dstv, srcv), writes=[key, ("prepslot", sl)], dma_sem=s_prep[sl])

        emit_prep(0, 10 ** 6)

        gblk = [0]
        total_blocks = n_tiles * n_layers * NWB

        def issue_wload(g):
            if g >= total_blocks:
                return
            bi = g % NWB
            l = (g // NWB) % n_layers
            sl = g % NSLOT
            n = WB_SIZE[bi]
            A("sp", DMA(wring[:, sl, 0:n], wb_d[l, bi, :, 0:n]), reads=wbd_keys[(l, bi)], writes=[("wr", sl)], dma_sem=s_w[sl])

        def next_block(expect_bi):
            g = gblk[0]
            assert g % NWB == expect_bi, (g % NWB, expect_bi)
            issue_wload(g + NSLOT - 1)
            gblk[0] += 1
            sl = g % NSLOT
            return wring[:, sl, :], ("wr", sl)

        def norm_begin():
            return PS.next()

        def norm_accum(nst, f):
            ps, pk = nst
            i = f % 2
            A("act", ACT(sqr[i][:], xT[:, f, :], AF.Square), reads=[("x", f)], writes=[("sq", i)])
            A("pe", MM(ps[:, 0:NT], onesm[:], sqr[i][:], f == 0, f == 7), reads=[("sq", i), "onesm"], writes=[pk])

        def norm_finish(nst, gcol, final_t0=None):
            ps, pk = nst
            A("act", ACT(rstd[:], ps[:, 0:NT], AF.Ln, bias=epsc[:, 0:1]), reads=[pk, "epsc"], writes=["rstd"])
            A("act", ACT(rstd[:], rstd[:], AF.Exp, scale=-0.5), reads=["rstd"], writes=["rstd"])
            for f in range(8):
                if final_t0 is not None:
                    i = f % 2
                    A("dve", STT(ntmp[i][:], xT[:, f, :], gv[:, gcol + f:gcol + f + 1], rstd[:], ALU.mult, ALU.mult),
                      reads=[("x", f), "rstd"] + SD, writes=[("ntmp", i)])
                    A("sp", DMA(yout_v[:, f, final_t0:final_t0 + NT], ntmp[i][:]), reads=[("ntmp", i)], writes=["yout"], dma_sem=s_yo[i])
                else:
                    A("dve", STT(hT[:, f, :], xT[:, f, :], gv[:, gcol + f:gcol + f + 1], rstd[:], ALU.mult, ALU.mult),
                      reads=[("x", f), "rstd"] + SD, writes=[("hT", f)])

        def norm(gcol, final_t0=None):
            nst = norm_begin()
            for f in range(8):
                norm_accum(nst, f)
            norm_finish(nst, gcol, final_t0)

        for g in range(NSLOT - 1):
            issue_wload(g)

        def layer_step(t, l, nst_in):
            t0 = t * NT
            tg = f"L{l}"
            S_.tag = tg + "norm1"
            A("pool", DMA(pT[:], pin[l].rearrange("(k p) s -> p k s", p=128)[:, :, t0:t0 + NT]), writes=["pT"], dma_sem=s_p)
            A("sp", DMA(lngb[:], lngb_d[l]), writes=["lngb"], dma_sem=s_ln)

            if nst_in is None:
                norm(l * 24)
            else:
                norm_finish(nst_in, l * 24)
            S_.tag = tg + "gates"
            psg, pgk = PS.next()
            for c in range(NCH):
                for k in range(8):
                    A("pe", MM(psg[:, c * 8:(c + 1) * 8], hT[:, k, c * 128:(c + 1) * 128], wg[:, l, k, :], k == 0, k == 7),
                      reads=[("hT", k), ("wg", l)], writes=[pgk])
            for c in range(NCH):
                A("dve", TT(gpre[:, c, :], psg[:, c * 8:(c + 1) * 8], bif[:, l, :], ALU.add), reads=[pgk] + SD, writes=["gpre"])
            A("act", ACT(e1[:], gpre[:, :, 4:8], AF.Exp, scale=-1.0), reads=["gpre"], writes=["e1"])
            A("act", ACT(l1[:], e1[:], AF.Ln, bias=1.0), reads=["e1"], writes=["l1"])
            def gates_tail():
                S_.tag = tg + "gates"
                psb, pbk = PS.next()
                l1v = l1[:].rearrange("p c h -> p (c h)")
                A("pe", MM(psb[:, 0:16], tri[:], l1v, True, True), reads=["l1"] + SD, writes=[pbk])
                A("pe", MM(psb[:, 16:32], onesf[:], l1v, True, True), reads=["l1", "onesf"], writes=[pbk])
                bneg = psb[:, 0:16].rearrange("p (c h) -> p c h", c=NCH)
                bLneg = psb[:, 16:32].rearrange("p (c h) -> p c h", c=NCH)
                A("dve", STT(t1[:], gpre[:, :, 0:4], math.log(128.0 ** -0.5), bneg, ALU.add, ALU.add), reads=["gpre", pbk], writes=["t1"])
                A("act", ACT(sc[:], t1[:], AF.Exp), reads=["t1"], writes=["sc"])
                A("act", ACT(einv[:], bneg, AF.Exp), reads=[pbk], writes=["einv"])
                A("act", ACT(ebL[:], bLneg, AF.Exp, scale=-1.0), reads=[pbk], writes=["ebL"])
                S_.tag = tg + "qk"

            S_.tag = tg + "qk"
            Wq = {}
            qpb = {}

            def qk_s0(f):
                wbi, j = divmod(f, 4)
                if j == 0:
                    W, wk = next_block(B_QK + wbi)
                    Wq[wbi] = (W[:, 0:4096].rearrange("p (k n) -> p k n", k=8), wk)
                Wv, wk = Wq[wbi]
                ps, pk = PB.next()
                qpb[f] = (ps, pk)
                for k in range(8):
                    A("pe", MM(ps[:, 0:NT], Wv[:, k, j * 128:(j + 1) * 128], hT[:, k, :], k == 0, k == 7), reads=[wk, ("hT", k)], writes=[pk])
                qk_ = ("qkraw", f)
                A("pool", CP(qkraw[:, f, 0:3], qhalo[:, l, f, 0:3]), reads=["qhalo"], writes=[qk_])
                A("act", ACT(qkraw[:, f, 3:3 + NT], ps[:, 0:NT], AF.Copy), reads=[pk, qk_], writes=[qk_])
                A("pool", CP(qhalo[:, l, f, 0:3], qkraw[:, f, NT:NT + 3]), reads=[qk_], writes=["qhalo"])

            def qk_s1(f):
                A("act", ACT(cacc[f % 4][:], qkraw[:, f, 0:NT], AF.Identity, scale=cwqk[:, l, f, 0:1], bias=cbqk[:, l, f:f + 1]),
                  reads=[("qkraw", f)] + SD, writes=[("cacc", f % 4)])

            def qk_s2(f):
                ps, pk = qpb.pop(f)
                ca = cacc[f % 4]
                ck = ("cacc", f % 4)
                A("dve", STT(ca[:], ps[:, 0:NT], cwqk[:, l, f, 3:4], ca[:], ALU.mult, ALU.add), reads=[pk, ck] + SD, writes=[ck])

            def qk_tap(tap):
                def fn(f):
                    ca = cacc[f % 4]
                    ck = ("cacc", f % 4)
                    A("dve", STT(ca[:], qkraw[:, f, tap:tap + NT], cwqk[:, l, f, tap:tap + 1], ca[:], ALU.mult, ALU.add), reads=[("qkraw", f), ck] + SD, writes=[ck])
                return fn

            def qk_s5(f):
                A("act", ACT(qkT[:, f, :], cacc[f % 4][:], AF.Silu), reads=[("cacc", f % 4)], writes=[("qkT", f)])

            gates_tail_done = [False]

            def qk_s0_wrap(f):
                qk_s0(f)
                if f == 1 and not gates_tail_done[0]:
                    gates_tail()
                    gates_tail_done[0] = True

            pipeline(8, [qk_s0_wrap, qk_s1, qk_s2, qk_tap(1), qk_tap(2), qk_s5])

            S_.tag = tg + "uvo"
            for which, bidx in (("u", B_U), ("v", B_V), ("vm", B_VM), ("o", B_O)):
                W, wk = next_block(bidx)
                Wv = W[:, 0:4096].rearrange("p (k n) -> p k n", k=8)
                for c in range(NCH):
                    ps, pk = PB.next()
                    for k in range(8):
                        A("pe", MM(ps[:, 0:512], hT[:, k, c * 128:(c + 1) * 128], Wv[:, k, :], k == 0, k == 7), reads=[wk, ("hT", k)], writes=[pk])
                    if which == "u":
                        A("act", ACT(ug[:, c, :], ps[:, 0:512], AF.Gelu), reads=[pk], writes=[("ug", c)])
                    elif which == "v":
                        A("act", ACT(vg[:, c, :], ps[:, 0:512], AF.Gelu), reads=[pk], writes=[("vg", c)])
                        A("dve", lambda e, c=c: e.bn_stats(out=st6[:, c, :], in_=vg[:, c, :]), reads=[("vg", c)], writes=[("st6", c)])
                        A("dve", lambda e, c=c: e.bn_aggr(out=mv[:, c, :], in_=st6[:, c, :]), reads=[("st6", c)], writes=[("mv", c)])
                    elif which == "vm":
                        for h in range(4):
                            A("act", ACT(vaug[:, c, h, 0:128], ps[:, h * 128:(h + 1) * 128], AF.Copy, scale=sc[:, c, h:h + 1]), reads=[pk, ("vaug", c), "sc"], writes=[("vaug", c)])
                        A("pool", CP(vaug[:, c, :, 128], sc[:, c, :]), reads=["sc", ("vaug", c)], writes=[("vaug", c)])
                    else:
                        A("act", ACT(so[:, c, :], ps[:, 0:512], AF.Sigmoid), reads=[pk], writes=[("so", c)])
                if which == "v":
                    A("act", ACT(lvar[:], mv[:, :, 1], AF.Ln, bias=epsc[:, 0:1]), reads=[("mv", c) for c in range(NCH)] + ["epsc"], writes=["lvar"])
                    A("act", ACT(lrstd[:], lvar[:], AF.Exp, scale=-0.5), reads=["lvar"], writes=["lrstd"])
                    for c in range(NCH):
                        v_ = vt[c % 2]
                        vk = ("vt", c % 2)
                        A("dve", TS(v_[:], vg[:, c, :], mv[:, c, 0:1], lrstd[:, c:c + 1], ALU.subtract, ALU.mult),
                          reads=[("vg", c), ("mv", c), "lrstd"], writes=[vk])
                        A("pool", TT(v_[:], v_[:], lngb[:, 0, :], ALU.mult), reads=[vk, "lngb"], writes=[vk])
                        A("pool", TT(vn[:, c, :], v_[:], lngb[:, 1, :], ALU.add), reads=[vk, "lngb"], writes=[("vn", c)])

            Wws, wsk = next_block(B_WS)
            wsv = Wws[:, 0:1024].rearrange("p (h t) -> p h t", h=8)
            stg = {}

            def mix_P1(c):
                S_.tag = tg + "mixP1"
                cs = slice(c * 128, (c + 1) * 128)
                gi = c % 2
                psm, pmk = PB.next()
                A("pe", MM(psm[:, 0:512], bs128[:, l, :], sel[:], True, False), reads=["bs128", "sel"], writes=[pmk])
                for h in range(8):
                    A("pe", MM(psm[:, h * 64:(h + 1) * 64], wsv[:, h, :], vn[:, c, h * 64:(h + 1) * 64], False, h == 7), reads=[wsk, ("vn", c)], writes=[pmk])
                st, stk = PS.next()
                for h in range(4):
                    A("pe", MM(st[:, h * 128:(h + 1) * 128], qkT[:, 4 + h, cs], qkT[:, h, cs], True, True), reads=[("qkT", 4 + h), ("qkT", h)], writes=[stk])
                kt, ktk = PT.next()
                for h in range(4):
                    A("pe", TR(kt[:, h * 128:(h + 1) * 128], qkT[:, 4 + h, cs], ident[:]), reads=[("qkT", 4 + h), "ident"], writes=[ktk])
                A("dve", TT(gm[gi][:], psm[:, 0:512], ug[:, c, :], ALU.mult), reads=[pmk, ("ug", c)], writes=[("gm", gi)])
                A("dve", TT(Amat[gi][:], st[:, 0:512].rearrange("p (h t) -> p h t", h=4), trib4[:], ALU.mult),
                  reads=[stk, "trib4"], writes=[("Amat", gi, h) for h in range(4)])
                A("dve", CP(Kp[gi][:], kt[:, 0:512].rearrange("p (h t) -> p h t", h=4)), reads=[ktk], writes=[("Kp", gi, h) for h in range(4)])

            def mix_P2(c):
                S_.tag = tg + "mixP2"
                cs = slice(c * 128, (c + 1) * 128)
                gi = c % 2
                pt, ptk = PT.next()
                for j in range(4):
                    A("pe", TR(pt[:, j * 128:(j + 1) * 128], gm[gi][:, j * 128:(j + 1) * 128], ident[:]), reads=[("gm", gi), "ident"], writes=[ptk])
                A("act", ACT(hT[:, 0:4, cs], pt[:, 0:512].rearrange("p (a b) -> p a b", a=4), AF.Copy), reads=[ptk], writes=HT[0:4])
                for pr in range(2):
                    nd, ndk = PB.next()
                    up, upk = PB.next()
                    for j in range(2):
                        h = 2 * pr + j
                        A("pe", MM(nd[:, j * 129:(j + 1) * 129], Amat[gi][:, h, :], vaug[:, c, h, 0:129], True, False), reads=[("Amat", gi, h), ("vaug", c)], writes=[ndk])
                        A("pe", MM(nd[:, j * 129:(j + 1) * 129], qkT[:, h, cs], Cbf[:, l, h, 0:129], False, True), reads=[("qkT", h), ("Cbf", l, h)], writes=[ndk])
                    for j in range(2):
                        h = 2 * pr + j
                        A("pe", MM(up[:, j * 129:(j + 1) * 129], Kp[gi][:, h, :], vaug[:, c, h, 0:129], True, True), reads=[("Kp", gi, h), ("vaug", c)], writes=[upk])
                    d_ = dtmp[gi]
                    dk = ("dtmp", gi, pr)
                    dsl = d_[:, 2 * pr:2 * pr + 2]
                    A("act", ACT(dsl, nd[:, 0:258].rearrange("p (j d) -> p j d", j=2)[:, :, 128], AF.Abs), reads=[ndk], writes=[dk])
                    A("dve", TT(dsl, dsl, einv[:, c, 2 * pr:2 * pr + 2], ALU.max), reads=[dk, "einv"], writes=[dk])
                    A("dve", lambda e, dsl=dsl: e.reciprocal(out=dsl, in_=dsl), reads=[dk], writes=[dk])
                    for j in range(2):
                        h = 2 * pr + j
                        A("dve", STT(hc[gi][:, h * 128:(h + 1) * 128], nd[:, j * 129:j * 129 + 128], d_[:, h:h + 1], so[:, c, h * 128:(h + 1) * 128], ALU.mult, ALU.mult),
                          reads=[ndk, dk, ("so", c)], writes=[("hc", gi, h)])
                    for j in range(2):
                        h = 2 * pr + j
                        prev = ebL[:, c - 1, h:h + 1] if c > 0 else ebLs[:, l, h:h + 1]
                        A("dve", STT(Dst[:, l, h, 0:129], Dst[:, l, h, 0:129], prev, up[:, j * 129:(j + 1) * 129], ALU.mult, ALU.add),
                          reads=[upk, ("Dst", l, h), "ebL", "ebLs"], writes=[("Dst", l, h)])
                A("pool", TT(Cbf[:, l, :, 0:129], Dst[:, l, :, 0:129], ebL[:, c, :].unsqueeze(2).broadcast_to([128, 4, 129]), ALU.mult),
                  reads=[("Dst", l, h) for h in range(4)] + ["ebL"], writes=[("Cbf", l, h) for h in range(4)])
                for h in range(4):
                    A("dve", lambda e, h=h, gi=gi: e.scalar_tensor_tensor(out=junk[:], in0=hc[gi][:, h * 128:(h + 1) * 128], scalar=1.0, in1=hc[gi][:, h * 128:(h + 1) * 128],
                                                                          op0=ALU.mult, op1=ALU.mult, accum_out=ss[gi][:, h:h + 1]),
                      reads=[("hc", gi, h)], writes=["junk", ("ss", gi)])
                A("act", ACT(lss[gi][:], ss[gi][:], AF.Ln, scale=1.0 / 128.0, bias=epsc[:, 0:1]), reads=[("ss", gi), "epsc"], writes=[("lss", gi)])
                A("act", ACT(hrs[gi][:], lss[gi][:], AF.Exp, scale=-0.5), reads=[("lss", gi)], writes=[("hrs", gi)])
                A("pool", TT(ml[gi][:].rearrange("p (h d) -> p h d", h=4), hc[gi][:].rearrange("p (h d) -> p h d", h=4),
                             hrs[gi][:].unsqueeze(2).broadcast_to([128, 4, 128]), ALU.mult),
                  reads=[("hc", gi, h) for h in range(4)] + [("hrs", gi)], writes=[("ml", gi)])

            def mix_P3(c):
                S_.tag = tg + "mixP3"
                cs = slice(c * 128, (c + 1) * 128)
                gi = c % 2
                pt, ptk = PT.next()
                for h in range(4):
                    A("pe", TR(pt[:, h * 128:(h + 1) * 128], ml[gi][:, h * 128:(h + 1) * 128], ident[:]), reads=[("ml", gi), "ident"], writes=[ptk])
                for h in range(4):
                    A("act", ACT(hT[:, 4 + h, cs], pt[:, h * 128:(h + 1) * 128], AF.Copy, scale=mlng[:, l * 4 + h:l * 4 + h + 1]), reads=[ptk] + SD, writes=[("hT", 4 + h)])

            mix_P1(0)
            for c in range(NCH):
                if c + 1 < NCH:
                    mix_P1(c + 1)
                mix_P2(c)
                if c >= 1:
                    mix_P3(c - 1)
            mix_P3(NCH - 1)
            A("dve", CP(ebLs[:, l, :], ebL[:, NCH - 1, :]), reads=["ebL", "ebLs"], writes=["ebLs"])

            S_.tag = tg + "wout"
            Wo = {}
            nst2 = norm_begin()

            def wout_s0(f):
                wbi, j = divmod(f, 4)
                if j == 0:
                    W, wk = next_block(B_OUT + wbi)
                    Wo[wbi] = (W[:, 0:4096].rearrange("p (k n) -> p k n", k=8), wk)
                Wv, wk = Wo[wbi]
                ps, pk = PB.next()
                for k in range(8):
                    A("pe", MM(ps[:, 0:NT], Wv[:, k, j * 128:(j + 1) * 128], hT[:, k, :], k == 0, k == 7), reads=[wk, ("hT", k)], writes=[pk])
                A("dve", TT(xT[:, f, :], ps[:, 0:NT], xT[:, f, :], ALU.add), reads=[pk, ("x", f)], writes=[("x", f)])

            pipeline(8, [wout_s0, lambda f: None, lambda f: norm_accum(nst2, f)])

            S_.tag = tg + "norm2"
            norm_finish(nst2, l * 24 + 8)
            S_.tag = tg + "wup"
            Wu = {}
            pbk_ = {}

            def up_pre(fi):
                A("pool", CP(raw[fi % 4][:, 0:2], halo[:, l, fi, :]), reads=["halo"], writes=[("raw", fi % 4)])

            def up_s0(fi):
                i, jj = divmod(fi, 4)
                if jj == 0:
                    W, wk = next_block(B_UP + i)
                    Wu[i] = (W[:, 0:4096].rearrange("p (k n) -> p k n", k=8), wk)
                Wv, wk = Wu[i]
                ps, pk = PB.next()
                pbk_[fi] = (ps, pk)
                for k in range(8):
                    A("pe", MM(ps[:, 0:NT], Wv[:, k, jj * 128:(jj + 1) * 128], hT[:, k, :], k == 0, k == 7), reads=[wk, ("hT", k)], writes=[pk])
                r_ = raw[fi % 4]
                rk_ = ("raw", fi % 4)
                A("act", ACT(r_[:, 2:2 + NT], ps[:, 0:NT], AF.Copy), reads=[pk, rk_], writes=[rk_])
                if t == 0:
                    emit_prep(l + 1, 2)

            def up_s1(fi):
                A("act", ACT(facc[fi % 3][:], raw[fi % 4][:, 0:NT], AF.Identity, scale=cwff[:, l, fi, 0:1], bias=cbff[:, l, fi:fi + 1]),
                  reads=[("raw", fi % 4)] + SD, writes=[("facc", fi % 3)])
                A("pool", CP(halo[:, l, fi, :], raw[fi % 4][:, NT:NT + 2]), reads=[("raw", fi % 4)], writes=["halo"])

            def up_s2(fi):
                ps, pk = pbk_.pop(fi)
                fa = facc[fi % 3]
                fk = ("facc", fi % 3)
                A("dve", STT(fa[:], ps[:, 0:NT], cwff[:, l, fi, 2:3], fa[:], ALU.mult, ALU.add), reads=[pk, fk] + SD, writes=[fk])

            def up_s3(fi):
                fa = facc[fi % 3]
                fk = ("facc", fi % 3)
                A("dve", STT(fa[:], raw[fi % 4][:, 1:1 + NT], cwff[:, l, fi, 1:2], fa[:], ALU.mult, ALU.add), reads=[("raw", fi % 4), fk] + SD, writes=[fk])

            def up_s4(fi):
                i, jj = divmod(fi, 4)
                fa = facc[fi % 3]
                fk = ("facc", fi % 3)
                if jj < 2:
                    A("act", ACT(sa[jj][:], fa[:], AF.Silu), reads=[fk], writes=[("sa", jj)])
                else:
                    A("pool", TT(actT[:, 2 * i + jj - 2, :], sa[jj - 2][:], fa[:], ALU.mult), reads=[("sa", jj - 2), fk], writes=[("actT", 2 * i + jj - 2)])

            pipeline(44, [up_pre, up_s0, up_s1, up_s2, up_s3, up_s4])
            if t == 0:
                emit_prep(l + 1, 10 ** 6)

            S_.tag = tg + "wdown"
            nst3 = norm_begin()

            def wdown_s0(f):
                W, wk = next_block(B_DOWN + f)
                Wv = W[:, 0:22 * 128].rearrange("p (k n) -> p k n", k=22)
                ps, pk = PB.next()
                for k in range(22):
                    A("pe", MM(ps[:, 0:NT], Wv[:, k, :], actT[:, k, :], k == 0, k == 21), reads=[wk, ("actT", k)], writes=[pk])
                A("dve", TT(xT[:, f, :], ps[:, 0:NT], xT[:, f, :], ALU.add), reads=[pk, ("x", f)], writes=[("x", f)])

            pipeline(8, [wdown_s0, lambda f: None, lambda f: norm_accum(nst3, f)])

            S_.tag = tg + "norm3"
            norm_finish(nst3, l * 24 + 16)
            S_.tag = tg + "ple"
            Wgp = {}
            nst_next = norm_begin()

            for it in range(8 + 3):
                f = it
                fa_ = it - 3
                if 0 <= fa_ < 8:
                    A("act", ACT(sqr[fa_ % 2][:], xT[:, fa_, :], AF.Square), reads=[("x", fa_)], writes=[("sq", fa_ % 2)])
                if f < 8:
                    jb, jj = divmod(f, 2)
                    if jj == 0:
                        W, wk = next_block(B_GP + jb)
                        Wgp[jb] = (W[:, 0:2048].rearrange("p (k n) -> p k n", k=8), W[:, 2048:2560].rearrange("p (k n) -> p k n", k=2), wk)
                    Wg, Wp, wk = Wgp[jb]
                    psg2, pgk2 = PB.next()
                    for k in range(8):
                        A("pe", MM(psg2[:, 0:NT], Wg[:, k, jj * 128:(jj + 1) * 128], hT[:, k, :], k == 0, k == 7), reads=[wk, ("hT", k)], writes=[pgk2])
                if 0 <= fa_ < 8:
                    ps_n, pk_n = nst_next
                    A("pe", MM(ps_n[:, 0:NT], onesm[:], sqr[fa_ % 2][:], fa_ == 0, fa_ == 7), reads=[("sq", fa_ % 2), "onesm"], writes=[pk_n])
                if f < 8:
                    g_ = gt[f % 2]
                    gk_ = ("gt", f % 2)
                    A("act", ACT(g_[:], psg2[:, 0:NT], AF.Sigmoid), reads=[pgk2], writes=[gk_])
                    psp, ppk = PB.next()
                    for k in range(2):
                        A("pe", MM(psp[:, 0:NT], Wp[:, k, jj * 128:(jj + 1) * 128], pT[:, k, :], k == 0, k == 1), reads=[wk, "pT"], writes=[ppk])
                    A("dve", TT(g_[:], psp[:, 0:NT], g_[:], ALU.mult), reads=[ppk, gk_], writes=[gk_])
                    A("dve", TT(xT[:, f, :], xT[:, f, :], g_[:], ALU.add), reads=[gk_, ("x", f)], writes=[("x", f)])
            return nst_next

        for t in range(n_tiles):
            S_.tag = "xload"
            A("sp", DMA(xT[:], xin_v[:, :, t * NT:(t + 1) * NT]), writes=XK, dma_sem=s_x)
            nst = None
            for l in range(n_layers):
                nst = layer_step(t, l, nst)
            S_.tag = "final"
            norm_finish(nst, NL * 24, final_t0=t * NT)

        A("sp", lambda e: e.nop(), reads=["yout"])

        S_.analyze(qsems)
        with nc.Block() as block:
            S_.emit(block)
    return nc


def _pk(v):
    v = np.asarray(v, np.float32)
    return np.ascontiguousarray(v.reshape(-1, 128).T)


def ffn_col_order():
    cols = []
    for i in range(11):
        for j in (2 * i, 2 * i + 1):
            cols.append(np.arange(128 * j, 128 * j + 128))
        for j in (2 * i, 2 * i + 1):
            cols.append(DFF + np.arange(128 * j, 128 * j + 128))
    return np.stack(cols)


def shared_inputs(inp):
    f32 = np.float32
    d = {}
    for nm in ("w_in", "w_out", "w_up", "w_down", "w_ple"):
        d[nm] = np.ascontiguousarray(inp[nm], f32)
    d["w_gate"] = np.ascontiguousarray(inp["w_ple_gate"], f32)
    cols = []
    for i in range(NL):
        cols += [_pk(inp["g_mix"][i]), _pk(inp["g_ffn"][i]), _pk(inp["g_ple"][i])]
    cols.append(_pk(inp["g_final"]))
    d["gv"] = np.ascontiguousarray(np.concatenate(cols, axis=1))
    d["mlng"] = np.ascontiguousarray(np.concatenate([_pk(inp["ml_norm_g"][i]) for i in range(NL)], axis=1))
    lngb = np.stack([np.stack([inp["gm_ln_g"][i], inp["gm_ln_b"][i]]) for i in range(NL)])
    d["lngb"] = np.ascontiguousarray(np.broadcast_to(lngb[:, None], (NL, 128, 2, 512)), f32)
    d["wsT"] = np.ascontiguousarray(np.transpose(np.asarray(inp["gm_ws"], f32), (0, 3, 1, 2)))
    d["bs8"] = np.ascontiguousarray(np.transpose(np.asarray(inp["gm_bs"], f32), (1, 0, 2)))
    cw = np.asarray(inp["ml_conv_w"], f32)
    d["cwqk"] = np.ascontiguousarray(np.transpose(cw.reshape(NL, 4, 8, 128), (3, 0, 2, 1)))
    cb = np.asarray(inp["ml_conv_b"], f32)
    d["cbqk"] = np.ascontiguousarray(np.transpose(cb.reshape(NL, 8, 128), (2, 0, 1)))
    order = ffn_col_order()
    fw = np.asarray(inp["ffn_conv_w"], f32)
    d["cwff"] = np.ascontiguousarray(np.transpose(fw[:, :, order], (3, 0, 2, 1)))
    fb = np.asarray(inp["ffn_conv_b"], f32)
    d["cbff"] = np.ascontiguousarray(np.transpose(fb[:, order], (2, 0, 1)))
    bif = np.concatenate([np.asarray(inp["ml_b_i"], f32), np.asarray(inp["ml_b_f"], f32)], axis=1)
    d["bif"] = np.ascontiguousarray(np.broadcast_to(bif[None], (128, NL, 8)), f32)
    d["ident"] = np.eye(128, dtype=f32)
    d["tri"] = np.triu(np.ones((128, 128), f32))
    sel = np.zeros((128, 512), f32)
    for h in range(8):
        sel[h, h * 64:(h + 1) * 64] = 1.0
    d["sel"] = sel
    return d


_PROG = {}


def get_prog(S, n_layers=NL):
    if (S, n_layers) not in _PROG:
        _PROG[(S, n_layers)] = build_program(S, n_layers)
    return _PROG[(S, n_layers)]


def kernel(**inp):
    x = np.asarray(inp["x"], np.float32)
    p = np.asarray(inp["p"], np.float32)
    B, S, _ = x.shape
    sh = shared_inputs(inp)
    maps = []
    for b in range(B):
        m = dict(sh)
        m["xin"] = np.ascontiguousarray(x[b].T)
        m["pin"] = np.ascontiguousarray(np.transpose(p[:, b], (0, 2, 1)))
        maps.append(m)
    res = run_bass_kernel_spmd(get_prog(S), maps, core_ids=list(range(B)))
    return np.stack([np.asarray(res.results[b]["yout"]).T for b in range(B)]).astype(np.float32)
```

```python
import math
import numpy as np
from contextlib import ExitStack
import concourse.bass as bass
import concourse.mybir as mybir
from concourse.bass_utils import run_bass_kernel_spmd

F32 = mybir.dt.float32
BF16 = mybir.dt.bfloat16
AF = mybir.ActivationFunctionType
ALU = mybir.AluOpType

D = 1024
NT = 512
NCH = NT // 128
DFF = 2816
EPS = 1e-6
NSLOT = 4
WBLK = 4096
SAME_ENGINE_SYNC = True
SAME_ENGINE_ALL = True
SAME_ENGINE_SYNC_Q = {"act", "dve", "pool"}
ANNOTATE = False


class Op:
    __slots__ = ("q", "fn", "reads", "writes", "sem", "inc", "needs_inc", "count", "waits", "tag")


class Sched:
    def __init__(self):
        self.ops = []
        self.tag = None

    def add(self, q, fn, reads=(), writes=(), dma_sem=None):
        o = Op()
        o.q, o.fn, o.reads, o.writes = q, fn, tuple(reads), tuple(writes)
        o.sem = dma_sem
        o.inc = 16 if dma_sem is not None else 1
        o.needs_inc = dma_sem is not None
        o.count = None
        o.waits = {}
        o.tag = self.tag
        self.ops.append(o)
        return o

    def analyze(self, qsems):
        last_w, readers, need = {}, {}, []
        for o in self.ops:
            deps = {}
            for r in o.reads:
                w = last_w.get(r)
                if w is not None:
                    deps[id(w)] = (w, True)
            for k in o.writes:
                w = last_w.get(k)
                if w is not None and id(w) not in deps:
                    deps[id(w)] = (w, False)
                for rd in readers.get(k, ()):
                    if id(rd) not in deps:
                        deps[id(rd)] = (rd, False)
            deps.pop(id(o), None)
            nd = []
            for a, raw in deps.values():
                if a.sem is None and a.q == o.q:
                    if o.q == "pe" or not SAME_ENGINE_SYNC or o.q not in SAME_ENGINE_SYNC_Q:
                        continue
                    if not raw and not SAME_ENGINE_ALL:
                        continue
                nd.append(a)
                a.needs_inc = True
            need.append(nd)
            for r in o.reads:
                readers.setdefault(r, []).append(o)
            for k in o.writes:
                last_w[k] = o
                readers[k] = []
        cnt = {}
        for o in self.ops:
            if o.needs_inc:
                s = o.sem if o.sem is not None else qsems[o.q]
                cnt[s] = cnt.get(s, 0) + o.inc
                o.count = (s, cnt[s])
        waited = {}
        for o, nd in zip(self.ops, need):
            w = {}
            for a in nd:
                s, c = a.count
                if c > w.get(s, 0):
                    w[s] = c
            qw = waited.setdefault(o.q, {})
            for s, c in list(w.items()):
                if qw.get(s, 0) >= c:
                    del w[s]
                else:
                    qw[s] = c
            o.waits = w
        self.final_counts = cnt

    def emit(self, block):
        byq = {}
        for o in self.ops:
            byq.setdefault(o.q, []).append(o)

        def run(eng, ops):
            for o in ops:
                for s, c in o.waits.items():
                    eng.wait_ge(s, c)
                ins = o.fn(eng)
                if ANNOTATE and o.tag is not None:
                    ins.annotate(o.tag)
                if o.needs_inc:
                    ins.then_inc(o.count[0], o.inc)

        names = {"pe": "tensor", "act": "scalar", "dve": "vector", "pool": "gpsimd", "sp": "sync"}
        for q in ["sp", "pe", "act", "dve", "pool"]:
            if q in byq:
                getattr(block, names[q])(lambda eng, ops=byq[q]: run(eng, ops))


class Ring:
    def __init__(self, items):
        self.items = items
        self.i = 0

    def next(self):
        it = self.items[self.i % len(self.items)]
        self.i += 1
        return it


NL = 4


def weight_blocks():
    b = []
    for i in range(2):
        b.append([("w_in", 8, 1024 + 512 * i, 512, 0, 512, 0)])
    for c0 in (0, 512, 2048, 2560):
        b.append([("w_in", 8, c0, 512, 0, 512, 0)])
    b.append(None)
    for i in range(2):
        b.append([("w_out", 8, 512 * i, 512, 0, 512, 0)])
    for i in range(11):
        b.append([("w_up", 8, 256 * i, 256, 0, 512, 0), ("w_up", 8, DFF + 256 * i, 256, 0, 512, 256)])
    for f in range(8):
        b.append([("w_down", 22, 128 * f, 128, 0, 128, 0)])
    for j in range(4):
        b.append([("w_gate", 8, 256 * j, 256, 0, 256, 0), ("w_ple", 2, 256 * j, 256, 2048, 256, 0)])
    return b


WB = weight_blocks()
NWB = len(WB)
B_QK, B_U, B_V, B_VM, B_O, B_WS, B_OUT, B_UP, B_DOWN, B_GP = 0, 2, 3, 4, 5, 6, 7, 9, 20, 28
WB_SIZE = []
for _b in WB:
    if _b is None:
        WB_SIZE.append(1024)
    else:
        WB_SIZE.append(max(base + K * W for (_, K, _, _, base, W, _) in _b))


def MM(out, lhsT, rhs, start, stop):
    return lambda e: e.matmul(out, lhsT=lhsT, rhs=rhs, start=start, stop=stop)


def TR(out, in_, ident):
    return lambda e: e.transpose(out, in_, ident)


def ACT(out, in_, func, **kw):
    return lambda e: e.activation(out=out, in_=in_, func=func, **kw)


def TT(out, in0, in1, op):
    return lambda e: e.tensor_tensor(out=out, in0=in0, in1=in1, op=op)


def TS(out, in0, s1, s2, op0, op1=None):
    if op1 is None:
        return lambda e: e.tensor_scalar(out=out, in0=in0, scalar1=s1, scalar2=None, op0=op0)
    return lambda e: e.tensor_scalar(out=out, in0=in0, scalar1=s1, scalar2=s2, op0=op0, op1=op1)


def TSS(out, in_, scalar, op):
    return lambda e: e.tensor_single_scalar(out=out, in_=in_, scalar=scalar, op=op)


def STT(out, in0, scalar, in1, op0, op1):
    return lambda e: e.scalar_tensor_tensor(out=out, in0=in0, scalar=scalar, in1=in1, op0=op0, op1=op1)


def CP(out, in_):
    return lambda e: e.tensor_copy(out=out, in_=in_)


def MS(out, val):
    return lambda e: e.memset(out, val)


def DMA(out, in_):
    return lambda e: e.dma_start(out=out, in_=in_)


def pipeline(n, stages):
    ns = len(stages)
    for i in range(n + ns - 1):
        for si in reversed(range(ns)):
            c = i - si
            if 0 <= c < n:
                stages[si](c)


def build_program(S, n_layers=NL):
    n_tiles = S // NT
    nc = bass.Bass("TRN2", target_bir_lowering=False)

    def din(name, shape):
        return nc.dram_tensor(name, list(shape), F32, kind="ExternalInput").ap()

    xin = din("xin", [D, S])
    pin = din("pin", [NL, 256, S])
    wsrc = {
        "w_in": din("w_in", [NL, D, 3080]), "w_out": din("w_out", [NL, D, D]), "w_up": din("w_up", [NL, D, 2 * DFF]),
        "w_down": din("w_down", [NL, DFF, D]), "w_gate": din("w_gate", [NL, D, D]), "w_ple": din("w_ple", [NL, 256, D]),
    }
    gv_d = din("gv", [128, NL * 24 + 8])
    mlng_d = din("mlng", [128, NL * 4])
    lngb_d = din("lngb", [NL, 128, 2, 512])
    wsT_d = din("wsT", [NL, 128, 8, 128])
    bs8_d = din("bs8", [8, NL, 128])
    cwqk_d = din("cwqk", [128, NL, 8, 4])
    cbqk_d = din("cbqk", [128, NL, 8])
    cwff_d = din("cwff", [128, NL, 44, 3])
    cbff_d = din("cbff", [128, NL, 44])
    bif_d = din("bif", [128, NL, 8])
    ident_d = din("ident", [128, 128])
    tri_d = din("tri", [128, 128])
    sel_d = din("sel", [128, 512])
    yout = nc.dram_tensor("yout", [D, S], F32, kind="ExternalOutput").ap()
    wb_d = nc.dram_tensor("wb_scratch", [NL, NWB, 128, WBLK], BF16).ap()

    xin_v = xin.rearrange("(f p) s -> p f s", p=128)
    yout_v = yout.rearrange("(f p) s -> p f s", p=128)

    with ExitStack() as es:
        def sb(name, shape, dt=F32):
            return es.enter_context(nc.sbuf_tensor("sb_" + name, list(shape), dt))

        def psum(name, shape, dt=F32):
            return es.enter_context(nc.psum_tensor(name, list(shape), dt))

        def sem(name):
            return es.enter_context(nc.semaphore(name))

        qsems = {q: sem("q_" + q) for q in ["pe", "act", "dve", "pool"]}
        S_ = Sched()
        A = S_.add

        xT = sb("xT", [128, 8, NT])
        hT = sb("hT", [128, 8, NT], BF16)
        sqr = [sb(f"sq{i}", [128, NT], BF16) for i in range(2)]
        rstd = sb("rstd", [128, NT])
        ntmp = [sb(f"ntmp{i}", [128, NT]) for i in range(2)]
        qkraw = sb("qkraw", [128, 8, NT + 4])
        qkT = sb("qkT", [128, 8, NT], BF16)
        cacc = [sb(f"cacc{i}", [128, NT]) for i in range(4)]
        gpre = sb("gpre", [128, NCH, 8])
        e1 = sb("e1", [128, NCH, 4])
        l1 = sb("l1", [128, NCH, 4])
        t1 = sb("t1", [128, NCH, 4])
        sc = sb("sc", [128, NCH, 4])
        einv = sb("einv", [128, NCH, 4])
        ebL = sb("ebL", [128, NCH, 4])
        ebLs = sb("ebLs", [128, NL, 4])
        ug = sb("ug", [128, NCH, 512], BF16)
        vg = sb("vg", [128, NCH, 512], BF16)
        vt = [sb(f"vt{i}", [128, 512]) for i in range(2)]
        vn = sb("vn", [128, NCH, 512], BF16)
        st6 = sb("st6", [128, NCH, 6])
        mv = sb("mv", [128, NCH, 2])
        lvar = sb("lvar", [128, NCH])
        lrstd = sb("lrstd", [128, NCH])
        vaug = sb("vaug", [128, NCH, 4, 132], BF16)
        so = sb("so", [128, NCH, 512], BF16)
        gm = [sb(f"gm{i}", [128, 512], BF16) for i in range(2)]
        Amat = [sb(f"Amat{i}", [128, 4, 128], BF16) for i in range(2)]
        Kp = [sb(f"Kp{i}", [128, 4, 128], BF16) for i in range(2)]
        dtmp = [sb(f"dtmp{i}", [128, 4]) for i in range(2)]
        hc = [sb(f"hc{i}", [128, 512]) for i in range(2)]
        junk = sb("junk", [128, 128])
        ss = [sb(f"ss{i}", [128, 4]) for i in range(2)]
        lss = [sb(f"lss{i}", [128, 4]) for i in range(2)]
        hrs = [sb(f"hrs{i}", [128, 4]) for i in range(2)]
        ml = [sb(f"ml{i}", [128, 512], BF16) for i in range(2)]
        Dst = sb("Dst", [128, NL, 4, 132])
        Cbf = sb("Cbf", [128, NL, 4, 132], BF16)
        raw = [sb(f"raw{i}", [128, NT + 2]) for i in range(4)]
        facc = [sb(f"facc{i}", [128, NT]) for i in range(3)]
        sa = [sb(f"sa{i}", [128, NT]) for i in range(2)]
        actT = sb("actT", [128, 22, NT], BF16)
        halo = sb("halo", [128, NL, 44, 2])
        qhalo = sb("qhalo", [128, NL, 8, 4])
        pT = sb("pT", [128, 2, NT], BF16)
        gt = [sb(f"gt{i}", [128, NT]) for i in range(2)]
        wring = sb("wring", [128, NSLOT, WBLK], BF16)
        lngb = sb("lngb", [128, 2, 512])
        ident = sb("ident", [128, 128], BF16)
        tri = sb("tri", [128, 128])
        trib = sb("trib", [128, 128], BF16)
        trib4 = sb("trib4", [128, 4, 128], BF16)
        onesf = sb("onesf", [128, 128])
        onesm = sb("onesm", [128, 128], BF16)
        sel = sb("sel", [128, 512], BF16)
        bs128 = sb("bs128", [128, NL, 128], BF16)
        gv = sb("gv", [128, NL * 24 + 8])
        mlng = sb("mlng", [128, NL * 4])
        cwqk = sb("cwqk", [128, NL, 8, 4])
        cbqk = sb("cbqk", [128, NL, 8])
        cwff = sb("cwff", [128, NL, 44, 3])
        cbff = sb("cbff", [128, NL, 44])
        bif = sb("bif", [128, NL, 8])
        epsc = sb("epsc", [128, 1])
        wg = sb("wg", [128, NL, 8, 8], BF16)

        PBt = [psum(f"pb{i}", [128, 512]) for i in range(4)]
        PTt = [psum(f"pt{i}", [128, 1024], BF16) for i in range(2)]
        PSt = [psum(f"ps{i}", [128, 512]) for i in range(2)]
        PB = Ring([(PBt[i], ("pb", i)) for i in range(4)])
        PT = Ring([(PTt[i], ("pt", i)) for i in range(2)])
        PS = Ring([(PSt[i], ("ps", i)) for i in range(2)])

        s_setup = sem("d_setup")
        s_w = [sem(f"d_w{i}") for i in range(NSLOT)]
        s_x = sem("d_x")
        s_p = sem("d_p")
        s_ln = sem("d_ln")
        s_yo = [sem(f"d_yo{i}") for i in range(2)]
        s_prep = [sem(f"d_prep{i}") for i in range(4)]
        s_ws = sem("d_ws")
        s_wg = sem("d_wg")
        s_ws2 = sem("d_ws2")

        XK = [("x", f) for f in range(8)]
        HT = [("hT", f) for f in range(8)]

        S_.tag = "setup"

        def load_const(dst_ap, src_ap, keys):
            A("sp", DMA(dst_ap, src_ap), writes=keys, dma_sem=s_setup)

        load_const(gv[:], gv_d, ["gv"])
        load_const(mlng[:], mlng_d, ["mlng"])
        load_const(cwqk[:], cwqk_d, ["cwqk"])
        load_const(cbqk[:], cbqk_d, ["cbqk"])
        load_const(cwff[:], cwff_d, ["cwff"])
        load_const(cbff[:], cbff_d, ["cbff"])
        load_const(bif[:], bif_d, ["bif"])
        load_const(tri[:], tri_d, ["tri"])
        load_const(xT[:, 0, :], sel_d, [("x", 0)])
        load_const(xT[:, 1, 0:128], ident_d, [("x", 1)])
        load_const(xT[0:8, 2, :].rearrange("p (l t) -> p l t", l=NL), bs8_d, [("x", 2)])
        setup_keys = ["gv", "mlng", "cwqk", "cbqk", "cwff", "cbff", "bif", "tri"] + XK[0:3]
        A("pool", MS(epsc[:], EPS), reads=setup_keys, writes=["epsc", "setup_done"])
        SD = ["setup_done"]
        A("pool", CP(sel[:], xT[:, 0, :]), reads=SD + [("x", 0)], writes=["sel"])
        A("pool", CP(ident[:], xT[:, 1, 0:128]), reads=SD + [("x", 1)], writes=["ident"])
        A("pool", CP(trib[:], tri[:]), reads=SD, writes=["trib"])
        for h in range(4):
            A("pool", CP(trib4[:, h, :], tri[:]), reads=SD, writes=["trib4"])
        A("pool", MS(bs128[:], 0.0), reads=SD, writes=["bs128"])
        A("pool", CP(bs128[0:8, :, :], xT[0:8, 2, :].rearrange("p (l t) -> p l t", l=NL)), reads=SD + ["bs128", ("x", 2)], writes=["bs128"])
        A("dve", MS(onesf[:], 1.0), writes=["onesf"])
        A("dve", MS(onesm[:], 1.0 / D), writes=["onesm"])
        A("dve", MS(qhalo[:], 0.0), writes=["qhalo"])
        A("dve", MS(halo[:], 0.0), writes=["halo"])
        A("dve", MS(Dst[:], 0.0), writes=[("Dst", l, h) for l in range(NL) for h in range(4)])
        A("dve", MS(Cbf[:], 0.0), writes=[("Cbf", l, h) for l in range(NL) for h in range(4)])
        A("dve", MS(ebLs[:], 1.0), writes=["ebLs"])
        A("pool", MS(vaug[:], 1.0), writes=[("vaug", c) for c in range(NCH)])
        for l in range(n_layers):
            A("pool", DMA(wg[:, l, :, :], wsrc["w_in"][l].rearrange("(k p) n -> p k n", p=128)[:, :, 3072:3080]),
              writes=[("wg", l), "wgslot"], dma_sem=s_wg)
        xflat = xT[:, 4:6, :].rearrange("p a b -> p (a b)")
        hflat = hT[:, 0:2, :].rearrange("p a b -> p (a b)")
        for l in range(n_layers):
            A("sp", DMA(xflat.rearrange("p (h t) -> p h t", h=8), wsT_d[l]), writes=[("x", 4), ("x", 5)], dma_sem=s_ws)
            for h in range(8):
                A("dve", TT(hflat[:, h * 128:(h + 1) * 128], xflat[:, h * 128:(h + 1) * 128], tri[:], ALU.mult),
                  reads=SD + [("x", 4), ("x", 5)], writes=[("hT", 0), ("hT", 1)])
            A("sp", DMA(wb_d[l, B_WS, :, 0:1024], hflat), reads=[("hT", 0), ("hT", 1)], writes=[("wbd", l, B_WS, 0)], dma_sem=s_ws2)

        np_ = [0]
        wbd_keys = {}
        prep_queue = {l: [] for l in range(n_layers)}
        for l in range(n_layers):
            for bi, parts in enumerate(WB):
                if parts is None:
                    wbd_keys[(l, bi)] = [("wbd", l, bi, 0)]
                    continue
                keys = []
                for pi, (src, K, c0, n, base, W, off) in enumerate(parts):
                    srcv = wsrc[src][l].rearrange("(k p) n -> p k n", p=128)[:, :, c0:c0 + n]
                    dstv = wb_d[l, bi, :, base:base + K * W].rearrange("p (k n) -> p k n", k=K)[:, :, off:off + n]
                    key = ("wbd", l, bi, pi)
                    prep_queue[l].append((dstv, srcv, key))
                    keys.append(key)
                wbd_keys[(l, bi)] = keys

        def emit_prep(l, n):
            for _ in range(n):
                if l >= n_layers or not prep_queue[l]:
                    return
                dstv, srcv, key = prep_queue[l].pop(0)
                sl = np_[0] % 4
                np_[0] += 1
                A("pool", DMA(dstv, srcv), writes=[key, ("prepslot", sl)], dma_sem=s_prep[sl])

        emit_prep(0, 10 ** 6)

        gblk = [0]
        total_blocks = n_tiles * n_layers * NWB

        def issue_wload(g):
            if g >= total_blocks:
                return
            bi = g % NWB
            l = (g // NWB) % n_layers
            sl = g % NSLOT
            n = WB_SIZE[bi]
            A("sp", DMA(wring[:, sl, 0:n], wb_d[l, bi, :, 0:n]), reads=wbd_keys[(l, bi)], writes=[("wr", sl)], dma_sem=s_w[sl])

        def next_block(expect_bi):
            g = gblk[0]
            assert g % NWB == expect_bi, (g % NWB, expect_bi)
            issue_wload(g + NSLOT - 1)
            gblk[0] += 1
            sl = g % NSLOT
            return wring[:, sl, :], ("wr", sl)

        def norm_begin():
            return PS.next()

        def norm_accum(nst, f):
            ps, pk = nst
            i = f % 2
            A("act", ACT(sqr[i][:], xT[:, f, :], AF.Square), reads=[("x", f)], writes=[("sq", i)])
            A("pe", MM(ps[:, 0:NT], onesm[:], sqr[i][:], f == 0, f == 7), reads=[("sq", i), "onesm"], writes=[pk])

        def norm_finish(nst, gcol, final_t0=None):
            ps, pk = nst
            A("act", ACT(rstd[:], ps[:, 0:NT], AF.Ln, bias=epsc[:, 0:1]), reads=[pk, "epsc"], writes=["rstd"])
            A("act", ACT(rstd[:], rstd[:], AF.Exp, scale=-0.5), reads=["rstd"], writes=["rstd"])
            for f in range(8):
                if final_t0 is not None:
                    i = f % 2
                    A("dve", STT(ntmp[i][:], xT[:, f, :], gv[:, gcol + f:gcol + f + 1], rstd[:], ALU.mult, ALU.mult),
                      reads=[("x", f), "rstd"] + SD, writes=[("ntmp", i)])
                    A("sp", DMA(yout_v[:, f, final_t0:final_t0 + NT], ntmp[i][:]), reads=[("ntmp", i)], writes=["yout"], dma_sem=s_yo[i])
                else:
                    A("dve", STT(hT[:, f, :], xT[:, f, :], gv[:, gcol + f:gcol + f + 1], rstd[:], ALU.mult, ALU.mult),
                      reads=[("x", f), "rstd"] + SD, writes=[("hT", f)])

        def norm(gcol, final_t0=None):
            nst = norm_begin()
            for f in range(8):
                norm_accum(nst, f)
            norm_finish(nst, gcol, final_t0)

        for g in range(NSLOT - 1):
            issue_wload(g)

        def layer_step(t, l, nst_in):
            t0 = t * NT
            tg = f"L{l}"
            S_.tag = tg + "norm1"
            A("pool", DMA(pT[:], pin[l].rearrange("(k p) s -> p k s", p=128)[:, :, t0:t0 + NT]), writes=["pT"], dma_sem=s_p)
            A("sp", DMA(lngb[:], lngb_d[l]), writes=["lngb"], dma_sem=s_ln)

            if nst_in is None:
                norm(l * 24)
            else:
                norm_finish(nst_in, l * 24)
            S_.tag = tg + "gates"
            psg, pgk = PS.next()
            for c in range(NCH):
                for k in range(8):
                    A("pe", MM(psg[:, c * 8:(c + 1) * 8], hT[:, k, c * 128:(c + 1) * 128], wg[:, l, k, :], k == 0, k == 7),
                      reads=[("hT", k), ("wg", l)], writes=[pgk])
            for c in range(NCH):
                A("dve", TT(gpre[:, c, :], psg[:, c * 8:(c + 1) * 8], bif[:, l, :], ALU.add), reads=[pgk] + SD, writes=["gpre"])
            A("act", ACT(e1[:], gpre[:, :, 4:8], AF.Exp, scale=-1.0), reads=["gpre"], writes=["e1"])
            A("act", ACT(l1[:], e1[:], AF.Ln, bias=1.0), reads=["e1"], writes=["l1"])
            def gates_tail():
                S_.tag = tg + "gates"
                psb, pbk = PS.next()
                l1v = l1[:].rearrange("p c h -> p (c h)")
                A("pe", MM(psb[:, 0:16], tri[:], l1v, True, True), reads=["l1"] + SD, writes=[pbk])
                A("pe", MM(psb[:, 16:32], onesf[:], l1v, True, True), reads=["l1", "onesf"], writes=[pbk])
                bneg = psb[:, 0:16].rearrange("p (c h) -> p c h", c=NCH)
                bLneg = psb[:, 16:32].rearrange("p (c h) -> p c h", c=NCH)
                A("dve", STT(t1[:], gpre[:, :, 0:4], math.log(128.0 ** -0.5), bneg, ALU.add, ALU.add), reads=["gpre", pbk], writes=["t1"])
                A("act", ACT(sc[:], t1[:], AF.Exp), reads=["t1"], writes=["sc"])
                A("act", ACT(einv[:], bneg, AF.Exp), reads=[pbk, "t1"], writes=["einv"])
                A("act", ACT(ebL[:], bLneg, AF.Exp, scale=-1.0), reads=[pbk, "t1"], writes=["ebL"])
                S_.tag = tg + "qk"

            S_.tag = tg + "qk"
            Wq = {}
            qpb = {}

            def qk_s0(f):
                wbi, j = divmod(f, 4)
                if j == 0:
                    W, wk = next_block(B_QK + wbi)
                    Wq[wbi] = (W[:, 0:4096].rearrange("p (k n) -> p k n", k=8), wk)
                Wv, wk = Wq[wbi]
                ps, pk = PB.next()
                qpb[f] = (ps, pk)
                for k in range(8):
                    A("pe", MM(ps[:, 0:NT], Wv[:, k, j * 128:(j + 1) * 128], hT[:, k, :], k == 0, k == 7), reads=[wk, ("hT", k)], writes=[pk])
                qk_ = ("qkraw", f)
                A("pool", CP(qkraw[:, f, 0:3], qhalo[:, l, f, 0:3]), reads=["qhalo"], writes=[qk_])
                A("act", ACT(qkraw[:, f, 3:3 + NT], ps[:, 0:NT], AF.Copy), reads=[pk, qk_], writes=[qk_])
                A("pool", CP(qhalo[:, l, f, 0:3], qkraw[:, f, NT:NT + 3]), reads=[qk_], writes=["qhalo"])

            def qk_s1(f):
                A("act", ACT(cacc[f % 4][:], qkraw[:, f, 0:NT], AF.Identity, scale=cwqk[:, l, f, 0:1], bias=cbqk[:, l, f:f + 1]),
                  reads=[("qkraw", f)] + SD, writes=[("cacc", f % 4)])

            def qk_s2(f):
                ps, pk = qpb.pop(f)
                ca = cacc[f % 4]
                ck = ("cacc", f % 4)
                A("dve", STT(ca[:], ps[:, 0:NT], cwqk[:, l, f, 3:4], ca[:], ALU.mult, ALU.add), reads=[pk, ck] + SD, writes=[ck])

            def qk_tap(tap):
                def fn(f):
                    ca = cacc[f % 4]
                    ck = ("cacc", f % 4)
                    A("dve", STT(ca[:], qkraw[:, f, tap:tap + NT], cwqk[:, l, f, tap:tap + 1], ca[:], ALU.mult, ALU.add), reads=[("qkraw", f), ck] + SD, writes=[ck])
                return fn

            def qk_s5(f):
                A("act", ACT(qkT[:, f, :], cacc[f % 4][:], AF.Silu), reads=[("cacc", f % 4)], writes=[("qkT", f)])

            gates_tail_done = [False]

            def qk_s0_wrap(f):
                qk_s0(f)
                if f == 1 and not gates_tail_done[0]:
                    gates_tail()
                    gates_tail_done[0] = True

            pipeline(8, [qk_s0_wrap, qk_s1, qk_s2, qk_tap(1), qk_tap(2), qk_s5])

            S_.tag = tg + "uvo"
            for which, bidx in (("u", B_U), ("v", B_V), ("vm", B_VM), ("o", B_O)):
                W, wk = next_block(bidx)
                Wv = W[:, 0:4096].rearrange("p (k n) -> p k n", k=8)
                for c in range(NCH):
                    ps, pk = PB.next()
                    for k in range(8):
                        A("pe", MM(ps[:, 0:512], hT[:, k, c * 128:(c + 1) * 128], Wv[:, k, :], k == 0, k == 7), reads=[wk, ("hT", k)], writes=[pk])
                    if which == "u":
                        A("act", ACT(ug[:, c, :], ps[:, 0:512], AF.Gelu), reads=[pk], writes=[("ug", c)])
                    elif which == "v":
                        A("act", ACT(vg[:, c, :], ps[:, 0:512], AF.Gelu), reads=[pk], writes=[("vg", c)])
                        A("dve", lambda e, c=c: e.bn_stats(out=st6[:, c, :], in_=vg[:, c, :]), reads=[("vg", c)], writes=[("st6", c)])
                        A("dve", lambda e, c=c: e.bn_aggr(out=mv[:, c, :], in_=st6[:, c, :]), reads=[("st6", c)], writes=[("mv", c)])
                    elif which == "vm":
                        for h in range(4):
                            A("act", ACT(vaug[:, c, h, 0:128], ps[:, h * 128:(h + 1) * 128], AF.Copy, scale=sc[:, c, h:h + 1]), reads=[pk, ("vaug", c), "sc"], writes=[("vaug", c)])
                        A("pool", CP(vaug[:, c, :, 128], sc[:, c, :]), reads=["sc", ("vaug", c)], writes=[("vaug", c)])
                    else:
                        A("act", ACT(so[:, c, :], ps[:, 0:512], AF.Sigmoid), reads=[pk], writes=[("so", c)])
                if which == "v":
                    A("act", ACT(lvar[:], mv[:, :, 1], AF.Ln, bias=epsc[:, 0:1]), reads=[("mv", c) for c in range(NCH)] + ["epsc"], writes=["lvar"])
                    A("act", ACT(lrstd[:], lvar[:], AF.Exp, scale=-0.5), reads=["lvar"], writes=["lrstd"])
                    for c in range(NCH):
                        v_ = vt[c % 2]
                        vk = ("vt", c % 2)
                        A("dve", TS(v_[:], vg[:, c, :], mv[:, c, 0:1], lrstd[:, c:c + 1], ALU.subtract, ALU.mult),
                          reads=[("vg", c), ("mv", c), "lrstd"], writes=[vk])
                        A("pool", TT(v_[:], v_[:], lngb[:, 0, :], ALU.mult), reads=[vk, "lngb"], writes=[vk])
                        A("pool", TT(vn[:, c, :], v_[:], lngb[:, 1, :], ALU.add), reads=[vk, "lngb"], writes=[("vn", c)])

            Wws, wsk = next_block(B_WS)
            wsv = Wws[:, 0:1024].rearrange("p (h t) -> p h t", h=8)
            stg = {}

            def mix_P1(c):
                S_.tag = tg + "mixP1"
                cs = slice(c * 128, (c + 1) * 128)
                gi = c % 2
                psm, pmk = PB.next()
                A("pe", MM(psm[:, 0:512], bs128[:, l, :], sel[:], True, False), reads=["bs128", "sel"], writes=[pmk])
                for h in range(8):
                    A("pe", MM(psm[:, h * 64:(h + 1) * 64], wsv[:, h, :], vn[:, c, h * 64:(h + 1) * 64], False, h == 7), reads=[wsk, ("vn", c)], writes=[pmk])
                st, stk = PS.next()
                for h in range(4):
                    A("pe", MM(st[:, h * 128:(h + 1) * 128], qkT[:, 4 + h, cs], qkT[:, h, cs], True, True), reads=[("qkT", 4 + h), ("qkT", h)], writes=[stk])
                kt, ktk = PT.next()
                for h in range(4):
                    A("pe", TR(kt[:, h * 128:(h + 1) * 128], qkT[:, 4 + h, cs], ident[:]), reads=[("qkT", 4 + h), "ident"], writes=[ktk])
                A("dve", TT(gm[gi][:], psm[:, 0:512], ug[:, c, :], ALU.mult), reads=[pmk, ("ug", c)], writes=[("gm", gi)])
                A("dve", TT(Amat[gi][:], st[:, 0:512].rearrange("p (h t) -> p h t", h=4), trib4[:], ALU.mult),
                  reads=[stk, "trib4"], writes=[("Amat", gi, h) for h in range(4)])
                A("dve", CP(Kp[gi][:], kt[:, 0:512].rearrange("p (h t) -> p h t", h=4)), reads=[ktk], writes=[("Kp", gi, h) for h in range(4)])

            def mix_P2(c):
                S_.tag = tg + "mixP2"
                cs = slice(c * 128, (c + 1) * 128)
                gi = c % 2
                pt, ptk = PT.next()
                for j in range(4):
                    A("pe", TR(pt[:, j * 128:(j + 1) * 128], gm[gi][:, j * 128:(j + 1) * 128], ident[:]), reads=[("gm", gi), "ident"], writes=[ptk])
                A("act", ACT(hT[:, 0:4, cs], pt[:, 0:512].rearrange("p (a b) -> p a b", a=4), AF.Copy), reads=[ptk], writes=HT[0:4])
                for pr in range(2):
                    nd, ndk = PB.next()
                    up, upk = PB.next()
                    for j in range(2):
                        h = 2 * pr + j
                        A("pe", MM(nd[:, j * 129:(j + 1) * 129], Amat[gi][:, h, :], vaug[:, c, h, 0:129], True, False), reads=[("Amat", gi, h), ("vaug", c)], writes=[ndk])
                        A("pe", MM(nd[:, j * 129:(j + 1) * 129], qkT[:, h, cs], Cbf[:, l, h, 0:129], False, True), reads=[("qkT", h), ("Cbf", l, h)], writes=[ndk])
                    for j in range(2):
                        h = 2 * pr + j
                        A("pe", MM(up[:, j * 129:(j + 1) * 129], Kp[gi][:, h, :], vaug[:, c, h, 0:129], True, True), reads=[("Kp", gi, h), ("vaug", c)], writes=[upk])
                    d_ = dtmp[gi]
                    dk = ("dtmp", gi, pr)
                    dsl = d_[:, 2 * pr:2 * pr + 2]
                    A("act", ACT(dsl, nd[:, 0:258].rearrange("p (j d) -> p j d", j=2)[:, :, 128], AF.Abs), reads=[ndk], writes=[dk])
                    A("dve", TT(dsl, dsl, einv[:, c, 2 * pr:2 * pr + 2], ALU.max), reads=[dk, "einv"], writes=[dk])
                    A("dve", lambda e, dsl=dsl: e.reciprocal(out=dsl, in_=dsl), reads=[dk], writes=[dk])
                    for j in range(2):
                        h = 2 * pr + j
                        A("dve", STT(hc[gi][:, h * 128:(h + 1) * 128], nd[:, j * 129:j * 129 + 128], d_[:, h:h + 1], so[:, c, h * 128:(h + 1) * 128], ALU.mult, ALU.mult),
                          reads=[ndk, dk, ("so", c)], writes=[("hc", gi, h)])
                    for j in range(2):
                        h = 2 * pr + j
                        prev = ebL[:, c - 1, h:h + 1] if c > 0 else ebLs[:, l, h:h + 1]
                        A("dve", STT(Dst[:, l, h, 0:129], Dst[:, l, h, 0:129], prev, up[:, j * 129:(j + 1) * 129], ALU.mult, ALU.add),
                          reads=[upk, ("Dst", l, h), "ebL", "ebLs"], writes=[("Dst", l, h)])
                A("pool", TT(Cbf[:, l, :, 0:129], Dst[:, l, :, 0:129], ebL[:, c, :].unsqueeze(2).broadcast_to([128, 4, 129]), ALU.mult),
                  reads=[("Dst", l, h) for h in range(4)] + ["ebL"], writes=[("Cbf", l, h) for h in range(4)])
                for h in range(4):
                    A("dve", lambda e, h=h, gi=gi: e.scalar_tensor_tensor(out=junk[:], in0=hc[gi][:, h * 128:(h + 1) * 128], scalar=1.0, in1=hc[gi][:, h * 128:(h + 1) * 128],
                                                                          op0=ALU.mult, op1=ALU.mult, accum_out=ss[gi][:, h:h + 1]),
                      reads=[("hc", gi, h)], writes=["junk", ("ss", gi)])
                A("act", ACT(lss[gi][:], ss[gi][:], AF.Ln, scale=1.0 / 128.0, bias=epsc[:, 0:1]), reads=[("ss", gi), "epsc"], writes=[("lss", gi)])
                A("act", ACT(hrs[gi][:], lss[gi][:], AF.Exp, scale=-0.5), reads=[("lss", gi)], writes=[("hrs", gi)])
                A("pool", TT(ml[gi][:].rearrange("p (h d) -> p h d", h=4), hc[gi][:].rearrange("p (h d) -> p h d", h=4),
                             hrs[gi][:].unsqueeze(2).broadcast_to([128, 4, 128]), ALU.mult),
                  reads=[("hc", gi, h) for h in range(4)] + [("hrs", gi)], writes=[("ml", gi)])

            def mix_P3(c):
                S_.tag = tg + "mixP3"
                cs = slice(c * 128, (c + 1) * 128)
                gi = c % 2
                pt, ptk = PT.next()
                for h in range(4):
                    A("pe", TR(pt[:, h * 128:(h + 1) * 128], ml[gi][:, h * 128:(h + 1) * 128], ident[:]), reads=[("ml", gi), "ident"], writes=[ptk])
                for h in range(4):
                    A("act", ACT(hT[:, 4 + h, cs], pt[:, h * 128:(h + 1) * 128], AF.Copy, scale=mlng[:, l * 4 + h:l * 4 + h + 1]), reads=[ptk] + SD, writes=[("hT", 4 + h)])

            mix_P1(0)
            for c in range(NCH):
                if c + 1 < NCH:
                    mix_P1(c + 1)
                mix_P2(c)
                if c >= 1:
                    mix_P3(c - 1)
            mix_P3(NCH - 1)
            A("dve", CP(ebLs[:, l, :], ebL[:, NCH - 1, :]), reads=["ebL", "ebLs"], writes=["ebLs"])

            S_.tag = tg + "wout"
            Wo = {}
            nst2 = norm_begin()

            def wout_s0(f):
                wbi, j = divmod(f, 4)
                if j == 0:
                    W, wk = next_block(B_OUT + wbi)
                    Wo[wbi] = (W[:, 0:4096].rearrange("p (k n) -> p k n", k=8), wk)
                Wv, wk = Wo[wbi]
                ps, pk = PB.next()
                for k in range(8):
                    A("pe", MM(ps[:, 0:NT], Wv[:, k, j * 128:(j + 1) * 128], hT[:, k, :], k == 0, k == 7), reads=[wk, ("hT", k)], writes=[pk])
                A("dve", TT(xT[:, f, :], ps[:, 0:NT], xT[:, f, :], ALU.add), reads=[pk, ("x", f)], writes=[("x", f)])

            pipeline(8, [wout_s0, lambda f: None, lambda f: norm_accum(nst2, f)])

            S_.tag = tg + "norm2"
            norm_finish(nst2, l * 24 + 8)
            S_.tag = tg + "wup"
            Wu = {}
            pbk_ = {}

            def up_pre(fi):
                A("pool", CP(raw[fi % 4][:, 0:2], halo[:, l, fi, :]), reads=["halo"], writes=[("raw", fi % 4)])

            def up_s0(fi):
                i, jj = divmod(fi, 4)
                if jj == 0:
                    W, wk = next_block(B_UP + i)
                    Wu[i] = (W[:, 0:4096].rearrange("p (k n) -> p k n", k=8), wk)
                Wv, wk = Wu[i]
                ps, pk = PB.next()
                pbk_[fi] = (ps, pk)
                for k in range(8):
                    A("pe", MM(ps[:, 0:NT], Wv[:, k, jj * 128:(jj + 1) * 128], hT[:, k, :], k == 0, k == 7), reads=[wk, ("hT", k)], writes=[pk])
                r_ = raw[fi % 4]
                rk_ = ("raw", fi % 4)
                A("act", ACT(r_[:, 2:2 + NT], ps[:, 0:NT], AF.Copy), reads=[pk, rk_], writes=[rk_])
                if t == 0:
                    emit_prep(l + 1, 2)

            def up_s1(fi):
                A("act", ACT(facc[fi % 3][:], raw[fi % 4][:, 0:NT], AF.Identity, scale=cwff[:, l, fi, 0:1], bias=cbff[:, l, fi:fi + 1]),
                  reads=[("raw", fi % 4)] + SD, writes=[("facc", fi % 3)])
                A("pool", CP(halo[:, l, fi, :], raw[fi % 4][:, NT:NT + 2]), reads=[("raw", fi % 4)], writes=["halo"])

            def up_s2(fi):
                ps, pk = pbk_.pop(fi)
                fa = facc[fi % 3]
                fk = ("facc", fi % 3)
                A("dve", STT(fa[:], ps[:, 0:NT], cwff[:, l, fi, 2:3], fa[:], ALU.mult, ALU.add), reads=[pk, fk] + SD, writes=[fk])

            def up_s3(fi):
                fa = facc[fi % 3]
                fk = ("facc", fi % 3)
                A("dve", STT(fa[:], raw[fi % 4][:, 1:1 + NT], cwff[:, l, fi, 1:2], fa[:], ALU.mult, ALU.add), reads=[("raw", fi % 4), fk] + SD, writes=[fk])

            def up_s4(fi):
                i, jj = divmod(fi, 4)
                fa = facc[fi % 3]
                fk = ("facc", fi % 3)
                if jj < 2:
                    A("act", ACT(sa[jj][:], fa[:], AF.Silu), reads=[fk], writes=[("sa", jj)])
                else:
                    A("pool", TT(actT[:, 2 * i + jj - 2, :], sa[jj - 2][:], fa[:], ALU.mult), reads=[("sa", jj - 2), fk], writes=[("actT", 2 * i + jj - 2)])

            pipeline(44, [up_pre, up_s0, up_s1, up_s2, up_s3, up_s4])
            if t == 0:
                emit_prep(l + 1, 10 ** 6)

            S_.tag = tg + "wdown"
            nst3 = norm_begin()

            def wdown_s0(f):
                W, wk = next_block(B_DOWN + f)
                Wv = W[:, 0:22 * 128].rearrange("p (k n) -> p k n", k=22)
                ps, pk = PB.next()
                for k in range(22):
                    A("pe", MM(ps[:, 0:NT], Wv[:, k, :], actT[:, k, :], k == 0, k == 21), reads=[wk, ("actT", k)], writes=[pk])
                A("dve", TT(xT[:, f, :], ps[:, 0:NT], xT[:, f, :], ALU.add), reads=[pk, ("x", f)], writes=[("x", f)])

            pipeline(8, [wdown_s0, lambda f: None, lambda f: norm_accum(nst3, f)])

            S_.tag = tg + "norm3"
            norm_finish(nst3, l * 24 + 16)
            S_.tag = tg + "ple"
            Wgp = {}
            nst_next = norm_begin()

            for it in range(8 + 3):
                f = it
                fa_ = it - 3
                if 0 <= fa_ < 8:
                    A("act", ACT(sqr[fa_ % 2][:], xT[:, fa_, :], AF.Square), reads=[("x", fa_)], writes=[("sq", fa_ % 2)])
                if f < 8:
                    jb, jj = divmod(f, 2)
                    if jj == 0:
                        W, wk = next_block(B_GP + jb)
                        Wgp[jb] = (W[:, 0:2048].rearrange("p (k n) -> p k n", k=8), W[:, 2048:2560].rearrange("p (k n) -> p k n", k=2), wk)
                    Wg, Wp, wk = Wgp[jb]
                    psg2, pgk2 = PB.next()
                    for k in range(8):
                        A("pe", MM(psg2[:, 0:NT], Wg[:, k, jj * 128:(jj + 1) * 128], hT[:, k, :], k == 0, k == 7), reads=[wk, ("hT", k)], writes=[pgk2])
                if 0 <= fa_ < 8:
                    ps_n, pk_n = nst_next
                    A("pe", MM(ps_n[:, 0:NT], onesm[:], sqr[fa_ % 2][:], fa_ == 0, fa_ == 7), reads=[("sq", fa_ % 2), "onesm"], writes=[pk_n])
                if f < 8:
                    g_ = gt[f % 2]
                    gk_ = ("gt", f % 2)
                    A("act", ACT(g_[:], psg2[:, 0:NT], AF.Sigmoid), reads=[pgk2], writes=[gk_])
                    psp, ppk = PB.next()
                    for k in range(2):
                        A("pe", MM(psp[:, 0:NT], Wp[:, k, jj * 128:(jj + 1) * 128], pT[:, k, :], k == 0, k == 1), reads=[wk, "pT"], writes=[ppk])
                    A("dve", TT(g_[:], psp[:, 0:NT], g_[:], ALU.mult), reads=[ppk, gk_], writes=[gk_])
                    A("dve", TT(xT[:, f, :], xT[:, f, :], g_[:], ALU.add), reads=[gk_, ("x", f)], writes=[("x", f)])
            return nst_next

        for t in range(n_tiles):
            S_.tag = "xload"
            A("sp", DMA(xT[:], xin_v[:, :, t * NT:(t + 1) * NT]), writes=XK, dma_sem=s_x)
            nst = None
            for l in range(n_layers):
                nst = layer_step(t, l, nst)
            S_.tag = "final"
            norm_finish(nst, NL * 24, final_t0=t * NT)

        A("sp", lambda e: e.nop(), reads=["yout"])

        S_.analyze(qsems)
        with nc.Block() as block:
            S_.emit(block)
    return nc


def _pk(v):
    v = np.asarray(v, np.float32)
    return np.ascontiguousarray(v.reshape(-1, 128).T)


def ffn_col_order():
    cols = []
    for i in range(11):
        for j in (2 * i, 2 * i + 1):
            cols.append(np.arange(128 * j, 128 * j + 128))
        for j in (2 * i, 2 * i + 1):
            cols.append(DFF + np.arange(128 * j, 128 * j + 128))
    return np.stack(cols)


def shared_inputs(inp):
    f32 = np.float32
    d = {}
    for nm in ("w_in", "w_out", "w_up", "w_down", "w_ple"):
        d[nm] = np.ascontiguousarray(inp[nm], f32)
    d["w_gate"] = np.ascontiguousarray(inp["w_ple_gate"], f32)
    cols = []
    for i in range(NL):
        cols += [_pk(inp["g_mix"][i]), _pk(inp["g_ffn"][i]), _pk(inp["g_ple"][i])]
    cols.append(_pk(inp["g_final"]))
    d["gv"] = np.ascontiguousarray(np.concatenate(cols, axis=1))
    d["mlng"] = np.ascontiguousarray(np.concatenate([_pk(inp["ml_norm_g"][i]) for i in range(NL)], axis=1))
    lngb = np.stack([np.stack([inp["gm_ln_g"][i], inp["gm_ln_b"][i]]) for i in range(NL)])
    d["lngb"] = np.ascontiguousarray(np.broadcast_to(lngb[:, None], (NL, 128, 2, 512)), f32)
    d["wsT"] = np.ascontiguousarray(np.transpose(np.asarray(inp["gm_ws"], f32), (0, 3, 1, 2)))
    d["bs8"] = np.ascontiguousarray(np.transpose(np.asarray(inp["gm_bs"], f32), (1, 0, 2)))
    cw = np.asarray(inp["ml_conv_w"], f32)
    d["cwqk"] = np.ascontiguousarray(np.transpose(cw.reshape(NL, 4, 8, 128), (3, 0, 2, 1)))
    cb = np.asarray(inp["ml_conv_b"], f32)
    d["cbqk"] = np.ascontiguousarray(np.transpose(cb.reshape(NL, 8, 128), (2, 0, 1)))
    order = ffn_col_order()
    fw = np.asarray(inp["ffn_conv_w"], f32)
    d["cwff"] = np.ascontiguousarray(np.transpose(fw[:, :, order], (3, 0, 2, 1)))
    fb = np.asarray(inp["ffn_conv_b"], f32)
    d["cbff"] = np.ascontiguousarray(np.transpose(fb[:, order], (2, 0, 1)))
    bif = np.concatenate([np.asarray(inp["ml_b_i"], f32), np.asarray(inp["ml_b_f"], f32)], axis=1)
    d["bif"] = np.ascontiguousarray(np.broadcast_to(bif[None], (128, NL, 8)), f32)
    d["ident"] = np.eye(128, dtype=f32)
    d["tri"] = np.triu(np.ones((128, 128), f32))
    sel = np.zeros((128, 512), f32)
    for h in range(8):
        sel[h, h * 64:(h + 1) * 64] = 1.0
    d["sel"] = sel
    return d


_PROG = {}


def get_prog(S, n_layers=NL):
    if (S, n_layers) not in _PROG:
        _PROG[(S, n_layers)] = build_program(S, n_layers)
    return _PROG[(S, n_layers)]


def kernel(**inp):
    x = np.asarray(inp["x"], np.float32)
    p = np.asarray(inp["p"], np.float32)
    B, S, _ = x.shape
    sh = shared_inputs(inp)
    maps = []
    for b in range(B):
        m = dict(sh)
        m["xin"] = np.ascontiguousarray(x[b].T)
        m["pin"] = np.ascontiguousarray(np.transpose(p[:, b], (0, 2, 1)))
        maps.append(m)
    res = run_bass_kernel_spmd(get_prog(S), maps, core_ids=list(range(B)))
    return np.stack([np.asarray(res.results[b]["yout"]).T for b in range(B)]).astype(np.float32)
```

```python
import math
import numpy as np
from contextlib import ExitStack
import concourse.bass as bass
import concourse.mybir as mybir
from concourse.bass_utils import run_bass_kernel_spmd

F32 = mybir.dt.float32
BF16 = mybir.dt.bfloat16
AF = mybir.ActivationFunctionType
ALU = mybir.AluOpType

D = 1024
NT = 512
NCH = NT // 128
DFF = 2816
EPS = 1e-6
NSLOT = 4
WBLK = 4096
SAME_ENGINE_SYNC = True
SAME_ENGINE_ALL = True
SAME_ENGINE_SYNC_Q = {"act", "dve", "pool"}
ANNOTATE = False


class Op:
    __slots__ = ("q", "fn", "reads", "writes", "sem", "inc", "needs_inc", "count", "waits", "tag")


class Sched:
    def __init__(self):
        self.ops = []
        self.tag = None

    def add(self, q, fn, reads=(), writes=(), dma_sem=None):
        o = Op()
        o.q, o.fn, o.reads, o.writes = q, fn, tuple(reads), tuple(writes)
        o.sem = dma_sem
        o.inc = 16 if dma_sem is not None else 1
        o.needs_inc = dma_sem is not None
        o.count = None
        o.waits = {}
        o.tag = self.tag
        self.ops.append(o)
        return o

    def analyze(self, qsems):
        last_w, readers, need = {}, {}, []
        for o in self.ops:
            deps = {}
            for r in o.reads:
                w = last_w.get(r)
                if w is not None:
                    deps[id(w)] = (w, True)
            for k in o.writes:
                w = last_w.get(k)
                if w is not None and id(w) not in deps:
                    deps[id(w)] = (w, False)
                for rd in readers.get(k, ()):
                    if id(rd) not in deps:
                        deps[id(rd)] = (rd, False)
            deps.pop(id(o), None)
            nd = []
            for a, raw in deps.values():
                if a.sem is None and a.q == o.q:
                    if o.q == "pe" or not SAME_ENGINE_SYNC or o.q not in SAME_ENGINE_SYNC_Q:
                        continue
                    if not raw and not SAME_ENGINE_ALL:
                        continue
                nd.append(a)
                a.needs_inc = True
            need.append(nd)
            for r in o.reads:
                readers.setdefault(r, []).append(o)
            for k in o.writes:
                last_w[k] = o
                readers[k] = []
        cnt = {}
        for o in self.ops:
            if o.needs_inc:
                s = o.sem if o.sem is not None else qsems[o.q]
                cnt[s] = cnt.get(s, 0) + o.inc
                o.count = (s, cnt[s])
        waited = {}
        for o, nd in zip(self.ops, need):
            w = {}
            for a in nd:
                s, c = a.count
                if c > w.get(s, 0):
                    w[s] = c
            qw = waited.setdefault(o.q, {})
            for s, c in list(w.items()):
                if qw.get(s, 0) >= c:
                    del w[s]
                else:
                    qw[s] = c
            o.waits = w
        self.final_counts = cnt

    def emit(self, block):
        byq = {}
        for o in self.ops:
            byq.setdefault(o.q, []).append(o)

        def run(eng, ops):
            for o in ops:
                for s, c in o.waits.items():
                    eng.wait_ge(s, c)
                ins = o.fn(eng)
                if ANNOTATE and o.tag is not None:
                    ins.annotate(o.tag)
                if o.needs_inc:
                    ins.then_inc(o.count[0], o.inc)

        names = {"pe": "tensor", "act": "scalar", "dve": "vector", "pool": "gpsimd", "sp": "sync"}
        for q in ["sp", "pe", "act", "dve", "pool"]:
            if q in byq:
                getattr(block, names[q])(lambda eng, ops=byq[q]: run(eng, ops))


class Ring:
    def __init__(self, items):
        self.items = items
        self.i = 0

    def next(self):
        it = self.items[self.i % len(self.items)]
        self.i += 1
        return it


NL = 4


def weight_blocks():
    b = []
    for i in range(2):
        b.append([("w_in", 8, 1024 + 512 * i, 512, 0, 512, 0)])
    for c0 in (0, 512, 2048, 2560):
        b.append([("w_in", 8, c0, 512, 0, 512, 0)])
    b.append(None)
    for i in range(2):
        b.append([("w_out", 8, 512 * i, 512, 0, 512, 0)])
    for i in range(11):
        b.append([("w_up", 8, 256 * i, 256, 0, 512, 0), ("w_up", 8, DFF + 256 * i, 256, 0, 512, 256)])
    for f in range(8):
        b.append([("w_down", 22, 128 * f, 128, 0, 128, 0)])
    for j in range(4):
        b.append([("w_gate", 8, 256 * j, 256, 0, 256, 0), ("w_ple", 2, 256 * j, 256, 2048, 256, 0)])
    return b


WB = weight_blocks()
NWB = len(WB)
B_QK, B_U, B_V, B_VM, B_O, B_WS, B_OUT, B_UP, B_DOWN, B_GP = 0, 2, 3, 4, 5, 6, 7, 9, 20, 28
WB_SIZE = []
for _b in WB:
    if _b is None:
        WB_SIZE.append(1024)
    else:
        WB_SIZE.append(max(base + K * W for (_, K, _, _, base, W, _) in _b))


def MM(out, lhsT, rhs, start, stop):
    return lambda e: e.matmul(out, lhsT=lhsT, rhs=rhs, start=start, stop=stop)


def TR(out, in_, ident):
    return lambda e: e.transpose(out, in_, ident)


def ACT(out, in_, func, **kw):
    return lambda e: e.activation(out=out, in_=in_, func=func, **kw)


def TT(out, in0, in1, op):
    return lambda e: e.tensor_tensor(out=out, in0=in0, in1=in1, op=op)


def TS(out, in0, s1, s2, op0, op1=None):
    if op1 is None:
        return lambda e: e.tensor_scalar(out=out, in0=in0, scalar1=s1, scalar2=None, op0=op0)
    return lambda e: e.tensor_scalar(out=out, in0=in0, scalar1=s1, scalar2=s2, op0=op0, op1=op1)


def TSS(out, in_, scalar, op):
    return lambda e: e.tensor_single_scalar(out=out, in_=in_, scalar=scalar, op=op)


def STT(out, in0, scalar, in1, op0, op1):
    return lambda e: e.scalar_tensor_tensor(out=out, in0=in0, scalar=scalar, in1=in1, op0=op0, op1=op1)


def CP(out, in_):
    return lambda e: e.tensor_copy(out=out, in_=in_)


def MS(out, val):
    return lambda e: e.memset(out, val)


def DMA(out, in_):
    return lambda e: e.dma_start(out=out, in_=in_)


def pipeline(n, stages):
    ns = len(stages)
    for i in range(n + ns - 1):
        for si in reversed(range(ns)):
            c = i - si
            if 0 <= c < n:
                stages[si](c)


def build_program(S, n_layers=NL):
    n_tiles = S // NT
    nc = bass.Bass("TRN2", target_bir_lowering=False)

    def din(name, shape):
        return nc.dram_tensor(name, list(shape), F32, kind="ExternalInput").ap()

    xin = din("xin", [D, S])
    pin = din("pin", [NL, 256, S])
    wsrc = {
        "w_in": din("w_in", [NL, D, 3080]), "w_out": din("w_out", [NL, D, D]), "w_up": din("w_up", [NL, D, 2 * DFF]),
        "w_down": din("w_down", [NL, DFF, D]), "w_gate": din("w_gate", [NL, D, D]), "w_ple": din("w_ple", [NL, 256, D]),
    }
    gv_d = din("gv", [128, NL * 24 + 8])
    mlng_d = din("mlng", [128, NL * 4])
    lngb_d = din("lngb", [NL, 128, 2, 512])
    wsT_d = din("wsT", [NL, 128, 8, 128])
    bs8_d = din("bs8", [8, NL, 128])
    cwqk_d = din("cwqk", [128, NL, 8, 4])
    cbqk_d = din("cbqk", [128, NL, 8])
    cwff_d = din("cwff", [128, NL, 44, 3])
    cbff_d = din("cbff", [128, NL, 44])
    bif_d = din("bif", [128, NL, 8])
    ident_d = din("ident", [128, 128])
    tri_d = din("tri", [128, 128])
    sel_d = din("sel", [128, 512])
    yout = nc.dram_tensor("yout", [D, S], F32, kind="ExternalOutput").ap()
    wb_d = nc.dram_tensor("wb_scratch", [NL, NWB, 128, WBLK], BF16).ap()

    xin_v = xin.rearrange("(f p) s -> p f s", p=128)
    yout_v = yout.rearrange("(f p) s -> p f s", p=128)

    with ExitStack() as es:
        def sb(name, shape, dt=F32):
            return es.enter_context(nc.sbuf_tensor("sb_" + name, list(shape), dt))

        def psum(name, shape, dt=F32):
            return es.enter_context(nc.psum_tensor(name, list(shape), dt))

        def sem(name):
            return es.enter_context(nc.semaphore(name))

        qsems = {q: sem("q_" + q) for q in ["pe", "act", "dve", "pool"]}
        S_ = Sched()
        A = S_.add

        xT = sb("xT", [128, 8, NT])
        hT = sb("hT", [128, 8, NT], BF16)
        sqr = [sb(f"sq{i}", [128, NT], BF16) for i in range(2)]
        rstd = sb("rstd", [128, NT])
        ntmp = [sb(f"ntmp{i}", [128, NT]) for i in range(2)]
        qkraw = sb("qkraw", [128, 8, NT + 4])
        qkT = sb("qkT", [128, 8, NT], BF16)
        cacc = [sb(f"cacc{i}", [128, NT]) for i in range(4)]
        gpre = sb("gpre", [128, NCH, 8])
        e1 = sb("e1", [128, NCH, 4])
        l1 = sb("l1", [128, NCH, 4])
        t1 = sb("t1", [128, NCH, 4])
        sc = sb("sc", [128, NCH, 4])
        einv = sb("einv", [128, NCH, 4])
        ebL = sb("ebL", [128, NCH, 4])
        ebLs = sb("ebLs", [128, NL, 4])
        ug = sb("ug", [128, NCH, 512], BF16)
        vg = sb("vg", [128, NCH, 512], BF16)
        vt = [sb(f"vt{i}", [128, 512]) for i in range(2)]
        vn = sb("vn", [128, NCH, 512], BF16)
        st6 = sb("st6", [128, NCH, 6])
        mv = sb("mv", [128, NCH, 2])
        lvar = sb("lvar", [128, NCH])
        lrstd = sb("lrstd", [128, NCH])
        vaug = sb("vaug", [128, NCH, 4, 132], BF16)
        so = sb("so", [128, NCH, 512], BF16)
        gm = [sb(f"gm{i}", [128, 512], BF16) for i in range(2)]
        Amat = [sb(f"Amat{i}", [128, 4, 128], BF16) for i in range(2)]
        Kp = [sb(f"Kp{i}", [128, 4, 128], BF16) for i in range(2)]
        dtmp = [sb(f"dtmp{i}", [128, 4]) for i in range(2)]
        hc = [sb(f"hc{i}", [128, 512]) for i in range(2)]
        junk = sb("junk", [128, 128])
        ss = [sb(f"ss{i}", [128, 4]) for i in range(2)]
        lss = [sb(f"lss{i}", [128, 4]) for i in range(2)]
        hrs = [sb(f"hrs{i}", [128, 4]) for i in range(2)]
        ml = [sb(f"ml{i}", [128, 512], BF16) for i in range(2)]
        Dst = sb("Dst", [128, NL, 4, 132])
        Cbf = sb("Cbf", [128, NL, 4, 132], BF16)
        raw = [sb(f"raw{i}", [128, NT + 2]) for i in range(4)]
        facc = [sb(f"facc{i}", [128, NT]) for i in range(3)]
        sa = [sb(f"sa{i}", [128, NT]) for i in range(2)]
        actT = sb("actT", [128, 22, NT], BF16)
        halo = sb("halo", [128, NL, 44, 2])
        qhalo = sb("qhalo", [128, NL, 8, 4])
        pT = sb("pT", [128, 2, NT], BF16)
        gt = [sb(f"gt{i}", [128, NT]) for i in range(2)]
        wring = sb("wring", [128, NSLOT, WBLK], BF16)
        lngb = sb("lngb", [128, 2, 512])
        ident = sb("ident", [128, 128], BF16)
        tri = sb("tri", [128, 128])
        trib = sb("trib", [128, 128], BF16)
        trib4 = sb("trib4", [128, 4, 128], BF16)
        onesf = sb("onesf", [128, 128])
        onesm = sb("onesm", [128, 128], BF16)
        sel = sb("sel", [128, 512], BF16)
        bs128 = sb("bs128", [128, NL, 128], BF16)
        gv = sb("gv", [128, NL * 24 + 8])
        mlng = sb("mlng", [128, NL * 4])
        cwqk = sb("cwqk", [128, NL, 8, 4])
        cbqk = sb("cbqk", [128, NL, 8])
        cwff = sb("cwff", [128, NL, 44, 3])
        cbff = sb("cbff", [128, NL, 44])
        bif = sb("bif", [128, NL, 8])
        epsc = sb("epsc", [128, 1])
        wg = sb("wg", [128, NL, 8, 8], BF16)

        PBt = [psum(f"pb{i}", [128, 512]) for i in range(4)]
        PTt = [psum(f"pt{i}", [128, 1024], BF16) for i in range(2)]
        PSt = [psum(f"ps{i}", [128, 512]) for i in range(2)]
        PB = Ring([(PBt[i], ("pb", i)) for i in range(4)])
        PT = Ring([(PTt[i], ("pt", i)) for i in range(2)])
        PS = Ring([(PSt[i], ("ps", i)) for i in range(2)])

        s_setup = sem("d_setup")
        s_w = [sem(f"d_w{i}") for i in range(NSLOT)]
        s_xf = [sem(f"d_x{f}") for f in range(8)]
        s_p = sem("d_p")
        s_ln = sem("d_ln")
        s_yo = [sem(f"d_yo{i}") for i in range(2)]
        s_prep = [sem(f"d_prep{i}") for i in range(4)]
        s_ws = sem("d_ws")
        s_wg = sem("d_wg")
        s_ws2 = sem("d_ws2")

        XK = [("x", f) for f in range(8)]
        HT = [("hT", f) for f in range(8)]

        S_.tag = "setup"

        def load_const(dst_ap, src_ap, keys):
            A("sp", DMA(dst_ap, src_ap), writes=keys, dma_sem=s_setup)

        load_const(gv[:], gv_d, ["gv"])
        load_const(mlng[:], mlng_d, ["mlng"])
        load_const(cwqk[:], cwqk_d, ["cwqk"])
        load_const(cbqk[:], cbqk_d, ["cbqk"])
        load_const(cwff[:], cwff_d, ["cwff"])
        load_const(cbff[:], cbff_d, ["cbff"])
        load_const(bif[:], bif_d, ["bif"])
        load_const(tri[:], tri_d, ["tri"])
        load_const(xT[:, 0, :], sel_d, [("x", 0)])
        load_const(xT[:, 1, 0:128], ident_d, [("x", 1)])
        load_const(xT[0:8, 2, :].rearrange("p (l t) -> p l t", l=NL), bs8_d, [("x", 2)])
        setup_keys = ["gv", "mlng", "cwqk", "cbqk", "cwff", "cbff", "bif", "tri"] + XK[0:3]
        A("pool", MS(epsc[:], EPS), reads=setup_keys, writes=["epsc", "setup_done"])
        SD = ["setup_done"]
        A("pool", CP(sel[:], xT[:, 0, :]), reads=SD + [("x", 0)], writes=["sel"])
        A("pool", CP(ident[:], xT[:, 1, 0:128]), reads=SD + [("x", 1)], writes=["ident"])
        A("pool", CP(trib[:], tri[:]), reads=SD, writes=["trib"])
        for h in range(4):
            A("pool", CP(trib4[:, h, :], tri[:]), reads=SD, writes=["trib4"])
        A("pool", MS(bs128[:], 0.0), reads=SD, writes=["bs128"])
        A("pool", CP(bs128[0:8, :, :], xT[0:8, 2, :].rearrange("p (l t) -> p l t", l=NL)), reads=SD + ["bs128", ("x", 2)], writes=["bs128"])
        A("dve", MS(onesf[:], 1.0), writes=["onesf"])
        A("dve", MS(onesm[:], 1.0 / D), writes=["onesm"])
        A("dve", MS(qhalo[:], 0.0), writes=["qhalo"])
        A("dve", MS(halo[:], 0.0), writes=["halo"])
        A("dve", MS(Dst[:], 0.0), writes=[("Dst", l, h) for l in range(NL) for h in range(4)])
        A("dve", MS(Cbf[:], 0.0), writes=[("Cbf", l, h) for l in range(NL) for h in range(4)])
        A("dve", MS(ebLs[:], 1.0), writes=["ebLs"])
        A("pool", MS(vaug[:], 1.0), writes=[("vaug", c) for c in range(NCH)])
        for l in range(n_layers):
            A("pool", DMA(wg[:, l, :, :], wsrc["w_in"][l].rearrange("(k p) n -> p k n", p=128)[:, :, 3072:3080]),
              writes=[("wg", l), "wgslot"], dma_sem=s_wg)
        xflat = xT[:, 4:6, :].rearrange("p a b -> p (a b)")
        hflat = hT[:, 0:2, :].rearrange("p a b -> p (a b)")
        for l in range(n_layers):
            A("sp", DMA(xflat.rearrange("p (h t) -> p h t", h=8), wsT_d[l]), writes=[("x", 4), ("x", 5)], dma_sem=s_ws)
            for h in range(8):
                A("dve", TT(hflat[:, h * 128:(h + 1) * 128], xflat[:, h * 128:(h + 1) * 128], tri[:], ALU.mult),
                  reads=SD + [("x", 4), ("x", 5)], writes=[("hT", 0), ("hT", 1)])
            A("sp", DMA(wb_d[l, B_WS, :, 0:1024], hflat), reads=[("hT", 0), ("hT", 1)], writes=[("wbd", l, B_WS, 0)], dma_sem=s_ws2)

        np_ = [0]
        wbd_keys = {}
        prep_queue = {l: [] for l in range(n_layers)}
        for l in range(n_layers):
            for bi, parts in enumerate(WB):
                if parts is None:
                    wbd_keys[(l, bi)] = [("wbd", l, bi, 0)]
                    continue
                keys = []
                for pi, (src, K, c0, n, base, W, off) in enumerate(parts):
                    srcv = wsrc[src][l].rearrange("(k p) n -> p k n", p=128)[:, :, c0:c0 + n]
                    dstv = wb_d[l, bi, :, base:base + K * W].rearrange("p (k n) -> p k n", k=K)[:, :, off:off + n]
                    key = ("wbd", l, bi, pi)
                    prep_queue[l].append((dstv, srcv, key))
                    keys.append(key)
                wbd_keys[(l, bi)] = keys

        def emit_prep(l, n):
            for _ in range(n):
                if l >= n_layers or not prep_queue[l]:
                    return
                dstv, srcv, key = prep_queue[l].pop(0)
                sl = np_[0] % 4
                np_[0] += 1
                A("pool", DMA(dstv, srcv), writes=[key, ("prepslot", sl)], dma_sem=s_prep[sl])

        emit_prep(0, 10 ** 6)

        gblk = [0]
        total_blocks = n_tiles * n_layers * NWB

        def issue_wload(g):
            if g >= total_blocks:
                return
            bi = g % NWB
            l = (g // NWB) % n_layers
            sl = g % NSLOT
            n = WB_SIZE[bi]
            A("sp", DMA(wring[:, sl, 0:n], wb_d[l, bi, :, 0:n]), reads=wbd_keys[(l, bi)], writes=[("wr", sl)], dma_sem=s_w[sl])

        def next_block(expect_bi):
            g = gblk[0]
            assert g % NWB == expect_bi, (g % NWB, expect_bi)
            issue_wload(g + NSLOT - 1)
            gblk[0] += 1
            sl = g % NSLOT
            return wring[:, sl, :], ("wr", sl)

        def norm_begin():
            return PS.next()

        def norm_accum(nst, f):
            ps, pk = nst
            i = f % 2
            A("act", ACT(sqr[i][:], xT[:, f, :], AF.Square), reads=[("x", f)], writes=[("sq", i)])
            A("pe", MM(ps[:, 0:NT], onesm[:], sqr[i][:], f == 0, f == 7), reads=[("sq", i), "onesm"], writes=[pk])

        def norm_finish(nst, gcol, final_t0=None):
            ps, pk = nst
            A("act", ACT(rstd[:], ps[:, 0:NT], AF.Ln, bias=epsc[:, 0:1]), reads=[pk, "epsc"], writes=["rstd"])
            A("act", ACT(rstd[:], rstd[:], AF.Exp, scale=-0.5), reads=["rstd"], writes=["rstd"])
            for f in range(8):
                if final_t0 is not None:
                    i = f % 2
                    A("dve", STT(ntmp[i][:], xT[:, f, :], gv[:, gcol + f:gcol + f + 1], rstd[:], ALU.mult, ALU.mult),
                      reads=[("x", f), "rstd"] + SD, writes=[("ntmp", i)])
                    A("sp", DMA(yout_v[:, f, final_t0:final_t0 + NT], ntmp[i][:]), reads=[("ntmp", i)], writes=["yout"], dma_sem=s_yo[i])
                else:
                    A("dve", STT(hT[:, f, :], xT[:, f, :], gv[:, gcol + f:gcol + f + 1], rstd[:], ALU.mult, ALU.mult),
                      reads=[("x", f), "rstd"] + SD, writes=[("hT", f)])

        def norm(gcol, final_t0=None):
            nst = norm_begin()
            for f in range(8):
                norm_accum(nst, f)
            norm_finish(nst, gcol, final_t0)

        for g in range(NSLOT - 1):
            issue_wload(g)

        def layer_step(t, l, nst_in):
            t0 = t * NT
            tg = f"L{l}"
            S_.tag = tg + "norm1"
            A("pool", DMA(pT[:], pin[l].rearrange("(k p) s -> p k s", p=128)[:, :, t0:t0 + NT]), writes=["pT"], dma_sem=s_p)
            A("sp", DMA(lngb[:], lngb_d[l]), writes=["lngb"], dma_sem=s_ln)

            if nst_in is None:
                norm(l * 24)
            else:
                norm_finish(nst_in, l * 24)
            S_.tag = tg + "gates"
            psg, pgk = PS.next()
            for c in range(NCH):
                for k in range(8):
                    A("pe", MM(psg[:, c * 8:(c + 1) * 8], hT[:, k, c * 128:(c + 1) * 128], wg[:, l, k, :], k == 0, k == 7),
                      reads=[("hT", k), ("wg", l)], writes=[pgk])
            for c in range(NCH):
                A("dve", TT(gpre[:, c, :], psg[:, c * 8:(c + 1) * 8], bif[:, l, :], ALU.add), reads=[pgk] + SD, writes=["gpre"])
            A("act", ACT(e1[:], gpre[:, :, 4:8], AF.Exp, scale=-1.0), reads=["gpre"], writes=["e1"])
            A("act", ACT(l1[:], e1[:], AF.Ln, bias=1.0), reads=["e1"], writes=["l1"])
            def gates_tail():
                S_.tag = tg + "gates"
                psb, pbk = PS.next()
                l1v = l1[:].rearrange("p c h -> p (c h)")
                A("pe", MM(psb[:, 0:16], tri[:], l1v, True, True), reads=["l1"] + SD, writes=[pbk])
                A("pe", MM(psb[:, 16:32], onesf[:], l1v, True, True), reads=["l1", "onesf"], writes=[pbk])
                bneg = psb[:, 0:16].rearrange("p (c h) -> p c h", c=NCH)
                bLneg = psb[:, 16:32].rearrange("p (c h) -> p c h", c=NCH)
                A("dve", STT(t1[:], gpre[:, :, 0:4], math.log(128.0 ** -0.5), bneg, ALU.add, ALU.add), reads=["gpre", pbk], writes=["t1"])
                A("act", ACT(sc[:], t1[:], AF.Exp), reads=["t1"], writes=["sc"])
                A("act", ACT(einv[:], bneg, AF.Exp), reads=[pbk, "t1"], writes=["einv"])
                A("act", ACT(ebL[:], bLneg, AF.Exp, scale=-1.0), reads=[pbk, "t1"], writes=["ebL"])
                S_.tag = tg + "qk"

            S_.tag = tg + "qk"
            Wq = {}
            qpb = {}

            def qk_s0(f):
                wbi, j = divmod(f, 4)
                if j == 0:
                    W, wk = next_block(B_QK + wbi)
                    Wq[wbi] = (W[:, 0:4096].rearrange("p (k n) -> p k n", k=8), wk)
                Wv, wk = Wq[wbi]
                ps, pk = PB.next()
                qpb[f] = (ps, pk)
                for k in range(8):
                    A("pe", MM(ps[:, 0:NT], Wv[:, k, j * 128:(j + 1) * 128], hT[:, k, :], k == 0, k == 7), reads=[wk, ("hT", k)], writes=[pk])
                qk_ = ("qkraw", f)
                A("pool", CP(qkraw[:, f, 0:3], qhalo[:, l, f, 0:3]), reads=["qhalo"], writes=[qk_])
                A("act", ACT(qkraw[:, f, 3:3 + NT], ps[:, 0:NT], AF.Copy), reads=[pk, qk_], writes=[qk_])
                A("pool", CP(qhalo[:, l, f, 0:3], qkraw[:, f, NT:NT + 3]), reads=[qk_], writes=["qhalo"])

            def qk_s1(f):
                A("act", ACT(cacc[f % 4][:], qkraw[:, f, 0:NT], AF.Identity, scale=cwqk[:, l, f, 0:1], bias=cbqk[:, l, f:f + 1]),
                  reads=[("qkraw", f)] + SD, writes=[("cacc", f % 4)])

            def qk_s2(f):
                ps, pk = qpb.pop(f)
                ca = cacc[f % 4]
                ck = ("cacc", f % 4)
                A("dve", STT(ca[:], ps[:, 0:NT], cwqk[:, l, f, 3:4], ca[:], ALU.mult, ALU.add), reads=[pk, ck] + SD, writes=[ck])

            def qk_tap(tap):
                def fn(f):
                    ca = cacc[f % 4]
                    ck = ("cacc", f % 4)
                    A("dve", STT(ca[:], qkraw[:, f, tap:tap + NT], cwqk[:, l, f, tap:tap + 1], ca[:], ALU.mult, ALU.add), reads=[("qkraw", f), ck] + SD, writes=[ck])
                return fn

            def qk_s5(f):
                A("act", ACT(qkT[:, f, :], cacc[f % 4][:], AF.Silu), reads=[("cacc", f % 4)], writes=[("qkT", f)])

            gates_tail_done = [False]

            def qk_s0_wrap(f):
                qk_s0(f)
                if f == 1 and not gates_tail_done[0]:
                    gates_tail()
                    gates_tail_done[0] = True

            pipeline(8, [qk_s0_wrap, qk_s1, qk_s2, qk_tap(1), qk_tap(2), qk_s5])

            S_.tag = tg + "uvo"
            for which, bidx in (("u", B_U), ("v", B_V), ("vm", B_VM), ("o", B_O)):
                W, wk = next_block(bidx)
                Wv = W[:, 0:4096].rearrange("p (k n) -> p k n", k=8)
                for c in range(NCH):
                    ps, pk = PB.next()
                    for k in range(8):
                        A("pe", MM(ps[:, 0:512], hT[:, k, c * 128:(c + 1) * 128], Wv[:, k, :], k == 0, k == 7), reads=[wk, ("hT", k)], writes=[pk])
                    if which == "u":
                        A("act", ACT(ug[:, c, :], ps[:, 0:512], AF.Gelu), reads=[pk], writes=[("ug", c)])
                    elif which == "v":
                        A("act", ACT(vg[:, c, :], ps[:, 0:512], AF.Gelu), reads=[pk], writes=[("vg", c)])
                        A("dve", lambda e, c=c: e.bn_stats(out=st6[:, c, :], in_=vg[:, c, :]), reads=[("vg", c)], writes=[("st6", c)])
                        A("dve", lambda e, c=c: e.bn_aggr(out=mv[:, c, :], in_=st6[:, c, :]), reads=[("st6", c)], writes=[("mv", c)])
                    elif which == "vm":
                        for h in range(4):
                            A("act", ACT(vaug[:, c, h, 0:128], ps[:, h * 128:(h + 1) * 128], AF.Copy, scale=sc[:, c, h:h + 1]), reads=[pk, ("vaug", c), "sc"], writes=[("vaug", c)])
                        A("pool", CP(vaug[:, c, :, 128], sc[:, c, :]), reads=["sc", ("vaug", c)], writes=[("vaug", c)])
                    else:
                        A("act", ACT(so[:, c, :], ps[:, 0:512], AF.Sigmoid), reads=[pk], writes=[("so", c)])
                if which == "v":
                    A("act", ACT(lvar[:], mv[:, :, 1], AF.Ln, bias=epsc[:, 0:1]), reads=[("mv", c) for c in range(NCH)] + ["epsc"], writes=["lvar"])
                    A("act", ACT(lrstd[:], lvar[:], AF.Exp, scale=-0.5), reads=["lvar"], writes=["lrstd"])
                    for c in range(NCH):
                        v_ = vt[c % 2]
                        vk = ("vt", c % 2)
                        A("dve", TS(v_[:], vg[:, c, :], mv[:, c, 0:1], lrstd[:, c:c + 1], ALU.subtract, ALU.mult),
                          reads=[("vg", c), ("mv", c), "lrstd"], writes=[vk])
                        A("pool", TT(v_[:], v_[:], lngb[:, 0, :], ALU.mult), reads=[vk, "lngb"], writes=[vk])
                        A("pool", TT(vn[:, c, :], v_[:], lngb[:, 1, :], ALU.add), reads=[vk, "lngb"], writes=[("vn", c)])

            Wws, wsk = next_block(B_WS)
            wsv = Wws[:, 0:1024].rearrange("p (h t) -> p h t", h=8)
            stg = {}

            def mix_P1(c):
                S_.tag = tg + "mixP1"
                cs = slice(c * 128, (c + 1) * 128)
                gi = c % 2
                psm, pmk = PB.next()
                A("pe", MM(psm[:, 0:512], bs128[:, l, :], sel[:], True, False), reads=["bs128", "sel"], writes=[pmk])
                for h in range(8):
                    A("pe", MM(psm[:, h * 64:(h + 1) * 64], wsv[:, h, :], vn[:, c, h * 64:(h + 1) * 64], False, h == 7), reads=[wsk, ("vn", c)], writes=[pmk])
                st, stk = PS.next()
                for h in range(4):
                    A("pe", MM(st[:, h * 128:(h + 1) * 128], qkT[:, 4 + h, cs], qkT[:, h, cs], True, True), reads=[("qkT", 4 + h), ("qkT", h)], writes=[stk])
                kt, ktk = PT.next()
                for h in range(4):
                    A("pe", TR(kt[:, h * 128:(h + 1) * 128], qkT[:, 4 + h, cs], ident[:]), reads=[("qkT", 4 + h), "ident"], writes=[ktk])
                A("dve", TT(gm[gi][:], psm[:, 0:512], ug[:, c, :], ALU.mult), reads=[pmk, ("ug", c)], writes=[("gm", gi)])
                A("dve", TT(Amat[gi][:], st[:, 0:512].rearrange("p (h t) -> p h t", h=4), trib4[:], ALU.mult),
                  reads=[stk, "trib4"], writes=[("Amat", gi, h) for h in range(4)])
                A("dve", CP(Kp[gi][:], kt[:, 0:512].rearrange("p (h t) -> p h t", h=4)), reads=[ktk], writes=[("Kp", gi, h) for h in range(4)])

            def mix_P2(c):
                S_.tag = tg + "mixP2"
                cs = slice(c * 128, (c + 1) * 128)
                gi = c % 2
                pt, ptk = PT.next()
                for j in range(4):
                    A("pe", TR(pt[:, j * 128:(j + 1) * 128], gm[gi][:, j * 128:(j + 1) * 128], ident[:]), reads=[("gm", gi), "ident"], writes=[ptk])
                A("act", ACT(hT[:, 0:4, cs], pt[:, 0:512].rearrange("p (a b) -> p a b", a=4), AF.Copy), reads=[ptk], writes=HT[0:4])
                for pr in range(2):
                    nd, ndk = PB.next()
                    up, upk = PB.next()
                    for j in range(2):
                        h = 2 * pr + j
                        A("pe", MM(nd[:, j * 129:(j + 1) * 129], Amat[gi][:, h, :], vaug[:, c, h, 0:129], True, False), reads=[("Amat", gi, h), ("vaug", c)], writes=[ndk])
                        A("pe", MM(nd[:, j * 129:(j + 1) * 129], qkT[:, h, cs], Cbf[:, l, h, 0:129], False, True), reads=[("qkT", h), ("Cbf", l, h)], writes=[ndk])
                    for j in range(2):
                        h = 2 * pr + j
                        A("pe", MM(up[:, j * 129:(j + 1) * 129], Kp[gi][:, h, :], vaug[:, c, h, 0:129], True, True), reads=[("Kp", gi, h), ("vaug", c)], writes=[upk])
                    d_ = dtmp[gi]
                    dk = ("dtmp", gi, pr)
                    dsl = d_[:, 2 * pr:2 * pr + 2]
                    A("act", ACT(dsl, nd[:, 0:258].rearrange("p (j d) -> p j d", j=2)[:, :, 128], AF.Abs), reads=[ndk], writes=[dk])
                    A("dve", TT(dsl, dsl, einv[:, c, 2 * pr:2 * pr + 2], ALU.max), reads=[dk, "einv"], writes=[dk])
                    A("dve", lambda e, dsl=dsl: e.reciprocal(out=dsl, in_=dsl), reads=[dk], writes=[dk])
                    for j in range(2):
                        h = 2 * pr + j
                        A("dve", STT(hc[gi][:, h * 128:(h + 1) * 128], nd[:, j * 129:j * 129 + 128], d_[:, h:h + 1], so[:, c, h * 128:(h + 1) * 128], ALU.mult, ALU.mult),
                          reads=[ndk, dk, ("so", c)], writes=[("hc", gi, h)])
                    for j in range(2):
                        h = 2 * pr + j
                        prev = ebL[:, c - 1, h:h + 1] if c > 0 else ebLs[:, l, h:h + 1]
                        A("dve", STT(Dst[:, l, h, 0:129], Dst[:, l, h, 0:129], prev, up[:, j * 129:(j + 1) * 129], ALU.mult, ALU.add),
                          reads=[upk, ("Dst", l, h), "ebL", "ebLs"], writes=[("Dst", l, h)])
                A("pool", TT(Cbf[:, l, :, 0:129], Dst[:, l, :, 0:129], ebL[:, c, :].unsqueeze(2).broadcast_to([128, 4, 129]), ALU.mult),
                  reads=[("Dst", l, h) for h in range(4)] + ["ebL"], writes=[("Cbf", l, h) for h in range(4)])
                for h in range(4):
                    A("dve", lambda e, h=h, gi=gi: e.scalar_tensor_tensor(out=junk[:], in0=hc[gi][:, h * 128:(h + 1) * 128], scalar=1.0, in1=hc[gi][:, h * 128:(h + 1) * 128],
                                                                          op0=ALU.mult, op1=ALU.mult, accum_out=ss[gi][:, h:h + 1]),
                      reads=[("hc", gi, h)], writes=["junk", ("ss", gi)])
                A("act", ACT(lss[gi][:], ss[gi][:], AF.Ln, scale=1.0 / 128.0, bias=epsc[:, 0:1]), reads=[("ss", gi), "epsc"], writes=[("lss", gi)])
                A("act", ACT(hrs[gi][:], lss[gi][:], AF.Exp, scale=-0.5), reads=[("lss", gi)], writes=[("hrs", gi)])
                A("pool", TT(ml[gi][:].rearrange("p (h d) -> p h d", h=4), hc[gi][:].rearrange("p (h d) -> p h d", h=4),
                             hrs[gi][:].unsqueeze(2).broadcast_to([128, 4, 128]), ALU.mult),
                  reads=[("hc", gi, h) for h in range(4)] + [("hrs", gi)], writes=[("ml", gi)])

            def mix_P3(c):
                S_.tag = tg + "mixP3"
                cs = slice(c * 128, (c + 1) * 128)
                gi = c % 2
                pt, ptk = PT.next()
                for h in range(4):
                    A("pe", TR(pt[:, h * 128:(h + 1) * 128], ml[gi][:, h * 128:(h + 1) * 128], ident[:]), reads=[("ml", gi), "ident"], writes=[ptk])
                for h in range(4):
                    A("act", ACT(hT[:, 4 + h, cs], pt[:, h * 128:(h + 1) * 128], AF.Copy, scale=mlng[:, l * 4 + h:l * 4 + h + 1]), reads=[ptk] + SD, writes=[("hT", 4 + h)])

            mix_P1(0)
            for c in range(NCH):
                if c + 1 < NCH:
                    mix_P1(c + 1)
                mix_P2(c)
                if c >= 1:
                    mix_P3(c - 1)
            mix_P3(NCH - 1)
            A("dve", CP(ebLs[:, l, :], ebL[:, NCH - 1, :]), reads=["ebL", "ebLs"], writes=["ebLs"])

            S_.tag = tg + "wout"
            Wo = {}
            nst2 = norm_begin()

            def wout_s0(f):
                wbi, j = divmod(f, 4)
                if j == 0:
                    W, wk = next_block(B_OUT + wbi)
                    Wo[wbi] = (W[:, 0:4096].rearrange("p (k n) -> p k n", k=8), wk)
                Wv, wk = Wo[wbi]
                ps, pk = PB.next()
                for k in range(8):
                    A("pe", MM(ps[:, 0:NT], Wv[:, k, j * 128:(j + 1) * 128], hT[:, k, :], k == 0, k == 7), reads=[wk, ("hT", k)], writes=[pk])
                A("dve", TT(xT[:, f, :], ps[:, 0:NT], xT[:, f, :], ALU.add), reads=[pk, ("x", f)], writes=[("x", f)])

            pipeline(8, [wout_s0, lambda f: None, lambda f: norm_accum(nst2, f)])

            S_.tag = tg + "norm2"
            norm_finish(nst2, l * 24 + 8)
            S_.tag = tg + "wup"
            Wu = {}
            pbk_ = {}

            def up_pre(fi):
                A("pool", CP(raw[fi % 4][:, 0:2], halo[:, l, fi, :]), reads=["halo"], writes=[("raw", fi % 4)])

            def up_s0(fi):
                i, jj = divmod(fi, 4)
                if jj == 0:
                    W, wk = next_block(B_UP + i)
                    Wu[i] = (W[:, 0:4096].rearrange("p (k n) -> p k n", k=8), wk)
                Wv, wk = Wu[i]
                ps, pk = PB.next()
                pbk_[fi] = (ps, pk)
                for k in range(8):
                    A("pe", MM(ps[:, 0:NT], Wv[:, k, jj * 128:(jj + 1) * 128], hT[:, k, :], k == 0, k == 7), reads=[wk, ("hT", k)], writes=[pk])
                r_ = raw[fi % 4]
                rk_ = ("raw", fi % 4)
                A("act", ACT(r_[:, 2:2 + NT], ps[:, 0:NT], AF.Copy), reads=[pk, rk_], writes=[rk_])
                if t == 0:
                    emit_prep(l + 1, 2)

            def up_s1(fi):
                A("act", ACT(facc[fi % 3][:], raw[fi % 4][:, 0:NT], AF.Identity, scale=cwff[:, l, fi, 0:1], bias=cbff[:, l, fi:fi + 1]),
                  reads=[("raw", fi % 4)] + SD, writes=[("facc", fi % 3)])
                A("pool", CP(halo[:, l, fi, :], raw[fi % 4][:, NT:NT + 2]), reads=[("raw", fi % 4)], writes=["halo"])

            def up_s2(fi):
                ps, pk = pbk_.pop(fi)
                fa = facc[fi % 3]
                fk = ("facc", fi % 3)
                A("dve", STT(fa[:], ps[:, 0:NT], cwff[:, l, fi, 2:3], fa[:], ALU.mult, ALU.add), reads=[pk, fk] + SD, writes=[fk])

            def up_s3(fi):
                fa = facc[fi % 3]
                fk = ("facc", fi % 3)
                A("dve", STT(fa[:], raw[fi % 4][:, 1:1 + NT], cwff[:, l, fi, 1:2], fa[:], ALU.mult, ALU.add), reads=[("raw", fi % 4), fk] + SD, writes=[fk])

            def up_s4(fi):
                i, jj = divmod(fi, 4)
                fa = facc[fi % 3]
                fk = ("facc", fi % 3)
                if jj < 2:
                    A("act", ACT(sa[jj][:], fa[:], AF.Silu), reads=[fk], writes=[("sa", jj)])
                else:
                    A("pool", TT(actT[:, 2 * i + jj - 2, :], sa[jj - 2][:], fa[:], ALU.mult), reads=[("sa", jj - 2), fk], writes=[("actT", 2 * i + jj - 2)])

            pipeline(44, [up_pre, up_s0, up_s1, up_s2, up_s3, up_s4])
            if t == 0:
                emit_prep(l + 1, 10 ** 6)

            S_.tag = tg + "wdown"
            nst3 = norm_begin()

            def wdown_s0(f):
                W, wk = next_block(B_DOWN + f)
                Wv = W[:, 0:22 * 128].rearrange("p (k n) -> p k n", k=22)
                ps, pk = PB.next()
                for k in range(22):
                    A("pe", MM(ps[:, 0:NT], Wv[:, k, :], actT[:, k, :], k == 0, k == 21), reads=[wk, ("actT", k)], writes=[pk])
                A("dve", TT(xT[:, f, :], ps[:, 0:NT], xT[:, f, :], ALU.add), reads=[pk, ("x", f)], writes=[("x", f)])

            pipeline(8, [wdown_s0, lambda f: None, lambda f: norm_accum(nst3, f)])

            S_.tag = tg + "norm3"
            norm_finish(nst3, l * 24 + 16)
            S_.tag = tg + "ple"
            Wgp = {}
            nst_next = norm_begin()

            for it in range(8 + 3):
                f = it
                fa_ = it - 3
                if 0 <= fa_ < 8:
                    A("act", ACT(sqr[fa_ % 2][:], xT[:, fa_, :], AF.Square), reads=[("x", fa_)], writes=[("sq", fa_ % 2)])
                if f < 8:
                    jb, jj = divmod(f, 2)
                    if jj == 0:
                        W, wk = next_block(B_GP + jb)
                        Wgp[jb] = (W[:, 0:2048].rearrange("p (k n) -> p k n", k=8), W[:, 2048:2560].rearrange("p (k n) -> p k n", k=2), wk)
                    Wg, Wp, wk = Wgp[jb]
                    psg2, pgk2 = PB.next()
                    for k in range(8):
                        A("pe", MM(psg2[:, 0:NT], Wg[:, k, jj * 128:(jj + 1) * 128], hT[:, k, :], k == 0, k == 7), reads=[wk, ("hT", k)], writes=[pgk2])
                if 0 <= fa_ < 8:
                    ps_n, pk_n = nst_next
                    A("pe", MM(ps_n[:, 0:NT], onesm[:], sqr[fa_ % 2][:], fa_ == 0, fa_ == 7), reads=[("sq", fa_ % 2), "onesm"], writes=[pk_n])
                if f < 8:
                    g_ = gt[f % 2]
                    gk_ = ("gt", f % 2)
                    A("act", ACT(g_[:], psg2[:, 0:NT], AF.Sigmoid), reads=[pgk2], writes=[gk_])
                    psp, ppk = PB.next()
                    for k in range(2):
                        A("pe", MM(psp[:, 0:NT], Wp[:, k, jj * 128:(jj + 1) * 128], pT[:, k, :], k == 0, k == 1), reads=[wk, "pT"], writes=[ppk])
                    A("dve", TT(g_[:], psp[:, 0:NT], g_[:], ALU.mult), reads=[ppk, gk_], writes=[gk_])
                    A("dve", TT(xT[:, f, :], xT[:, f, :], g_[:], ALU.add), reads=[gk_, ("x", f)], writes=[("x", f)])
            return nst_next

        for t in range(n_tiles):
            S_.tag = "xload"
            for f in range(8):
                A("sp", DMA(xT[:, f, :], xin_v[:, f, t * NT:(t + 1) * NT]), writes=[("x", f)], dma_sem=s_xf[f])
            nst = None
            for l in range(n_layers):
                nst = layer_step(t, l, nst)
            S_.tag = "final"
            norm_finish(nst, NL * 24, final_t0=t * NT)

        A("sp", lambda e: e.nop(), reads=["yout"])

        S_.analyze(qsems)
        with nc.Block() as block:
            S_.emit(block)
    return nc


def _pk(v):
    v = np.asarray(v, np.float32)
    return np.ascontiguousarray(v.reshape(-1, 128).T)


def ffn_col_order():
    cols = []
    for i in range(11):
        for j in (2 * i, 2 * i + 1):
            cols.append(np.arange(128 * j, 128 * j + 128))
        for j in (2 * i, 2 * i + 1):
            cols.append(DFF + np.arange(128 * j, 128 * j + 128))
    return np.stack(cols)


def shared_inputs(inp):
    f32 = np.float32
    d = {}
    for nm in ("w_in", "w_out", "w_up", "w_down", "w_ple"):
        d[nm] = np.ascontiguousarray(inp[nm], f32)
    d["w_gate"] = np.ascontiguousarray(inp["w_ple_gate"], f32)
    cols = []
    for i in range(NL):
        cols += [_pk(inp["g_mix"][i]), _pk(inp["g_ffn"][i]), _pk(inp["g_ple"][i])]
    cols.append(_pk(inp["g_final"]))
    d["gv"] = np.ascontiguousarray(np.concatenate(cols, axis=1))
    d["mlng"] = np.ascontiguousarray(np.concatenate([_pk(inp["ml_norm_g"][i]) for i in range(NL)], axis=1))
    lngb = np.stack([np.stack([inp["gm_ln_g"][i], inp["gm_ln_b"][i]]) for i in range(NL)])
    d["lngb"] = np.ascontiguousarray(np.broadcast_to(lngb[:, None], (NL, 128, 2, 512)), f32)
    d["wsT"] = np.ascontiguousarray(np.transpose(np.asarray(inp["gm_ws"], f32), (0, 3, 1, 2)))
    d["bs8"] = np.ascontiguousarray(np.transpose(np.asarray(inp["gm_bs"], f32), (1, 0, 2)))
    cw = np.asarray(inp["ml_conv_w"], f32)
    d["cwqk"] = np.ascontiguousarray(np.transpose(cw.reshape(NL, 4, 8, 128), (3, 0, 2, 1)))
    cb = np.asarray(inp["ml_conv_b"], f32)
    d["cbqk"] = np.ascontiguousarray(np.transpose(cb.reshape(NL, 8, 128), (2, 0, 1)))
    order = ffn_col_order()
    fw = np.asarray(inp["ffn_conv_w"], f32)
    d["cwff"] = np.ascontiguousarray(np.transpose(fw[:, :, order], (3, 0, 2, 1)))
    fb = np.asarray(inp["ffn_conv_b"], f32)
    d["cbff"] = np.ascontiguousarray(np.transpose(fb[:, order], (2, 0, 1)))
    bif = np.concatenate([np.asarray(inp["ml_b_i"], f32), np.asarray(inp["ml_b_f"], f32)], axis=1)
    d["bif"] = np.ascontiguousarray(np.broadcast_to(bif[None], (128, NL, 8)), f32)
    d["ident"] = np.eye(128, dtype=f32)
    d["tri"] = np.triu(np.ones((128, 128), f32))
    sel = np.zeros((128, 512), f32)
    for h in range(8):
        sel[h, h * 64:(h + 1) * 64] = 1.0
    d["sel"] = sel
    return d


_PROG = {}


def get_prog(S, n_layers=NL):
    if (S, n_layers) not in _PROG:
        _PROG[(S, n_layers)] = build_program(S, n_layers)
    return _PROG[(S, n_layers)]


def kernel(**inp):
    x = np.asarray(inp["x"], np.float32)
    p = np.asarray(inp["p"], np.float32)
    B, S, _ = x.shape
    sh = shared_inputs(inp)
    maps = []
    for b in range(B):
        m = dict(sh)
        m["xin"] = np.ascontiguousarray(x[b].T)
        m["pin"] = np.ascontiguousarray(np.transpose(p[:, b], (0, 2, 1)))
        maps.append(m)
    res = run_bass_kernel_spmd(get_prog(S), maps, core_ids=list(range(B)))
    return np.stack([np.asarray(res.results[b]["yout"]).T for b in range(B)]).astype(np.float32)
```

```python
import math
import numpy as np
from contextlib import ExitStack
import concourse.bass as bass
import concourse.mybir as mybir
from concourse.bass_utils import run_bass_kernel_spmd

F32 = mybir.dt.float32
BF16 = mybir.dt.bfloat16
AF = mybir.ActivationFunctionType
ALU = mybir.AluOpType

D = 1024
NT = 512
NCH = NT // 128
DFF = 2816
EPS = 1e-6
NSLOT = 4
WBLK = 4096
SAME_ENGINE_SYNC = True
SAME_ENGINE_ALL = False
SAME_ENGINE_SYNC_Q = {"act", "dve", "pool"}
ANNOTATE = False


class Op:
    __slots__ = ("q", "fn", "reads", "writes", "sem", "inc", "needs_inc", "count", "waits", "tag")


class Sched:
    def __init__(self):
        self.ops = []
        self.tag = None

    def add(self, q, fn, reads=(), writes=(), dma_sem=None):
        o = Op()
        o.q, o.fn, o.reads, o.writes = q, fn, tuple(reads), tuple(writes)
        o.sem = dma_sem
        o.inc = 16 if dma_sem is not None else 1
        o.needs_inc = dma_sem is not None
        o.count = None
        o.waits = {}
        o.tag = self.tag
        self.ops.append(o)
        return o

    def analyze(self, qsems):
        last_w, readers, need = {}, {}, []
        for o in self.ops:
            deps = {}
            for r in o.reads:
                w = last_w.get(r)
                if w is not None:
                    deps[id(w)] = (w, True)
            for k in o.writes:
                w = last_w.get(k)
                if w is not None and id(w) not in deps:
                    deps[id(w)] = (w, False)
                for rd in readers.get(k, ()):
                    if id(rd) not in deps:
                        deps[id(rd)] = (rd, False)
            deps.pop(id(o), None)
            nd = []
            for a, raw in deps.values():
                if a.sem is None and a.q == o.q:
                    if o.q == "pe" or not SAME_ENGINE_SYNC or o.q not in SAME_ENGINE_SYNC_Q:
                        continue
                    if not raw and not SAME_ENGINE_ALL:
                        continue
                nd.append(a)
                a.needs_inc = True
            need.append(nd)
            for r in o.reads:
                readers.setdefault(r, []).append(o)
            for k in o.writes:
                last_w[k] = o
                readers[k] = []
        cnt = {}
        for o in self.ops:
            if o.needs_inc:
                s = o.sem if o.sem is not None else qsems[o.q]
                cnt[s] = cnt.get(s, 0) + o.inc
                o.count = (s, cnt[s])
        waited = {}
        for o, nd in zip(self.ops, need):
            w = {}
            for a in nd:
                s, c = a.count
                if c > w.get(s, 0):
                    w[s] = c
            qw = waited.setdefault(o.q, {})
            for s, c in list(w.items()):
                if qw.get(s, 0) >= c:
                    del w[s]
                else:
                    qw[s] = c
            o.waits = w
        self.final_counts = cnt

    def emit(self, block):
        byq = {}
        for o in self.ops:
            byq.setdefault(o.q, []).append(o)

        def run(eng, ops):
            for o in ops:
                for s, c in o.waits.items():
                    eng.wait_ge(s, c)
                ins = o.fn(eng)
                if ANNOTATE and o.tag is not None:
                    ins.annotate(o.tag)
                if o.needs_inc:
                    ins.then_inc(o.count[0], o.inc)

        names = {"pe": "tensor", "act": "scalar", "dve": "vector", "pool": "gpsimd", "sp": "sync"}
        for q in ["sp", "pe", "act", "dve", "pool"]:
            if q in byq:
                getattr(block, names[q])(lambda eng, ops=byq[q]: run(eng, ops))


class Ring:
    def __init__(self, items):
        self.items = items
        self.i = 0

    def next(self):
        it = self.items[self.i % len(self.items)]
        self.i += 1
        return it


NL = 4


def weight_blocks():
    b = []
    for i in range(2):
        b.append([("w_in", 8, 1024 + 512 * i, 512, 0, 512, 0)])
    for c0 in (0, 512, 2048, 2560):
        b.append([("w_in", 8, c0, 512, 0, 512, 0)])
    b.append(None)
    for i in range(2):
        b.append([("w_out", 8, 512 * i, 512, 0, 512, 0)])
    for i in range(11):
        b.append([("w_up", 8, 256 * i, 256, 0, 512, 0), ("w_up", 8, DFF + 256 * i, 256, 0, 512, 256)])
    for f in range(8):
        b.append([("w_down", 22, 128 * f, 128, 0, 128, 0)])
    for j in range(4):
        b.append([("w_gate", 8, 256 * j, 256, 0, 256, 0), ("w_ple", 2, 256 * j, 256, 2048, 256, 0)])
    return b


WB = weight_blocks()
NWB = len(WB)
B_QK, B_U, B_V, B_VM, B_O, B_WS, B_OUT, B_UP, B_DOWN, B_GP = 0, 2, 3, 4, 5, 6, 7, 9, 20, 28
WB_SIZE = []
for _b in WB:
    if _b is None:
        WB_SIZE.append(1024)
    else:
        WB_SIZE.append(max(base + K * W for (_, K, _, _, base, W, _) in _b))


def MM(out, lhsT, rhs, start, stop):
    return lambda e: e.matmul(out, lhsT=lhsT, rhs=rhs, start=start, stop=stop)


def TR(out, in_, ident):
    return lambda e: e.transpose(out, in_, ident)


def ACT(out, in_, func, **kw):
    return lambda e: e.activation(out=out, in_=in_, func=func, **kw)


def TT(out, in0, in1, op):
    return lambda e: e.tensor_tensor(out=out, in0=in0, in1=in1, op=op)


def TS(out, in0, s1, s2, op0, op1=None):
    if op1 is None:
        return lambda e: e.tensor_scalar(out=out, in0=in0, scalar1=s1, scalar2=None, op0=op0)
    return lambda e: e.tensor_scalar(out=out, in0=in0, scalar1=s1, scalar2=s2, op0=op0, op1=op1)


def TSS(out, in_, scalar, op):
    return lambda e: e.tensor_single_scalar(out=out, in_=in_, scalar=scalar, op=op)


def STT(out, in0, scalar, in1, op0, op1):
    return lambda e: e.scalar_tensor_tensor(out=out, in0=in0, scalar=scalar, in1=in1, op0=op0, op1=op1)


def CP(out, in_):
    return lambda e: e.tensor_copy(out=out, in_=in_)


def MS(out, val):
    return lambda e: e.memset(out, val)


def DMA(out, in_):
    return lambda e: e.dma_start(out=out, in_=in_)


def pipeline(n, stages):
    ns = len(stages)
    for i in range(n + ns - 1):
        for si in reversed(range(ns)):
            c = i - si
            if 0 <= c < n:
                stages[si](c)


def build_program(S, n_layers=NL):
    n_tiles = S // NT
    nc = bass.Bass("TRN2", target_bir_lowering=False)

    def din(name, shape):
        return nc.dram_tensor(name, list(shape), F32, kind="ExternalInput").ap()

    xin = din("xin", [D, S])
    pin = din("pin", [NL, 256, S])
    wsrc = {
        "w_in": din("w_in", [NL, D, 3080]), "w_out": din("w_out", [NL, D, D]), "w_up": din("w_up", [NL, D, 2 * DFF]),
        "w_down": din("w_down", [NL, DFF, D]), "w_gate": din("w_gate", [NL, D, D]), "w_ple": din("w_ple", [NL, 256, D]),
    }
    gv_d = din("gv", [128, NL * 24 + 8])
    mlng_d = din("mlng", [128, NL * 4])
    lngb_d = din("lngb", [NL, 128, 2, 512])
    wsT_d = din("wsT", [NL, 128, 8, 128])
    bs8_d = din("bs8", [8, NL, 128])
    cwqk_d = din("cwqk", [128, NL, 8, 4])
    cbqk_d = din("cbqk", [128, NL, 8])
    cwff_d = din("cwff", [128, NL, 44, 3])
    cbff_d = din("cbff", [128, NL, 44])
    bif_d = din("bif", [128, NL, 8])
    ident_d = din("ident", [128, 128])
    tri_d = din("tri", [128, 128])
    sel_d = din("sel", [128, 512])
    yout = nc.dram_tensor("yout", [D, S], F32, kind="ExternalOutput").ap()
    wb_d = nc.dram_tensor("wb_scratch", [NL, NWB, 128, WBLK], BF16).ap()

    xin_v = xin.rearrange("(f p) s -> p f s", p=128)
    yout_v = yout.rearrange("(f p) s -> p f s", p=128)

    with ExitStack() as es:
        def sb(name, shape, dt=F32):
            return es.enter_context(nc.sbuf_tensor("sb_" + name, list(shape), dt))

        def psum(name, shape, dt=F32):
            return es.enter_context(nc.psum_tensor(name, list(shape), dt))

        def sem(name):
            return es.enter_context(nc.semaphore(name))

        qsems = {q: sem("q_" + q) for q in ["pe", "act", "dve", "pool"]}
        S_ = Sched()
        A = S_.add

        xT = sb("xT", [128, 8, NT])
        hT = sb("hT", [128, 8, NT], BF16)
        sqr = [sb(f"sq{i}", [128, NT], BF16) for i in range(2)]
        rstd = sb("rstd", [128, NT])
        ntmp = [sb(f"ntmp{i}", [128, NT]) for i in range(2)]
        qkraw = sb("qkraw", [128, 8, NT + 4])
        qkT = sb("qkT", [128, 8, NT], BF16)
        cacc = [sb(f"cacc{i}", [128, NT]) for i in range(4)]
        gpre = sb("gpre", [128, NCH, 8])
        e1 = sb("e1", [128, NCH, 4])
        l1 = sb("l1", [128, NCH, 4])
        t1 = sb("t1", [128, NCH, 4])
        sc = sb("sc", [128, NCH, 4])
        einv = sb("einv", [128, NCH, 4])
        ebL = sb("ebL", [128, NCH, 4])
        ebLs = sb("ebLs", [128, NL, 4])
        ug = sb("ug", [128, NCH, 512], BF16)
        vg = sb("vg", [128, NCH, 512], BF16)
        vt = [sb(f"vt{i}", [128, 512]) for i in range(2)]
        vn = sb("vn", [128, NCH, 512], BF16)
        st6 = sb("st6", [128, NCH, 6])
        mv = sb("mv", [128, NCH, 2])
        lvar = sb("lvar", [128, NCH])
        lrstd = sb("lrstd", [128, NCH])
        vaug = sb("vaug", [128, NCH, 4, 132], BF16)
        so = sb("so", [128, NCH, 512], BF16)
        gm = [sb(f"gm{i}", [128, 512], BF16) for i in range(2)]
        Amat = [sb(f"Amat{i}", [128, 4, 128], BF16) for i in range(2)]
        Kp = [sb(f"Kp{i}", [128, 4, 128], BF16) for i in range(2)]
        dtmp = [sb(f"dtmp{i}", [128, 4]) for i in range(2)]
        hc = [sb(f"hc{i}", [128, 512]) for i in range(2)]
        junk = sb("junk", [128, 128])
        ss = [sb(f"ss{i}", [128, 4]) for i in range(2)]
        lss = [sb(f"lss{i}", [128, 4]) for i in range(2)]
        hrs = [sb(f"hrs{i}", [128, 4]) for i in range(2)]
        ml = [sb(f"ml{i}", [128, 512], BF16) for i in range(2)]
        Dst = sb("Dst", [128, NL, 4, 132])
        Cbf = sb("Cbf", [128, NL, 4, 132], BF16)
        raw = [sb(f"raw{i}", [128, NT + 2]) for i in range(4)]
        facc = [sb(f"facc{i}", [128, NT]) for i in range(3)]
        sa = [sb(f"sa{i}", [128, NT]) for i in range(2)]
        actT = sb("actT", [128, 22, NT], BF16)
        halo = sb("halo", [128, NL, 44, 2])
        qhalo = sb("qhalo", [128, NL, 8, 4])
        pT = sb("pT", [128, 2, NT], BF16)
        gt = [sb(f"gt{i}", [128, NT]) for i in range(2)]
        wring = sb("wring", [128, NSLOT, WBLK], BF16)
        lngb = sb("lngb", [128, 2, 512])
        ident = sb("ident", [128, 128], BF16)
        tri = sb("tri", [128, 128])
        trib = sb("trib", [128, 128], BF16)
        trib4 = sb("trib4", [128, 4, 128], BF16)
        onesf = sb("onesf", [128, 128])
        onesm = sb("onesm", [128, 128], BF16)
        sel = sb("sel", [128, 512], BF16)
        bs128 = sb("bs128", [128, NL, 128], BF16)
        gv = sb("gv", [128, NL * 24 + 8])
        mlng = sb("mlng", [128, NL * 4])
        cwqk = sb("cwqk", [128, NL, 8, 4])
        cbqk = sb("cbqk", [128, NL, 8])
        cwff = sb("cwff", [128, NL, 44, 3])
        cbff = sb("cbff", [128, NL, 44])
        bif = sb("bif", [128, NL, 8])
        epsc = sb("epsc", [128, 1])
        wg = sb("wg", [128, NL, 8, 8], BF16)

        PBt = [psum(f"pb{i}", [128, 512]) for i in range(4)]
        PTt = [psum(f"pt{i}", [128, 1024], BF16) for i in range(2)]
        PSt = [psum(f"ps{i}", [128, 512]) for i in range(2)]
        PB = Ring([(PBt[i], ("pb", i)) for i in range(4)])
        PT = Ring([(PTt[i], ("pt", i)) for i in range(2)])
        PS = Ring([(PSt[i], ("ps", i)) for i in range(2)])

        s_setup = sem("d_setup")
        s_w = [sem(f"d_w{i}") for i in range(NSLOT)]
        s_xf = [sem(f"d_x{f}") for f in range(8)]
        s_p = sem("d_p")
        s_ln = sem("d_ln")
        s_yo = [sem(f"d_yo{i}") for i in range(2)]
        s_prep = [sem(f"d_prep{i}") for i in range(4)]
        s_ws = sem("d_ws")
        s_wg = sem("d_wg")
        s_ws2 = sem("d_ws2")

        XK = [("x", f) for f in range(8)]
        HT = [("hT", f) for f in range(8)]

        S_.tag = "setup"

        def load_const(dst_ap, src_ap, keys):
            A("sp", DMA(dst_ap, src_ap), writes=keys, dma_sem=s_setup)

        load_const(gv[:], gv_d, ["gv"])
        load_const(mlng[:], mlng_d, ["mlng"])
        load_const(cwqk[:], cwqk_d, ["cwqk"])
        load_const(cbqk[:], cbqk_d, ["cbqk"])
        load_const(cwff[:], cwff_d, ["cwff"])
        load_const(cbff[:], cbff_d, ["cbff"])
        load_const(bif[:], bif_d, ["bif"])
        load_const(tri[:], tri_d, ["tri"])
        load_const(xT[:, 0, :], sel_d, [("x", 0)])
        load_const(xT[:, 1, 0:128], ident_d, [("x", 1)])
        load_const(xT[0:8, 2, :].rearrange("p (l t) -> p l t", l=NL), bs8_d, [("x", 2)])
        setup_keys = ["gv", "mlng", "cwqk", "cbqk", "cwff", "cbff", "bif", "tri"] + XK[0:3]
        A("pool", MS(epsc[:], EPS), reads=setup_keys, writes=["epsc", "setup_done"])
        SD = ["setup_done"]
        A("pool", CP(sel[:], xT[:, 0, :]), reads=SD + [("x", 0)], writes=["sel"])
        A("pool", CP(ident[:], xT[:, 1, 0:128]), reads=SD + [("x", 1)], writes=["ident"])
        A("pool", CP(trib[:], tri[:]), reads=SD, writes=["trib"])
        for h in range(4):
            A("pool", CP(trib4[:, h, :], tri[:]), reads=SD, writes=["trib4"])
        A("pool", MS(bs128[:], 0.0), reads=SD, writes=["bs128"])
        A("pool", CP(bs128[0:8, :, :], xT[0:8, 2, :].rearrange("p (l t) -> p l t", l=NL)), reads=SD + ["bs128", ("x", 2)], writes=["bs128"])
        A("dve", MS(onesf[:], 1.0), writes=["onesf"])
        A("dve", MS(onesm[:], 1.0 / D), writes=["onesm"])
        A("dve", MS(qhalo[:], 0.0), writes=["qhalo"])
        A("dve", MS(halo[:], 0.0), writes=["halo"])
        A("dve", MS(Dst[:], 0.0), writes=[("Dst", l, h) for l in range(NL) for h in range(4)])
        A("dve", MS(Cbf[:], 0.0), writes=[("Cbf", l, h) for l in range(NL) for h in range(4)])
        A("dve", MS(ebLs[:], 1.0), writes=["ebLs"])
        A("pool", MS(vaug[:], 1.0), writes=[("vaug", c) for c in range(NCH)])
        for l in range(n_layers):
            A("pool", DMA(wg[:, l, :, :], wsrc["w_in"][l].rearrange("(k p) n -> p k n", p=128)[:, :, 3072:3080]),
              writes=[("wg", l), "wgslot"], dma_sem=s_wg)
        xflat = xT[:, 4:6, :].rearrange("p a b -> p (a b)")
        hflat = hT[:, 0:2, :].rearrange("p a b -> p (a b)")
        for l in range(n_layers):
            A("sp", DMA(xflat.rearrange("p (h t) -> p h t", h=8), wsT_d[l]), writes=[("x", 4), ("x", 5)], dma_sem=s_ws)
            for h in range(8):
                A("dve", TT(hflat[:, h * 128:(h + 1) * 128], xflat[:, h * 128:(h + 1) * 128], tri[:], ALU.mult),
                  reads=SD + [("x", 4), ("x", 5)], writes=[("hT", 0), ("hT", 1)])
            A("sp", DMA(wb_d[l, B_WS, :, 0:1024], hflat), reads=[("hT", 0), ("hT", 1)], writes=[("wbd", l, B_WS, 0)], dma_sem=s_ws2)

        np_ = [0]
        wbd_keys = {}
        prep_queue = {l: [] for l in range(n_layers)}
        for l in range(n_layers):
            for bi, parts in enumerate(WB):
                if parts is None:
                    wbd_keys[(l, bi)] = [("wbd", l, bi, 0)]
                    continue
                keys = []
                for pi, (src, K, c0, n, base, W, off) in enumerate(parts):
                    srcv = wsrc[src][l].rearrange("(k p) n -> p k n", p=128)[:, :, c0:c0 + n]
                    dstv = wb_d[l, bi, :, base:base + K * W].rearrange("p (k n) -> p k n", k=K)[:, :, off:off + n]
                    key = ("wbd", l, bi, pi)
                    prep_queue[l].append((dstv, srcv, key))
                    keys.append(key)
                wbd_keys[(l, bi)] = keys

        def emit_prep(l, n):
            for _ in range(n):
                if l >= n_layers or not prep_queue[l]:
                    return
                dstv, srcv, key = prep_queue[l].pop(0)
                sl = np_[0] % 4
                np_[0] += 1
                A("pool", DMA(dstv, srcv), writes=[key, ("prepslot", sl)], dma_sem=s_prep[sl])

        emit_prep(0, 10 ** 6)

        gblk = [0]
        total_blocks = n_tiles * n_layers * NWB

        def issue_wload(g):
            if g >= total_blocks:
                return
            bi = g % NWB
            l = (g // NWB) % n_layers
            sl = g % NSLOT
            n = WB_SIZE[bi]
            A("sp", DMA(wring[:, sl, 0:n], wb_d[l, bi, :, 0:n]), reads=wbd_keys[(l, bi)], writes=[("wr", sl)], dma_sem=s_w[sl])

        def next_block(expect_bi):
            g = gblk[0]
            assert g % NWB == expect_bi, (g % NWB, expect_bi)
            issue_wload(g + NSLOT - 1)
            gblk[0] += 1
            sl = g % NSLOT
            return wring[:, sl, :], ("wr", sl)

        def norm_begin():
            return PS.next()

        def norm_accum(nst, f):
            ps, pk = nst
            i = f % 2
            A("act", ACT(sqr[i][:], xT[:, f, :], AF.Square), reads=[("x", f)], writes=[("sq", i)])
            A("pe", MM(ps[:, 0:NT], onesm[:], sqr[i][:], f == 0, f == 7), reads=[("sq", i), "onesm"], writes=[pk])

        def norm_finish(nst, gcol, final_t0=None):
            ps, pk = nst
            A("act", ACT(rstd[:], ps[:, 0:NT], AF.Ln, bias=epsc[:, 0:1]), reads=[pk, "epsc"], writes=["rstd"])
            A("act", ACT(rstd[:], rstd[:], AF.Exp, scale=-0.5), reads=["rstd"], writes=["rstd"])
            for f in range(8):
                if final_t0 is not None:
                    i = f % 2
                    A("dve", STT(ntmp[i][:], xT[:, f, :], gv[:, gcol + f:gcol + f + 1], rstd[:], ALU.mult, ALU.mult),
                      reads=[("x", f), "rstd"] + SD, writes=[("ntmp", i)])
                    A("sp", DMA(yout_v[:, f, final_t0:final_t0 + NT], ntmp[i][:]), reads=[("ntmp", i)], writes=["yout"], dma_sem=s_yo[i])
                else:
                    A("dve", STT(hT[:, f, :], xT[:, f, :], gv[:, gcol + f:gcol + f + 1], rstd[:], ALU.mult, ALU.mult),
                      reads=[("x", f), "rstd"] + SD, writes=[("hT", f)])

        def norm(gcol, final_t0=None):
            nst = norm_begin()
            for f in range(8):
                norm_accum(nst, f)
            norm_finish(nst, gcol, final_t0)

        for g in range(NSLOT - 1):
            issue_wload(g)

        def layer_step(t, l, nst_in):
            t0 = t * NT
            tg = f"L{l}"
            S_.tag = tg + "norm1"
            A("pool", DMA(pT[:], pin[l].rearrange("(k p) s -> p k s", p=128)[:, :, t0:t0 + NT]), writes=["pT"], dma_sem=s_p)
            A("sp", DMA(lngb[:], lngb_d[l]), writes=["lngb"], dma_sem=s_ln)

            if nst_in is None:
                norm(l * 24)
            else:
                norm_finish(nst_in, l * 24)
            S_.tag = tg + "gates"
            psg, pgk = PS.next()
            for c in range(NCH):
                for k in range(8):
                    A("pe", MM(psg[:, c * 8:(c + 1) * 8], hT[:, k, c * 128:(c + 1) * 128], wg[:, l, k, :], k == 0, k == 7),
                      reads=[("hT", k), ("wg", l)], writes=[pgk])
            for c in range(NCH):
                A("dve", TT(gpre[:, c, :], psg[:, c * 8:(c + 1) * 8], bif[:, l, :], ALU.add), reads=[pgk] + SD, writes=["gpre"])
            A("act", ACT(e1[:], gpre[:, :, 4:8], AF.Exp, scale=-1.0), reads=["gpre"], writes=["e1"])
            A("act", ACT(l1[:], e1[:], AF.Ln, bias=1.0), reads=["e1"], writes=["l1"])
            def gates_tail():
                S_.tag = tg + "gates"
                psb, pbk = PS.next()
                l1v = l1[:].rearrange("p c h -> p (c h)")
                A("pe", MM(psb[:, 0:16], tri[:], l1v, True, True), reads=["l1"] + SD, writes=[pbk])
                A("pe", MM(psb[:, 16:32], onesf[:], l1v, True, True), reads=["l1", "onesf"], writes=[pbk])
                bneg = psb[:, 0:16].rearrange("p (c h) -> p c h", c=NCH)
                bLneg = psb[:, 16:32].rearrange("p (c h) -> p c h", c=NCH)
                A("dve", STT(t1[:], gpre[:, :, 0:4], math.log(128.0 ** -0.5), bneg, ALU.add, ALU.add), reads=["gpre", pbk], writes=["t1"])
                A("act", ACT(sc[:], t1[:], AF.Exp), reads=["t1"], writes=["sc"])
                A("act", ACT(einv[:], bneg, AF.Exp), reads=[pbk, "t1"], writes=["einv"])
                A("act", ACT(ebL[:], bLneg, AF.Exp, scale=-1.0), reads=[pbk, "t1"], writes=["ebL"])
                S_.tag = tg + "qk"

            S_.tag = tg + "qk"
            Wq = {}
            qpb = {}

            def qk_s0(f):
                wbi, j = divmod(f, 4)
                if j == 0:
                    W, wk = next_block(B_QK + wbi)
                    Wq[wbi] = (W[:, 0:4096].rearrange("p (k n) -> p k n", k=8), wk)
                Wv, wk = Wq[wbi]
                ps, pk = PB.next()
                qpb[f] = (ps, pk)
                for k in range(8):
                    A("pe", MM(ps[:, 0:NT], Wv[:, k, j * 128:(j + 1) * 128], hT[:, k, :], k == 0, k == 7), reads=[wk, ("hT", k)], writes=[pk])
                qk_ = ("qkraw", f)
                A("pool", CP(qkraw[:, f, 0:3], qhalo[:, l, f, 0:3]), reads=["qhalo"], writes=[qk_])
                A("act", ACT(qkraw[:, f, 3:3 + NT], ps[:, 0:NT], AF.Copy), reads=[pk, qk_], writes=[qk_])
                A("pool", CP(qhalo[:, l, f, 0:3], qkraw[:, f, NT:NT + 3]), reads=[qk_], writes=["qhalo"])

            def qk_s1(f):
                A("act", ACT(cacc[f % 4][:], qkraw[:, f, 0:NT], AF.Identity, scale=cwqk[:, l, f, 0:1], bias=cbqk[:, l, f:f + 1]),
                  reads=[("qkraw", f)] + SD, writes=[("cacc", f % 4)])

            def qk_s2(f):
                ps, pk = qpb.pop(f)
                ca = cacc[f % 4]
                ck = ("cacc", f % 4)
                A("dve", STT(ca[:], ps[:, 0:NT], cwqk[:, l, f, 3:4], ca[:], ALU.mult, ALU.add), reads=[pk, ck] + SD, writes=[ck])

            def qk_tap(tap):
                def fn(f):
                    ca = cacc[f % 4]
                    ck = ("cacc", f % 4)
                    A("dve", STT(ca[:], qkraw[:, f, tap:tap + NT], cwqk[:, l, f, tap:tap + 1], ca[:], ALU.mult, ALU.add), reads=[("qkraw", f), ck] + SD, writes=[ck])
                return fn

            def qk_s5(f):
                A("act", ACT(qkT[:, f, :], cacc[f % 4][:], AF.Silu), reads=[("cacc", f % 4)], writes=[("qkT", f)])

            gates_tail_done = [False]

            def qk_s0_wrap(f):
                qk_s0(f)
                if f == 1 and not gates_tail_done[0]:
                    gates_tail()
                    gates_tail_done[0] = True

            pipeline(8, [qk_s0_wrap, qk_s1, qk_s2, qk_tap(1), qk_tap(2), qk_s5])

            S_.tag = tg + "uvo"
            for which, bidx in (("u", B_U), ("v", B_V), ("vm", B_VM), ("o", B_O)):
                W, wk = next_block(bidx)
                Wv = W[:, 0:4096].rearrange("p (k n) -> p k n", k=8)
                for c in range(NCH):
                    ps, pk = PB.next()
                    for k in range(8):
                        A("pe", MM(ps[:, 0:512], hT[:, k, c * 128:(c + 1) * 128], Wv[:, k, :], k == 0, k == 7), reads=[wk, ("hT", k)], writes=[pk])
                    if which == "u":
                        A("act", ACT(ug[:, c, :], ps[:, 0:512], AF.Gelu), reads=[pk], writes=[("ug", c)])
                    elif which == "v":
                        A("act", ACT(vg[:, c, :], ps[:, 0:512], AF.Gelu), reads=[pk], writes=[("vg", c)])
                        A("dve", lambda e, c=c: e.bn_stats(out=st6[:, c, :], in_=vg[:, c, :]), reads=[("vg", c)], writes=[("st6", c)])
                        A("dve", lambda e, c=c: e.bn_aggr(out=mv[:, c, :], in_=st6[:, c, :]), reads=[("st6", c)], writes=[("mv", c)])
                    elif which == "vm":
                        for h in range(4):
                            A("act", ACT(vaug[:, c, h, 0:128], ps[:, h * 128:(h + 1) * 128], AF.Copy, scale=sc[:, c, h:h + 1]), reads=[pk, ("vaug", c), "sc"], writes=[("vaug", c)])
                        A("pool", CP(vaug[:, c, :, 128], sc[:, c, :]), reads=["sc", ("vaug", c)], writes=[("vaug", c)])
                    else:
                        A("act", ACT(so[:, c, :], ps[:, 0:512], AF.Sigmoid), reads=[pk], writes=[("so", c)])
                if which == "v":
                    A("act", ACT(lvar[:], mv[:, :, 1], AF.Ln, bias=epsc[:, 0:1]), reads=[("mv", c) for c in range(NCH)] + ["epsc"], writes=["lvar"])
                    A("act", ACT(lrstd[:], lvar[:], AF.Exp, scale=-0.5), reads=["lvar"], writes=["lrstd"])
                    for c in range(NCH):
                        v_ = vt[c % 2]
                        vk = ("vt", c % 2)
                        A("dve", TS(v_[:], vg[:, c, :], mv[:, c, 0:1], lrstd[:, c:c + 1], ALU.subtract, ALU.mult),
                          reads=[("vg", c), ("mv", c), "lrstd"], writes=[vk])
                        A("pool", TT(v_[:], v_[:], lngb[:, 0, :], ALU.mult), reads=[vk, "lngb"], writes=[vk])
                        A("pool", TT(vn[:, c, :], v_[:], lngb[:, 1, :], ALU.add), reads=[vk, "lngb"], writes=[("vn", c)])

            Wws, wsk = next_block(B_WS)
            wsv = Wws[:, 0:1024].rearrange("p (h t) -> p h t", h=8)
            stg = {}

            def mix_P1(c):
                S_.tag = tg + "mixP1"
                cs = slice(c * 128, (c + 1) * 128)
                gi = c % 2
                psm, pmk = PB.next()
                A("pe", MM(psm[:, 0:512], bs128[:, l, :], sel[:], True, False), reads=["bs128", "sel"], writes=[pmk])
                for h in range(8):
                    A("pe", MM(psm[:, h * 64:(h + 1) * 64], wsv[:, h, :], vn[:, c, h * 64:(h + 1) * 64], False, h == 7), reads=[wsk, ("vn", c)], writes=[pmk])
                st, stk = PS.next()
                for h in range(4):
                    A("pe", MM(st[:, h * 128:(h + 1) * 128], qkT[:, 4 + h, cs], qkT[:, h, cs], True, True), reads=[("qkT", 4 + h), ("qkT", h)], writes=[stk])
                kt, ktk = PT.next()
                for h in range(4):
                    A("pe", TR(kt[:, h * 128:(h + 1) * 128], qkT[:, 4 + h, cs], ident[:]), reads=[("qkT", 4 + h), "ident"], writes=[ktk])
                A("dve", TT(gm[gi][:], psm[:, 0:512], ug[:, c, :], ALU.mult), reads=[pmk, ("ug", c)], writes=[("gm", gi)])
                A("dve", TT(Amat[gi][:], st[:, 0:512].rearrange("p (h t) -> p h t", h=4), trib4[:], ALU.mult),
                  reads=[stk, "trib4"], writes=[("Amat", gi, h) for h in range(4)])
                A("dve", CP(Kp[gi][:], kt[:, 0:512].rearrange("p (h t) -> p h t", h=4)), reads=[ktk], writes=[("Kp", gi, h) for h in range(4)])

            def mix_P2(c):
                S_.tag = tg + "mixP2"
                cs = slice(c * 128, (c + 1) * 128)
                gi = c % 2
                pt, ptk = PT.next()
                for j in range(4):
                    A("pe", TR(pt[:, j * 128:(j + 1) * 128], gm[gi][:, j * 128:(j + 1) * 128], ident[:]), reads=[("gm", gi), "ident"], writes=[ptk])
                A("act", ACT(hT[:, 0:4, cs], pt[:, 0:512].rearrange("p (a b) -> p a b", a=4), AF.Copy), reads=[ptk], writes=HT[0:4])
                for pr in range(2):
                    nd, ndk = PB.next()
                    up, upk = PB.next()
                    for j in range(2):
                        h = 2 * pr + j
                        A("pe", MM(nd[:, j * 129:(j + 1) * 129], Amat[gi][:, h, :], vaug[:, c, h, 0:129], True, False), reads=[("Amat", gi, h), ("vaug", c)], writes=[ndk])
                        A("pe", MM(nd[:, j * 129:(j + 1) * 129], qkT[:, h, cs], Cbf[:, l, h, 0:129], False, True), reads=[("qkT", h), ("Cbf", l, h)], writes=[ndk])
                    for j in range(2):
                        h = 2 * pr + j
                        A("pe", MM(up[:, j * 129:(j + 1) * 129], Kp[gi][:, h, :], vaug[:, c, h, 0:129], True, True), reads=[("Kp", gi, h), ("vaug", c)], writes=[upk])
                    d_ = dtmp[gi]
                    dk = ("dtmp", gi, pr)
                    dsl = d_[:, 2 * pr:2 * pr + 2]
                    A("act", ACT(dsl, nd[:, 0:258].rearrange("p (j d) -> p j d", j=2)[:, :, 128], AF.Abs), reads=[ndk], writes=[dk])
                    A("dve", TT(dsl, dsl, einv[:, c, 2 * pr:2 * pr + 2], ALU.max), reads=[dk, "einv"], writes=[dk])
                    A("dve", lambda e, dsl=dsl: e.reciprocal(out=dsl, in_=dsl), reads=[dk], writes=[dk])
                    for j in range(2):
                        h = 2 * pr + j
                        A("dve", STT(hc[gi][:, h * 128:(h + 1) * 128], nd[:, j * 129:j * 129 + 128], d_[:, h:h + 1], so[:, c, h * 128:(h + 1) * 128], ALU.mult, ALU.mult),
                          reads=[ndk, dk, ("so", c)], writes=[("hc", gi, h)])
                    for j in range(2):
                        h = 2 * pr + j
                        prev = ebL[:, c - 1, h:h + 1] if c > 0 else ebLs[:, l, h:h + 1]
                        A("dve", STT(Dst[:, l, h, 0:129], Dst[:, l, h, 0:129], prev, up[:, j * 129:(j + 1) * 129], ALU.mult, ALU.add),
                          reads=[upk, ("Dst", l, h), "ebL", "ebLs"], writes=[("Dst", l, h)])
                A("pool", TT(Cbf[:, l, :, 0:129], Dst[:, l, :, 0:129], ebL[:, c, :].unsqueeze(2).broadcast_to([128, 4, 129]), ALU.mult),
                  reads=[("Dst", l, h) for h in range(4)] + ["ebL"], writes=[("Cbf", l, h) for h in range(4)])
                for h in range(4):
                    A("dve", lambda e, h=h, gi=gi: e.scalar_tensor_tensor(out=junk[:], in0=hc[gi][:, h * 128:(h + 1) * 128], scalar=1.0, in1=hc[gi][:, h * 128:(h + 1) * 128],
                                                                          op0=ALU.mult, op1=ALU.mult, accum_out=ss[gi][:, h:h + 1]),
                      reads=[("hc", gi, h)], writes=["junk", ("ss", gi)])
                A("act", ACT(lss[gi][:], ss[gi][:], AF.Ln, scale=1.0 / 128.0, bias=epsc[:, 0:1]), reads=[("ss", gi), "epsc"], writes=[("lss", gi)])
                A("act", ACT(hrs[gi][:], lss[gi][:], AF.Exp, scale=-0.5), reads=[("lss", gi)], writes=[("hrs", gi)])
                A("pool", TT(ml[gi][:].rearrange("p (h d) -> p h d", h=4), hc[gi][:].rearrange("p (h d) -> p h d", h=4),
                             hrs[gi][:].unsqueeze(2).broadcast_to([128, 4, 128]), ALU.mult),
                  reads=[("hc", gi, h) for h in range(4)] + [("hrs", gi)], writes=[("ml", gi)])

            def mix_P3(c):
                S_.tag = tg + "mixP3"
                cs = slice(c * 128, (c + 1) * 128)
                gi = c % 2
                pt, ptk = PT.next()
                for h in range(4):
                    A("pe", TR(pt[:, h * 128:(h + 1) * 128], ml[gi][:, h * 128:(h + 1) * 128], ident[:]), reads=[("ml", gi), "ident"], writes=[ptk])
                for h in range(4):
                    A("act", ACT(hT[:, 4 + h, cs], pt[:, h * 128:(h + 1) * 128], AF.Copy, scale=mlng[:, l * 4 + h:l * 4 + h + 1]), reads=[ptk] + SD, writes=[("hT", 4 + h)])

            mix_P1(0)
            for c in range(NCH):
                if c + 1 < NCH:
                    mix_P1(c + 1)
                mix_P2(c)
                if c >= 1:
                    mix_P3(c - 1)
            mix_P3(NCH - 1)
            A("dve", CP(ebLs[:, l, :], ebL[:, NCH - 1, :]), reads=["ebL", "ebLs"], writes=["ebLs"])

            S_.tag = tg + "wout"
            Wo = {}
            nst2 = norm_begin()

            def wout_s0(f):
                wbi, j = divmod(f, 4)
                if j == 0:
                    W, wk = next_block(B_OUT + wbi)
                    Wo[wbi] = (W[:, 0:4096].rearrange("p (k n) -> p k n", k=8), wk)
                Wv, wk = Wo[wbi]
                ps, pk = PB.next()
                for k in range(8):
                    A("pe", MM(ps[:, 0:NT], Wv[:, k, j * 128:(j + 1) * 128], hT[:, k, :], k == 0, k == 7), reads=[wk, ("hT", k)], writes=[pk])
                A("dve", TT(xT[:, f, :], ps[:, 0:NT], xT[:, f, :], ALU.add), reads=[pk, ("x", f)], writes=[("x", f)])

            pipeline(8, [wout_s0, lambda f: None, lambda f: norm_accum(nst2, f)])

            S_.tag = tg + "norm2"
            norm_finish(nst2, l * 24 + 8)
            S_.tag = tg + "wup"
            Wu = {}
            pbk_ = {}

            def up_pre(fi):
                A("pool", CP(raw[fi % 4][:, 0:2], halo[:, l, fi, :]), reads=["halo"], writes=[("raw", fi % 4)])

            def up_s0(fi):
                i, jj = divmod(fi, 4)
                if jj == 0:
                    W, wk = next_block(B_UP + i)
                    Wu[i] = (W[:, 0:4096].rearrange("p (k n) -> p k n", k=8), wk)
                Wv, wk = Wu[i]
                ps, pk = PB.next()
                pbk_[fi] = (ps, pk)
                for k in range(8):
                    A("pe", MM(ps[:, 0:NT], Wv[:, k, jj * 128:(jj + 1) * 128], hT[:, k, :], k == 0, k == 7), reads=[wk, ("hT", k)], writes=[pk])
                r_ = raw[fi % 4]
                rk_ = ("raw", fi % 4)
                A("act", ACT(r_[:, 2:2 + NT], ps[:, 0:NT], AF.Copy), reads=[pk, rk_], writes=[rk_])
                if t == 0:
                    emit_prep(l + 1, 2)

            def up_s1(fi):
                A("act", ACT(facc[fi % 3][:], raw[fi % 4][:, 0:NT], AF.Identity, scale=cwff[:, l, fi, 0:1], bias=cbff[:, l, fi:fi + 1]),
                  reads=[("raw", fi % 4)] + SD, writes=[("facc", fi % 3)])
                A("pool", CP(halo[:, l, fi, :], raw[fi % 4][:, NT:NT + 2]), reads=[("raw", fi % 4)], writes=["halo"])

            def up_s2(fi):
                ps, pk = pbk_.pop(fi)
                fa = facc[fi % 3]
                fk = ("facc", fi % 3)
                A("dve", STT(fa[:], ps[:, 0:NT], cwff[:, l, fi, 2:3], fa[:], ALU.mult, ALU.add), reads=[pk, fk] + SD, writes=[fk])

            def up_s3(fi):
                fa = facc[fi % 3]
                fk = ("facc", fi % 3)
                A("dve", STT(fa[:], raw[fi % 4][:, 1:1 + NT], cwff[:, l, fi, 1:2], fa[:], ALU.mult, ALU.add), reads=[("raw", fi % 4), fk] + SD, writes=[fk])

            def up_s4(fi):
                i, jj = divmod(fi, 4)
                fa = facc[fi % 3]
                fk = ("facc", fi % 3)
                if jj < 2:
                    A("act", ACT(sa[jj][:], fa[:], AF.Silu), reads=[fk], writes=[("sa", jj)])
                else:
                    A("pool", TT(actT[:, 2 * i + jj - 2, :], sa[jj - 2][:], fa[:], ALU.mult), reads=[("sa", jj - 2), fk], writes=[("actT", 2 * i + jj - 2)])

            pipeline(44, [up_pre, up_s0, up_s1, up_s2, up_s3, up_s4])
            if t == 0:
                emit_prep(l + 1, 10 ** 6)

            S_.tag = tg + "wdown"
            nst3 = norm_begin()

            def wdown_s0(f):
                W, wk = next_block(B_DOWN + f)
                Wv = W[:, 0:22 * 128].rearrange("p (k n) -> p k n", k=22)
                ps, pk = PB.next()
                for k in range(22):
                    A("pe", MM(ps[:, 0:NT], Wv[:, k, :], actT[:, k, :], k == 0, k == 21), reads=[wk, ("actT", k)], writes=[pk])
                A("dve", TT(xT[:, f, :], ps[:, 0:NT], xT[:, f, :], ALU.add), reads=[pk, ("x", f)], writes=[("x", f)])

            pipeline(8, [wdown_s0, lambda f: None, lambda f: norm_accum(nst3, f)])

            S_.tag = tg + "norm3"
            norm_finish(nst3, l * 24 + 16)
            S_.tag = tg + "ple"
            Wgp = {}
            nst_next = norm_begin()

            for it in range(8 + 3):
                f = it
                fa_ = it - 3
                if 0 <= fa_ < 8:
                    A("act", ACT(sqr[fa_ % 2][:], xT[:, fa_, :], AF.Square), reads=[("x", fa_)], writes=[("sq", fa_ % 2)])
                if f < 8:
                    jb, jj = divmod(f, 2)
                    if jj == 0:
                        W, wk = next_block(B_GP + jb)
                        Wgp[jb] = (W[:, 0:2048].rearrange("p (k n) -> p k n", k=8), W[:, 2048:2560].rearrange("p (k n) -> p k n", k=2), wk)
                    Wg, Wp, wk = Wgp[jb]
                    psg2, pgk2 = PB.next()
                    for k in range(8):
                        A("pe", MM(psg2[:, 0:NT], Wg[:, k, jj * 128:(jj + 1) * 128], hT[:, k, :], k == 0, k == 7), reads=[wk, ("hT", k)], writes=[pgk2])
                if 0 <= fa_ < 8:
                    ps_n, pk_n = nst_next
                    A("pe", MM(ps_n[:, 0:NT], onesm[:], sqr[fa_ % 2][:], fa_ == 0, fa_ == 7), reads=[("sq", fa_ % 2), "onesm"], writes=[pk_n])
                if f < 8:
                    g_ = gt[f % 2]
                    gk_ = ("gt", f % 2)
                    A("act", ACT(g_[:], psg2[:, 0:NT], AF.Sigmoid), reads=[pgk2], writes=[gk_])
                    psp, ppk = PB.next()
                    for k in range(2):
                        A("pe", MM(psp[:, 0:NT], Wp[:, k, jj * 128:(jj + 1) * 128], pT[:, k, :], k == 0, k == 1), reads=[wk, "pT"], writes=[ppk])
                    A("dve", TT(g_[:], psp[:, 0:NT], g_[:], ALU.mult), reads=[ppk, gk_], writes=[gk_])
                    A("dve", TT(xT[:, f, :], xT[:, f, :], g_[:], ALU.add), reads=[gk_, ("x", f)], writes=[("x", f)])
            return nst_next

        for t in range(n_tiles):
            S_.tag = "xload"
            for f in range(8):
                A("sp", DMA(xT[:, f, :], xin_v[:, f, t * NT:(t + 1) * NT]), writes=[("x", f)], dma_sem=s_xf[f])
            nst = None
            for l in range(n_layers):
                nst = layer_step(t, l, nst)
            S_.tag = "final"
            norm_finish(nst, NL * 24, final_t0=t * NT)

        A("sp", lambda e: e.nop(), reads=["yout"])

        S_.analyze(qsems)
        with nc.Block() as block:
            S_.emit(block)
    return nc


def _pk(v):
    v = np.asarray(v, np.float32)
    return np.ascontiguousarray(v.reshape(-1, 128).T)


def ffn_col_order():
    cols = []
    for i in range(11):
        for j in (2 * i, 2 * i + 1):
            cols.append(np.arange(128 * j, 128 * j + 128))
        for j in (2 * i, 2 * i + 1):
            cols.append(DFF + np.arange(128 * j, 128 * j + 128))
    return np.stack(cols)


def shared_inputs(inp):
    f32 = np.float32
    d = {}
    for nm in ("w_in", "w_out", "w_up", "w_down", "w_ple"):
        d[nm] = np.ascontiguousarray(inp[nm], f32)
    d["w_gate"] = np.ascontiguousarray(inp["w_ple_gate"], f32)
    cols = []
    for i in range(NL):
        cols += [_pk(inp["g_mix"][i]), _pk(inp["g_ffn"][i]), _pk(inp["g_ple"][i])]
    cols.append(_pk(inp["g_final"]))
    d["gv"] = np.ascontiguousarray(np.concatenate(cols, axis=1))
    d["mlng"] = np.ascontiguousarray(np.concatenate([_pk(inp["ml_norm_g"][i]) for i in range(NL)], axis=1))
    lngb = np.stack([np.stack([inp["gm_ln_g"][i], inp["gm_ln_b"][i]]) for i in range(NL)])
    d["lngb"] = np.ascontiguousarray(np.broadcast_to(lngb[:, None], (NL, 128, 2, 512)), f32)
    d["wsT"] = np.ascontiguousarray(np.transpose(np.asarray(inp["gm_ws"], f32), (0, 3, 1, 2)))
    d["bs8"] = np.ascontiguousarray(np.transpose(np.asarray(inp["gm_bs"], f32), (1, 0, 2)))
    cw = np.asarray(inp["ml_conv_w"], f32)
    d["cwqk"] = np.ascontiguousarray(np.transpose(cw.reshape(NL, 4, 8, 128), (3, 0, 2, 1)))
    cb = np.asarray(inp["ml_conv_b"], f32)
    d["cbqk"] = np.ascontiguousarray(np.transpose(cb.reshape(NL, 8, 128), (2, 0, 1)))
    order = ffn_col_order()
    fw = np.asarray(inp["ffn_conv_w"], f32)
    d["cwff"] = np.ascontiguousarray(np.transpose(fw[:, :, order], (3, 0, 2, 1)))
    fb = np.asarray(inp["ffn_conv_b"], f32)
    d["cbff"] = np.ascontiguousarray(np.transpose(fb[:, order], (2, 0, 1)))
    bif = np.concatenate([np.asarray(inp["ml_b_i"], f32), np.asarray(inp["ml_b_f"], f32)], axis=1)
    d["bif"] = np.ascontiguousarray(np.broadcast_to(bif[None], (128, NL, 8)), f32)
    d["ident"] = np.eye(128, dtype=f32)
    d["tri"] = np.triu(np.ones((128, 128), f32))
    sel = np.zeros((128, 512), f32)
    for h in range(8):
        sel[h, h * 64:(h + 1) * 64] = 1.0
    d["sel"] = sel
    return d


_PROG = {}


def get_prog(S, n_layers=NL):
    if (S, n_layers) not in _PROG:
        _PROG[(S, n_layers)] = build_program(S, n_layers)
    return _PROG[(S, n_layers)]


def kernel(**inp):
    x = np.asarray(inp["x"], np.float32)
    p = np.asarray(inp["p"], np.float32)
    B, S, _ = x.shape
    sh = shared_inputs(inp)
    maps = []
    for b in range(B):
        m = dict(sh)
        m["xin"] = np.ascontiguousarray(x[b].T)
        m["pin"] = np.ascontiguousarray(np.transpose(p[:, b], (0, 2, 1)))
        maps.append(m)
    res = run_bass_kernel_spmd(get_prog(S), maps, core_ids=list(range(B)))
    return np.stack([np.asarray(res.results[b]["yout"]).T for b in range(B)]).astype(np.float32)
```

```python
import math
import numpy as np
from contextlib import ExitStack
import concourse.bass as bass
import concourse.mybir as mybir
from concourse.bass_utils import run_bass_kernel_spmd

F32 = mybir.dt.float32
BF16 = mybir.dt.bfloat16
AF = mybir.ActivationFunctionType
ALU = mybir.AluOpType

D = 1024
NT = 512
NCH = NT // 128
DFF = 2816
EPS = 1e-6
NSLOT = 4
WBLK = 4096
SAME_ENGINE_SYNC = True
SAME_ENGINE_ALL = False
SAME_ENGINE_SYNC_Q = {"act", "dve", "pool"}
ANNOTATE = False


class Op:
    __slots__ = ("q", "fn", "reads", "writes", "sem", "inc", "needs_inc", "count", "waits", "tag")


class Sched:
    def __init__(self):
        self.ops = []
        self.tag = None

    def add(self, q, fn, reads=(), writes=(), dma_sem=None):
        o = Op()
        o.q, o.fn, o.reads, o.writes = q, fn, tuple(reads), tuple(writes)
        o.sem = dma_sem
        o.inc = 16 if dma_sem is not None else 1
        o.needs_inc = dma_sem is not None
        o.count = None
        o.waits = {}
        o.tag = self.tag
        self.ops.append(o)
        return o

    def analyze(self, qsems):
        last_w, readers, need = {}, {}, []
        for o in self.ops:
            deps = {}
            for r in o.reads:
                w = last_w.get(r)
                if w is not None:
                    deps[id(w)] = (w, True)
            for k in o.writes:
                w = last_w.get(k)
                if w is not None and id(w) not in deps:
                    deps[id(w)] = (w, False)
                for rd in readers.get(k, ()):
                    if id(rd) not in deps:
                        deps[id(rd)] = (rd, False)
            deps.pop(id(o), None)
            nd = []
            for a, raw in deps.values():
                if a.sem is None and a.q == o.q:
                    if o.q == "pe" or not SAME_ENGINE_SYNC or o.q not in SAME_ENGINE_SYNC_Q:
                        continue
                    if not raw and not SAME_ENGINE_ALL:
                        continue
                nd.append(a)
                a.needs_inc = True
            need.append(nd)
            for r in o.reads:
                readers.setdefault(r, []).append(o)
            for k in o.writes:
                last_w[k] = o
                readers[k] = []
        cnt = {}
        for o in self.ops:
            if o.needs_inc:
                s = o.sem if o.sem is not None else qsems[o.q]
                cnt[s] = cnt.get(s, 0) + o.inc
                o.count = (s, cnt[s])
        waited = {}
        for o, nd in zip(self.ops, need):
            w = {}
            for a in nd:
                s, c = a.count
                if c > w.get(s, 0):
                    w[s] = c
            qw = waited.setdefault(o.q, {})
            for s, c in list(w.items()):
                if qw.get(s, 0) >= c:
                    del w[s]
                else:
                    qw[s] = c
            o.waits = w
        self.final_counts = cnt

    def emit(self, block):
        byq = {}
        for o in self.ops:
            byq.setdefault(o.q, []).append(o)

        def run(eng, ops):
            for o in ops:
                for s, c in o.waits.items():
                    eng.wait_ge(s, c)
                ins = o.fn(eng)
                if ANNOTATE and o.tag is not None:
                    ins.annotate(o.tag)
                if o.needs_inc:
                    ins.then_inc(o.count[0], o.inc)

        names = {"pe": "tensor", "act": "scalar", "dve": "vector", "pool": "gpsimd", "sp": "sync"}
        for q in ["sp", "pe", "act", "dve", "pool"]:
            if q in byq:
                getattr(block, names[q])(lambda eng, ops=byq[q]: run(eng, ops))


class Ring:
    def __init__(self, items):
        self.items = items
        self.i = 0

    def next(self):
        it = self.items[self.i % len(self.items)]
        self.i += 1
        return it


NL = 4


def weight_blocks():
    b = []
    for i in range(2):
        b.append([("w_in", 8, 1024 + 512 * i, 512, 0, 512, 0)])
    for c0 in (0, 512, 2048, 2560):
        b.append([("w_in", 8, c0, 512, 0, 512, 0)])
    b.append(None)
    for i in range(2):
        b.append([("w_out", 8, 512 * i, 512, 0, 512, 0)])
    for i in range(11):
        b.append([("w_up", 8, 256 * i, 256, 0, 512, 0), ("w_up", 8, DFF + 256 * i, 256, 0, 512, 256)])
    for f in range(8):
        b.append([("w_down", 22, 128 * f, 128, 0, 128, 0)])
    for j in range(4):
        b.append([("w_gate", 8, 256 * j, 256, 0, 256, 0), ("w_ple", 2, 256 * j, 256, 2048, 256, 0)])
    return b


WB = weight_blocks()
NWB = len(WB)
B_QK, B_U, B_V, B_VM, B_O, B_WS, B_OUT, B_UP, B_DOWN, B_GP = 0, 2, 3, 4, 5, 6, 7, 9, 20, 28
WB_SIZE = []
for _b in WB:
    if _b is None:
        WB_SIZE.append(1024)
    else:
        WB_SIZE.append(max(base + K * W for (_, K, _, _, base, W, _) in _b))


def MM(out, lhsT, rhs, start, stop):
    return lambda e: e.matmul(out, lhsT=lhsT, rhs=rhs, start=start, stop=stop)


def TR(out, in_, ident):
    return lambda e: e.transpose(out, in_, ident)


def ACT(out, in_, func, **kw):
    return lambda e: e.activation(out=out, in_=in_, func=func, **kw)


def TT(out, in0, in1, op):
    return lambda e: e.tensor_tensor(out=out, in0=in0, in1=in1, op=op)


def TS(out, in0, s1, s2, op0, op1=None):
    if op1 is None:
        return lambda e: e.tensor_scalar(out=out, in0=in0, scalar1=s1, scalar2=None, op0=op0)
    return lambda e: e.tensor_scalar(out=out, in0=in0, scalar1=s1, scalar2=s2, op0=op0, op1=op1)


def TSS(out, in_, scalar, op):
    return lambda e: e.tensor_single_scalar(out=out, in_=in_, scalar=scalar, op=op)


def STT(out, in0, scalar, in1, op0, op1):
    return lambda e: e.scalar_tensor_tensor(out=out, in0=in0, scalar=scalar, in1=in1, op0=op0, op1=op1)


def CP(out, in_):
    return lambda e: e.tensor_copy(out=out, in_=in_)


def MS(out, val):
    return lambda e: e.memset(out, val)


def DMA(out, in_):
    return lambda e: e.dma_start(out=out, in_=in_)


def pipeline(n, stages):
    ns = len(stages)
    for i in range(n + ns - 1):
        for si in reversed(range(ns)):
            c = i - si
            if 0 <= c < n:
                stages[si](c)


def build_program(S, n_layers=NL):
    n_tiles = S // NT
    nc = bass.Bass("TRN2", target_bir_lowering=False)

    def din(name, shape):
        return nc.dram_tensor(name, list(shape), F32, kind="ExternalInput").ap()

    xin = din("xin", [D, S])
    pin = din("pin", [NL, 256, S])
    wsrc = {
        "w_in": din("w_in", [NL, D, 3080]), "w_out": din("w_out", [NL, D, D]), "w_up": din("w_up", [NL, D, 2 * DFF]),
        "w_down": din("w_down", [NL, DFF, D]), "w_gate": din("w_gate", [NL, D, D]), "w_ple": din("w_ple", [NL, 256, D]),
    }
    gv_d = din("gv", [128, NL * 24 + 8])
    mlng_d = din("mlng", [128, NL * 4])
    lngb_d = din("lngb", [NL, 128, 2, 512])
    wsT_d = din("wsT", [NL, 128, 8, 128])
    bs8_d = din("bs8", [8, NL, 128])
    cwqk_d = din("cwqk", [128, NL, 8, 4])
    cbqk_d = din("cbqk", [128, NL, 8])
    cwff_d = din("cwff", [128, NL, 44, 3])
    cbff_d = din("cbff", [128, NL, 44])
    bif_d = din("bif", [128, NL, 8])
    ident_d = din("ident", [128, 128])
    tri_d = din("tri", [128, 128])
    sel_d = din("sel", [128, 512])
    yout = nc.dram_tensor("yout", [D, S], F32, kind="ExternalOutput").ap()
    wb_d = nc.dram_tensor("wb_scratch", [NL, NWB, 128, WBLK], BF16).ap()

    xin_v = xin.rearrange("(f p) s -> p f s", p=128)
    yout_v = yout.rearrange("(f p) s -> p f s", p=128)

    with ExitStack() as es:
        def sb(name, shape, dt=F32):
            return es.enter_context(nc.sbuf_tensor("sb_" + name, list(shape), dt))

        def psum(name, shape, dt=F32):
            return es.enter_context(nc.psum_tensor(name, list(shape), dt))

        def sem(name):
            return es.enter_context(nc.semaphore(name))

        qsems = {q: sem("q_" + q) for q in ["pe", "act", "dve", "pool"]}
        S_ = Sched()
        A = S_.add

        xT = sb("xT", [128, 8, NT])
        hT = sb("hT", [128, 8, NT], BF16)
        sqr = [sb(f"sq{i}", [128, NT], BF16) for i in range(2)]
        rstd = sb("rstd", [128, NT])
        ntmp = [sb(f"ntmp{i}", [128, NT]) for i in range(2)]
        qkraw = sb("qkraw", [128, 8, NT + 4])
        qkT = sb("qkT", [128, 8, NT], BF16)
        cacc = [sb(f"cacc{i}", [128, NT]) for i in range(4)]
        gpre = sb("gpre", [128, NCH, 8])
        e1 = sb("e1", [128, NCH, 4])
        l1 = sb("l1", [128, NCH, 4])
        t1 = sb("t1", [128, NCH, 4])
        sc = sb("sc", [128, NCH, 4])
        einv = sb("einv", [128, NCH, 4])
        ebL = sb("ebL", [128, NCH, 4])
        ebLs = sb("ebLs", [128, NL, 4])
        ug = sb("ug", [128, NCH, 512], BF16)
        vg = sb("vg", [128, NCH, 512], BF16)
        vt = [sb(f"vt{i}", [128, 512]) for i in range(2)]
        vn = sb("vn", [128, NCH, 512], BF16)
        st6 = sb("st6", [128, NCH, 6])
        mv = sb("mv", [128, NCH, 2])
        lvar = sb("lvar", [128, NCH])
        lrstd = sb("lrstd", [128, NCH])
        vaug = sb("vaug", [128, NCH, 4, 132], BF16)
        so = sb("so", [128, NCH, 512], BF16)
        gm = [sb(f"gm{i}", [128, 512], BF16) for i in range(2)]
        Amat = [sb(f"Amat{i}", [128, 4, 128], BF16) for i in range(2)]
        Kp = [sb(f"Kp{i}", [128, 4, 128], BF16) for i in range(2)]
        dtmp = [sb(f"dtmp{i}", [128, 4]) for i in range(2)]
        hc = [sb(f"hc{i}", [128, 512]) for i in range(2)]
        junk = sb("junk", [128, 128])
        ss = [sb(f"ss{i}", [128, 4]) for i in range(2)]
        lss = [sb(f"lss{i}", [128, 4]) for i in range(2)]
        hrs = [sb(f"hrs{i}", [128, 4]) for i in range(2)]
        ml = [sb(f"ml{i}", [128, 512], BF16) for i in range(2)]
        Dst = sb("Dst", [128, NL, 4, 132])
        Cbf = sb("Cbf", [128, NL, 4, 132], BF16)
        raw = [sb(f"raw{i}", [128, NT + 2]) for i in range(4)]
        facc = [sb(f"facc{i}", [128, NT]) for i in range(3)]
        sa = [sb(f"sa{i}", [128, NT]) for i in range(2)]
        actT = sb("actT", [128, 22, NT], BF16)
        halo = sb("halo", [128, NL, 44, 2])
        qhalo = sb("qhalo", [128, NL, 8, 4])
        pT = sb("pT", [128, 2, NT], BF16)
        gt = [sb(f"gt{i}", [128, NT]) for i in range(2)]
        wring = sb("wring", [128, NSLOT, WBLK], BF16)
        lngb = sb("lngb", [128, 2, 512])
        ident = sb("ident", [128, 128], BF16)
        tri = sb("tri", [128, 128])
        trib = sb("trib", [128, 128], BF16)
        trib4 = sb("trib4", [128, 4, 128], BF16)
        onesf = sb("onesf", [128, 128])
        onesm = sb("onesm", [128, 128], BF16)
        sel = sb("sel", [128, 512], BF16)
        bs128 = sb("bs128", [128, NL, 128], BF16)
        gv = sb("gv", [128, NL * 24 + 8])
        mlng = sb("mlng", [128, NL * 4])
        cwqk = sb("cwqk", [128, NL, 8, 4])
        cbqk = sb("cbqk", [128, NL, 8])
        cwff = sb("cwff", [128, NL, 44, 3])
        cbff = sb("cbff", [128, NL, 44])
        bif = sb("bif", [128, NL, 8])
        epsc = sb("epsc", [128, 1])
        wg = sb("wg", [128, NL, 8, 8], BF16)

        PBt = [psum(f"pb{i}", [128, 512]) for i in range(4)]
        PTt = [psum(f"pt{i}", [128, 1024], BF16) for i in range(2)]
        PSt = [psum(f"ps{i}", [128, 512]) for i in range(2)]
        PB = Ring([(PBt[i], ("pb", i)) for i in range(4)])
        PT = Ring([(PTt[i], ("pt", i)) for i in range(2)])
        PS = Ring([(PSt[i], ("ps", i)) for i in range(2)])

        s_setup = sem("d_setup")
        s_w = [sem(f"d_w{i}") for i in range(NSLOT)]
        s_xf = [sem(f"d_x{f}") for f in range(8)]
        s_p = sem("d_p")
        s_ln = sem("d_ln")
        s_yo = [sem(f"d_yo{i}") for i in range(2)]
        s_prep = [sem(f"d_prep{i}") for i in range(4)]
        s_ws = sem("d_ws")
        s_wg = sem("d_wg")
        s_ws2 = sem("d_ws2")

        XK = [("x", f) for f in range(8)]
        HT = [("hT", f) for f in range(8)]

        S_.tag = "setup"

        def load_const(dst_ap, src_ap, keys):
            A("sp", DMA(dst_ap, src_ap), writes=keys, dma_sem=s_setup)

        load_const(gv[:], gv_d, ["gv"])
        load_const(mlng[:], mlng_d, ["mlng"])
        load_const(cwqk[:], cwqk_d, ["cwqk"])
        load_const(cbqk[:], cbqk_d, ["cbqk"])
        load_const(cwff[:], cwff_d, ["cwff"])
        load_const(cbff[:], cbff_d, ["cbff"])
        load_const(bif[:], bif_d, ["bif"])
        load_const(tri[:], tri_d, ["tri"])
        load_const(xT[:, 0, :], sel_d, [("x", 0)])
        load_const(xT[:, 1, 0:128], ident_d, [("x", 1)])
        load_const(xT[0:8, 2, :].rearrange("p (l t) -> p l t", l=NL), bs8_d, [("x", 2)])
        setup_keys = ["gv", "mlng", "cwqk", "cbqk", "cwff", "cbff", "bif", "tri"] + XK[0:3]
        A("pool", MS(epsc[:], EPS), reads=setup_keys, writes=["epsc", "setup_done"])
        SD = ["setup_done"]
        A("pool", CP(sel[:], xT[:, 0, :]), reads=SD + [("x", 0)], writes=["sel"])
        A("pool", CP(ident[:], xT[:, 1, 0:128]), reads=SD + [("x", 1)], writes=["ident"])
        A("pool", CP(trib[:], tri[:]), reads=SD, writes=["trib"])
        for h in range(4):
            A("pool", CP(trib4[:, h, :], tri[:]), reads=SD, writes=["trib4"])
        A("pool", MS(bs128[:], 0.0), reads=SD, writes=["bs128"])
        A("pool", CP(bs128[0:8, :, :], xT[0:8, 2, :].rearrange("p (l t) -> p l t", l=NL)), reads=SD + ["bs128", ("x", 2)], writes=["bs128"])
        A("dve", MS(onesf[:], 1.0), writes=["onesf"])
        A("dve", MS(onesm[:], 1.0 / D), writes=["onesm"])
        A("dve", MS(qhalo[:], 0.0), writes=["qhalo"])
        A("dve", MS(halo[:], 0.0), writes=["halo"])
        A("dve", MS(Dst[:], 0.0), writes=[("Dst", l, h) for l in range(NL) for h in range(4)])
        A("dve", MS(Cbf[:], 0.0), writes=[("Cbf", l, h) for l in range(NL) for h in range(4)])
        A("dve", MS(ebLs[:], 1.0), writes=["ebLs"])
        A("pool", MS(vaug[:], 1.0), writes=[("vaug", c) for c in range(NCH)])
        for l in range(n_layers):
            A("pool", DMA(wg[:, l, :, :], wsrc["w_in"][l].rearrange("(k p) n -> p k n", p=128)[:, :, 3072:3080]),
              writes=[("wg", l), "wgslot"], dma_sem=s_wg)
        xflat = xT[:, 4:6, :].rearrange("p a b -> p (a b)")
        hflat = hT[:, 0:2, :].rearrange("p a b -> p (a b)")
        for l in range(n_layers):
            A("sp", DMA(xflat.rearrange("p (h t) -> p h t", h=8), wsT_d[l]), writes=[("x", 4), ("x", 5)], dma_sem=s_ws)
            for h in range(8):
                A("dve", TT(hflat[:, h * 128:(h + 1) * 128], xflat[:, h * 128:(h + 1) * 128], tri[:], ALU.mult),
                  reads=SD + [("x", 4), ("x", 5)], writes=[("hT", 0), ("hT", 1)])
            A("sp", DMA(wb_d[l, B_WS, :, 0:1024], hflat), reads=[("hT", 0), ("hT", 1)], writes=[("wbd", l, B_WS, 0)], dma_sem=s_ws2)

        np_ = [0]
        wbd_keys = {}
        prep_queue = {l: [] for l in range(n_layers)}
        for l in range(n_layers):
            for bi, parts in enumerate(WB):
                if parts is None:
                    wbd_keys[(l, bi)] = [("wbd", l, bi, 0)]
                    continue
                keys = []
                for pi, (src, K, c0, n, base, W, off) in enumerate(parts):
                    srcv = wsrc[src][l].rearrange("(k p) n -> p k n", p=128)[:, :, c0:c0 + n]
                    dstv = wb_d[l, bi, :, base:base + K * W].rearrange("p (k n) -> p k n", k=K)[:, :, off:off + n]
                    key = ("wbd", l, bi, pi)
                    prep_queue[l].append((dstv, srcv, key))
                    keys.append(key)
                wbd_keys[(l, bi)] = keys

        def emit_prep(l, n):
            for _ in range(n):
                if l >= n_layers or not prep_queue[l]:
                    return
                dstv, srcv, key = prep_queue[l].pop(0)
                sl = np_[0] % 4
                np_[0] += 1
                A("pool", DMA(dstv, srcv), writes=[key, ("prepslot", sl)], dma_sem=s_prep[sl])

        emit_prep(0, 10 ** 6)

        gblk = [0]
        total_blocks = n_tiles * n_layers * NWB

        def issue_wload(g):
            if g >= total_blocks:
                return
            bi = g % NWB
            l = (g // NWB) % n_layers
            sl = g % NSLOT
            n = WB_SIZE[bi]
            A("sp", DMA(wring[:, sl, 0:n], wb_d[l, bi, :, 0:n]), reads=wbd_keys[(l, bi)], writes=[("wr", sl)], dma_sem=s_w[sl])

        def next_block(expect_bi):
            g = gblk[0]
            assert g % NWB == expect_bi, (g % NWB, expect_bi)
            issue_wload(g + NSLOT - 1)
            gblk[0] += 1
            sl = g % NSLOT
            return wring[:, sl, :], ("wr", sl)

        def norm_begin():
            return PS.next()

        def norm_accum(nst, f):
            ps, pk = nst
            i = f % 2
            A("act", ACT(sqr[i][:], xT[:, f, :], AF.Square), reads=[("x", f)], writes=[("sq", i)])
            A("pe", MM(ps[:, 0:NT], onesm[:], sqr[i][:], f == 0, f == 7), reads=[("sq", i), "onesm"], writes=[pk])

        def norm_finish(nst, gcol, final_t0=None):
            ps, pk = nst
            A("act", ACT(rstd[:], ps[:, 0:NT], AF.Ln, bias=epsc[:, 0:1]), reads=[pk, "epsc"], writes=["rstd"])
            A("act", ACT(rstd[:], rstd[:], AF.Exp, scale=-0.5), reads=["rstd"], writes=["rstd"])
            for f in range(8):
                if final_t0 is not None:
                    i = f % 2
                    A("dve", STT(ntmp[i][:], xT[:, f, :], gv[:, gcol + f:gcol + f + 1], rstd[:], ALU.mult, ALU.mult),
                      reads=[("x", f), "rstd"] + SD, writes=[("ntmp", i)])
                    A("sp", DMA(yout_v[:, f, final_t0:final_t0 + NT], ntmp[i][:]), reads=[("ntmp", i)], writes=["yout"], dma_sem=s_yo[i])
                else:
                    A("dve", STT(hT[:, f, :], xT[:, f, :], gv[:, gcol + f:gcol + f + 1], rstd[:], ALU.mult, ALU.mult),
                      reads=[("x", f), "rstd"] + SD, writes=[("hT", f)])

        def norm(gcol, final_t0=None):
            nst = norm_begin()
            for f in range(8):
                norm_accum(nst, f)
            norm_finish(nst, gcol, final_t0)

        for g in range(NSLOT - 1):
            issue_wload(g)

        def layer_step(t, l, nst_in):
            t0 = t * NT
            tg = f"L{l}"
            S_.tag = tg + "norm1"
            A("pool", DMA(pT[:], pin[l].rearrange("(k p) s -> p k s", p=128)[:, :, t0:t0 + NT]), writes=["pT"], dma_sem=s_p)
            A("sp", DMA(lngb[:], lngb_d[l]), writes=["lngb"], dma_sem=s_ln)

            if nst_in is None:
                norm(l * 24)
            else:
                norm_finish(nst_in, l * 24)
            S_.tag = tg + "gates"
            psg, pgk = PS.next()
            for c in range(NCH):
                for k in range(8):
                    A("pe", MM(psg[:, c * 8:(c + 1) * 8], hT[:, k, c * 128:(c + 1) * 128], wg[:, l, k, :], k == 0, k == 7),
                      reads=[("hT", k), ("wg", l)], writes=[pgk])
            for c in range(NCH):
                A("dve", TT(gpre[:, c, :], psg[:, c * 8:(c + 1) * 8], bif[:, l, :], ALU.add), reads=[pgk] + SD, writes=["gpre"])
            A("act", ACT(e1[:], gpre[:, :, 4:8], AF.Exp, scale=-1.0), reads=["gpre"], writes=["e1"])
            A("act", ACT(l1[:], e1[:], AF.Ln, bias=1.0), reads=["e1"], writes=["l1"])
            def gates_tail():
                S_.tag = tg + "gates"
                psb, pbk = PS.next()
                l1v = l1[:].rearrange("p c h -> p (c h)")
                A("pe", MM(psb[:, 0:16], tri[:], l1v, True, True), reads=["l1"] + SD, writes=[pbk])
                A("pe", MM(psb[:, 16:32], onesf[:], l1v, True, True), reads=["l1", "onesf"], writes=[pbk])
                bneg = psb[:, 0:16].rearrange("p (c h) -> p c h", c=NCH)
                bLneg = psb[:, 16:32].rearrange("p (c h) -> p c h", c=NCH)
                A("dve", STT(t1[:], gpre[:, :, 0:4], math.log(128.0 ** -0.5), bneg, ALU.add, ALU.add), reads=["gpre", pbk], writes=["t1"])
                A("act", ACT(sc[:], t1[:], AF.Exp), reads=["t1"], writes=["sc"])
                A("act", ACT(einv[:], bneg, AF.Exp), reads=[pbk, "t1"], writes=["einv"])
                A("act", ACT(ebL[:], bLneg, AF.Exp, scale=-1.0), reads=[pbk, "t1"], writes=["ebL"])
                S_.tag = tg + "qk"

            S_.tag = tg + "qk"
            Wq = {}
            qpb = {}

            def qk_s0(f):
                wbi, j = divmod(f, 4)
                if j == 0:
                    W, wk = next_block(B_QK + wbi)
                    Wq[wbi] = (W[:, 0:4096].rearrange("p (k n) -> p k n", k=8), wk)
                Wv, wk = Wq[wbi]
                ps, pk = PB.next()
                qpb[f] = (ps, pk)
                for k in range(8):
                    A("pe", MM(ps[:, 0:NT], Wv[:, k, j * 128:(j + 1) * 128], hT[:, k, :], k == 0, k == 7), reads=[wk, ("hT", k)], writes=[pk])
                qk_ = ("qkraw", f)
                A("pool", CP(qkraw[:, f, 0:3], qhalo[:, l, f, 0:3]), reads=["qhalo"], writes=[qk_])
                A("act", ACT(qkraw[:, f, 3:3 + NT], ps[:, 0:NT], AF.Copy), reads=[pk, qk_], writes=[qk_])
                A("pool", CP(qhalo[:, l, f, 0:3], qkraw[:, f, NT:NT + 3]), reads=[qk_], writes=["qhalo"])

            def qk_s1(f):
                A("act", ACT(cacc[f % 4][:], qkraw[:, f, 0:NT], AF.Identity, scale=cwqk[:, l, f, 0:1], bias=cbqk[:, l, f:f + 1]),
                  reads=[("qkraw", f)] + SD, writes=[("cacc", f % 4)])

            def qk_s2(f):
                ps, pk = qpb.pop(f)
                ca = cacc[f % 4]
                ck = ("cacc", f % 4)
                A("dve", STT(ca[:], ps[:, 0:NT], cwqk[:, l, f, 3:4], ca[:], ALU.mult, ALU.add), reads=[pk, ck] + SD, writes=[ck])

            def qk_tap(tap):
                def fn(f):
                    ca = cacc[f % 4]
                    ck = ("cacc", f % 4)
                    A("dve", STT(ca[:], qkraw[:, f, tap:tap + NT], cwqk[:, l, f, tap:tap + 1], ca[:], ALU.mult, ALU.add), reads=[("qkraw", f), ck] + SD, writes=[ck])
                return fn

            def qk_s5(f):
                A("act", ACT(qkT[:, f, :], cacc[f % 4][:], AF.Silu), reads=[("cacc", f % 4)], writes=[("qkT", f)])

            gates_tail_done = [False]

            def qk_s0_wrap(f):
                qk_s0(f)
                if f == 1 and not gates_tail_done[0]:
                    gates_tail()
                    gates_tail_done[0] = True

            pipeline(8, [qk_s0_wrap, qk_s1, qk_s2, qk_tap(1), qk_tap(2), qk_s5])

            S_.tag = tg + "uvo"
            for which, bidx in (("u", B_U), ("v", B_V), ("vm", B_VM), ("o", B_O)):
                W, wk = next_block(bidx)
                Wv = W[:, 0:4096].rearrange("p (k n) -> p k n", k=8)
                for c in range(NCH):
                    ps, pk = PB.next()
                    for k in range(8):
                        A("pe", MM(ps[:, 0:512], hT[:, k, c * 128:(c + 1) * 128], Wv[:, k, :], k == 0, k == 7), reads=[wk, ("hT", k)], writes=[pk])
                    if which == "u":
                        A("act", ACT(ug[:, c, :], ps[:, 0:512], AF.Gelu), reads=[pk], writes=[("ug", c)])
                    elif which == "v":
                        A("act", ACT(vg[:, c, :], ps[:, 0:512], AF.Gelu), reads=[pk], writes=[("vg", c)])
                        A("dve", lambda e, c=c: e.bn_stats(out=st6[:, c, :], in_=vg[:, c, :]), reads=[("vg", c)], writes=[("st6", c)])
                        A("dve", lambda e, c=c: e.bn_aggr(out=mv[:, c, :], in_=st6[:, c, :]), reads=[("st6", c)], writes=[("mv", c)])
                    elif which == "vm":
                        for h in range(4):
                            A("act", ACT(vaug[:, c, h, 0:128], ps[:, h * 128:(h + 1) * 128], AF.Copy, scale=sc[:, c, h:h + 1]), reads=[pk, ("vaug", c), "sc"], writes=[("vaug", c)])
                        A("pool", CP(vaug[:, c, :, 128], sc[:, c, :]), reads=["sc", ("vaug", c)], writes=[("vaug", c)])
                    else:
                        A("act", ACT(so[:, c, :], ps[:, 0:512], AF.Sigmoid), reads=[pk], writes=[("so", c)])
                if which == "v":
                    A("act", ACT(lvar[:], mv[:, :, 1], AF.Ln, bias=epsc[:, 0:1]), reads=[("mv", c) for c in range(NCH)] + ["epsc"], writes=["lvar"])
                    A("act", ACT(lrstd[:], lvar[:], AF.Exp, scale=-0.5), reads=["lvar"], writes=["lrstd"])
                    for c in range(NCH):
                        v_ = vt[c % 2]
                        vk = ("vt", c % 2)
                        A("dve", TS(v_[:], vg[:, c, :], mv[:, c, 0:1], lrstd[:, c:c + 1], ALU.subtract, ALU.mult),
                          reads=[("vg", c), ("mv", c), "lrstd"], writes=[vk])
                        A("pool", TT(v_[:], v_[:], lngb[:, 0, :], ALU.mult), reads=[vk, "lngb"], writes=[vk])
                        A("pool", TT(vn[:, c, :], v_[:], lngb[:, 1, :], ALU.add), reads=[vk, "lngb"], writes=[("vn", c)])

            Wws, wsk = next_block(B_WS)
            wsv = Wws[:, 0:1024].rearrange("p (h t) -> p h t", h=8)
            stg = {}

            def mix_P1(c):
                S_.tag = tg + "mixP1"
                cs = slice(c * 128, (c + 1) * 128)
                gi = c % 2
                psm, pmk = PB.next()
                A("pe", MM(psm[:, 0:512], bs128[:, l, :], sel[:], True, False), reads=["bs128", "sel"], writes=[pmk])
                for h in range(8):
                    A("pe", MM(psm[:, h * 64:(h + 1) * 64], wsv[:, h, :], vn[:, c, h * 64:(h + 1) * 64], False, h == 7), reads=[wsk, ("vn", c)], writes=[pmk])
                st, stk = PS.next()
                for h in range(4):
                    A("pe", MM(st[:, h * 128:(h + 1) * 128], qkT[:, 4 + h, cs], qkT[:, h, cs], True, True), reads=[("qkT", 4 + h), ("qkT", h)], writes=[stk])
                kt, ktk = PT.next()
                for h in range(4):
                    A("pe", TR(kt[:, h * 128:(h + 1) * 128], qkT[:, 4 + h, cs], ident[:]), reads=[("qkT", 4 + h), "ident"], writes=[ktk])
                A("dve", TT(gm[gi][:], psm[:, 0:512], ug[:, c, :], ALU.mult), reads=[pmk, ("ug", c)], writes=[("gm", gi)])
                A("dve", TT(Amat[gi][:], st[:, 0:512].rearrange("p (h t) -> p h t", h=4), trib4[:], ALU.mult),
                  reads=[stk, "trib4"], writes=[("Amat", gi, h) for h in range(4)])
                A("dve", CP(Kp[gi][:], kt[:, 0:512].rearrange("p (h t) -> p h t", h=4)), reads=[ktk], writes=[("Kp", gi, h) for h in range(4)])

            def mix_P2(c):
                S_.tag = tg + "mixP2"
                cs = slice(c * 128, (c + 1) * 128)
                gi = c % 2
                pt, ptk = PT.next()
                for j in range(4):
                    A("pe", TR(pt[:, j * 128:(j + 1) * 128], gm[gi][:, j * 128:(j + 1) * 128], ident[:]), reads=[("gm", gi), "ident"], writes=[ptk])
                A("act", ACT(hT[:, 0:4, cs], pt[:, 0:512].rearrange("p (a b) -> p a b", a=4), AF.Copy), reads=[ptk], writes=HT[0:4])
                for pr in range(2):
                    nd, ndk = PB.next()
                    up, upk = PB.next()
                    for j in range(2):
                        h = 2 * pr + j
                        A("pe", MM(nd[:, j * 129:(j + 1) * 129], Amat[gi][:, h, :], vaug[:, c, h, 0:129], True, False), reads=[("Amat", gi, h), ("vaug", c)], writes=[ndk])
                        A("pe", MM(nd[:, j * 129:(j + 1) * 129], qkT[:, h, cs], Cbf[:, l, h, 0:129], False, True), reads=[("qkT", h), ("Cbf", l, h)], writes=[ndk])
                    for j in range(2):
                        h = 2 * pr + j
                        A("pe", MM(up[:, j * 129:(j + 1) * 129], Kp[gi][:, h, :], vaug[:, c, h, 0:129], True, True), reads=[("Kp", gi, h), ("vaug", c)], writes=[upk])
                    for j in range(2):
                        h = 2 * pr + j
                        prev = ebL[:, c - 1, h:h + 1] if c > 0 else ebLs[:, l, h:h + 1]
                        A("dve", STT(Dst[:, l, h, 0:129], Dst[:, l, h, 0:129], prev, up[:, j * 129:(j + 1) * 129], ALU.mult, ALU.add),
                          reads=[upk, ("Dst", l, h), "ebL", "ebLs"], writes=[("Dst", l, h)])
                    d_ = dtmp[gi]
                    dk = ("dtmp", gi, pr)
                    dsl = d_[:, 2 * pr:2 * pr + 2]
                    A("act", ACT(dsl, nd[:, 0:258].rearrange("p (j d) -> p j d", j=2)[:, :, 128], AF.Abs), reads=[ndk], writes=[dk])
                    A("dve", TT(dsl, dsl, einv[:, c, 2 * pr:2 * pr + 2], ALU.max), reads=[dk, "einv"], writes=[dk])
                    A("dve", lambda e, dsl=dsl: e.reciprocal(out=dsl, in_=dsl), reads=[dk], writes=[dk])
                    for j in range(2):
                        h = 2 * pr + j
                        A("dve", STT(hc[gi][:, h * 128:(h + 1) * 128], nd[:, j * 129:j * 129 + 128], d_[:, h:h + 1], so[:, c, h * 128:(h + 1) * 128], ALU.mult, ALU.mult),
                          reads=[ndk, dk, ("so", c)], writes=[("hc", gi, h)])
                A("pool", TT(Cbf[:, l, :, 0:129], Dst[:, l, :, 0:129], ebL[:, c, :].unsqueeze(2).broadcast_to([128, 4, 129]), ALU.mult),
                  reads=[("Dst", l, h) for h in range(4)] + ["ebL"], writes=[("Cbf", l, h) for h in range(4)])
                for h in range(4):
                    A("dve", lambda e, h=h, gi=gi: e.scalar_tensor_tensor(out=junk[:], in0=hc[gi][:, h * 128:(h + 1) * 128], scalar=1.0, in1=hc[gi][:, h * 128:(h + 1) * 128],
                                                                          op0=ALU.mult, op1=ALU.mult, accum_out=ss[gi][:, h:h + 1]),
                      reads=[("hc", gi, h)], writes=["junk", ("ss", gi)])
                A("act", ACT(lss[gi][:], ss[gi][:], AF.Ln, scale=1.0 / 128.0, bias=epsc[:, 0:1]), reads=[("ss", gi), "epsc"], writes=[("lss", gi)])
                A("act", ACT(hrs[gi][:], lss[gi][:], AF.Exp, scale=-0.5), reads=[("lss", gi)], writes=[("hrs", gi)])
                A("pool", TT(ml[gi][:].rearrange("p (h d) -> p h d", h=4), hc[gi][:].rearrange("p (h d) -> p h d", h=4),
                             hrs[gi][:].unsqueeze(2).broadcast_to([128, 4, 128]), ALU.mult),
                  reads=[("hc", gi, h) for h in range(4)] + [("hrs", gi)], writes=[("ml", gi)])

            def mix_P3(c):
                S_.tag = tg + "mixP3"
                cs = slice(c * 128, (c + 1) * 128)
                gi = c % 2
                pt, ptk = PT.next()
                for h in range(4):
                    A("pe", TR(pt[:, h * 128:(h + 1) * 128], ml[gi][:, h * 128:(h + 1) * 128], ident[:]), reads=[("ml", gi), "ident"], writes=[ptk])
                for h in range(4):
                    A("act", ACT(hT[:, 4 + h, cs], pt[:, h * 128:(h + 1) * 128], AF.Copy, scale=mlng[:, l * 4 + h:l * 4 + h + 1]), reads=[ptk] + SD, writes=[("hT", 4 + h)])

            mix_P1(0)
            for c in range(NCH):
                if c + 1 < NCH:
                    mix_P1(c + 1)
                mix_P2(c)
                if c >= 1:
                    mix_P3(c - 1)
            mix_P3(NCH - 1)
            A("dve", CP(ebLs[:, l, :], ebL[:, NCH - 1, :]), reads=["ebL", "ebLs"], writes=["ebLs"])

            S_.tag = tg + "wout"
            Wo = {}
            nst2 = norm_begin()

            def wout_s0(f):
                wbi, j = divmod(f, 4)
                if j == 0:
                    W, wk = next_block(B_OUT + wbi)
                    Wo[wbi] = (W[:, 0:4096].rearrange("p (k n) -> p k n", k=8), wk)
                Wv, wk = Wo[wbi]
                ps, pk = PB.next()
                for k in range(8):
                    A("pe", MM(ps[:, 0:NT], Wv[:, k, j * 128:(j + 1) * 128], hT[:, k, :], k == 0, k == 7), reads=[wk, ("hT", k)], writes=[pk])
                A("dve", TT(xT[:, f, :], ps[:, 0:NT], xT[:, f, :], ALU.add), reads=[pk, ("x", f)], writes=[("x", f)])

            pipeline(8, [wout_s0, lambda f: None, lambda f: norm_accum(nst2, f)])

            S_.tag = tg + "norm2"
            norm_finish(nst2, l * 24 + 8)
            S_.tag = tg + "wup"
            Wu = {}
            pbk_ = {}

            def up_pre(fi):
                A("pool", CP(raw[fi % 4][:, 0:2], halo[:, l, fi, :]), reads=["halo"], writes=[("raw", fi % 4)])

            def up_s0(fi):
                i, jj = divmod(fi, 4)
                if jj == 0:
                    W, wk = next_block(B_UP + i)
                    Wu[i] = (W[:, 0:4096].rearrange("p (k n) -> p k n", k=8), wk)
                Wv, wk = Wu[i]
                ps, pk = PB.next()
                pbk_[fi] = (ps, pk)
                for k in range(8):
                    A("pe", MM(ps[:, 0:NT], Wv[:, k, jj * 128:(jj + 1) * 128], hT[:, k, :], k == 0, k == 7), reads=[wk, ("hT", k)], writes=[pk])
                r_ = raw[fi % 4]
                rk_ = ("raw", fi % 4)
                A("act", ACT(r_[:, 2:2 + NT], ps[:, 0:NT], AF.Copy), reads=[pk, rk_], writes=[rk_])
                if t == 0:
                    emit_prep(l + 1, 2)

            def up_s1(fi):
                A("act", ACT(facc[fi % 3][:], raw[fi % 4][:, 0:NT], AF.Identity, scale=cwff[:, l, fi, 0:1], bias=cbff[:, l, fi:fi + 1]),
                  reads=[("raw", fi % 4)] + SD, writes=[("facc", fi % 3)])
                A("pool", CP(halo[:, l, fi, :], raw[fi % 4][:, NT:NT + 2]), reads=[("raw", fi % 4)], writes=["halo"])

            def up_s2(fi):
                ps, pk = pbk_.pop(fi)
                fa = facc[fi % 3]
                fk = ("facc", fi % 3)
                A("dve", STT(fa[:], ps[:, 0:NT], cwff[:, l, fi, 2:3], fa[:], ALU.mult, ALU.add), reads=[pk, fk] + SD, writes=[fk])

            def up_s3(fi):
                fa = facc[fi % 3]
                fk = ("facc", fi % 3)
                A("dve", STT(fa[:], raw[fi % 4][:, 1:1 + NT], cwff[:, l, fi, 1:2], fa[:], ALU.mult, ALU.add), reads=[("raw", fi % 4), fk] + SD, writes=[fk])

            def up_s4(fi):
                i, jj = divmod(fi, 4)
                fa = facc[fi % 3]
                fk = ("facc", fi % 3)
                if jj < 2:
                    A("act", ACT(sa[jj][:], fa[:], AF.Silu), reads=[fk], writes=[("sa", jj)])
                else:
                    A("pool", TT(actT[:, 2 * i + jj - 2, :], sa[jj - 2][:], fa[:], ALU.mult), reads=[("sa", jj - 2), fk], writes=[("actT", 2 * i + jj - 2)])

            pipeline(44, [up_pre, up_s0, up_s1, up_s2, up_s3, up_s4])
            if t == 0:
                emit_prep(l + 1, 10 ** 6)

            S_.tag = tg + "wdown"
            nst3 = norm_begin()

            def wdown_s0(f):
                W, wk = next_block(B_DOWN + f)
                Wv = W[:, 0:22 * 128].rearrange("p (k n) -> p k n", k=22)
                ps, pk = PB.next()
                for k in range(22):
                    A("pe", MM(ps[:, 0:NT], Wv[:, k, :], actT[:, k, :], k == 0, k == 21), reads=[wk, ("actT", k)], writes=[pk])
                A("dve", TT(xT[:, f, :], ps[:, 0:NT], xT[:, f, :], ALU.add), reads=[pk, ("x", f)], writes=[("x", f)])

            pipeline(8, [wdown_s0, lambda f: None, lambda f: norm_accum(nst3, f)])

            S_.tag = tg + "norm3"
            norm_finish(nst3, l * 24 + 16)
            S_.tag = tg + "ple"
            Wgp = {}
            nst_next = norm_begin()

            for it in range(8 + 3):
                f = it
                fa_ = it - 3
                if 0 <= fa_ < 8:
                    A("act", ACT(sqr[fa_ % 2][:], xT[:, fa_, :], AF.Square), reads=[("x", fa_)], writes=[("sq", fa_ % 2)])
                if f < 8:
                    jb, jj = divmod(f, 2)
                    if jj == 0:
                        W, wk = next_block(B_GP + jb)
                        Wgp[jb] = (W[:, 0:2048].rearrange("p (k n) -> p k n", k=8), W[:, 2048:2560].rearrange("p (k n) -> p k n", k=2), wk)
                    Wg, Wp, wk = Wgp[jb]
                    psg2, pgk2 = PB.next()
                    for k in range(8):
                        A("pe", MM(psg2[:, 0:NT], Wg[:, k, jj * 128:(jj + 1) * 128], hT[:, k, :], k == 0, k == 7), reads=[wk, ("hT", k)], writes=[pgk2])
                if 0 <= fa_ < 8:
                    ps_n, pk_n = nst_next
                    A("pe", MM(ps_n[:, 0:NT], onesm[:], sqr[fa_ % 2][:], fa_ == 0, fa_ == 7), reads=[("sq", fa_ % 2), "onesm"], writes=[pk_n])
                if f < 8:
                    g_ = gt[f % 2]
                    gk_ = ("gt", f % 2)
                    A("act", ACT(g_[:], psg2[:, 0:NT], AF.Sigmoid), reads=[pgk2], writes=[gk_])
                    psp, ppk = PB.next()
                    for k in range(2):
                        A("pe", MM(psp[:, 0:NT], Wp[:, k, jj * 128:(jj + 1) * 128], pT[:, k, :], k == 0, k == 1), reads=[wk, "pT"], writes=[ppk])
                    A("dve", TT(g_[:], psp[:, 0:NT], g_[:], ALU.mult), reads=[ppk, gk_], writes=[gk_])
                    A("dve", TT(xT[:, f, :], xT[:, f, :], g_[:], ALU.add), reads=[gk_, ("x", f)], writes=[("x", f)])
            return nst_next

        for t in range(n_tiles):
            S_.tag = "xload"
            for f in range(8):
                A("sp", DMA(xT[:, f, :], xin_v[:, f, t * NT:(t + 1) * NT]), writes=[("x", f)], dma_sem=s_xf[f])
            nst = None
            for l in range(n_layers):
                nst = layer_step(t, l, nst)
            S_.tag = "final"
            norm_finish(nst, NL * 24, final_t0=t * NT)

        A("sp", lambda e: e.nop(), reads=["yout"])

        S_.analyze(qsems)
        with nc.Block() as block:
            S_.emit(block)
    return nc


def _pk(v):
    v = np.asarray(v, np.float32)
    return np.ascontiguousarray(v.reshape(-1, 128).T)


def ffn_col_order():
    cols = []
    for i in range(11):
        for j in (2 * i, 2 * i + 1):
            cols.append(np.arange(128 * j, 128 * j + 128))
        for j in (2 * i, 2 * i + 1):
            cols.append(DFF + np.arange(128 * j, 128 * j + 128))
    return np.stack(cols)


def shared_inputs(inp):
    f32 = np.float32
    d = {}
    for nm in ("w_in", "w_out", "w_up", "w_down", "w_ple"):
        d[nm] = np.ascontiguousarray(inp[nm], f32)
    d["w_gate"] = np.ascontiguousarray(inp["w_ple_gate"], f32)
    cols = []
    for i in range(NL):
        cols += [_pk(inp["g_mix"][i]), _pk(inp["g_ffn"][i]), _pk(inp["g_ple"][i])]
    cols.append(_pk(inp["g_final"]))
    d["gv"] = np.ascontiguousarray(np.concatenate(cols, axis=1))
    d["mlng"] = np.ascontiguousarray(np.concatenate([_pk(inp["ml_norm_g"][i]) for i in range(NL)], axis=1))
    lngb = np.stack([np.stack([inp["gm_ln_g"][i], inp["gm_ln_b"][i]]) for i in range(NL)])
    d["lngb"] = np.ascontiguousarray(np.broadcast_to(lngb[:, None], (NL, 128, 2, 512)), f32)
    d["wsT"] = np.ascontiguousarray(np.transpose(np.asarray(inp["gm_ws"], f32), (0, 3, 1, 2)))
    d["bs8"] = np.ascontiguousarray(np.transpose(np.asarray(inp["gm_bs"], f32), (1, 0, 2)))
    cw = np.asarray(inp["ml_conv_w"], f32)
    d["cwqk"] = np.ascontiguousarray(np.transpose(cw.reshape(NL, 4, 8, 128), (3, 0, 2, 1)))
    cb = np.asarray(inp["ml_conv_b"], f32)
    d["cbqk"] = np.ascontiguousarray(np.transpose(cb.reshape(NL, 8, 128), (2, 0, 1)))
    order = ffn_col_order()
    fw = np.asarray(inp["ffn_conv_w"], f32)
    d["cwff"] = np.ascontiguousarray(np.transpose(fw[:, :, order], (3, 0, 2, 1)))
    fb = np.asarray(inp["ffn_conv_b"], f32)
    d["cbff"] = np.ascontiguousarray(np.transpose(fb[:, order], (2, 0, 1)))
    bif = np.concatenate([np.asarray(inp["ml_b_i"], f32), np.asarray(inp["ml_b_f"], f32)], axis=1)
    d["bif"] = np.ascontiguousarray(np.broadcast_to(bif[None], (128, NL, 8)), f32)
    d["ident"] = np.eye(128, dtype=f32)
    d["tri"] = np.triu(np.ones((128, 128), f32))
    sel = np.zeros((128, 512), f32)
    for h in range(8):
        sel[h, h * 64:(h + 1) * 64] = 1.0
    d["sel"] = sel
    return d


_PROG = {}


def get_prog(S, n_layers=NL):
    if (S, n_layers) not in _PROG:
        _PROG[(S, n_layers)] = build_program(S, n_layers)
    return _PROG[(S, n_layers)]


def kernel(**inp):
    x = np.asarray(inp["x"], np.float32)
    p = np.asarray(inp["p"], np.float32)
    B, S, _ = x.shape
    sh = shared_inputs(inp)
    maps = []
    for b in range(B):
        m = dict(sh)
        m["xin"] = np.ascontiguousarray(x[b].T)
        m["pin"] = np.ascontiguousarray(np.transpose(p[:, b], (0, 2, 1)))
        maps.append(m)
    res = run_bass_kernel_spmd(get_prog(S), maps, core_ids=list(range(B)))
    return np.stack([np.asarray(res.results[b]["yout"]).T for b in range(B)]).astype(np.float32)
```

```python
import math
import numpy as np
from contextlib import ExitStack
import concourse.bass as bass
import concourse.mybir as mybir
from concourse.bass_utils import run_bass_kernel_spmd

F32 = mybir.dt.float32
BF16 = mybir.dt.bfloat16
AF = mybir.ActivationFunctionType
ALU = mybir.AluOpType

D = 1024
NT = 512
NCH = NT // 128
DFF = 2816
EPS = 1e-6
NSLOT = 4
WBLK = 4096
SAME_ENGINE_SYNC = True
SAME_ENGINE_ALL = False
SAME_ENGINE_SYNC_Q = {"act", "dve", "pool"}
ANNOTATE = False


class Op:
    __slots__ = ("q", "fn", "reads", "writes", "sem", "inc", "needs_inc", "count", "waits", "tag")


class Sched:
    def __init__(self):
        self.ops = []
        self.tag = None

    def add(self, q, fn, reads=(), writes=(), dma_sem=None):
        o = Op()
        o.q, o.fn, o.reads, o.writes = q, fn, tuple(reads), tuple(writes)
        o.sem = dma_sem
        o.inc = 16 if dma_sem is not None else 1
        o.needs_inc = dma_sem is not None
        o.count = None
        o.waits = {}
        o.tag = self.tag
        self.ops.append(o)
        return o

    def analyze(self, qsems):
        last_w, readers, need = {}, {}, []
        for o in self.ops:
            deps = {}
            for r in o.reads:
                w = last_w.get(r)
                if w is not None:
                    deps[id(w)] = (w, True)
            for k in o.writes:
                w = last_w.get(k)
                if w is not None and id(w) not in deps:
                    deps[id(w)] = (w, False)
                for rd in readers.get(k, ()):
                    if id(rd) not in deps:
                        deps[id(rd)] = (rd, False)
            deps.pop(id(o), None)
            nd = []
            for a, raw in deps.values():
                if a.sem is None and a.q == o.q:
                    if o.q == "pe" or not SAME_ENGINE_SYNC or o.q not in SAME_ENGINE_SYNC_Q:
                        continue
                    if not raw and not SAME_ENGINE_ALL:
                        continue
                nd.append(a)
                a.needs_inc = True
            need.append(nd)
            for r in o.reads:
                readers.setdefault(r, []).append(o)
            for k in o.writes:
                last_w[k] = o
                readers[k] = []
        cnt = {}
        for o in self.ops:
            if o.needs_inc:
                s = o.sem if o.sem is not None else qsems[o.q]
                cnt[s] = cnt.get(s, 0) + o.inc
                o.count = (s, cnt[s])
        waited = {}
        for o, nd in zip(self.ops, need):
            w = {}
            for a in nd:
                s, c = a.count
                if c > w.get(s, 0):
                    w[s] = c
            qw = waited.setdefault(o.q, {})
            for s, c in list(w.items()):
                if qw.get(s, 0) >= c:
                    del w[s]
                else:
                    qw[s] = c
            o.waits = w
        self.final_counts = cnt

    def emit(self, block):
        byq = {}
        for o in self.ops:
            byq.setdefault(o.q, []).append(o)

        def run(eng, ops):
            for o in ops:
                for s, c in o.waits.items():
                    eng.wait_ge(s, c)
                ins = o.fn(eng)
                if ANNOTATE and o.tag is not None:
                    ins.annotate(o.tag)
                if o.needs_inc:
                    ins.then_inc(o.count[0], o.inc)

        names = {"pe": "tensor", "act": "scalar", "dve": "vector", "pool": "gpsimd", "sp": "sync"}
        for q in ["sp", "pe", "act", "dve", "pool"]:
            if q in byq:
                getattr(block, names[q])(lambda eng, ops=byq[q]: run(eng, ops))


class Ring:
    def __init__(self, items):
        self.items = items
        self.i = 0

    def next(self):
        it = self.items[self.i % len(self.items)]
        self.i += 1
        return it


NL = 4


def weight_blocks():
    b = []
    for i in range(2):
        b.append([("w_in", 8, 1024 + 512 * i, 512, 0, 512, 0)])
    for c0 in (0, 512, 2048, 2560):
        b.append([("w_in", 8, c0, 512, 0, 512, 0)])
    b.append(None)
    for i in range(2):
        b.append([("w_out", 8, 512 * i, 512, 0, 512, 0)])
    for i in range(11):
        b.append([("w_up", 8, 256 * i, 256, 0, 512, 0), ("w_up", 8, DFF + 256 * i, 256, 0, 512, 256)])
    for f in range(8):
        b.append([("w_down", 22, 128 * f, 128, 0, 128, 0)])
    for j in range(4):
        b.append([("w_gate", 8, 256 * j, 256, 0, 256, 0), ("w_ple", 2, 256 * j, 256, 2048, 256, 0)])
    return b


WB = weight_blocks()
NWB = len(WB)
B_QK, B_U, B_V, B_VM, B_O, B_WS, B_OUT, B_UP, B_DOWN, B_GP = 0, 2, 3, 4, 5, 6, 7, 9, 20, 28
WB_SIZE = []
for _b in WB:
    if _b is None:
        WB_SIZE.append(1024)
    else:
        WB_SIZE.append(max(base + K * W for (_, K, _, _, base, W, _) in _b))


def MM(out, lhsT, rhs, start, stop):
    return lambda e: e.matmul(out, lhsT=lhsT, rhs=rhs, start=start, stop=stop)


def TR(out, in_, ident):
    return lambda e: e.transpose(out, in_, ident)


def ACT(out, in_, func, **kw):
    return lambda e: e.activation(out=out, in_=in_, func=func, **kw)


def TT(out, in0, in1, op):
    return lambda e: e.tensor_tensor(out=out, in0=in0, in1=in1, op=op)


def TS(out, in0, s1, s2, op0, op1=None):
    if op1 is None:
        return lambda e: e.tensor_scalar(out=out, in0=in0, scalar1=s1, scalar2=None, op0=op0)
    return lambda e: e.tensor_scalar(out=out, in0=in0, scalar1=s1, scalar2=s2, op0=op0, op1=op1)


def TSS(out, in_, scalar, op):
    return lambda e: e.tensor_single_scalar(out=out, in_=in_, scalar=scalar, op=op)


def STT(out, in0, scalar, in1, op0, op1):
    return lambda e: e.scalar_tensor_tensor(out=out, in0=in0, scalar=scalar, in1=in1, op0=op0, op1=op1)


def CP(out, in_):
    return lambda e: e.tensor_copy(out=out, in_=in_)


def MS(out, val):
    return lambda e: e.memset(out, val)


def DMA(out, in_):
    return lambda e: e.dma_start(out=out, in_=in_)


def pipeline(n, stages):
    ns = len(stages)
    for i in range(n + ns - 1):
        for si in reversed(range(ns)):
            c = i - si
            if 0 <= c < n:
                stages[si](c)


def build_program(S, n_layers=NL):
    n_tiles = S // NT
    nc = bass.Bass("TRN2", target_bir_lowering=False)

    def din(name, shape):
        return nc.dram_tensor(name, list(shape), F32, kind="ExternalInput").ap()

    xin = din("xin", [D, S])
    pin = din("pin", [NL, 256, S])
    wsrc = {
        "w_in": din("w_in", [NL, D, 3080]), "w_out": din("w_out", [NL, D, D]), "w_up": din("w_up", [NL, D, 2 * DFF]),
        "w_down": din("w_down", [NL, DFF, D]), "w_gate": din("w_gate", [NL, D, D]), "w_ple": din("w_ple", [NL, 256, D]),
    }
    gv_d = din("gv", [128, NL * 24 + 8])
    mlng_d = din("mlng", [128, NL * 4])
    lngb_d = din("lngb", [NL, 128, 2, 512])
    wsT_d = din("wsT", [NL, 128, 8, 128])
    bs8_d = din("bs8", [8, NL, 128])
    cwqk_d = din("cwqk", [128, NL, 8, 4])
    cbqk_d = din("cbqk", [128, NL, 8])
    cwff_d = din("cwff", [128, NL, 44, 3])
    cbff_d = din("cbff", [128, NL, 44])
    bif_d = din("bif", [128, NL, 8])
    ident_d = din("ident", [128, 128])
    tri_d = din("tri", [128, 128])
    sel_d = din("sel", [128, 512])
    yout = nc.dram_tensor("yout", [D, S], F32, kind="ExternalOutput").ap()
    wb_d = nc.dram_tensor("wb_scratch", [NL, NWB, 128, WBLK], BF16).ap()

    xin_v = xin.rearrange("(f p) s -> p f s", p=128)
    yout_v = yout.rearrange("(f p) s -> p f s", p=128)

    with ExitStack() as es:
        def sb(name, shape, dt=F32):
            return es.enter_context(nc.sbuf_tensor("sb_" + name, list(shape), dt))

        def psum(name, shape, dt=F32):
            return es.enter_context(nc.psum_tensor(name, list(shape), dt))

        def sem(name):
            return es.enter_context(nc.semaphore(name))

        qsems = {q: sem("q_" + q) for q in ["pe", "act", "dve", "pool"]}
        S_ = Sched()
        A = S_.add

        xT = sb("xT", [128, 8, NT])
        hT = sb("hT", [128, 8, NT], BF16)
        sqr = [sb(f"sq{i}", [128, NT], BF16) for i in range(2)]
        rstd = sb("rstd", [128, NT])
        ntmp = [sb(f"ntmp{i}", [128, NT]) for i in range(2)]
        qkraw = sb("qkraw", [128, 8, NT + 4])
        qkT = sb("qkT", [128, 8, NT], BF16)
        cacc = [sb(f"cacc{i}", [128, NT]) for i in range(4)]
        gpre = sb("gpre", [128, NCH, 8])
        e1 = sb("e1", [128, NCH, 4])
        l1 = sb("l1", [128, NCH, 4])
        t1 = sb("t1", [128, NCH, 4])
        sc = sb("sc", [128, NCH, 4])
        einv = sb("einv", [128, NCH, 4])
        ebL = sb("ebL", [128, NCH, 4])
        ebLs = sb("ebLs", [128, NL, 4])
        ug = sb("ug", [128, NCH, 512], BF16)
        vg = sb("vg", [128, NCH, 512], BF16)
        vt = [sb(f"vt{i}", [128, 512]) for i in range(2)]
        vn = sb("vn", [128, NCH, 512], BF16)
        st6 = sb("st6", [128, NCH, 6])
        mv = sb("mv", [128, NCH, 2])
        lvar = sb("lvar", [128, NCH])
        lrstd = sb("lrstd", [128, NCH])
        vaug = sb("vaug", [128, NCH, 4, 132], BF16)
        so = sb("so", [128, NCH, 512], BF16)
        gm = [sb(f"gm{i}", [128, 512], BF16) for i in range(2)]
        Amat = [sb(f"Amat{i}", [128, 4, 128], BF16) for i in range(2)]
        Kp = [sb(f"Kp{i}", [128, 4, 128], BF16) for i in range(2)]
        dtmp = [sb(f"dtmp{i}", [128, 4]) for i in range(2)]
        hc = [sb(f"hc{i}", [128, 512]) for i in range(2)]
        junk = sb("junk", [128, 128])
        ss = [sb(f"ss{i}", [128, 4]) for i in range(2)]
        lss = [sb(f"lss{i}", [128, 4]) for i in range(2)]
        hrs = [sb(f"hrs{i}", [128, 4]) for i in range(2)]
        ml = [sb(f"ml{i}", [128, 512], BF16) for i in range(2)]
        Dst = sb("Dst", [128, NL, 4, 132])
        Cbf = sb("Cbf", [128, NL, 4, 132], BF16)
        raw = [sb(f"raw{i}", [128, NT + 2]) for i in range(4)]
        facc = [sb(f"facc{i}", [128, NT]) for i in range(3)]
        sa = [sb(f"sa{i}", [128, NT]) for i in range(2)]
        actT = sb("actT", [128, 22, NT], BF16)
        halo = sb("halo", [128, NL, 44, 2])
        qhalo = sb("qhalo", [128, NL, 8, 4])
        pT = sb("pT", [128, 2, NT], BF16)
        gt = [sb(f"gt{i}", [128, NT]) for i in range(2)]
        wring = sb("wring", [128, NSLOT, WBLK], BF16)
        lngb = sb("lngb", [128, 2, 512])
        ident = sb("ident", [128, 128], BF16)
        tri = sb("tri", [128, 128])
        trib = sb("trib", [128, 128], BF16)
        trib4 = sb("trib4", [128, 4, 128], BF16)
        onesf = sb("onesf", [128, 128])
        onesm = sb("onesm", [128, 128], BF16)
        sel = sb("sel", [128, 512], BF16)
        bs128 = sb("bs128", [128, NL, 128], BF16)
        gv = sb("gv", [128, NL * 24 + 8])
        mlng = sb("mlng", [128, NL * 4])
        cwqk = sb("cwqk", [128, NL, 8, 4])
        cbqk = sb("cbqk", [128, NL, 8])
        cwff = sb("cwff", [128, NL, 44, 3])
        cbff = sb("cbff", [128, NL, 44])
        bif = sb("bif", [128, NL, 8])
        epsc = sb("epsc", [128, 1])
        wg = sb("wg", [128, NL, 8, 8], BF16)

        PBt = [psum(f"pb{i}", [128, 512]) for i in range(4)]
        PTt = [psum(f"pt{i}", [128, 1024], BF16) for i in range(2)]
        PSt = [psum(f"ps{i}", [128, 512]) for i in range(2)]
        PB = Ring([(PBt[i], ("pb", i)) for i in range(4)])
        PT = Ring([(PTt[i], ("pt", i)) for i in range(2)])
        PS = Ring([(PSt[i], ("ps", i)) for i in range(2)])

        s_setup = sem("d_setup")
        s_w = [sem(f"d_w{i}") for i in range(NSLOT)]
        s_xf = [sem(f"d_x{f}") for f in range(8)]
        s_p = sem("d_p")
        s_ln = sem("d_ln")
        s_yo = [sem(f"d_yo{i}") for i in range(2)]
        s_prep = [sem(f"d_prep{i}") for i in range(4)]
        s_ws = sem("d_ws")
        s_wg = sem("d_wg")
        s_ws2 = sem("d_ws2")

        XK = [("x", f) for f in range(8)]
        HT = [("hT", f) for f in range(8)]

        S_.tag = "setup"

        def load_const(dst_ap, src_ap, keys):
            A("sp", DMA(dst_ap, src_ap), writes=keys, dma_sem=s_setup)

        load_const(gv[:], gv_d, ["gv"])
        load_const(mlng[:], mlng_d, ["mlng"])
        load_const(cwqk[:], cwqk_d, ["cwqk"])
        load_const(cbqk[:], cbqk_d, ["cbqk"])
        load_const(cwff[:], cwff_d, ["cwff"])
        load_const(cbff[:], cbff_d, ["cbff"])
        load_const(bif[:], bif_d, ["bif"])
        load_const(tri[:], tri_d, ["tri"])
        load_const(xT[:, 0, :], sel_d, [("x", 0)])
        load_const(xT[:, 1, 0:128], ident_d, [("x", 1)])
        load_const(xT[0:8, 2, :].rearrange("p (l t) -> p l t", l=NL), bs8_d, [("x", 2)])
        setup_keys = ["gv", "mlng", "cwqk", "cbqk", "cwff", "cbff", "bif", "tri"] + XK[0:3]
        A("pool", MS(epsc[:], EPS), reads=setup_keys, writes=["epsc", "setup_done"])
        SD = ["setup_done"]
        A("pool", CP(sel[:], xT[:, 0, :]), reads=SD + [("x", 0)], writes=["sel"])
        A("pool", CP(ident[:], xT[:, 1, 0:128]), reads=SD + [("x", 1)], writes=["ident"])
        A("pool", CP(trib[:], tri[:]), reads=SD, writes=["trib"])
        for h in range(4):
            A("pool", CP(trib4[:, h, :], tri[:]), reads=SD, writes=["trib4"])
        A("pool", MS(bs128[:], 0.0), reads=SD, writes=["bs128"])
        A("pool", CP(bs128[0:8, :, :], xT[0:8, 2, :].rearrange("p (l t) -> p l t", l=NL)), reads=SD + ["bs128", ("x", 2)], writes=["bs128"])
        A("dve", MS(onesf[:], 1.0), writes=["onesf"])
        A("dve", MS(onesm[:], 1.0 / D), writes=["onesm"])
        A("dve", MS(qhalo[:], 0.0), writes=["qhalo"])
        A("dve", MS(halo[:], 0.0), writes=["halo"])
        A("dve", MS(Dst[:], 0.0), writes=[("Dst", l, h) for l in range(NL) for h in range(4)])
        A("dve", MS(Cbf[:], 0.0), writes=[("Cbf", l, h) for l in range(NL) for h in range(4)])
        A("dve", MS(ebLs[:], 1.0), writes=["ebLs"])
        A("pool", MS(vaug[:], 1.0), writes=[("vaug", c) for c in range(NCH)])
        for l in range(n_layers):
            A("pool", DMA(wg[:, l, :, :], wsrc["w_in"][l].rearrange("(k p) n -> p k n", p=128)[:, :, 3072:3080]),
              writes=[("wg", l), "wgslot"], dma_sem=s_wg)
        xflat = xT[:, 4:6, :].rearrange("p a b -> p (a b)")
        hflat = hT[:, 0:2, :].rearrange("p a b -> p (a b)")
        for l in range(n_layers):
            A("sp", DMA(xflat.rearrange("p (h t) -> p h t", h=8), wsT_d[l]), writes=[("x", 4), ("x", 5)], dma_sem=s_ws)
            for h in range(8):
                A("dve", TT(hflat[:, h * 128:(h + 1) * 128], xflat[:, h * 128:(h + 1) * 128], tri[:], ALU.mult),
                  reads=SD + [("x", 4), ("x", 5)], writes=[("hT", 0), ("hT", 1)])
            A("sp", DMA(wb_d[l, B_WS, :, 0:1024], hflat), reads=[("hT", 0), ("hT", 1)], writes=[("wbd", l, B_WS, 0)], dma_sem=s_ws2)

        np_ = [0]
        wbd_keys = {}
        prep_queue = {l: [] for l in range(n_layers)}
        for l in range(n_layers):
            for bi, parts in enumerate(WB):
                if parts is None:
                    wbd_keys[(l, bi)] = [("wbd", l, bi, 0)]
                    continue
                keys = []
                for pi, (src, K, c0, n, base, W, off) in enumerate(parts):
                    srcv = wsrc[src][l].rearrange("(k p) n -> p k n", p=128)[:, :, c0:c0 + n]
                    dstv = wb_d[l, bi, :, base:base + K * W].rearrange("p (k n) -> p k n", k=K)[:, :, off:off + n]
                    key = ("wbd", l, bi, pi)
                    prep_queue[l].append((dstv, srcv, key))
                    keys.append(key)
                wbd_keys[(l, bi)] = keys

        def emit_prep(l, n):
            for _ in range(n):
                if l >= n_layers or not prep_queue[l]:
                    return
                dstv, srcv, key = prep_queue[l].pop(0)
                sl = np_[0] % 4
                np_[0] += 1
                A("pool", DMA(dstv, srcv), writes=[key, ("prepslot", sl)], dma_sem=s_prep[sl])

        emit_prep(0, 10 ** 6)

        gblk = [0]
        total_blocks = n_tiles * n_layers * NWB

        def issue_wload(g):
            if g >= total_blocks:
                return
            bi = g % NWB
            l = (g // NWB) % n_layers
            sl = g % NSLOT
            n = WB_SIZE[bi]
            A("sp", DMA(wring[:, sl, 0:n], wb_d[l, bi, :, 0:n]), reads=wbd_keys[(l, bi)], writes=[("wr", sl)], dma_sem=s_w[sl])

        def next_block(expect_bi):
            g = gblk[0]
            assert g % NWB == expect_bi, (g % NWB, expect_bi)
            issue_wload(g + NSLOT - 1)
            gblk[0] += 1
            sl = g % NSLOT
            return wring[:, sl, :], ("wr", sl)

        def norm_begin():
            return PS.next()

        def norm_accum(nst, f):
            ps, pk = nst
            i = f % 2
            A("act", ACT(sqr[i][:], xT[:, f, :], AF.Square), reads=[("x", f)], writes=[("sq", i)])
            A("pe", MM(ps[:, 0:NT], onesm[:], sqr[i][:], f == 0, f == 7), reads=[("sq", i), "onesm"], writes=[pk])

        def norm_finish(nst, gcol, final_t0=None):
            ps, pk = nst
            A("act", ACT(rstd[:], ps[:, 0:NT], AF.Ln, bias=epsc[:, 0:1]), reads=[pk, "epsc"], writes=["rstd"])
            A("act", ACT(rstd[:], rstd[:], AF.Exp, scale=-0.5), reads=["rstd"], writes=["rstd"])
            for f in range(8):
                if final_t0 is not None:
                    i = f % 2
                    A("dve", STT(ntmp[i][:], xT[:, f, :], gv[:, gcol + f:gcol + f + 1], rstd[:], ALU.mult, ALU.mult),
                      reads=[("x", f), "rstd"] + SD, writes=[("ntmp", i)])
                    A("sp", DMA(yout_v[:, f, final_t0:final_t0 + NT], ntmp[i][:]), reads=[("ntmp", i)], writes=["yout"], dma_sem=s_yo[i])
                else:
                    A("dve", STT(hT[:, f, :], xT[:, f, :], gv[:, gcol + f:gcol + f + 1], rstd[:], ALU.mult, ALU.mult),
                      reads=[("x", f), "rstd"] + SD, writes=[("hT", f)])

        def norm(gcol, final_t0=None):
            nst = norm_begin()
            for f in range(8):
                norm_accum(nst, f)
            norm_finish(nst, gcol, final_t0)

        for g in range(NSLOT - 1):
            issue_wload(g)

        def layer_step(t, l, nst_in):
            t0 = t * NT
            tg = f"L{l}"
            S_.tag = tg + "norm1"
            A("pool", DMA(pT[:], pin[l].rearrange("(k p) s -> p k s", p=128)[:, :, t0:t0 + NT]), writes=["pT"], dma_sem=s_p)
            A("sp", DMA(lngb[:], lngb_d[l]), writes=["lngb"], dma_sem=s_ln)

            if nst_in is None:
                norm(l * 24)
            else:
                norm_finish(nst_in, l * 24)
            S_.tag = tg + "gates"
            psg, pgk = PS.next()
            for c in range(NCH):
                for k in range(8):
                    A("pe", MM(psg[:, c * 8:(c + 1) * 8], hT[:, k, c * 128:(c + 1) * 128], wg[:, l, k, :], k == 0, k == 7),
                      reads=[("hT", k), ("wg", l)], writes=[pgk])
            for c in range(NCH):
                A("dve", TT(gpre[:, c, :], psg[:, c * 8:(c + 1) * 8], bif[:, l, :], ALU.add), reads=[pgk] + SD, writes=["gpre"])
            A("act", ACT(e1[:], gpre[:, :, 4:8], AF.Exp, scale=-1.0), reads=["gpre"], writes=["e1"])
            A("act", ACT(l1[:], e1[:], AF.Ln, bias=1.0), reads=["e1"], writes=["l1"])
            def gates_tail():
                S_.tag = tg + "gates"
                psb, pbk = PS.next()
                l1v = l1[:].rearrange("p c h -> p (c h)")
                A("pe", MM(psb[:, 0:16], tri[:], l1v, True, True), reads=["l1"] + SD, writes=[pbk])
                A("pe", MM(psb[:, 16:32], onesf[:], l1v, True, True), reads=["l1", "onesf"], writes=[pbk])
                bneg = psb[:, 0:16].rearrange("p (c h) -> p c h", c=NCH)
                bLneg = psb[:, 16:32].rearrange("p (c h) -> p c h", c=NCH)
                A("dve", STT(t1[:], gpre[:, :, 0:4], math.log(128.0 ** -0.5), bneg, ALU.add, ALU.add), reads=["gpre", pbk], writes=["t1"])
                A("act", ACT(sc[:], t1[:], AF.Exp), reads=["t1"], writes=["sc"])
                A("act", ACT(einv[:], bneg, AF.Exp), reads=[pbk, "t1"], writes=["einv"])
                A("act", ACT(ebL[:], bLneg, AF.Exp, scale=-1.0), reads=[pbk, "t1"], writes=["ebL"])
                S_.tag = tg + "qk"

            S_.tag = tg + "qk"
            Wq = {}
            qpb = {}

            def qk_s0(f):
                wbi, j = divmod(f, 4)
                if j == 0:
                    W, wk = next_block(B_QK + wbi)
                    Wq[wbi] = (W[:, 0:4096].rearrange("p (k n) -> p k n", k=8), wk)
                Wv, wk = Wq[wbi]
                ps, pk = PB.next()
                qpb[f] = (ps, pk)
                for k in range(8):
                    A("pe", MM(ps[:, 0:NT], Wv[:, k, j * 128:(j + 1) * 128], hT[:, k, :], k == 0, k == 7), reads=[wk, ("hT", k)], writes=[pk])
                qk_ = ("qkraw", f)
                A("pool", CP(qkraw[:, f, 0:3], qhalo[:, l, f, 0:3]), reads=["qhalo"], writes=[qk_])
                A("act", ACT(qkraw[:, f, 3:3 + NT], ps[:, 0:NT], AF.Copy), reads=[pk, qk_], writes=[qk_])
                A("pool", CP(qhalo[:, l, f, 0:3], qkraw[:, f, NT:NT + 3]), reads=[qk_], writes=["qhalo"])

            def qk_s1(f):
                A("act", ACT(cacc[f % 4][:], qkraw[:, f, 0:NT], AF.Identity, scale=cwqk[:, l, f, 0:1], bias=cbqk[:, l, f:f + 1]),
                  reads=[("qkraw", f)] + SD, writes=[("cacc", f % 4)])

            def qk_s2(f):
                ps, pk = qpb.pop(f)
                ca = cacc[f % 4]
                ck = ("cacc", f % 4)
                A("dve", STT(ca[:], ps[:, 0:NT], cwqk[:, l, f, 3:4], ca[:], ALU.mult, ALU.add), reads=[pk, ck] + SD, writes=[ck])

            def qk_tap(tap):
                def fn(f):
                    ca = cacc[f % 4]
                    ck = ("cacc", f % 4)
                    A("dve", STT(ca[:], qkraw[:, f, tap:tap + NT], cwqk[:, l, f, tap:tap + 1], ca[:], ALU.mult, ALU.add), reads=[("qkraw", f), ck] + SD, writes=[ck])
                return fn

            def qk_s5(f):
                A("act", ACT(qkT[:, f, :], cacc[f % 4][:], AF.Silu), reads=[("cacc", f % 4)], writes=[("qkT", f)])

            gates_tail_done = [False]

            def qk_s0_wrap(f):
                qk_s0(f)
                if f == 1 and not gates_tail_done[0]:
                    gates_tail()
                    gates_tail_done[0] = True

            pipeline(8, [qk_s0_wrap, qk_s1, qk_s2, qk_tap(1), qk_tap(2), qk_s5])

            S_.tag = tg + "uvo"
            for which, bidx in (("u", B_U), ("v", B_V), ("vm", B_VM), ("o", B_O)):
                W, wk = next_block(bidx)
                Wv = W[:, 0:4096].rearrange("p (k n) -> p k n", k=8)
                for c in range(NCH):
                    ps, pk = PB.next()
                    for k in range(8):
                        A("pe", MM(ps[:, 0:512], hT[:, k, c * 128:(c + 1) * 128], Wv[:, k, :], k == 0, k == 7), reads=[wk, ("hT", k)], writes=[pk])
                    if which == "u":
                        A("act", ACT(ug[:, c, :], ps[:, 0:512], AF.Gelu), reads=[pk], writes=[("ug", c)])
                    elif which == "v":
                        A("act", ACT(vg[:, c, :], ps[:, 0:512], AF.Gelu), reads=[pk], writes=[("vg", c)])
                        A("dve", lambda e, c=c: e.bn_stats(out=st6[:, c, :], in_=vg[:, c, :]), reads=[("vg", c)], writes=[("st6", c)])
                        A("dve", lambda e, c=c: e.bn_aggr(out=mv[:, c, :], in_=st6[:, c, :]), reads=[("st6", c)], writes=[("mv", c)])
                    elif which == "vm":
                        for h in range(4):
                            A("act", ACT(vaug[:, c, h, 0:128], ps[:, h * 128:(h + 1) * 128], AF.Copy, scale=sc[:, c, h:h + 1]), reads=[pk, ("vaug", c), "sc"], writes=[("vaug", c)])
                        A("pool", CP(vaug[:, c, :, 128], sc[:, c, :]), reads=["sc", ("vaug", c)], writes=[("vaug", c)])
                    else:
                        A("act", ACT(so[:, c, :], ps[:, 0:512], AF.Sigmoid), reads=[pk], writes=[("so", c)])
                if which == "v":
                    A("act", ACT(lvar[:], mv[:, :, 1], AF.Ln, bias=epsc[:, 0:1]), reads=[("mv", c) for c in range(NCH)] + ["epsc"], writes=["lvar"])
                    A("act", ACT(lrstd[:], lvar[:], AF.Exp, scale=-0.5), reads=["lvar"], writes=["lrstd"])
                    for c in range(NCH):
                        v_ = vt[c % 2]
                        vk = ("vt", c % 2)
                        A("dve", TS(v_[:], vg[:, c, :], mv[:, c, 0:1], lrstd[:, c:c + 1], ALU.subtract, ALU.mult),
                          reads=[("vg", c), ("mv", c), "lrstd"], writes=[vk])
                        A("pool", TT(v_[:], v_[:], lngb[:, 0, :], ALU.mult), reads=[vk, "lngb"], writes=[vk])
                        A("pool", TT(vn[:, c, :], v_[:], lngb[:, 1, :], ALU.add), reads=[vk, "lngb"], writes=[("vn", c)])

            Wws, wsk = next_block(B_WS)
            wsv = Wws[:, 0:1024].rearrange("p (h t) -> p h t", h=8)
            stg = {}

            def mix_P1(c):
                S_.tag = tg + "mixP1"
                cs = slice(c * 128, (c + 1) * 128)
                gi = c % 2
                psm, pmk = PB.next()
                A("pe", MM(psm[:, 0:512], bs128[:, l, :], sel[:], True, False), reads=["bs128", "sel"], writes=[pmk])
                for h in range(8):
                    A("pe", MM(psm[:, h * 64:(h + 1) * 64], wsv[:, h, :], vn[:, c, h * 64:(h + 1) * 64], False, h == 7), reads=[wsk, ("vn", c)], writes=[pmk])
                st, stk = PS.next()
                for h in range(4):
                    A("pe", MM(st[:, h * 128:(h + 1) * 128], qkT[:, 4 + h, cs], qkT[:, h, cs], True, True), reads=[("qkT", 4 + h), ("qkT", h)], writes=[stk])
                kt, ktk = PT.next()
                for h in range(4):
                    A("pe", TR(kt[:, h * 128:(h + 1) * 128], qkT[:, 4 + h, cs], ident[:]), reads=[("qkT", 4 + h), "ident"], writes=[ktk])
                A("dve", TT(gm[gi][:], psm[:, 0:512], ug[:, c, :], ALU.mult), reads=[pmk, ("ug", c)], writes=[("gm", gi)])
                A("dve", TT(Amat[gi][:], st[:, 0:512].rearrange("p (h t) -> p h t", h=4), trib4[:], ALU.mult),
                  reads=[stk, "trib4"], writes=[("Amat", gi, h) for h in range(4)])
                A("dve", CP(Kp[gi][:], kt[:, 0:512].rearrange("p (h t) -> p h t", h=4)), reads=[ktk], writes=[("Kp", gi, h) for h in range(4)])

            def mix_P2(c):
                S_.tag = tg + "mixP2"
                cs = slice(c * 128, (c + 1) * 128)
                gi = c % 2
                for pr in range(2):
                    nd, ndk = PB.next()
                    up, upk = PB.next()
                    for j in range(2):
                        h = 2 * pr + j
                        A("pe", MM(nd[:, j * 129:(j + 1) * 129], Amat[gi][:, h, :], vaug[:, c, h, 0:129], True, False), reads=[("Amat", gi, h), ("vaug", c)], writes=[ndk])
                        A("pe", MM(nd[:, j * 129:(j + 1) * 129], qkT[:, h, cs], Cbf[:, l, h, 0:129], False, True), reads=[("qkT", h), ("Cbf", l, h)], writes=[ndk])
                    for j in range(2):
                        h = 2 * pr + j
                        A("pe", MM(up[:, j * 129:(j + 1) * 129], Kp[gi][:, h, :], vaug[:, c, h, 0:129], True, True), reads=[("Kp", gi, h), ("vaug", c)], writes=[upk])
                    for j in range(2):
                        h = 2 * pr + j
                        prev = ebL[:, c - 1, h:h + 1] if c > 0 else ebLs[:, l, h:h + 1]
                        A("dve", STT(Dst[:, l, h, 0:129], Dst[:, l, h, 0:129], prev, up[:, j * 129:(j + 1) * 129], ALU.mult, ALU.add),
                          reads=[upk, ("Dst", l, h), "ebL", "ebLs"], writes=[("Dst", l, h)])
                    d_ = dtmp[gi]
                    dk = ("dtmp", gi, pr)
                    dsl = d_[:, 2 * pr:2 * pr + 2]
                    A("act", ACT(dsl, nd[:, 0:258].rearrange("p (j d) -> p j d", j=2)[:, :, 128], AF.Abs), reads=[ndk], writes=[dk])
                    A("dve", TT(dsl, dsl, einv[:, c, 2 * pr:2 * pr + 2], ALU.max), reads=[dk, "einv"], writes=[dk])
                    A("dve", lambda e, dsl=dsl: e.reciprocal(out=dsl, in_=dsl), reads=[dk], writes=[dk])
                    for j in range(2):
                        h = 2 * pr + j
                        A("dve", STT(hc[gi][:, h * 128:(h + 1) * 128], nd[:, j * 129:j * 129 + 128], d_[:, h:h + 1], so[:, c, h * 128:(h + 1) * 128], ALU.mult, ALU.mult),
                          reads=[ndk, dk, ("so", c)], writes=[("hc", gi, h)])
                A("pool", TT(Cbf[:, l, :, 0:129], Dst[:, l, :, 0:129], ebL[:, c, :].unsqueeze(2).broadcast_to([128, 4, 129]), ALU.mult),
                  reads=[("Dst", l, h) for h in range(4)] + ["ebL"], writes=[("Cbf", l, h) for h in range(4)])
                for h in range(4):
                    A("dve", lambda e, h=h, gi=gi: e.scalar_tensor_tensor(out=junk[:], in0=hc[gi][:, h * 128:(h + 1) * 128], scalar=1.0, in1=hc[gi][:, h * 128:(h + 1) * 128],
                                                                          op0=ALU.mult, op1=ALU.mult, accum_out=ss[gi][:, h:h + 1]),
                      reads=[("hc", gi, h)], writes=["junk", ("ss", gi)])
                A("act", ACT(lss[gi][:], ss[gi][:], AF.Ln, scale=1.0 / 128.0, bias=epsc[:, 0:1]), reads=[("ss", gi), "epsc"], writes=[("lss", gi)])
                A("act", ACT(hrs[gi][:], lss[gi][:], AF.Exp, scale=-0.5), reads=[("lss", gi)], writes=[("hrs", gi)])
                A("pool", TT(ml[gi][:].rearrange("p (h d) -> p h d", h=4), hc[gi][:].rearrange("p (h d) -> p h d", h=4),
                             hrs[gi][:].unsqueeze(2).broadcast_to([128, 4, 128]), ALU.mult),
                  reads=[("hc", gi, h) for h in range(4)] + [("hrs", gi)], writes=[("ml", gi)])

                pt, ptk = PT.next()
                for j in range(4):
                    A("pe", TR(pt[:, j * 128:(j + 1) * 128], gm[gi][:, j * 128:(j + 1) * 128], ident[:]), reads=[("gm", gi), "ident"], writes=[ptk])
                A("act", ACT(hT[:, 0:4, cs], pt[:, 0:512].rearrange("p (a b) -> p a b", a=4), AF.Copy), reads=[ptk], writes=HT[0:4])

            def mix_P3(c):
                S_.tag = tg + "mixP3"
                cs = slice(c * 128, (c + 1) * 128)
                gi = c % 2
                pt, ptk = PT.next()
                for h in range(4):
                    A("pe", TR(pt[:, h * 128:(h + 1) * 128], ml[gi][:, h * 128:(h + 1) * 128], ident[:]), reads=[("ml", gi), "ident"], writes=[ptk])
                for h in range(4):
                    A("act", ACT(hT[:, 4 + h, cs], pt[:, h * 128:(h + 1) * 128], AF.Copy, scale=mlng[:, l * 4 + h:l * 4 + h + 1]), reads=[ptk] + SD, writes=[("hT", 4 + h)])

            mix_P1(0)
            for c in range(NCH):
                if c + 1 < NCH:
                    mix_P1(c + 1)
                mix_P2(c)
                if c >= 1:
                    mix_P3(c - 1)
            mix_P3(NCH - 1)
            A("dve", CP(ebLs[:, l, :], ebL[:, NCH - 1, :]), reads=["ebL", "ebLs"], writes=["ebLs"])

            S_.tag = tg + "wout"
            Wo = {}
            nst2 = norm_begin()

            def wout_s0(f):
                wbi, j = divmod(f, 4)
                if j == 0:
                    W, wk = next_block(B_OUT + wbi)
                    Wo[wbi] = (W[:, 0:4096].rearrange("p (k n) -> p k n", k=8), wk)
                Wv, wk = Wo[wbi]
                ps, pk = PB.next()
                for k in range(8):
                    A("pe", MM(ps[:, 0:NT], Wv[:, k, j * 128:(j + 1) * 128], hT[:, k, :], k == 0, k == 7), reads=[wk, ("hT", k)], writes=[pk])
                A("dve", TT(xT[:, f, :], ps[:, 0:NT], xT[:, f, :], ALU.add), reads=[pk, ("x", f)], writes=[("x", f)])

            pipeline(8, [wout_s0, lambda f: None, lambda f: norm_accum(nst2, f)])

            S_.tag = tg + "norm2"
            norm_finish(nst2, l * 24 + 8)
            S_.tag = tg + "wup"
            Wu = {}
            pbk_ = {}

            def up_pre(fi):
                A("pool", CP(raw[fi % 4][:, 0:2], halo[:, l, fi, :]), reads=["halo"], writes=[("raw", fi % 4)])

            def up_s0(fi):
                i, jj = divmod(fi, 4)
                if jj == 0:
                    W, wk = next_block(B_UP + i)
                    Wu[i] = (W[:, 0:4096].rearrange("p (k n) -> p k n", k=8), wk)
                Wv, wk = Wu[i]
                ps, pk = PB.next()
                pbk_[fi] = (ps, pk)
                for k in range(8):
                    A("pe", MM(ps[:, 0:NT], Wv[:, k, jj * 128:(jj + 1) * 128], hT[:, k, :], k == 0, k == 7), reads=[wk, ("hT", k)], writes=[pk])
                r_ = raw[fi % 4]
                rk_ = ("raw", fi % 4)
                A("act", ACT(r_[:, 2:2 + NT], ps[:, 0:NT], AF.Copy), reads=[pk, rk_], writes=[rk_])
                if t == 0:
                    emit_prep(l + 1, 2)

            def up_s1(fi):
                A("act", ACT(facc[fi % 3][:], raw[fi % 4][:, 0:NT], AF.Identity, scale=cwff[:, l, fi, 0:1], bias=cbff[:, l, fi:fi + 1]),
                  reads=[("raw", fi % 4)] + SD, writes=[("facc", fi % 3)])
                A("pool", CP(halo[:, l, fi, :], raw[fi % 4][:, NT:NT + 2]), reads=[("raw", fi % 4)], writes=["halo"])

            def up_s2(fi):
                ps, pk = pbk_.pop(fi)
                fa = facc[fi % 3]
                fk = ("facc", fi % 3)
                A("dve", STT(fa[:], ps[:, 0:NT], cwff[:, l, fi, 2:3], fa[:], ALU.mult, ALU.add), reads=[pk, fk] + SD, writes=[fk])

            def up_s3(fi):
                fa = facc[fi % 3]
                fk = ("facc", fi % 3)
                A("dve", STT(fa[:], raw[fi % 4][:, 1:1 + NT], cwff[:, l, fi, 1:2], fa[:], ALU.mult, ALU.add), reads=[("raw", fi % 4), fk] + SD, writes=[fk])

            def up_s4(fi):
                i, jj = divmod(fi, 4)
                fa = facc[fi % 3]
                fk = ("facc", fi % 3)
                if jj < 2:
                    A("act", ACT(sa[jj][:], fa[:], AF.Silu), reads=[fk], writes=[("sa", jj)])
                else:
                    A("pool", TT(actT[:, 2 * i + jj - 2, :], sa[jj - 2][:], fa[:], ALU.mult), reads=[("sa", jj - 2), fk], writes=[("actT", 2 * i + jj - 2)])

            pipeline(44, [up_pre, up_s0, up_s1, up_s2, up_s3, up_s4])
            if t == 0:
                emit_prep(l + 1, 10 ** 6)

            S_.tag = tg + "wdown"
            nst3 = norm_begin()

            def wdown_s0(f):
                W, wk = next_block(B_DOWN + f)
                Wv = W[:, 0:22 * 128].rearrange("p (k n) -> p k n", k=22)
                ps, pk = PB.next()
                for k in range(22):
                    A("pe", MM(ps[:, 0:NT], Wv[:, k, :], actT[:, k, :], k == 0, k == 21), reads=[wk, ("actT", k)], writes=[pk])
                A("dve", TT(xT[:, f, :], ps[:, 0:NT], xT[:, f, :], ALU.add), reads=[pk, ("x", f)], writes=[("x", f)])

            pipeline(8, [wdown_s0, lambda f: None, lambda f: norm_accum(nst3, f)])

            S_.tag = tg + "norm3"
            norm_finish(nst3, l * 24 + 16)
            S_.tag = tg + "ple"
            Wgp = {}
            nst_next = norm_begin()

            for it in range(8 + 3):
                f = it
                fa_ = it - 3
                if 0 <= fa_ < 8:
                    A("act", ACT(sqr[fa_ % 2][:], xT[:, fa_, :], AF.Square), reads=[("x", fa_)], writes=[("sq", fa_ % 2)])
                if f < 8:
                    jb, jj = divmod(f, 2)
                    if jj == 0:
                        W, wk = next_block(B_GP + jb)
                        Wgp[jb] = (W[:, 0:2048].rearrange("p (k n) -> p k n", k=8), W[:, 2048:2560].rearrange("p (k n) -> p k n", k=2), wk)
                    Wg, Wp, wk = Wgp[jb]
                    psg2, pgk2 = PB.next()
                    for k in range(8):
                        A("pe", MM(psg2[:, 0:NT], Wg[:, k, jj * 128:(jj + 1) * 128], hT[:, k, :], k == 0, k == 7), reads=[wk, ("hT", k)], writes=[pgk2])
                if 0 <= fa_ < 8:
                    ps_n, pk_n = nst_next
                    A("pe", MM(ps_n[:, 0:NT], onesm[:], sqr[fa_ % 2][:], fa_ == 0, fa_ == 7), reads=[("sq", fa_ % 2), "onesm"], writes=[pk_n])
                if f < 8:
                    g_ = gt[f % 2]
                    gk_ = ("gt", f % 2)
                    A("act", ACT(g_[:], psg2[:, 0:NT], AF.Sigmoid), reads=[pgk2], writes=[gk_])
                    psp, ppk = PB.next()
                    for k in range(2):
                        A("pe", MM(psp[:, 0:NT], Wp[:, k, jj * 128:(jj + 1) * 128], pT[:, k, :], k == 0, k == 1), reads=[wk, "pT"], writes=[ppk])
                    A("dve", TT(g_[:], psp[:, 0:NT], g_[:], ALU.mult), reads=[ppk, gk_], writes=[gk_])
                    A("dve", TT(xT[:, f, :], xT[:, f, :], g_[:], ALU.add), reads=[gk_, ("x", f)], writes=[("x", f)])
            return nst_next

        for t in range(n_tiles):
            S_.tag = "xload"
            for f in range(8):
                A("sp", DMA(xT[:, f, :], xin_v[:, f, t * NT:(t + 1) * NT]), writes=[("x", f)], dma_sem=s_xf[f])
            nst = None
            for l in range(n_layers):
                nst = layer_step(t, l, nst)
            S_.tag = "final"
            norm_finish(nst, NL * 24, final_t0=t * NT)

        A("sp", lambda e: e.nop(), reads=["yout"])

        S_.analyze(qsems)
        with nc.Block() as block:
            S_.emit(block)
    return nc


def _pk(v):
    v = np.asarray(v, np.float32)
    return np.ascontiguousarray(v.reshape(-1, 128).T)


def ffn_col_order():
    cols = []
    for i in range(11):
        for j in (2 * i, 2 * i + 1):
            cols.append(np.arange(128 * j, 128 * j + 128))
        for j in (2 * i, 2 * i + 1):
            cols.append(DFF + np.arange(128 * j, 128 * j + 128))
    return np.stack(cols)


def shared_inputs(inp):
    f32 = np.float32
    d = {}
    for nm in ("w_in", "w_out", "w_up", "w_down", "w_ple"):
        d[nm] = np.ascontiguousarray(inp[nm], f32)
    d["w_gate"] = np.ascontiguousarray(inp["w_ple_gate"], f32)
    cols = []
    for i in range(NL):
        cols += [_pk(inp["g_mix"][i]), _pk(inp["g_ffn"][i]), _pk(inp["g_ple"][i])]
    cols.append(_pk(inp["g_final"]))
    d["gv"] = np.ascontiguousarray(np.concatenate(cols, axis=1))
    d["mlng"] = np.ascontiguousarray(np.concatenate([_pk(inp["ml_norm_g"][i]) for i in range(NL)], axis=1))
    lngb = np.stack([np.stack([inp["gm_ln_g"][i], inp["gm_ln_b"][i]]) for i in range(NL)])
    d["lngb"] = np.ascontiguousarray(np.broadcast_to(lngb[:, None], (NL, 128, 2, 512)), f32)
    d["wsT"] = np.ascontiguousarray(np.transpose(np.asarray(inp["gm_ws"], f32), (0, 3, 1, 2)))
    d["bs8"] = np.ascontiguousarray(np.transpose(np.asarray(inp["gm_bs"], f32), (1, 0, 2)))
    cw = np.asarray(inp["ml_conv_w"], f32)
    d["cwqk"] = np.ascontiguousarray(np.transpose(cw.reshape(NL, 4, 8, 128), (3, 0, 2, 1)))
    cb = np.asarray(inp["ml_conv_b"], f32)
    d["cbqk"] = np.ascontiguousarray(np.transpose(cb.reshape(NL, 8, 128), (2, 0, 1)))
    order = ffn_col_order()
    fw = np.asarray(inp["ffn_conv_w"], f32)
    d["cwff"] = np.ascontiguousarray(np.transpose(fw[:, :, order], (3, 0, 2, 1)))
    fb = np.asarray(inp["ffn_conv_b"], f32)
    d["cbff"] = np.ascontiguousarray(np.transpose(fb[:, order], (2, 0, 1)))
    bif = np.concatenate([np.asarray(inp["ml_b_i"], f32), np.asarray(inp["ml_b_f"], f32)], axis=1)
    d["bif"] = np.ascontiguousarray(np.broadcast_to(bif[None], (128, NL, 8)), f32)
    d["ident"] = np.eye(128, dtype=f32)
    d["tri"] = np.triu(np.ones((128, 128), f32))
    sel = np.zeros((128, 512), f32)
    for h in range(8):
        sel[h, h * 64:(h + 1) * 64] = 1.0
    d["sel"] = sel
    return d


_PROG = {}


def get_prog(S, n_layers=NL):
    if (S, n_layers) not in _PROG:
        _PROG[(S, n_layers)] = build_program(S, n_layers)
    return _PROG[(S, n_layers)]


def kernel(**inp):
    x = np.asarray(inp["x"], np.float32)
    p = np.asarray(inp["p"], np.float32)
    B, S, _ = x.shape
    sh = shared_inputs(inp)
    maps = []
    for b in range(B):
        m = dict(sh)
        m["xin"] = np.ascontiguousarray(x[b].T)
        m["pin"] = np.ascontiguousarray(np.transpose(p[:, b], (0, 2, 1)))
        maps.append(m)
    res = run_bass_kernel_spmd(get_prog(S), maps, core_ids=list(range(B)))
    return np.stack([np.asarray(res.results[b]["yout"]).T for b in range(B)]).astype(np.float32)
```

```python
import math
import numpy as np
from contextlib import ExitStack
import concourse.bass as bass
import concourse.mybir as mybir
from concourse.bass_utils import run_bass_kernel_spmd

F32 = mybir.dt.float32
BF16 = mybir.dt.bfloat16
AF = mybir.ActivationFunctionType
ALU = mybir.AluOpType

D = 1024
NT = 512
NCH = NT // 128
DFF = 2816
EPS = 1e-6
NSLOT = 4
WBLK = 4096
SAME_ENGINE_SYNC = True
SAME_ENGINE_ALL = False
SAME_ENGINE_SYNC_Q = {"act", "dve"}
ANNOTATE = False


class Op:
    __slots__ = ("q", "fn", "reads", "writes", "sem", "inc", "needs_inc", "count", "waits", "tag")


class Sched:
    def __init__(self):
        self.ops = []
        self.tag = None

    def add(self, q, fn, reads=(), writes=(), dma_sem=None):
        o = Op()
        o.q, o.fn, o.reads, o.writes = q, fn, tuple(reads), tuple(writes)
        o.sem = dma_sem
        o.inc = 16 if dma_sem is not None else 1
        o.needs_inc = dma_sem is not None
        o.count = None
        o.waits = {}
        o.tag = self.tag
        self.ops.append(o)
        return o

    def analyze(self, qsems):
        last_w, readers, need = {}, {}, []
        for o in self.ops:
            deps = {}
            for r in o.reads:
                w = last_w.get(r)
                if w is not None:
                    deps[id(w)] = (w, True)
            for k in o.writes:
                w = last_w.get(k)
                if w is not None and id(w) not in deps:
                    deps[id(w)] = (w, False)
                for rd in readers.get(k, ()):
                    if id(rd) not in deps:
                        deps[id(rd)] = (rd, False)
            deps.pop(id(o), None)
            nd = []
            for a, raw in deps.values():
                if a.sem is None and a.q == o.q:
                    if o.q == "pe" or not SAME_ENGINE_SYNC or o.q not in SAME_ENGINE_SYNC_Q:
                        continue
                    if not raw and not SAME_ENGINE_ALL:
                        continue
                nd.append(a)
                a.needs_inc = True
            need.append(nd)
            for r in o.reads:
                readers.setdefault(r, []).append(o)
            for k in o.writes:
                last_w[k] = o
                readers[k] = []
        cnt = {}
        for o in self.ops:
            if o.needs_inc:
                s = o.sem if o.sem is not None else qsems[o.q]
                cnt[s] = cnt.get(s, 0) + o.inc
                o.count = (s, cnt[s])
        waited = {}
        for o, nd in zip(self.ops, need):
            w = {}
            for a in nd:
                s, c = a.count
                if c > w.get(s, 0):
                    w[s] = c
            qw = waited.setdefault(o.q, {})
            for s, c in list(w.items()):
                if qw.get(s, 0) >= c:
                    del w[s]
                else:
                    qw[s] = c
            o.waits = w
        self.final_counts = cnt

    def emit(self, block):
        byq = {}
        for o in self.ops:
            byq.setdefault(o.q, []).append(o)

        def run(eng, ops):
            for o in ops:
                for s, c in o.waits.items():
                    eng.wait_ge(s, c)
                ins = o.fn(eng)
                if ANNOTATE and o.tag is not None:
                    ins.annotate(o.tag)
                if o.needs_inc:
                    ins.then_inc(o.count[0], o.inc)

        names = {"pe": "tensor", "act": "scalar", "dve": "vector", "pool": "gpsimd", "sp": "sync"}
        for q in ["sp", "pe", "act", "dve", "pool"]:
            if q in byq:
                getattr(block, names[q])(lambda eng, ops=byq[q]: run(eng, ops))


class Ring:
    def __init__(self, items):
        self.items = items
        self.i = 0

    def next(self):
        it = self.items[self.i % len(self.items)]
        self.i += 1
        return it


NL = 4


def weight_blocks():
    b = []
    for i in range(2):
        b.append([("w_in", 8, 1024 + 512 * i, 512, 0, 512, 0)])
    for c0 in (0, 512, 2048, 2560):
        b.append([("w_in", 8, c0, 512, 0, 512, 0)])
    b.append(None)
    for i in range(2):
        b.append([("w_out", 8, 512 * i, 512, 0, 512, 0)])
    for i in range(11):
        b.append([("w_up", 8, 256 * i, 256, 0, 512, 0), ("w_up", 8, DFF + 256 * i, 256, 0, 512, 256)])
    for f in range(8):
        b.append([("w_down", 22, 128 * f, 128, 0, 128, 0)])
    for j in range(4):
        b.append([("w_gate", 8, 256 * j, 256, 0, 256, 0), ("w_ple", 2, 256 * j, 256, 2048, 256, 0)])
    return b


WB = weight_blocks()
NWB = len(WB)
B_QK, B_U, B_V, B_VM, B_O, B_WS, B_OUT, B_UP, B_DOWN, B_GP = 0, 2, 3, 4, 5, 6, 7, 9, 20, 28
WB_SIZE = []
for _b in WB:
    if _b is None:
        WB_SIZE.append(1024)
    else:
        WB_SIZE.append(max(base + K * W for (_, K, _, _, base, W, _) in _b))


def MM(out, lhsT, rhs, start, stop):
    return lambda e: e.matmul(out, lhsT=lhsT, rhs=rhs, start=start, stop=stop)


def TR(out, in_, ident):
    return lambda e: e.transpose(out, in_, ident)


def ACT(out, in_, func, **kw):
    return lambda e: e.activation(out=out, in_=in_, func=func, **kw)


def TT(out, in0, in1, op):
    return lambda e: e.tensor_tensor(out=out, in0=in0, in1=in1, op=op)


def TS(out, in0, s1, s2, op0, op1=None):
    if op1 is None:
        return lambda e: e.tensor_scalar(out=out, in0=in0, scalar1=s1, scalar2=None, op0=op0)
    return lambda e: e.tensor_scalar(out=out, in0=in0, scalar1=s1, scalar2=s2, op0=op0, op1=op1)


def TSS(out, in_, scalar, op):
    return lambda e: e.tensor_single_scalar(out=out, in_=in_, scalar=scalar, op=op)


def STT(out, in0, scalar, in1, op0, op1):
    return lambda e: e.scalar_tensor_tensor(out=out, in0=in0, scalar=scalar, in1=in1, op0=op0, op1=op1)


def CP(out, in_):
    return lambda e: e.tensor_copy(out=out, in_=in_)


def MS(out, val):
    return lambda e: e.memset(out, val)


def DMA(out, in_):
    return lambda e: e.dma_start(out=out, in_=in_)


def pipeline(n, stages):
    ns = len(stages)
    for i in range(n + ns - 1):
        for si in reversed(range(ns)):
            c = i - si
            if 0 <= c < n:
                stages[si](c)


def build_program(S, n_layers=NL):
    n_tiles = S // NT
    nc = bass.Bass("TRN2", target_bir_lowering=False)

    def din(name, shape):
        return nc.dram_tensor(name, list(shape), F32, kind="ExternalInput").ap()

    xin = din("xin", [D, S])
    pin = din("pin", [NL, 256, S])
    wsrc = {
        "w_in": din("w_in", [NL, D, 3080]), "w_out": din("w_out", [NL, D, D]), "w_up": din("w_up", [NL, D, 2 * DFF]),
        "w_down": din("w_down", [NL, DFF, D]), "w_gate": din("w_gate", [NL, D, D]), "w_ple": din("w_ple", [NL, 256, D]),
    }
    gv_d = din("gv", [128, NL * 24 + 8])
    mlng_d = din("mlng", [128, NL * 4])
    lngb_d = din("lngb", [NL, 128, 2, 512])
    wsT_d = din("wsT", [NL, 128, 8, 128])
    bs8_d = din("bs8", [8, NL, 128])
    cwqk_d = din("cwqk", [128, NL, 8, 4])
    cbqk_d = din("cbqk", [128, NL, 8])
    cwff_d = din("cwff", [128, NL, 44, 3])
    cbff_d = din("cbff", [128, NL, 44])
    bif_d = din("bif", [128, NL, 8])
    ident_d = din("ident", [128, 128])
    tri_d = din("tri", [128, 128])
    sel_d = din("sel", [128, 512])
    yout = nc.dram_tensor("yout", [D, S], F32, kind="ExternalOutput").ap()
    wb_d = nc.dram_tensor("wb_scratch", [NL, NWB, 128, WBLK], BF16).ap()

    xin_v = xin.rearrange("(f p) s -> p f s", p=128)
    yout_v = yout.rearrange("(f p) s -> p f s", p=128)

    with ExitStack() as es:
        def sb(name, shape, dt=F32):
            return es.enter_context(nc.sbuf_tensor("sb_" + name, list(shape), dt))

        def psum(name, shape, dt=F32):
            return es.enter_context(nc.psum_tensor(name, list(shape), dt))

        def sem(name):
            return es.enter_context(nc.semaphore(name))

        qsems = {q: sem("q_" + q) for q in ["pe", "act", "dve", "pool"]}
        S_ = Sched()
        A = S_.add

        xT = sb("xT", [128, 8, NT])
        hT = sb("hT", [128, 8, NT], BF16)
        sqr = [sb(f"sq{i}", [128, NT], BF16) for i in range(2)]
        rstd = sb("rstd", [128, NT])
        ntmp = [sb(f"ntmp{i}", [128, NT]) for i in range(2)]
        qkraw = sb("qkraw", [128, 8, NT + 4])
        qkT = sb("qkT", [128, 8, NT], BF16)
        cacc = [sb(f"cacc{i}", [128, NT]) for i in range(4)]
        gpre = sb("gpre", [128, NCH, 8])
        e1 = sb("e1", [128, NCH, 4])
        l1 = sb("l1", [128, NCH, 4])
        t1 = sb("t1", [128, NCH, 4])
        sc = sb("sc", [128, NCH, 4])
        einv = sb("einv", [128, NCH, 4])
        ebL = sb("ebL", [128, NCH, 4])
        ebLs = sb("ebLs", [128, NL, 4])
        ug = sb("ug", [128, NCH, 512], BF16)
        vg = sb("vg", [128, NCH, 512], BF16)
        vt = [sb(f"vt{i}", [128, 512]) for i in range(2)]
        vn = sb("vn", [128, NCH, 512], BF16)
        st6 = sb("st6", [128, NCH, 6])
        mv = sb("mv", [128, NCH, 2])
        lvar = sb("lvar", [128, NCH])
        lrstd = sb("lrstd", [128, NCH])
        vaug = sb("vaug", [128, NCH, 4, 132], BF16)
        so = sb("so", [128, NCH, 512], BF16)
        gm = [sb(f"gm{i}", [128, 512], BF16) for i in range(2)]
        Amat = [sb(f"Amat{i}", [128, 4, 128], BF16) for i in range(2)]
        Kp = [sb(f"Kp{i}", [128, 4, 128], BF16) for i in range(2)]
        dtmp = [sb(f"dtmp{i}", [128, 4]) for i in range(2)]
        hc = [sb(f"hc{i}", [128, 512]) for i in range(2)]
        junk = sb("junk", [128, 128])
        ss = [sb(f"ss{i}", [128, 4]) for i in range(2)]
        lss = [sb(f"lss{i}", [128, 4]) for i in range(2)]
        hrs = [sb(f"hrs{i}", [128, 4]) for i in range(2)]
        ml = [sb(f"ml{i}", [128, 512], BF16) for i in range(2)]
        Dst = sb("Dst", [128, NL, 4, 132])
        Cbf = sb("Cbf", [128, NL, 4, 132], BF16)
        raw = [sb(f"raw{i}", [128, NT + 2]) for i in range(4)]
        facc = [sb(f"facc{i}", [128, NT]) for i in range(3)]
        sa = [sb(f"sa{i}", [128, NT]) for i in range(2)]
        actT = sb("actT", [128, 22, NT], BF16)
        halo = sb("halo", [128, NL, 44, 2])
        qhalo = sb("qhalo", [128, NL, 8, 4])
        pT = sb("pT", [128, 2, NT], BF16)
        gt = [sb(f"gt{i}", [128, NT]) for i in range(2)]
        wring = sb("wring", [128, NSLOT, WBLK], BF16)
        lngb = sb("lngb", [128, 2, 512])
        ident = sb("ident", [128, 128], BF16)
        tri = sb("tri", [128, 128])
        trib = sb("trib", [128, 128], BF16)
        trib4 = sb("trib4", [128, 4, 128], BF16)
        onesf = sb("onesf", [128, 128])
        onesm = sb("onesm", [128, 128], BF16)
        sel = sb("sel", [128, 512], BF16)
        bs128 = sb("bs128", [128, NL, 128], BF16)
        gv = sb("gv", [128, NL * 24 + 8])
        mlng = sb("mlng", [128, NL * 4])
        cwqk = sb("cwqk", [128, NL, 8, 4])
        cbqk = sb("cbqk", [128, NL, 8])
        cwff = sb("cwff", [128, NL, 44, 3])
        cbff = sb("cbff", [128, NL, 44])
        bif = sb("bif", [128, NL, 8])
        epsc = sb("epsc", [128, 1])
        wg = sb("wg", [128, NL, 8, 8], BF16)

        PBt = [psum(f"pb{i}", [128, 512]) for i in range(4)]
        PTt = [psum(f"pt{i}", [128, 1024], BF16) for i in range(2)]
        PSt = [psum(f"ps{i}", [128, 512]) for i in range(2)]
        PB = Ring([(PBt[i], ("pb", i)) for i in range(4)])
        PT = Ring([(PTt[i], ("pt", i)) for i in range(2)])
        PS = Ring([(PSt[i], ("ps", i)) for i in range(2)])

        s_setup = sem("d_setup")
        s_w = [sem(f"d_w{i}") for i in range(NSLOT)]
        s_xf = [sem(f"d_x{f}") for f in range(8)]
        s_p = sem("d_p")
        s_ln = sem("d_ln")
        s_yo = [sem(f"d_yo{i}") for i in range(2)]
        s_prep = [sem(f"d_prep{i}") for i in range(4)]
        s_ws = sem("d_ws")
        s_wg = sem("d_wg")
        s_ws2 = sem("d_ws2")

        XK = [("x", f) for f in range(8)]
        HT = [("hT", f) for f in range(8)]

        S_.tag = "setup"

        def load_const(dst_ap, src_ap, keys):
            A("sp", DMA(dst_ap, src_ap), writes=keys, dma_sem=s_setup)

        load_const(gv[:], gv_d, ["gv"])
        load_const(mlng[:], mlng_d, ["mlng"])
        load_const(cwqk[:], cwqk_d, ["cwqk"])
        load_const(cbqk[:], cbqk_d, ["cbqk"])
        load_const(cwff[:], cwff_d, ["cwff"])
        load_const(cbff[:], cbff_d, ["cbff"])
        load_const(bif[:], bif_d, ["bif"])
        load_const(tri[:], tri_d, ["tri"])
        load_const(xT[:, 0, :], sel_d, [("x", 0)])
        load_const(xT[:, 1, 0:128], ident_d, [("x", 1)])
        load_const(xT[0:8, 2, :].rearrange("p (l t) -> p l t", l=NL), bs8_d, [("x", 2)])
        setup_keys = ["gv", "mlng", "cwqk", "cbqk", "cwff", "cbff", "bif", "tri"] + XK[0:3]
        A("pool", MS(epsc[:], EPS), reads=setup_keys, writes=["epsc", "setup_done"])
        SD = ["setup_done"]
        A("pool", CP(sel[:], xT[:, 0, :]), reads=SD + [("x", 0)], writes=["sel"])
        A("pool", CP(ident[:], xT[:, 1, 0:128]), reads=SD + [("x", 1)], writes=["ident"])
        A("pool", CP(trib[:], tri[:]), reads=SD, writes=["trib"])
        for h in range(4):
            A("pool", CP(trib4[:, h, :], tri[:]), reads=SD, writes=["trib4"])
        A("pool", MS(bs128[:], 0.0), reads=SD, writes=["bs128"])
        A("pool", CP(bs128[0:8, :, :], xT[0:8, 2, :].rearrange("p (l t) -> p l t", l=NL)), reads=SD + ["bs128", ("x", 2)], writes=["bs128"])
        A("dve", MS(onesf[:], 1.0), writes=["onesf"])
        A("dve", MS(onesm[:], 1.0 / D), writes=["onesm"])
        A("dve", MS(qhalo[:], 0.0), writes=["qhalo"])
        A("dve", MS(halo[:], 0.0), writes=["halo"])
        A("dve", MS(Dst[:], 0.0), writes=[("Dst", l, h) for l in range(NL) for h in range(4)])
        A("dve", MS(Cbf[:], 0.0), writes=[("Cbf", l, h) for l in range(NL) for h in range(4)])
        A("dve", MS(ebLs[:], 1.0), writes=["ebLs"])
        A("pool", MS(vaug[:], 1.0), writes=[("vaug", c) for c in range(NCH)])
        for l in range(n_layers):
            A("pool", DMA(wg[:, l, :, :], wsrc["w_in"][l].rearrange("(k p) n -> p k n", p=128)[:, :, 3072:3080]),
              writes=[("wg", l), "wgslot"], dma_sem=s_wg)
        xflat = xT[:, 4:6, :].rearrange("p a b -> p (a b)")
        hflat = hT[:, 0:2, :].rearrange("p a b -> p (a b)")
        for l in range(n_layers):
            A("sp", DMA(xflat.rearrange("p (h t) -> p h t", h=8), wsT_d[l]), writes=[("x", 4), ("x", 5)], dma_sem=s_ws)
            for h in range(8):
                A("dve", TT(hflat[:, h * 128:(h + 1) * 128], xflat[:, h * 128:(h + 1) * 128], tri[:], ALU.mult),
                  reads=SD + [("x", 4), ("x", 5)], writes=[("hT", 0), ("hT", 1)])
            A("sp", DMA(wb_d[l, B_WS, :, 0:1024], hflat), reads=[("hT", 0), ("hT", 1)], writes=[("wbd", l, B_WS, 0)], dma_sem=s_ws2)

        np_ = [0]
        wbd_keys = {}
        prep_queue = {l: [] for l in range(n_layers)}
        for l in range(n_layers):
            for bi, parts in enumerate(WB):
                if parts is None:
                    wbd_keys[(l, bi)] = [("wbd", l, bi, 0)]
                    continue
                keys = []
                for pi, (src, K, c0, n, base, W, off) in enumerate(parts):
                    srcv = wsrc[src][l].rearrange("(k p) n -> p k n", p=128)[:, :, c0:c0 + n]
                    dstv = wb_d[l, bi, :, base:base + K * W].rearrange("p (k n) -> p k n", k=K)[:, :, off:off + n]
                    key = ("wbd", l, bi, pi)
                    prep_queue[l].append((dstv, srcv, key))
                    keys.append(key)
                wbd_keys[(l, bi)] = keys

        def emit_prep(l, n):
            for _ in range(n):
                if l >= n_layers or not prep_queue[l]:
                    return
                dstv, srcv, key = prep_queue[l].pop(0)
                sl = np_[0] % 4
                np_[0] += 1
                A("pool", DMA(dstv, srcv), writes=[key, ("prepslot", sl)], dma_sem=s_prep[sl])

        emit_prep(0, 10 ** 6)

        gblk = [0]
        total_blocks = n_tiles * n_layers * NWB

        def issue_wload(g):
            if g >= total_blocks:
                return
            bi = g % NWB
            l = (g // NWB) % n_layers
            sl = g % NSLOT
            n = WB_SIZE[bi]
            A("sp", DMA(wring[:, sl, 0:n], wb_d[l, bi, :, 0:n]), reads=wbd_keys[(l, bi)], writes=[("wr", sl)], dma_sem=s_w[sl])

        def next_block(expect_bi):
            g = gblk[0]
            assert g % NWB == expect_bi, (g % NWB, expect_bi)
            issue_wload(g + NSLOT - 1)
            gblk[0] += 1
            sl = g % NSLOT
            return wring[:, sl, :], ("wr", sl)

        def norm_begin():
            return PS.next()

        def norm_accum(nst, f):
            ps, pk = nst
            i = f % 2
            A("act", ACT(sqr[i][:], xT[:, f, :], AF.Square), reads=[("x", f)], writes=[("sq", i)])
            A("pe", MM(ps[:, 0:NT], onesm[:], sqr[i][:], f == 0, f == 7), reads=[("sq", i), "onesm"], writes=[pk])

        def norm_finish(nst, gcol, final_t0=None):
            ps, pk = nst
            A("act", ACT(rstd[:], ps[:, 0:NT], AF.Ln, bias=epsc[:, 0:1]), reads=[pk, "epsc"], writes=["rstd"])
            A("act", ACT(rstd[:], rstd[:], AF.Exp, scale=-0.5), reads=["rstd"], writes=["rstd"])
            for f in range(8):
                if final_t0 is not None:
                    i = f % 2
                    A("dve", STT(ntmp[i][:], xT[:, f, :], gv[:, gcol + f:gcol + f + 1], rstd[:], ALU.mult, ALU.mult),
                      reads=[("x", f), "rstd"] + SD, writes=[("ntmp", i)])
                    A("sp", DMA(yout_v[:, f, final_t0:final_t0 + NT], ntmp[i][:]), reads=[("ntmp", i)], writes=["yout"], dma_sem=s_yo[i])
                else:
                    A("dve", STT(hT[:, f, :], xT[:, f, :], gv[:, gcol + f:gcol + f + 1], rstd[:], ALU.mult, ALU.mult),
                      reads=[("x", f), "rstd"] + SD, writes=[("hT", f)])

        def norm(gcol, final_t0=None):
            nst = norm_begin()
            for f in range(8):
                norm_accum(nst, f)
            norm_finish(nst, gcol, final_t0)

        for g in range(NSLOT - 1):
            issue_wload(g)

        def layer_step(t, l, nst_in):
            t0 = t * NT
            tg = f"L{l}"
            S_.tag = tg + "norm1"
            A("pool", DMA(pT[:], pin[l].rearrange("(k p) s -> p k s", p=128)[:, :, t0:t0 + NT]), writes=["pT"], dma_sem=s_p)
            A("sp", DMA(lngb[:], lngb_d[l]), writes=["lngb"], dma_sem=s_ln)

            if nst_in is None:
                norm(l * 24)
            else:
                norm_finish(nst_in, l * 24)
            S_.tag = tg + "gates"
            psg, pgk = PS.next()
            for c in range(NCH):
                for k in range(8):
                    A("pe", MM(psg[:, c * 8:(c + 1) * 8], hT[:, k, c * 128:(c + 1) * 128], wg[:, l, k, :], k == 0, k == 7),
                      reads=[("hT", k), ("wg", l)], writes=[pgk])
            for c in range(NCH):
                A("dve", TT(gpre[:, c, :], psg[:, c * 8:(c + 1) * 8], bif[:, l, :], ALU.add), reads=[pgk] + SD, writes=["gpre"])
            A("act", ACT(e1[:], gpre[:, :, 4:8], AF.Exp, scale=-1.0), reads=["gpre"], writes=["e1"])
            A("act", ACT(l1[:], e1[:], AF.Ln, bias=1.0), reads=["e1"], writes=["l1"])
            def gates_tail():
                S_.tag = tg + "gates"
                psb, pbk = PS.next()
                l1v = l1[:].rearrange("p c h -> p (c h)")
                A("pe", MM(psb[:, 0:16], tri[:], l1v, True, True), reads=["l1"] + SD, writes=[pbk])
                A("pe", MM(psb[:, 16:32], onesf[:], l1v, True, True), reads=["l1", "onesf"], writes=[pbk])
                bneg = psb[:, 0:16].rearrange("p (c h) -> p c h", c=NCH)
                bLneg = psb[:, 16:32].rearrange("p (c h) -> p c h", c=NCH)
                A("dve", STT(t1[:], gpre[:, :, 0:4], math.log(128.0 ** -0.5), bneg, ALU.add, ALU.add), reads=["gpre", pbk], writes=["t1"])
                A("act", ACT(sc[:], t1[:], AF.Exp), reads=["t1"], writes=["sc"])
                A("act", ACT(einv[:], bneg, AF.Exp), reads=[pbk, "t1"], writes=["einv"])
                A("act", ACT(ebL[:], bLneg, AF.Exp, scale=-1.0), reads=[pbk, "t1"], writes=["ebL"])
                S_.tag = tg + "qk"

            S_.tag = tg + "qk"
            Wq = {}
            qpb = {}

            def qk_s0(f):
                wbi, j = divmod(f, 4)
                if j == 0:
                    W, wk = next_block(B_QK + wbi)
                    Wq[wbi] = (W[:, 0:4096].rearrange("p (k n) -> p k n", k=8), wk)
                Wv, wk = Wq[wbi]
                ps, pk = PB.next()
                qpb[f] = (ps, pk)
                for k in range(8):
                    A("pe", MM(ps[:, 0:NT], Wv[:, k, j * 128:(j + 1) * 128], hT[:, k, :], k == 0, k == 7), reads=[wk, ("hT", k)], writes=[pk])
                qk_ = ("qkraw", f)
                A("pool", CP(qkraw[:, f, 0:3], qhalo[:, l, f, 0:3]), reads=["qhalo"], writes=[qk_])
                A("act", ACT(qkraw[:, f, 3:3 + NT], ps[:, 0:NT], AF.Copy), reads=[pk, qk_], writes=[qk_])
                A("pool", CP(qhalo[:, l, f, 0:3], qkraw[:, f, NT:NT + 3]), reads=[qk_], writes=["qhalo"])

            def qk_s1(f):
                A("act", ACT(cacc[f % 4][:], qkraw[:, f, 0:NT], AF.Identity, scale=cwqk[:, l, f, 0:1], bias=cbqk[:, l, f:f + 1]),
                  reads=[("qkraw", f)] + SD, writes=[("cacc", f % 4)])

            def qk_s2(f):
                ps, pk = qpb.pop(f)
                ca = cacc[f % 4]
                ck = ("cacc", f % 4)
                A("dve", STT(ca[:], ps[:, 0:NT], cwqk[:, l, f, 3:4], ca[:], ALU.mult, ALU.add), reads=[pk, ck] + SD, writes=[ck])

            def qk_tap(tap):
                def fn(f):
                    ca = cacc[f % 4]
                    ck = ("cacc", f % 4)
                    A("dve", STT(ca[:], qkraw[:, f, tap:tap + NT], cwqk[:, l, f, tap:tap + 1], ca[:], ALU.mult, ALU.add), reads=[("qkraw", f), ck] + SD, writes=[ck])
                return fn

            def qk_s5(f):
                A("act", ACT(qkT[:, f, :], cacc[f % 4][:], AF.Silu), reads=[("cacc", f % 4)], writes=[("qkT", f)])

            gates_tail_done = [False]

            def qk_s0_wrap(f):
                qk_s0(f)
                if f == 1 and not gates_tail_done[0]:
                    gates_tail()
                    gates_tail_done[0] = True

            pipeline(8, [qk_s0_wrap, qk_s1, qk_s2, qk_tap(1), qk_tap(2), qk_s5])

            S_.tag = tg + "uvo"
            for which, bidx in (("u", B_U), ("v", B_V), ("vm", B_VM), ("o", B_O)):
                W, wk = next_block(bidx)
                Wv = W[:, 0:4096].rearrange("p (k n) -> p k n", k=8)
                for c in range(NCH):
                    ps, pk = PB.next()
                    for k in range(8):
                        A("pe", MM(ps[:, 0:512], hT[:, k, c * 128:(c + 1) * 128], Wv[:, k, :], k == 0, k == 7), reads=[wk, ("hT", k)], writes=[pk])
                    if which == "u":
                        A("act", ACT(ug[:, c, :], ps[:, 0:512], AF.Gelu), reads=[pk], writes=[("ug", c)])
                    elif which == "v":
                        A("act", ACT(vg[:, c, :], ps[:, 0:512], AF.Gelu), reads=[pk], writes=[("vg", c)])
                        A("dve", lambda e, c=c: e.bn_stats(out=st6[:, c, :], in_=vg[:, c, :]), reads=[("vg", c)], writes=[("st6", c)])
                        A("dve", lambda e, c=c: e.bn_aggr(out=mv[:, c, :], in_=st6[:, c, :]), reads=[("st6", c)], writes=[("mv", c)])
                    elif which == "vm":
                        for h in range(4):
                            A("act", ACT(vaug[:, c, h, 0:128], ps[:, h * 128:(h + 1) * 128], AF.Copy, scale=sc[:, c, h:h + 1]), reads=[pk, ("vaug", c), "sc"], writes=[("vaug", c)])
                        A("pool", CP(vaug[:, c, :, 128], sc[:, c, :]), reads=["sc", ("vaug", c)], writes=[("vaug", c)])
                    else:
                        A("act", ACT(so[:, c, :], ps[:, 0:512], AF.Sigmoid), reads=[pk], writes=[("so", c)])
                if which == "v":
                    A("act", ACT(lvar[:], mv[:, :, 1], AF.Ln, bias=epsc[:, 0:1]), reads=[("mv", c) for c in range(NCH)] + ["epsc"], writes=["lvar"])
                    A("act", ACT(lrstd[:], lvar[:], AF.Exp, scale=-0.5), reads=["lvar"], writes=["lrstd"])
                    for c in range(NCH):
                        v_ = vt[c % 2]
                        vk = ("vt", c % 2)
                        A("dve", TS(v_[:], vg[:, c, :], mv[:, c, 0:1], lrstd[:, c:c + 1], ALU.subtract, ALU.mult),
                          reads=[("vg", c), ("mv", c), "lrstd"], writes=[vk])
                        A("pool", TT(v_[:], v_[:], lngb[:, 0, :], ALU.mult), reads=[vk, "lngb"], writes=[vk])
                        A("pool", TT(vn[:, c, :], v_[:], lngb[:, 1, :], ALU.add), reads=[vk, "lngb"], writes=[("vn", c)])

            Wws, wsk = next_block(B_WS)
            wsv = Wws[:, 0:1024].rearrange("p (h t) -> p h t", h=8)
            stg = {}

            def mix_P1(c):
                S_.tag = tg + "mixP1"
                cs = slice(c * 128, (c + 1) * 128)
                gi = c % 2
                psm, pmk = PB.next()
                A("pe", MM(psm[:, 0:512], bs128[:, l, :], sel[:], True, False), reads=["bs128", "sel"], writes=[pmk])
                for h in range(8):
                    A("pe", MM(psm[:, h * 64:(h + 1) * 64], wsv[:, h, :], vn[:, c, h * 64:(h + 1) * 64], False, h == 7), reads=[wsk, ("vn", c)], writes=[pmk])
                st, stk = PS.next()
                for h in range(4):
                    A("pe", MM(st[:, h * 128:(h + 1) * 128], qkT[:, 4 + h, cs], qkT[:, h, cs], True, True), reads=[("qkT", 4 + h), ("qkT", h)], writes=[stk])
                kt, ktk = PT.next()
                for h in range(4):
                    A("pe", TR(kt[:, h * 128:(h + 1) * 128], qkT[:, 4 + h, cs], ident[:]), reads=[("qkT", 4 + h), "ident"], writes=[ktk])
                A("dve", TT(gm[gi][:], psm[:, 0:512], ug[:, c, :], ALU.mult), reads=[pmk, ("ug", c)], writes=[("gm", gi)])
                A("dve", TT(Amat[gi][:], st[:, 0:512].rearrange("p (h t) -> p h t", h=4), trib4[:], ALU.mult),
                  reads=[stk, "trib4"], writes=[("Amat", gi, h) for h in range(4)])
                A("dve", CP(Kp[gi][:], kt[:, 0:512].rearrange("p (h t) -> p h t", h=4)), reads=[ktk], writes=[("Kp", gi, h) for h in range(4)])

            def mix_P2(c):
                S_.tag = tg + "mixP2"
                cs = slice(c * 128, (c + 1) * 128)
                gi = c % 2
                for pr in range(2):
                    nd, ndk = PB.next()
                    up, upk = PB.next()
                    for j in range(2):
                        h = 2 * pr + j
                        A("pe", MM(nd[:, j * 129:(j + 1) * 129], Amat[gi][:, h, :], vaug[:, c, h, 0:129], True, False), reads=[("Amat", gi, h), ("vaug", c)], writes=[ndk])
                        A("pe", MM(nd[:, j * 129:(j + 1) * 129], qkT[:, h, cs], Cbf[:, l, h, 0:129], False, True), reads=[("qkT", h), ("Cbf", l, h)], writes=[ndk])
                    for j in range(2):
                        h = 2 * pr + j
                        A("pe", MM(up[:, j * 129:(j + 1) * 129], Kp[gi][:, h, :], vaug[:, c, h, 0:129], True, True), reads=[("Kp", gi, h), ("vaug", c)], writes=[upk])
                    for j in range(2):
                        h = 2 * pr + j
                        prev = ebL[:, c - 1, h:h + 1] if c > 0 else ebLs[:, l, h:h + 1]
                        A("dve", STT(Dst[:, l, h, 0:129], Dst[:, l, h, 0:129], prev, up[:, j * 129:(j + 1) * 129], ALU.mult, ALU.add),
                          reads=[upk, ("Dst", l, h), "ebL", "ebLs"], writes=[("Dst", l, h)])
                    d_ = dtmp[gi]
                    dk = ("dtmp", gi, pr)
                    dsl = d_[:, 2 * pr:2 * pr + 2]
                    A("act", ACT(dsl, nd[:, 0:258].rearrange("p (j d) -> p j d", j=2)[:, :, 128], AF.Abs), reads=[ndk], writes=[dk])
                    A("dve", TT(dsl, dsl, einv[:, c, 2 * pr:2 * pr + 2], ALU.max), reads=[dk, "einv"], writes=[dk])
                    A("dve", lambda e, dsl=dsl: e.reciprocal(out=dsl, in_=dsl), reads=[dk], writes=[dk])
                    for j in range(2):
                        h = 2 * pr + j
                        A("dve", STT(hc[gi][:, h * 128:(h + 1) * 128], nd[:, j * 129:j * 129 + 128], d_[:, h:h + 1], so[:, c, h * 128:(h + 1) * 128], ALU.mult, ALU.mult),
                          reads=[ndk, dk, ("so", c)], writes=[("hc", gi, h)])
                A("pool", TT(Cbf[:, l, :, 0:129], Dst[:, l, :, 0:129], ebL[:, c, :].unsqueeze(2).broadcast_to([128, 4, 129]), ALU.mult),
                  reads=[("Dst", l, h) for h in range(4)] + ["ebL"], writes=[("Cbf", l, h) for h in range(4)])
                for h in range(4):
                    A("dve", lambda e, h=h, gi=gi: e.scalar_tensor_tensor(out=junk[:], in0=hc[gi][:, h * 128:(h + 1) * 128], scalar=1.0, in1=hc[gi][:, h * 128:(h + 1) * 128],
                                                                          op0=ALU.mult, op1=ALU.mult, accum_out=ss[gi][:, h:h + 1]),
                      reads=[("hc", gi, h)], writes=["junk", ("ss", gi)])
                A("act", ACT(lss[gi][:], ss[gi][:], AF.Ln, scale=1.0 / 128.0, bias=epsc[:, 0:1]), reads=[("ss", gi), "epsc"], writes=[("lss", gi)])
                A("act", ACT(hrs[gi][:], lss[gi][:], AF.Exp, scale=-0.5), reads=[("lss", gi)], writes=[("hrs", gi)])
                A("pool", TT(ml[gi][:].rearrange("p (h d) -> p h d", h=4), hc[gi][:].rearrange("p (h d) -> p h d", h=4),
                             hrs[gi][:].unsqueeze(2).broadcast_to([128, 4, 128]), ALU.mult),
                  reads=[("hc", gi, h) for h in range(4)] + [("hrs", gi)], writes=[("ml", gi)])

                pt, ptk = PT.next()
                for j in range(4):
                    A("pe", TR(pt[:, j * 128:(j + 1) * 128], gm[gi][:, j * 128:(j + 1) * 128], ident[:]), reads=[("gm", gi), "ident"], writes=[ptk])
                A("act", ACT(hT[:, 0:4, cs], pt[:, 0:512].rearrange("p (a b) -> p a b", a=4), AF.Copy), reads=[ptk], writes=HT[0:4])

            def mix_P3(c):
                S_.tag = tg + "mixP3"
                cs = slice(c * 128, (c + 1) * 128)
                gi = c % 2
                pt, ptk = PT.next()
                for h in range(4):
                    A("pe", TR(pt[:, h * 128:(h + 1) * 128], ml[gi][:, h * 128:(h + 1) * 128], ident[:]), reads=[("ml", gi), "ident"], writes=[ptk])
                for h in range(4):
                    A("act", ACT(hT[:, 4 + h, cs], pt[:, h * 128:(h + 1) * 128], AF.Copy, scale=mlng[:, l * 4 + h:l * 4 + h + 1]), reads=[ptk] + SD, writes=[("hT", 4 + h)])

            mix_P1(0)
            for c in range(NCH):
                if c + 1 < NCH:
                    mix_P1(c + 1)
                mix_P2(c)
                if c >= 1:
                    mix_P3(c - 1)
            mix_P3(NCH - 1)
            A("dve", CP(ebLs[:, l, :], ebL[:, NCH - 1, :]), reads=["ebL", "ebLs"], writes=["ebLs"])

            S_.tag = tg + "wout"
            Wo = {}
            nst2 = norm_begin()

            def wout_s0(f):
                wbi, j = divmod(f, 4)
                if j == 0:
                    W, wk = next_block(B_OUT + wbi)
                    Wo[wbi] = (W[:, 0:4096].rearrange("p (k n) -> p k n", k=8), wk)
                Wv, wk = Wo[wbi]
                ps, pk = PB.next()
                for k in range(8):
                    A("pe", MM(ps[:, 0:NT], Wv[:, k, j * 128:(j + 1) * 128], hT[:, k, :], k == 0, k == 7), reads=[wk, ("hT", k)], writes=[pk])
                A("dve", TT(xT[:, f, :], ps[:, 0:NT], xT[:, f, :], ALU.add), reads=[pk, ("x", f)], writes=[("x", f)])

            pipeline(8, [wout_s0, lambda f: None, lambda f: norm_accum(nst2, f)])

            S_.tag = tg + "norm2"
            norm_finish(nst2, l * 24 + 8)
            S_.tag = tg + "wup"
            Wu = {}
            pbk_ = {}

            def up_pre(fi):
                A("pool", CP(raw[fi % 4][:, 0:2], halo[:, l, fi, :]), reads=["halo"], writes=[("raw", fi % 4)])

            def up_s0(fi):
                i, jj = divmod(fi, 4)
                if jj == 0:
                    W, wk = next_block(B_UP + i)
                    Wu[i] = (W[:, 0:4096].rearrange("p (k n) -> p k n", k=8), wk)
                Wv, wk = Wu[i]
                ps, pk = PB.next()
                pbk_[fi] = (ps, pk)
                for k in range(8):
                    A("pe", MM(ps[:, 0:NT], Wv[:, k, jj * 128:(jj + 1) * 128], hT[:, k, :], k == 0, k == 7), reads=[wk, ("hT", k)], writes=[pk])
                r_ = raw[fi % 4]
                rk_ = ("raw", fi % 4)
                A("act", ACT(r_[:, 2:2 + NT], ps[:, 0:NT], AF.Copy), reads=[pk, rk_], writes=[rk_])
                if t == 0:
                    emit_prep(l + 1, 2)

            def up_s1(fi):
                A("act", ACT(facc[fi % 3][:], raw[fi % 4][:, 0:NT], AF.Identity, scale=cwff[:, l, fi, 0:1], bias=cbff[:, l, fi:fi + 1]),
                  reads=[("raw", fi % 4)] + SD, writes=[("facc", fi % 3)])
                A("pool", CP(halo[:, l, fi, :], raw[fi % 4][:, NT:NT + 2]), reads=[("raw", fi % 4)], writes=["halo"])

            def up_s2(fi):
                ps, pk = pbk_.pop(fi)
                fa = facc[fi % 3]
                fk = ("facc", fi % 3)
                A("dve", STT(fa[:], ps[:, 0:NT], cwff[:, l, fi, 2:3], fa[:], ALU.mult, ALU.add), reads=[pk, fk] + SD, writes=[fk])

            def up_s3(fi):
                fa = facc[fi % 3]
                fk = ("facc", fi % 3)
                A("dve", STT(fa[:], raw[fi % 4][:, 1:1 + NT], cwff[:, l, fi, 1:2], fa[:], ALU.mult, ALU.add), reads=[("raw", fi % 4), fk] + SD, writes=[fk])

            def up_s4(fi):
                i, jj = divmod(fi, 4)
                fa = facc[fi % 3]
                fk = ("facc", fi % 3)
                if jj < 2:
                    A("act", ACT(sa[jj][:], fa[:], AF.Silu), reads=[fk], writes=[("sa", jj)])
                else:
                    A("pool", TT(actT[:, 2 * i + jj - 2, :], sa[jj - 2][:], fa[:], ALU.mult), reads=[("sa", jj - 2), fk], writes=[("actT", 2 * i + jj - 2)])

            pipeline(44, [up_pre, up_s0, up_s1, up_s2, up_s3, up_s4])
            if t == 0:
                emit_prep(l + 1, 10 ** 6)

            S_.tag = tg + "wdown"
            nst3 = norm_begin()

            def wdown_s0(f):
                W, wk = next_block(B_DOWN + f)
                Wv = W[:, 0:22 * 128].rearrange("p (k n) -> p k n", k=22)
                ps, pk = PB.next()
                for k in range(22):
                    A("pe", MM(ps[:, 0:NT], Wv[:, k, :], actT[:, k, :], k == 0, k == 21), reads=[wk, ("actT", k)], writes=[pk])
                A("dve", TT(xT[:, f, :], ps[:, 0:NT], xT[:, f, :], ALU.add), reads=[pk, ("x", f)], writes=[("x", f)])

            pipeline(8, [wdown_s0, lambda f: None, lambda f: norm_accum(nst3, f)])

            S_.tag = tg + "norm3"
            norm_finish(nst3, l * 24 + 16)
            S_.tag = tg + "ple"
            Wgp = {}
            nst_next = norm_begin()

            for it in range(8 + 3):
                f = it
                fa_ = it - 3
                if 0 <= fa_ < 8:
                    A("act", ACT(sqr[fa_ % 2][:], xT[:, fa_, :], AF.Square), reads=[("x", fa_)], writes=[("sq", fa_ % 2)])
                if f < 8:
                    jb, jj = divmod(f, 2)
                    if jj == 0:
                        W, wk = next_block(B_GP + jb)
                        Wgp[jb] = (W[:, 0:2048].rearrange("p (k n) -> p k n", k=8), W[:, 2048:2560].rearrange("p (k n) -> p k n", k=2), wk)
                    Wg, Wp, wk = Wgp[jb]
                    psg2, pgk2 = PB.next()
                    for k in range(8):
                        A("pe", MM(psg2[:, 0:NT], Wg[:, k, jj * 128:(jj + 1) * 128], hT[:, k, :], k == 0, k == 7), reads=[wk, ("hT", k)], writes=[pgk2])
                if 0 <= fa_ < 8:
                    ps_n, pk_n = nst_next
                    A("pe", MM(ps_n[:, 0:NT], onesm[:], sqr[fa_ % 2][:], fa_ == 0, fa_ == 7), reads=[("sq", fa_ % 2), "onesm"], writes=[pk_n])
                if f < 8:
                    g_ = gt[f % 2]
                    gk_ = ("gt", f % 2)
                    A("act", ACT(g_[:], psg2[:, 0:NT], AF.Sigmoid), reads=[pgk2], writes=[gk_])
                    psp, ppk = PB.next()
                    for k in range(2):
                        A("pe", MM(psp[:, 0:NT], Wp[:, k, jj * 128:(jj + 1) * 128], pT[:, k, :], k == 0, k == 1), reads=[wk, "pT"], writes=[ppk])
                    A("dve", TT(g_[:], psp[:, 0:NT], g_[:], ALU.mult), reads=[ppk, gk_], writes=[gk_])
                    A("dve", TT(xT[:, f, :], xT[:, f, :], g_[:], ALU.add), reads=[gk_, ("x", f)], writes=[("x", f)])
            return nst_next

        for t in range(n_tiles):
            S_.tag = "xload"
            for f in range(8):
                A("sp", DMA(xT[:, f, :], xin_v[:, f, t * NT:(t + 1) * NT]), writes=[("x", f)], dma_sem=s_xf[f])
            nst = None
            for l in range(n_layers):
                nst = layer_step(t, l, nst)
            S_.tag = "final"
            norm_finish(nst, NL * 24, final_t0=t * NT)

        A("sp", lambda e: e.nop(), reads=["yout"])

        S_.analyze(qsems)
        with nc.Block() as block:
            S_.emit(block)
    return nc


def _pk(v):
    v = np.asarray(v, np.float32)
    return np.ascontiguousarray(v.reshape(-1, 128).T)


def ffn_col_order():
    cols = []
    for i in range(11):
        for j in (2 * i, 2 * i + 1):
            cols.append(np.arange(128 * j, 128 * j + 128))
        for j in (2 * i, 2 * i + 1):
            cols.append(DFF + np.arange(128 * j, 128 * j + 128))
    return np.stack(cols)


def shared_inputs(inp):
    f32 = np.float32
    d = {}
    for nm in ("w_in", "w_out", "w_up", "w_down", "w_ple"):
        d[nm] = np.ascontiguousarray(inp[nm], f32)
    d["w_gate"] = np.ascontiguousarray(inp["w_ple_gate"], f32)
    cols = []
    for i in range(NL):
        cols += [_pk(inp["g_mix"][i]), _pk(inp["g_ffn"][i]), _pk(inp["g_ple"][i])]
    cols.append(_pk(inp["g_final"]))
    d["gv"] = np.ascontiguousarray(np.concatenate(cols, axis=1))
    d["mlng"] = np.ascontiguousarray(np.concatenate([_pk(inp["ml_norm_g"][i]) for i in range(NL)], axis=1))
    lngb = np.stack([np.stack([inp["gm_ln_g"][i], inp["gm_ln_b"][i]]) for i in range(NL)])
    d["lngb"] = np.ascontiguousarray(np.broadcast_to(lngb[:, None], (NL, 128, 2, 512)), f32)
    d["wsT"] = np.ascontiguousarray(np.transpose(np.asarray(inp["gm_ws"], f32), (0, 3, 1, 2)))
    d["bs8"] = np.ascontiguousarray(np.transpose(np.asarray(inp["gm_bs"], f32), (1, 0, 2)))
    cw = np.asarray(inp["ml_conv_w"], f32)
    d["cwqk"] = np.ascontiguousarray(np.transpose(cw.reshape(NL, 4, 8, 128), (3, 0, 2, 1)))
    cb = np.asarray(inp["ml_conv_b"], f32)
    d["cbqk"] = np.ascontiguousarray(np.transpose(cb.reshape(NL, 8, 128), (2, 0, 1)))
    order = ffn_col_order()
    fw = np.asarray(inp["ffn_conv_w"], f32)
    d["cwff"] = np.ascontiguousarray(np.transpose(fw[:, :, order], (3, 0, 2, 1)))
    fb = np.asarray(inp["ffn_conv_b"], f32)
    d["cbff"] = np.ascontiguousarray(np.transpose(fb[:, order], (2, 0, 1)))
    bif = np.concatenate([np.asarray(inp["ml_b_i"], f32), np.asarray(inp["ml_b_f"], f32)], axis=1)
    d["bif"] = np.ascontiguousarray(np.broadcast_to(bif[None], (128, NL, 8)), f32)
    d["ident"] = np.eye(128, dtype=f32)
    d["tri"] = np.triu(np.ones((128, 128), f32))
    sel = np.zeros((128, 512), f32)
    for h in range(8):
        sel[h, h * 64:(h + 1) * 64] = 1.0
    d["sel"] = sel
    return d


_PROG = {}


def get_prog(S, n_layers=NL):
    if (S, n_layers) not in _PROG:
        _PROG[(S, n_layers)] = build_program(S, n_layers)
    return _PROG[(S, n_layers)]


def kernel(**inp):
    x = np.asarray(inp["x"], np.float32)
    p = np.asarray(inp["p"], np.float32)
    B, S, _ = x.shape
    sh = shared_inputs(inp)
    maps = []
    for b in range(B):
        m = dict(sh)
        m["xin"] = np.ascontiguousarray(x[b].T)
        m["pin"] = np.ascontiguousarray(np.transpose(p[:, b], (0, 2, 1)))
        maps.append(m)
    res = run_bass_kernel_spmd(get_prog(S), maps, core_ids=list(range(B)))
    return np.stack([np.asarray(res.results[b]["yout"]).T for b in range(B)]).astype(np.float32)
```
